# Optimizing a Trainium2 kernel written in Bass

```python
import jax, jax.numpy as jnp
from jax import lax
import numpy as np

D_MODEL = 1024
BATCH = 8
SEQ = 2048
DEPTH = 1
DEC_BATCH = 128
DEC_SEQ = 1
PAST_LEN = 16384
PAGE_SIZE = 128

N_META = 16
CHUNK = 128
ML_HEADS = 4
ML_DK = D_MODEL // ML_HEADS
ML_DV = D_MODEL // ML_HEADS
RT_HEADS = 4
RT_DK = D_MODEL // RT_HEADS
RT_DV = D_MODEL // RT_HEADS
ML_W = ML_HEADS * ML_DV
RT_W = RT_HEADS * RT_DV
D_FF = ((-(-8 * D_MODEL // 3) + 255) // 256) * 256
ROPE_BASE = 10000.0
ALPHA = (2 * DEPTH) ** 0.25
BETA = (8 * DEPTH) ** -0.25
LN_EPS = 1e-5
SPLIT_SIZES = (ML_HEADS * ML_DK, ML_HEADS * ML_DK, ML_W, ML_W, ML_HEADS, ML_HEADS,
               RT_HEADS * RT_DK, RT_HEADS * RT_DK, RT_W, RT_W, D_MODEL, D_MODEL)
D_IN = sum(SPLIT_SIZES)

kernel_name = "mlstm_retention_gated_hybrid_step"


def _starts():
    return [int(s) for s in np.cumsum((0,) + SPLIT_SIZES)]


def layer_norm(x, g, b):
    xf = x.astype(jnp.float32)
    mu = jnp.mean(xf, axis=-1, keepdims=True)
    var = jnp.mean(jnp.square(xf - mu), axis=-1, keepdims=True)
    return ((xf - mu) * lax.rsqrt(var + LN_EPS) * g.astype(jnp.float32) + b.astype(jnp.float32)).astype(x.dtype)


def to_heads(t, n_heads):
    b_, l_, _ = t.shape
    return t.reshape(b_, l_, n_heads, -1).transpose(0, 2, 1, 3)


def head_norm_merge(h, g):
    n_heads, d = h.shape[1], h.shape[3]
    mu = jnp.mean(h, axis=-1, keepdims=True)
    var = jnp.mean(jnp.square(h - mu), axis=-1, keepdims=True)
    h = (h - mu) * lax.rsqrt(var + LN_EPS) * g.astype(jnp.float32).reshape(n_heads, 1, d)
    b_, _, l_, _ = h.shape
    return h.transpose(0, 2, 1, 3).reshape(b_, l_, n_heads * d)


def rotary(t, pos):
    d = t.shape[-1]
    inv = ROPE_BASE ** (-jnp.arange(0, d, 2, dtype=jnp.float32) / d)
    ang = pos[:, None] * inv[None, :]
    cos, sin = jnp.cos(ang), jnp.sin(ang)
    t1, t2 = t[..., : d // 2], t[..., d // 2:]
    return jnp.concatenate([t1 * cos - t2 * sin, t1 * sin + t2 * cos], axis=-1)


def mlstm_step(carry, xs):
    C, n, m = carry
    q, k, v, logi, logf = xs
    L = q.shape[2]
    b = jnp.cumsum(logf, axis=-1)
    causal = jnp.tril(jnp.ones((L, L), dtype=bool))
    dmat = jnp.where(causal, b[..., :, None] - b[..., None, :] + logi[..., None, :], -jnp.inf)
    inter = b + m[..., None]
    m_t = jnp.maximum(inter, jnp.max(dmat, axis=-1))
    w_intra = jnp.exp(dmat - m_t[..., None])
    w_inter = jnp.exp(inter - m_t)
    s = jnp.einsum('bhtd,bhsd->bhts', q, k) * w_intra
    num = (w_inter[..., None] * jnp.einsum('bhtd,bhde->bhte', q, C)
           + jnp.einsum('bhts,bhse->bhte', s, v))
    den = w_inter * jnp.einsum('bhtd,bhd->bht', q, n) + jnp.sum(s, axis=-1)
    h = num / jnp.maximum(jnp.abs(den), jnp.exp(-m_t))[..., None]
    b_last = b[..., -1]
    g_s = b_last[..., None] - b + logi
    m_new = jnp.maximum(b_last + m, jnp.max(g_s, axis=-1))
    w_s = jnp.exp(g_s - m_new[..., None])
    w_c = jnp.exp(b_last + m - m_new)
    C_new = w_c[..., None, None] * C + jnp.einsum('bhs,bhsd,bhse->bhde', w_s, k, v)
    n_new = w_c[..., None] * n + jnp.einsum('bhs,bhsd->bhd', w_s, k)
    return (C_new, n_new, m_new), h


def retention_step(S, xs):
    q, k, v = xs
    L = q.shape[2]
    lg = jnp.log1p(-jnp.exp2(-5.0 - jnp.arange(RT_HEADS, dtype=jnp.float32)))
    t = jnp.arange(L, dtype=jnp.float32)
    causal = t[:, None] >= t[None, :]
    decay = jnp.where(causal[None], jnp.exp((t[:, None] - t[None, :])[None] * lg[:, None, None]), 0.0)
    scores = jnp.einsum('bhtd,bhsd->bhts', q, k) * decay[None]
    o = (jnp.einsum('bhts,bhse->bhte', scores, v)
         + jnp.exp((t + 1.0)[None, :] * lg[:, None])[None, :, :, None] * jnp.einsum('bhtd,bhde->bhte', q, S))
    w = jnp.exp((L - 1.0 - t)[None, :] * lg[:, None])
    S_new = jnp.exp(L * lg)[None, :, None, None] * S + jnp.einsum('hs,bhsd,bhse->bhde', w, k, v)
    return S_new, o


def run_prompt(step, carry, xs):
    lead = jax.tree_util.tree_map(lambda a: a[:, :, :N_META], xs)
    carry, h_lead = step(carry, lead)

    def to_chunks(a):
        b_, h_, l_ = a.shape[:3]
        a = a[:, :, N_META:].reshape(b_, h_, (l_ - N_META) // CHUNK, CHUNK, *a.shape[3:])
        return jnp.moveaxis(a, 2, 0)

    carry, h_rest = lax.scan(step, carry, jax.tree_util.tree_map(to_chunks, xs))
    h_rest = jnp.moveaxis(h_rest, 0, 2)
    h_rest = h_rest.reshape(h_rest.shape[0], h_rest.shape[1], -1, h_rest.shape[-1])
    return carry, jnp.concatenate([h_lead, h_rest], axis=2)


def run_single(step, carry, xs):
    return step(carry, xs)


def token_mix(x, pos, run, ml_state, rt_state, w_in, b_in, ml_norm_g, rt_norm_g, w_out):
    f32 = jnp.float32
    proj = x @ w_in + b_in
    st = _starts()
    (mq, mk, mv, mo, mi, mf, rq, rk, rv, rg, ga, gb) = jnp.split(proj, st[1:-1], axis=-1)
    q = to_heads(mq, ML_HEADS).astype(f32)
    k = to_heads(mk, ML_HEADS).astype(f32) * (ML_DK ** -0.5)
    v = to_heads(mv, ML_HEADS).astype(f32)
    logi = mi.astype(f32).transpose(0, 2, 1)
    logf = jax.nn.log_sigmoid(mf.astype(f32)).transpose(0, 2, 1)
    ml_state, h_a = run(mlstm_step, ml_state, (q, k, v, logi, logf))
    h_a = head_norm_merge(h_a, ml_norm_g) * jax.nn.sigmoid(mo.astype(f32))
    rq_h = rotary(to_heads(rq, RT_HEADS).astype(f32), pos)
    rk_h = rotary(to_heads(rk, RT_HEADS).astype(f32), pos) * (RT_DK ** -0.5)
    rv_h = to_heads(rv, RT_HEADS).astype(f32)
    rt_state, h_b = run(retention_step, rt_state, (rq_h, rk_h, rv_h))
    h_b = head_norm_merge(h_b, rt_norm_g) * jax.nn.silu(rg.astype(f32))
    merged = jax.nn.sigmoid(ga.astype(f32)) * h_a + jax.nn.sigmoid(gb.astype(f32)) * h_b
    return merged.astype(x.dtype) @ w_out, ml_state, rt_state


def swiglu(x, w_gate, w_up, w_down):
    return (jax.nn.silu(x @ w_gate) * (x @ w_up)) @ w_down


def trunk(x, pos, run, ml_states, rt_states, w_in, b_in, ml_norm_g, rt_norm_g, w_out,
          ln1_g, ln1_b, w_gate, w_up, w_down, ln2_g, ln2_b):
    out_ml, out_rt = [], []
    for l in range(DEPTH):
        mix, ml_s, rt_s = token_mix(x, pos, run, ml_states[l], rt_states[l], w_in[l], b_in[l],
                                    ml_norm_g[l], rt_norm_g[l], w_out[l])
        x = layer_norm(ALPHA * x + mix, ln1_g[l], ln1_b[l])
        x = layer_norm(ALPHA * x + swiglu(x, w_gate[l], w_up[l], w_down[l]), ln2_g[l], ln2_b[l])
        out_ml.append(ml_s)
        out_rt.append(rt_s)
    C = jnp.stack([s[0] for s in out_ml])
    n = jnp.stack([s[1] for s in out_ml])
    m = jnp.stack([s[2] for s in out_ml])
    S = jnp.stack(out_rt)
    return x, C, n, m, S


def setup_inputs(seed: int = 0) -> dict:
    key = jax.random.key(seed)
    ks = jax.random.split(key, 24)
    f32 = jnp.float32
    st = _starts()
    col_scale = jnp.ones((D_IN,), f32)
    col_scale = col_scale.at[st[2]:st[3]].set(BETA).at[st[8]:st[9]].set(BETA)
    w_in = jax.random.normal(ks[0], (DEPTH, D_MODEL, D_IN), f32) * (D_MODEL ** -0.5) * col_scale
    b_in = jax.random.normal(ks[1], (DEPTH, D_IN), f32) * 0.02
    f_bias = jnp.linspace(3.0, 6.0, ML_HEADS, dtype=f32) + 0.1 * jax.random.normal(ks[2], (DEPTH, ML_HEADS), f32)
    b_in = b_in.at[:, st[5]:st[6]].set(f_bias)
    return {
        "x_prompt": jax.random.normal(ks[3], (BATCH, SEQ, D_MODEL), f32),
        "x_sample": jax.random.normal(ks[4], (DEC_BATCH, DEC_SEQ, D_MODEL), f32),
        "state_mlstm_C": jax.random.normal(ks[5], (DEPTH, DEC_BATCH, ML_HEADS, ML_DK, ML_DV), f32) * 0.1,
        "state_mlstm_n": jax.random.normal(ks[6], (DEPTH, DEC_BATCH, ML_HEADS, ML_DK), f32) * 0.1,
        "state_mlstm_m": jax.random.normal(ks[7], (DEPTH, DEC_BATCH, ML_HEADS), f32),
        "state_ret_S": jax.random.normal(ks[8], (DEPTH, DEC_BATCH, RT_HEADS, RT_DK, RT_DV), f32) * 0.1,
        "meta_tokens": jax.random.normal(ks[9], (N_META, D_MODEL), f32),
        "w_in": w_in,
        "b_in": b_in,
        "ml_norm_g": 1.0 + 0.02 * jax.random.normal(ks[10], (DEPTH, ML_W), f32),
        "rt_norm_g": 1.0 + 0.02 * jax.random.normal(ks[11], (DEPTH, RT_W), f32),
        "w_out": jax.random.normal(ks[12], (DEPTH, D_MODEL, D_MODEL), f32) * (D_MODEL ** -0.5) * BETA,
        "ln1_g": 1.0 + 0.02 * jax.random.normal(ks[13], (DEPTH, D_MODEL), f32),
        "ln1_b": 0.02 * jax.random.normal(ks[14], (DEPTH, D_MODEL), f32),
        "w_gate": jax.random.normal(ks[15], (DEPTH, D_MODEL, D_FF), f32) * (D_MODEL ** -0.5) * BETA,
        "w_up": jax.random.normal(ks[16], (DEPTH, D_MODEL, D_FF), f32) * (D_MODEL ** -0.5) * BETA,
        "w_down": jax.random.normal(ks[17], (DEPTH, D_FF, D_MODEL), f32) * (D_FF ** -0.5) * BETA,
        "ln2_g": 1.0 + 0.02 * jax.random.normal(ks[18], (DEPTH, D_MODEL), f32),
        "ln2_b": 0.02 * jax.random.normal(ks[19], (DEPTH, D_MODEL), f32),
    }


def reference(x_prompt, x_sample, state_mlstm_C, state_mlstm_n, state_mlstm_m, state_ret_S,
              meta_tokens, w_in, b_in, ml_norm_g, rt_norm_g, w_out,
              ln1_g, ln1_b, w_gate, w_up, w_down, ln2_g, ln2_b):
    f32 = jnp.float32
    meta = jnp.broadcast_to(meta_tokens.astype(x_prompt.dtype)[None], (BATCH, N_META, D_MODEL))
    xp = jnp.concatenate([meta, x_prompt], axis=1)
    pos_p = jnp.arange(N_META + SEQ, dtype=f32)
    ml0 = [(jnp.zeros((BATCH, ML_HEADS, ML_DK, ML_DV), f32), jnp.zeros((BATCH, ML_HEADS, ML_DK), f32),
            jnp.zeros((BATCH, ML_HEADS), f32)) for _ in range(DEPTH)]
    rt0 = [jnp.zeros((BATCH, RT_HEADS, RT_DK, RT_DV), f32) for _ in range(DEPTH)]
    yp, p_C, p_n, p_m, p_S = trunk(xp, pos_p, run_prompt, ml0, rt0, w_in, b_in, ml_norm_g, rt_norm_g, w_out,
                                   ln1_g, ln1_b, w_gate, w_up, w_down, ln2_g, ln2_b)
    y_prompt = yp[:, N_META:]
    pos_s = PAST_LEN + jnp.arange(DEC_SEQ, dtype=f32)
    mls = [(state_mlstm_C[l].astype(f32), state_mlstm_n[l].astype(f32), state_mlstm_m[l].astype(f32))
           for l in range(DEPTH)]
    rts = [state_ret_S[l].astype(f32) for l in range(DEPTH)]
    y_sample, s_C, s_n, s_m, s_S = trunk(x_sample, pos_s, run_single, mls, rts, w_in, b_in, ml_norm_g, rt_norm_g,
                                         w_out, ln1_g, ln1_b, w_gate, w_up, w_down, ln2_g, ln2_b)
    return (y_prompt, y_sample, p_C, p_n, p_m, p_S, s_C, s_n, s_m, s_S)
```

```python
from contextlib import ExitStack

import numpy as np
import concourse.bass as bass
import concourse.mybir as mybir
from concourse.bass_utils import run_bass_kernel_spmd

F32 = mybir.dt.float32
BF16 = mybir.dt.bfloat16
AF = mybir.ActivationFunctionType
ALU = mybir.AluOpType
AX = mybir.AxisListType

NCORES = 8
D = 1024
SEQ = 2048
NMETA = 16
NS = 16
SP0 = 32
DIN = 10248
DFF = 2816
NFF = DFF // 128
LN_EPS = 1e-5
ALPHA = 2.0 ** 0.25
PAST = 16384
C_MI = 4096

K_ES128, K_ES16, K_WC128, K_WC16, K_EPS128, K_EPS16, K_G1, K_EPS1, K_ONE = (0, 4, 8, 12, 16, 20, 24, 28, 32)
NCST = 40
DEBUG = False


class Buf:
    __slots__ = ("name", "w", "r", "sem", "cnt")

    def __init__(self, name):
        self.name = name
        self.w = None
        self.r = {}
        self.sem = None
        self.cnt = 0


class FW:
    ENG = ("pe", "act", "dve", "pool", "sp")

    def __init__(self, nc, stack):
        self.nc = nc
        self.stack = stack
        self.sem = {e: stack.enter_context(nc.semaphore("s_" + e)) for e in self.ENG}
        self.n = {e: 0 for e in self.ENG}
        self.known = {e: {} for e in self.ENG}
        self.rec = {e: [] for e in self.ENG}
        self.nsem = len(self.ENG)
        self.out_toks = []

    def _waits(self, E, reads, writes):
        need = {}

        def add(tok):
            if tok is None:
                return
            s, v = tok
            if need.get(s, 0) < v:
                need[s] = v
        for b in reads:
            add(b.w)
        for b in writes:
            add(b.w)
            for t in b.r.values():
                add(t)
        out = []
        kn = self.known[E]
        for s, v in need.items():
            if E == "pe" and s is self.sem["pe"]:
                continue
            if kn.get(s, 0) < v:
                kn[s] = v
                out.append((s, v))
        return out

    def op(self, E, fn, reads=(), writes=(), extra=()):
        waits = self._waits(E, reads, list(writes) + list(extra))
        self.n[E] += 1
        sem = self.sem[E]
        tok = (sem, self.n[E])
        self.rec[E].append((waits, fn, sem, 1))
        for b in reads:
            b.r[E] = tok
        for b in writes:
            b.w = tok
            b.r = {}
        return tok

    def dma(self, Q, fn, sb, reads=(), writes=(), nowait=False, extra=()):
        waits = [] if nowait else self._waits(Q, reads, list(writes) + list(extra))
        if sb.sem is None:
            sb.sem = self.stack.enter_context(self.nc.semaphore("d_" + sb.name))
            self.nsem += 1
        sb.cnt += 16
        tok = (sb.sem, sb.cnt)
        self.rec[Q].append((waits, fn, sb.sem, 16))
        for b in reads:
            b.r["dma_" + sb.name] = tok
        for b in writes:
            b.w = tok
            b.r = {}
        return tok

    def emit(self):
        nc = self.nc
        need = {}
        for s, v in self.out_toks:
            if need.get(s, 0) < v:
                need[s] = v
        self.rec["sp"].append((list(need.items()), None, None, 0))
        with nc.Block() as block:
            def run(eng, lst):
                for waits, fn, sem, inc in lst:
                    for s, v in waits:
                        eng.wait_ge(s, v)
                    if fn is not None:
                        fn(eng).then_inc(sem, inc)

            @block.tensor
            def _(e):
                run(e, self.rec["pe"])

            @block.scalar
            def _(e):
                run(e, self.rec["act"])

            @block.vector
            def _(e):
                run(e, self.rec["dve"])

            @block.gpsimd
            def _(e):
                run(e, self.rec["pool"])

            @block.sync
            def _(e):
                run(e, self.rec["sp"])


class Tl:
    def __init__(self, t, name):
        self.t = t
        self.name = name
        self._b = {}
        self.coarse = False

    def b(self, key=0):
        if self.coarse:
            key = 0
        if key not in self._b:
            self._b[key] = Buf("%s_%s" % (self.name, key))
        return self._b[key]

    def bs(self, keys):
        return [self.b(k) for k in keys]


class TokSet:
    pass


class Prog:
    def __init__(self):
        self.nc = bass.Bass("TRN2", target_bir_lowering=False)
        self.st = ExitStack()
        self.dram = {}
        self.xbi = 0
        self.pending_ln = []
        self.ps_reserved = set()
        self.mini_down = None
        self.mgi = 0
        self.tbi = 0
        self.psi = 0
        self.sbytes = 0

    def din(self, name, shape):
        self.dram[name] = self.nc.dram_tensor(name, list(shape), F32, kind="ExternalInput").ap()

    def dout(self, name, shape):
        self.dram[name] = self.nc.dram_tensor(name, list(shape), F32, kind="ExternalOutput").ap()

    def sb(self, name, shape, dt):
        n = 1
        for x in shape[1:]:
            n *= x
        self.sbytes += n * (4 if dt == F32 else 2)
        return Tl(self.st.enter_context(self.nc.sbuf_tensor("sb_" + name, list(shape), dt)), name)

    def load(self, Q, tl, key, out_ap, in_ap, nowait=False):
        return self.fw.dma(Q, lambda e: e.dma_start(out=out_ap, in_=in_ap), tl.b(key),
                           writes=[tl.b(key)], nowait=nowait)

    def store(self, Q, tl, key, out_ap, in_ap, reads=None, semkey=None):
        tok = self.fw.dma(Q, lambda e: e.dma_start(out=out_ap, in_=in_ap), tl.b(key if semkey is None else semkey),
                          reads=[tl.b(key)] if reads is None else reads)
        self.fw.out_toks.append(tok)
        return tok

    def build(self):
        nc = self.nc
        with self.st:
            self.fw = FW(nc, self.st)
            self.declare()
            self.alloc()
            self.program()
            self.fw.emit()
        return nc

    def declare(self):
        d = self.din
        d("xp", (SEQ, D)); d("meta", (NMETA, D)); d("xs", (NS, D))
        d("sC", (NS, 4, 256, 256)); d("sn", (NS, 1024)); d("sm", (NS, 4)); d("sS", (NS, 4, 256, 256))
        d("w_in", (D, DIN)); d("b_in", (1, DIN))
        d("ml_g", (1, D)); d("rt_g", (1, D)); d("w_out", (D, D))
        d("ln1_g", (1, D)); d("ln1_b", (1, D))
        d("w_gate", (D, DFF)); d("w_up", (D, DFF)); d("w_down", (DFF, D))
        d("ln2_g", (1, D)); d("ln2_b", (1, D))
        d("ident", (128, 128)); d("maskT", (128, 128)); d("cst", (128, NCST))
        d("ropeC", (128, SEQ)); d("ropeS", (128, SEQ))
        d("ropeCm", (128, 64)); d("ropeSm", (128, 64))
        d("identrow", (128, 1024))
        o = self.dout
        o("yp", (SEQ, D)); o("ys", (NS, D))
        if DEBUG:
            o("dbg_mrg", (64, D)); o("dbg_x1", (64, D))
        o("pC", (4, 256, 256)); o("pn", (4, 256)); o("pm", (4, 1)); o("pS", (4, 256, 256))
        o("oC", (NS, 4, 256, 256)); o("on", (NS, 1024)); o("om", (NS, 4)); o("oS", (NS, 4, 256, 256))

    def mkset(self, name, T, tsz):
        s = TokSet()
        s.name, s.T, s.tsz, s.ntile = name, T, tsz, T // tsz
        sb = self.sb
        s.alias = (tsz == 128)
        s.xT = sb(name + "xT", [128, 8, T], BF16)
        s.fmbig = sb(name + "fm", [128, 32, T], BF16)
        s.xres = sb(name + "xres", [tsz, s.ntile, 1024], F32)
        s.G = [sb(name + "G%d" % i, [tsz, s.ntile, 1024], BF16) for i in range(2)]
        s.gcol = sb(name + "gcol", [tsz, s.ntile, 8], F32)
        if s.alias:
            s.xT.coarse = True
            mv_ = s.xT.t[:, :, :].rearrange("p k t -> p (k t)").rearrange("p (i n) -> p i n", i=s.ntile)
            s.merged = Tl(mv_, s.xT.name)
            s.merged._b = s.xT._b
            s.merged.coarse = True
        else:
            s.merged = sb(name + "mrg", [tsz, s.ntile, 1024], BF16)
        s.rc = sb(name + "rc", [128, T], F32)
        s.rs = sb(name + "rs", [128, T], F32)
        return s

    def fm(self, s, which, chunk):
        return s.fmbig.t[:, 8 * which + chunk, :]

    def vview(self, s, mix):
        nt = s.ntile
        half = s.xres.t[:, :, :].rearrange("p a n -> p (a n)").bitcast(BF16)
        return half[:, mix * nt * 1024:(mix + 1) * nt * 1024].rearrange("p (a n) -> p a n", a=nt)

    def alloc(self):
        nc, st, sb = self.nc, self.st, self.sb
        self.ps = [Tl(st.enter_context(nc.psum_tensor("ps%d" % i, [128, 512], F32)), "ps%d" % i)
                   for i in range(8)]
        self.MAIN = self.mkset("M", 512, 128)
        self.MINI = self.mkset("E", 64, 64)
        self.NWB = 3
        self.wbuf = [sb("wb%d" % i, [128, 9 * 512], BF16) for i in range(self.NWB)]
        self.gw = sb("gw", [128, 9, 8], BF16)
        self.ident_b = sb("ident_b", [128, 128], BF16)
        self.ident_f = sb("ident_f", [128, 128], F32)
        self.maskT = sb("maskT", [128, 128], F32)
        self.cst = sb("cst", [128, NCST], F32)
        self.ones_b = sb("ones_b", [128, 512], BF16)
        self.ones_f = sb("ones_f", [128, 128], F32)
        self.gbc = [sb("gbc%d" % i, [128, 1024], BF16) for i in range(2)]
        self.lnp = [sb("lnp%d" % i, [128, 1024], BF16) for i in range(4)]
        self.xbf = [sb("xbf%d" % i, [128, 1024], BF16) for i in range(2)]
        self.tmpb = [sb("tmpb%d" % i, [128, 512], BF16) for i in range(2)]
        self.rot = sb("rot", [128, 4, 256], F32)
        self.Cf = sb("Cf", [128, 8, 2, 257], F32)
        self.Cb = sb("Cb", [128, 8, 2, 257], BF16)
        self.ktm = [sb("ktm%d" % i, [128, 4, 256], BF16) for i in range(2)]
        self.vp = [sb("vp%d" % i, [128, 4, 257], BF16) for i in range(2)]
        self.stm = [sb("stm%d" % i, [128, 4, 128], BF16) for i in range(2)]
        self.u = sb("u", [128, 8, 256], BF16)
        self.st6 = sb("st6", [128, 8, 6], F32)
        self.mv = sb("mv", [128, 8, 2], F32)
        self.den = sb("den", [128, 4], F32)
        self.sm = sb("smalls", [128, 64], F32)
        self.tmpa = sb("tmpa", [128, 256], F32)
        self.gm = sb("gm", [128, 4, 40], F32)
        self.gsm = sb("gsm", [4, 96], F32)
        self.gbcst = sb("gbcst", [128, 5, 8], F32)
        self.wcb = sb("wcb", [128, 5, 4], F32)
        self.mcur = sb("mcur", [4, 1], F32)
        self.lst = sb("lst", [128, 2, 6], F32)
        self.lmv = sb("lmv", [128, 4], F32)
        self.NCIN = 3
        self.cin = [sb("cin%d" % i, [128, 2, 256], F32) for i in range(self.NCIN)]
        self.cin_owner = {}
        for i, wb in enumerate(self.wbuf):
            for q in range(4):
                v_ = wb.t[:, q * 1024:(q + 1) * 1024].bitcast(F32).rearrange("p (j e) -> p j e", j=2)
                tl = Tl(v_, "cs%d_%d" % (i, q))
                self.cin.append(tl)
                self.cin_owner[tl.name] = wb
        wb = self.wbuf[-1]
        for _ in range(2):
            tl = self.cin.pop()
            del self.cin_owner[tl.name]
        self.qm2 = Tl(wb.t[:, 3 * 1024:3 * 1024 + 512].rearrange("p (k n) -> p k n", k=8), "qm2")
        self.vm2 = Tl(wb.t[0:64, 2 * 1024:3 * 1024], "vm2")
        self.borrowed2 = [self.qm2, self.vm2]
        self.cbf = [sb("cbf%d" % i, [128, 2, 256], BF16) for i in range(2)]
        self.qm = [sb("qm0", [128, 8, 64], BF16)] * 2
        self.vmk = [sb("vmk0", [64, 1024], BF16)] * 2
        self.identrow = sb("identrow", [128, 16, 64], BF16)
        self.s_num = sb("s_num", [64, 256], F32)
        self.s_tm = [sb("s_tm%d" % i, [64, 1024], BF16) for i in range(3)]
        self.s_tm = [self.s_tm[0], self.s_tm[1], self.s_tm[0], self.s_tm[2]]
        self.s_n = Tl(self.rot.t[0:64, :, :].rearrange("p a n -> p (a n)"), "rot")
        self.s_n._b = self.rot._b
        self.s_n.coarse = True
        self.rot.coarse = True
        self.s_sm = sb("s_sm", [64, 64], F32)
        self.s_wb = sb("s_wb", [128, 128], F32)
        self.s_dg = sb("s_dg", [64, 16, 8], F32)

    def nextps(self):
        while (self.psi % 8) in self.ps_reserved:
            self.psi += 1
        p = self.ps[self.psi % 8]
        self.psi += 1
        return p

    def wspec_list(self, npass):
        L = []
        for p in range(npass):
            for n_, c0 in enumerate(list(range(0, 4096, 512)) + list(range(4104, DIN, 512))):
                L.append(("in", c0))
                if p == 1 and n_ < 6:
                    L.append(("down", n_ * 512))
            for c0 in (0, 512):
                L.append(("out", c0))
            for c0 in range(0, DFF, 512):
                L.append(("gate", c0))
                L.append(("up", c0))
            for r0 in range(0, DFF, 512):
                L.append(("down", r0))
        return L

    def wload(self, idx):
        kind, c0 = self.wspecs[idx]
        tl = self.wbuf[idx % self.NWB]
        d = self.dram
        uid = (kind, c0)
        def regions(t2):
            if kind == "in":
                return [t2[:, 0:8 * 512], t2[0:1, 8 * 512:9 * 512]]
            if kind == "out":
                return [t2[:, 0:8 * 512]]
            if kind in ("gate", "up"):
                n = min(512, DFF - c0)
                return [t2[:, 0:8 * 512].rearrange("p (k n) -> p k n", k=8)[:, :, 0:n]]
            nch = min(4, (DFF - c0) // 128)
            return [t2[:, 0:nch * 1024]]
        if uid in self.scr_slot:
            slot = self.scr_slot[uid]
            for n_, (o_, i_) in enumerate(zip(regions(tl.t), regions(self.wscr[slot]))):
                self.fw.dma("pool", lambda e, o_=o_, i_=i_: e.dma_start(out=o_, in_=i_), tl.b(0),
                            reads=[self.scr_buf[slot]], writes=[tl.b(0)], nowait=(n_ > 0))
            return
        self._wload_cast(idx)
        slot = len(self.scr_slot)
        self.scr_slot[uid] = slot
        self.scr_buf[slot] = Buf("scr%d" % slot)
        for n_, (o_, i_) in enumerate(zip(regions(self.wscr[slot]), regions(tl.t))):
            self.fw.dma("sp", lambda e, o_=o_, i_=i_: e.dma_start(out=o_, in_=i_), tl.b("st"),
                        reads=[tl.b(0)], writes=[self.scr_buf[slot]], nowait=(n_ > 0))

    def _wload_cast(self, idx):
        kind, c0 = self.wspecs[idx]
        tl = self.wbuf[idx % self.NWB]
        d = self.dram
        t3 = tl.t[:, 0:8 * 512].rearrange("p (k n) -> p k n", k=8)
        if kind == "in":
            self.load("pool", tl, 0, t3, d["w_in"][:, c0:c0 + 512].rearrange("(k p) n -> p k n", p=128))
            self.load("pool", tl, 0, tl.t[0:1, 8 * 512:9 * 512], d["b_in"][:, c0:c0 + 512], nowait=True)
        elif kind == "out":
            self.load("pool", tl, 0, t3, d["w_out"][:, c0:c0 + 512].rearrange("(k p) n -> p k n", p=128))
        elif kind in ("gate", "up"):
            w = d["w_gate"] if kind == "gate" else d["w_up"]
            n = min(512, DFF - c0)
            self.load("pool", tl, 0, t3[:, :, 0:n], w[:, c0:c0 + n].rearrange("(k p) n -> p k n", p=128))
        else:
            nch = min(4, (DFF - c0) // 128)
            t4 = tl.t[:, 0:4 * 1024].rearrange("p (c n) -> p c n", c=4)
            self.load("pool", tl, 0, t4[:, 0:nch, :],
                      d["w_down"][c0:c0 + nch * 128, :].rearrange("(c p) n -> p c n", p=128))

    def wnext(self, kind, c0, hold=0):
        i = self.wi
        assert self.wspecs[i] == (kind, c0), (self.wspecs[i], kind, c0)
        lim = min(len(self.wspecs), i + self.NWB - hold)
        if self.no_prefetch_beyond is not None:
            lim = min(lim, self.no_prefetch_beyond + 1)
        while self.wloaded < lim:
            self.wload(self.wloaded)
            self.wloaded += 1
        self.wi += 1
        return self.wbuf[i % self.NWB]

    def load_consts(self):
        d = self.dram
        fw = self.fw
        self.load("sp", self.ident_f, 0, self.ident_f.t[:], d["ident"])
        self.load("pool", self.ident_b, 0, self.ident_b.t[:], d["ident"])
        self.load("sp", self.maskT, 0, self.maskT.t[:], d["maskT"])
        self.load("sp", self.cst, 0, self.cst.t[:], d["cst"])
        self.load("pool", self.identrow, 0, self.identrow.t[:].rearrange("p a b -> p (a b)"), d["identrow"])
        fw.op("dve", lambda e: e.memset(self.ones_b.t[:], 1.0), writes=[self.ones_b.b()])
        fw.op("dve", lambda e: e.memset(self.ones_f.t[:], 1.0), writes=[self.ones_f.b()])
        fw.op("dve", lambda e: e.memset(self.Cf.t[:], 0.0), writes=self.Cf.bs(range(8)))
        fw.op("dve", lambda e: e.memset(self.mcur.t[:], 0.0), writes=[self.mcur.b()])

    def load_consts_late(self):
        d = self.dram
        for tl, nm in ((self.gbc[0], "ml_g"), (self.gbc[1], "rt_g"), (self.lnp[0], "ln1_g"),
                       (self.lnp[1], "ln1_b"), (self.lnp[2], "ln2_g"), (self.lnp[3], "ln2_b")):
            self.load("pool", tl, 0, tl.t[:], d[nm].partition_broadcast(128))

    def transpose_in(self, s, src_bufs, src_ap_fn, dst_ap, dst_bufs):
        tsz = s.tsz
        ps = self.nextps()
        pv = ps.t[:].bitcast(BF16).rearrange("p (k n) -> p k n", k=8)

        def tr(e):
            for k in range(8):
                ins = e.transpose(out=pv[:, k, 0:tsz], in_=src_ap_fn(k), identity=self.ident_b.t[0:tsz, 0:tsz])
            return ins
        self.fw.op("pe", tr, reads=list(src_bufs) + [self.ident_b.b()], writes=[ps.b()])
        self.fw.op("act", lambda e: e.copy(out=dst_ap, in_=pv[:, :, 0:tsz]), reads=[ps.b()], writes=list(dst_bufs))

    def load_x_main(self, p):
        s = self.MAIN
        d = self.dram
        for i in range(4):
            xb = self.xbf[self.xbi % 2]
            self.xbi += 1
            r0 = p * 512 + i * 128
            self.load("pool", xb, 0, xb.t[:, :], d["xp"][r0:r0 + 128, :])
            self.transpose_in(s, [xb.b()], lambda k, xb=xb: xb.t[:, k * 128:(k + 1) * 128],
                              s.xT.t[:, :, i * 128:(i + 1) * 128], [s.xT.b(i)])
        self.load("sp", s.rc, 0, s.rc.t[:], d["ropeC"][:, p * 512:(p + 1) * 512])
        self.load("sp", s.rs, 0, s.rs.t[:], d["ropeS"][:, p * 512:(p + 1) * 512])

    def load_x_mini(self):
        s = self.MINI
        d = self.dram
        xb = self.vmk[0]
        self.fw.op("dve", lambda e: e.memset(xb.t[:], 0.0), writes=[xb.b()])
        self.load("pool", xb, 0, xb.t[0:NMETA, :], d["meta"])
        self.load("pool", xb, 0, xb.t[SP0:SP0 + NS, :], d["xs"], nowait=True)
        self.transpose_in(s, [xb.b()], lambda k: xb.t[:, k * 128:(k + 1) * 128], s.xT.t[:, :, :], [s.xT.b(0)])
        self.load("sp", s.rc, 0, s.rc.t[:], d["ropeCm"])
        self.load("sp", s.rs, 0, s.rs.t[:], d["ropeSm"])

    def proj_gates(self, s):
        fw = self.fw
        gw = self.gw
        for i in range(s.ntile):
            ps = self.nextps()

            def mm(e, i=i, ps=ps):
                for k in range(8):
                    e.matmul(ps.t[0:s.tsz, 0:8], lhsT=s.xT.t[:, k, i * s.tsz:(i + 1) * s.tsz],
                             rhs=gw.t[:, k, :], start=(k == 0), stop=False)
                return e.matmul(ps.t[0:s.tsz, 0:8], lhsT=self.ones_b.t[0:1, 0:s.tsz],
                                rhs=gw.t[0:1, 8, :], start=False, stop=True)
            fw.op("pe", mm, reads=[s.xT.b(i), gw.b(), self.ones_b.b()], writes=[ps.b()])
            fw.op("act", lambda e, i=i, ps=ps: e.copy(out=s.gcol.t[:, i, :], in_=ps.t[0:s.tsz, 0:8]),
                  reads=[ps.b()], writes=[s.gcol.b()])

    def proj_fm(self, s, wt, cbase, which, scale, rot):
        fw = self.fw
        T = s.T
        w3 = wt.t[:, 0:8 * 512].rearrange("p (k n) -> p k n", k=8)
        brow = wt.t[0:1, 8 * 512:9 * 512]
        xr = s.xT.bs(range(s.ntile))
        pss = []
        for m in range(4):
            ps = self.nextps()
            pss.append(ps)

            def mm(e, m=m, ps=ps):
                for k in range(8):
                    e.matmul(ps.t[:, 0:T], lhsT=w3[:, k, m * 128:(m + 1) * 128], rhs=s.xT.t[:, k, :],
                             start=(k == 0), stop=False)
                return e.matmul(ps.t[:, 0:T], lhsT=brow[:, m * 128:(m + 1) * 128], rhs=self.ones_b.t[0:1, 0:T],
                                start=False, stop=True)
            fw.op("pe", mm, reads=xr + [wt.b(), self.ones_b.b()], writes=[ps.b()])
            if not rot:
                fw.op("act", lambda e, m=m, ps=ps: e.activation(out=self.fm(s, which, cbase + m), in_=ps.t[:, 0:T],
                                                               func=AF.Copy, scale=scale),
                      reads=[ps.b()], writes=[s.fmbig.b(8 * which + cbase + m)])
            elif m % 2 == 1:
                p1, p2 = pss[m - 1], ps
                c1, c2 = cbase + m - 1, cbase + m
                r = self.rot
                rd = [s.rc.b(), s.rs.b()]
                for q0 in range(0, T, 256):
                    n = min(256, T - q0)
                    cos, sin = s.rc.t[:, q0:q0 + n], s.rs.t[:, q0:q0 + n]
                    a1, a2 = p1.t[:, q0:q0 + n], p2.t[:, q0:q0 + n]

                    def stt(o, i0, i1):
                        return lambda e: e.scalar_tensor_tensor(out=o, in0=i0, scalar=scale, in1=i1,
                                                                op0=ALU.mult, op1=ALU.mult)
                    fw.op("dve", stt(r.t[:, 0, 0:n], a1, cos), reads=[p1.b()] + rd, writes=[r.b(0)])
                    fw.op("dve", stt(r.t[:, 1, 0:n], a2, sin), reads=[p2.b()] + rd, writes=[r.b(1)])
                    fw.op("dve", stt(r.t[:, 2, 0:n], a1, sin), reads=[p1.b()] + rd, writes=[r.b(2)])
                    fw.op("dve", stt(r.t[:, 3, 0:n], a2, cos), reads=[p2.b()] + rd, writes=[r.b(3)])
                    o1 = self.fm(s, which, c1)[:, q0:q0 + n]
                    o2 = self.fm(s, which, c2)[:, q0:q0 + n]
                    fw.op("dve", lambda e, o1=o1, n=n: e.tensor_tensor(out=o1, in0=r.t[:, 0, 0:n], in1=r.t[:, 1, 0:n],
                                                                      op=ALU.subtract),
                          reads=[r.b(0), r.b(1)], writes=[s.fmbig.b(8 * which + c1)])
                    fw.op("dve", lambda e, o2=o2, n=n: e.tensor_tensor(out=o2, in0=r.t[:, 2, 0:n], in1=r.t[:, 3, 0:n],
                                                                      op=ALU.add),
                          reads=[r.b(2), r.b(3)], writes=[s.fmbig.b(8 * which + c2)])

    def proj_tm(self, s, wt, kind, half):
        fw = self.fw
        w3 = wt.t[:, 0:8 * 512].rearrange("p (k n) -> p k n", k=8)
        brow = wt.t[0:1, 8 * 512:9 * 512]
        cs = slice(half * 512, (half + 1) * 512)
        tsz = s.tsz
        for i in range(s.ntile):
            ps = self.nextps()

            def mm(e, i=i, ps=ps):
                for k in range(8):
                    e.matmul(ps.t[0:tsz, :], lhsT=s.xT.t[:, k, i * tsz:(i + 1) * tsz], rhs=w3[:, k, :],
                             start=(k == 0), stop=False)
                return e.matmul(ps.t[0:tsz, :], lhsT=self.ones_b.t[0:1, 0:tsz], rhs=brow, start=False, stop=True)
            fw.op("pe", mm, reads=[s.xT.b(i), wt.b(), self.ones_b.b()], writes=[ps.b()])
            pin = ps.t[0:tsz, :]
            if kind in ("mv", "rv"):
                dst = self.vview(s, 0 if kind == "mv" else 1)
                fw.op("act", lambda e, dst=dst, i=i, pin=pin: e.copy(out=dst[:, i, cs], in_=pin),
                      reads=[ps.b()], writes=s.xres.bs(range(s.ntile)))
            elif kind in ("mo", "rg"):
                G = s.G[0 if kind == "mo" else 1]
                gb = self.gbc[0 if kind == "mo" else 1]
                tb = self.tmpb[self.tbi % 2]
                self.tbi += 1
                fn = AF.Sigmoid if kind == "mo" else AF.Silu
                fw.op("act", lambda e, tb=tb, pin=pin, fn=fn: e.activation(out=tb.t[0:tsz, :], in_=pin, func=fn),
                      reads=[ps.b()], writes=[tb.b()])
                fw.op("dve", lambda e, tb=tb, G=G, i=i, gb=gb: e.tensor_tensor(
                    out=G.t[:, i, cs], in0=tb.t[0:tsz, :], in1=gb.t[0:tsz, cs], op=ALU.mult),
                    reads=[tb.b(), gb.b()], writes=[G.b((i, half))])
            else:
                G = s.G[0 if kind == "ga" else 1]
                tb = self.tmpb[self.tbi % 2]
                self.tbi += 1
                fw.op("act", lambda e, tb=tb, pin=pin: e.activation(out=tb.t[0:tsz, :], in_=pin, func=AF.Sigmoid),
                      reads=[ps.b()], writes=[tb.b()])
                fw.op("dve", lambda e, tb=tb, G=G, i=i: e.tensor_tensor(
                    out=G.t[:, i, cs], in0=tb.t[0:tsz, :], in1=G.t[:, i, cs], op=ALU.mult),
                    reads=[tb.b(), G.b((i, half))], writes=[G.b((i, half))])

    def projection(self, sets):
        d = self.dram
        gw = self.gw
        if not self.gw_loaded:
            self.load("pool", gw, 0, gw.t[:, 0:8, :], d["w_in"][:, C_MI:C_MI + 8].rearrange("(k p) n -> p k n", p=128))
            self.load("pool", gw, 0, gw.t[0:1, 8, :], d["b_in"][:, C_MI:C_MI + 8], nowait=True)
            self.gw_loaded = True
        for s in sets:
            self.proj_gates(s)
        plan = [(0, "fm", 0, 0, 1.0, False), (512, "fm", 0, 4, 1.0, False),
                (1024, "fm", 1, 0, 1.0 / 16, False), (1536, "fm", 1, 4, 1.0 / 16, False),
                (2048, "tm", "mv", 0), (2560, "tm", "mv", 1), (3072, "tm", "mo", 0), (3584, "tm", "mo", 1),
                (4104, "fm", 2, 0, 1.0, True), (4616, "fm", 2, 4, 1.0, True),
                (5128, "fm", 3, 0, 1.0 / 16, True), (5640, "fm", 3, 4, 1.0 / 16, True),
                (6152, "tm", "rv", 0), (6664, "tm", "rv", 1), (7176, "tm", "rg", 0), (7688, "tm", "rg", 1),
                (8200, "tm", "ga", 0), (8712, "tm", "ga", 1), (9224, "tm", "gb", 0), (9736, "tm", "gb", 1)]
        md_banks = [[self.ps[6], self.ps[7]]]
        if self.mini_down:
            self.ps_reserved = {6, 7}
        for n_ent, ent in enumerate(plan):
            if self.pending_ln:
                self.pending_ln.pop(0)()
            if self.mini_down and n_ent == 6:
                self.down(self.mini_down, banks=md_banks, blocks="finish")
                self.mini_down = None
                self.ps_reserved = set()
            wt = self.wnext("in", ent[0])
            for s in sets:
                if ent[1] == "fm":
                    self.proj_fm(s, wt, ent[3], ent[2], ent[4], ent[5])
                else:
                    self.proj_tm(s, wt, ent[2], ent[3])
            if ent[0] == 1536:
                for s in sets:
                    if s is not self.MAIN:
                        for _ in self.gate_math(s):
                            pass
                gens = [self.gate_math(self.MAIN)]
            if ent[0] in (1536, 2048, 2560):
                for g_ in gens:
                    next(g_, None)
            if self.mini_down and n_ent < 6:
                self.down(self.mini_down, banks=md_banks, blocks=n_ent * 512)

    def gate_math(self, s):
        fw = self.fw
        main = s is self.MAIN
        L = 128 if main else NMETA
        nt = s.ntile
        gm, gcol = self.gm, s.gcol
        one = self.cst.t[0:L, K_ONE:K_ONE + 1]
        slot0 = 0 if main else 4
        o = s.gofs = (24 if main else 32)
        fw.op("act", lambda e: e.activation(out=gm.t[0:L, 0:nt, 20:24], in_=gcol.t[0:L, :, 4:8], func=AF.Exp, scale=-1.0),
              reads=[gcol.b()], writes=[gm.b()])
        fw.op("act", lambda e: e.activation(out=gm.t[0:L, 0:nt, 0:4], in_=gm.t[0:L, 0:nt, 20:24], func=AF.Ln,
                                            bias=one, scale=1.0),
              reads=[gm.b(), self.cst.b()], writes=[gm.b()])
        fw.op("dve", lambda e: e.tensor_scalar(out=gm.t[0:L, 0:nt, 0:4], in0=gm.t[0:L, 0:nt, 0:4], scalar1=-1.0,
                                               scalar2=None, op0=ALU.mult),
              reads=[gm.b()], writes=[gm.b()])
        gsm = self.gsm
        for i in range(nt):
            ps = self.nextps()

            def mmc(e, i=i, ps=ps):
                e.matmul(ps.t[0:L, 0:4], lhsT=self.maskT.t[0:L, 0:L], rhs=gm.t[0:L, i, 0:4], start=True, stop=True)
                return e.matmul(ps.t[0:4, 8:9], lhsT=gm.t[0:L, i, 0:4], rhs=self.ones_f.t[0:L, 0:1], start=True, stop=True)
            fw.op("pe", mmc, reads=[self.maskT.b(), gm.b(), self.ones_f.b()], writes=[ps.b()])
            fw.op("dve", lambda e, i=i, ps=ps: e.tensor_copy(out=gm.t[0:L, i, 4:8], in_=ps.t[0:L, 0:4]),
                  reads=[ps.b()], writes=[gm.b()])
            fw.op("dve", lambda e, i=i, ps=ps: e.tensor_copy(out=gsm.t[:, 4 + i:5 + i], in_=ps.t[0:4, 8:9]),
                  reads=[ps.b()], writes=[gsm.b()])
        fw.op("dve", lambda e: e.tensor_tensor(out=gm.t[0:L, 0:nt, 8:12], in0=gcol.t[0:L, :, 0:4],
                                               in1=gm.t[0:L, 0:nt, 4:8], op=ALU.subtract),
              reads=[gm.b(), gcol.b()], writes=[gm.b()])
        yield
        ps = self.nextps()

        def tr(e, ps=ps):
            for i in range(nt):
                ins = e.transpose(out=ps.t[0:4, i * L:(i + 1) * L], in_=gm.t[0:L, i, 8:12],
                                  identity=self.ident_f.t[0:L, 0:L])
            return ins
        fw.op("pe", tr, reads=[gm.b(), self.ident_f.b()], writes=[ps.b()])
        fw.op("dve", lambda e, ps=ps: e.tensor_reduce(out=gsm.t[:, 0:nt],
                                                      in_=ps.t[0:4, 0:nt * L].rearrange("p (c l) -> p c l", l=L),
                                                      axis=AX.X, op=ALU.max),
              reads=[ps.b()], writes=[gsm.b()])
        for c in range(nt):
            fw.op("dve", lambda e, c=c: e.tensor_copy(out=gsm.t[:, 9 + 2 * c:10 + 2 * c], in_=self.mcur.t[:]),
                  reads=[self.mcur.b(), gsm.b()], writes=[gsm.b()])
            fw.op("dve", lambda e, c=c: e.tensor_tensor(out=gsm.t[:, 8 + 2 * c:9 + 2 * c], in0=self.mcur.t[:],
                                                        in1=gsm.t[:, c:c + 1], op=ALU.max),
                  reads=[self.mcur.b(), gsm.b()], writes=[gsm.b()])
            fw.op("dve", lambda e, c=c: e.tensor_tensor(out=self.mcur.t[:], in0=gsm.t[:, 8 + 2 * c:9 + 2 * c],
                                                        in1=gsm.t[:, 4 + c:5 + c], op=ALU.add),
                  reads=[gsm.b(), self.mcur.b()], writes=[self.mcur.b()])
        dg = gsm.t[:, 24:24 + 8 * nt].rearrange("p (c h) -> p c h", h=4)
        fw.op("dve", lambda e: e.tensor_tensor(
            out=dg, in0=self.ident_f.t[0:4, 0:4].unsqueeze(1).to_broadcast([4, 2 * nt, 4]),
            in1=gsm.t[:, 8:8 + 2 * nt].unsqueeze(2).to_broadcast([4, 2 * nt, 4]), op=ALU.mult),
            reads=[gsm.b(), self.ident_f.b()], writes=[gsm.b()])
        yield
        ps = self.nextps()
        fw.op("pe", lambda e, ps=ps: e.matmul(ps.t[:, 0:8 * nt], lhsT=self.ones_f.t[0:4, :],
                                            rhs=gsm.t[:, 24:24 + 8 * nt], start=True, stop=True),
              reads=[gsm.b(), self.ones_f.b()], writes=[ps.b()])
        gb = self.gbcst
        fw.op("dve", lambda e, ps=ps: e.tensor_copy(out=gb.t[:, slot0:slot0 + nt, :],
                                                   in_=ps.t[:, 0:8 * nt].rearrange("p (c k) -> p c k", k=8)),
              reads=[ps.b()], writes=[gb.b()])
        wcb = self.wcb
        fw.op("dve", lambda e: e.tensor_tensor(out=wcb.t[:, slot0:slot0 + nt, :], in0=gb.t[:, slot0:slot0 + nt, 4:8],
                                               in1=gb.t[:, slot0:slot0 + nt, 0:4], op=ALU.subtract),
              reads=[gb.b(), wcb.b()], writes=[wcb.b()])
        fw.op("act", lambda e: e.activation(out=wcb.t[:, slot0:slot0 + nt, :], in_=wcb.t[:, slot0:slot0 + nt, :],
                                            func=AF.Exp),
              reads=[wcb.b()], writes=[wcb.b()])
        fw.op("dve", lambda e: e.tensor_tensor(out=gm.t[0:L, 0:nt, 12:16], in0=gm.t[0:L, 0:nt, 8:12],
                                               in1=gb.t[0:L, slot0:slot0 + nt, 0:4], op=ALU.subtract),
              reads=[gm.b(), gb.b()], writes=[gm.b()])
        fw.op("dve", lambda e: e.tensor_tensor(out=gm.t[0:L, 0:nt, 16:20], in0=gm.t[0:L, 0:nt, 4:8],
                                               in1=gb.t[0:L, slot0:slot0 + nt, 0:4], op=ALU.add),
              reads=[gm.b(), gb.b()], writes=[gm.b()])
        fw.op("act", lambda e: e.activation(out=gm.t[0:L, 0:nt, o:o + 4], in_=gm.t[0:L, 0:nt, 12:16], func=AF.Exp),
              reads=[gm.b()], writes=[gm.b()])
        fw.op("act", lambda e: e.activation(out=gm.t[0:L, 0:nt, o + 4:o + 8], in_=gm.t[0:L, 0:nt, 16:20], func=AF.Exp,
                                            scale=-1.0),
              reads=[gm.b()], writes=[gm.b()])

    def rstd(self, out_ap, in_ap, tl):
        fw = self.fw
        fw.op("act", lambda e: e.activation(out=out_ap, in_=in_ap, func=AF.Ln), reads=[tl.b()], writes=[tl.b()])
        fw.op("act", lambda e: e.activation(out=out_ap, in_=out_ap, func=AF.Exp, scale=-0.5),
              reads=[tl.b()], writes=[tl.b()])

    def gate_aps(self, s, mix, h, c, L):
        cst = self.cst
        if mix == 0:
            slot0 = 0 if s is self.MAIN else 4
            return (self.gm.t[0:L, c, s.gofs + h:s.gofs + h + 1], self.wcb.t[:, slot0 + c, h:h + 1],
                    [self.gm.b(), self.wcb.b()])
        ke, kw = (K_ES128, K_WC128) if L == 128 else (K_ES16, K_WC16)
        return cst.t[0:L, ke + h:ke + h + 1], cst.t[:, kw + h:kw + h + 1], [cst.b()]

    def make_cb(self, s, hm, c, L):
        fw = self.fw
        Cb, Cf = self.Cb, self.Cf
        _, wc, rd_g = self.gate_aps(s, hm // 4, hm % 4, c, L)
        fw.op("act", lambda e: e.activation(out=Cb.t[:, hm, :, :], in_=Cf.t[:, hm, :, :], func=AF.Copy, scale=wc),
              reads=[Cf.b(hm)] + rd_g, writes=[Cb.b(hm)])

    def mixers(self, s):
        fw = self.fw
        main = s is self.MAIN
        L = 128 if main else NMETA
        nt = s.ntile
        cst = self.cst
        Cb, Cf = self.Cb, self.Cf
        st6, mv, u = self.st6, self.mv, self.u
        for hm in range(8):
            self.make_cb(s, hm, 0, L)

        def prologue(c, g):
            t0 = c * L
            gi = self.mgi % 2
            self.mgi += 1
            P = TokSet()
            P.hms = hms = [2 * g, 4 + 2 * g, 2 * g + 1, 4 + 2 * g + 1]
            pT, pS = self.ps[4 + gi], self.ps[6 + gi]
            P.ktm, P.vp, P.stm = ktm, vp, stm = self.ktm[gi], self.vp[gi], self.stm[gi]
            pv = pT.t[:].bitcast(BF16)
            P.qa, P.qb = qa, qb = {}, {}
            ka, kb = {}, {}
            for hm in hms:
                mix, h = hm // 4, hm % 4
                qa[hm] = [self.fm(s, 2 * mix, 2 * h + j)[:, t0:t0 + L] for j in range(2)]
                ka[hm] = [self.fm(s, 2 * mix + 1, 2 * h + j)[:, t0:t0 + L] for j in range(2)]
                qb[hm] = s.fmbig.bs([16 * mix + 2 * h, 16 * mix + 2 * h + 1])
                kb[hm] = s.fmbig.bs([16 * mix + 8 + 2 * h, 16 * mix + 8 + 2 * h + 1])
            allq = [b for hm in hms for b in qb[hm]]
            allk = [b for hm in hms for b in kb[hm]]

            def trk(e):
                for k_, hm in enumerate(hms):
                    for j in range(2):
                        ins = e.transpose(out=pv[0:L, k_ * 256 + j * 128:k_ * 256 + (j + 1) * 128], in_=ka[hm][j],
                                          identity=self.ident_b.t[:])
                return ins
            fw.op("pe", trk, reads=allk + [self.ident_b.b()], writes=[pT.b()])

            def mms(e):
                for k_, hm in enumerate(hms):
                    for j in range(2):
                        ins = e.matmul(pS.t[0:L, k_ * 128:k_ * 128 + L], lhsT=ka[hm][j], rhs=qa[hm][j],
                                       start=(j == 0), stop=(j == 1))
                return ins
            fw.op("pe", mms, reads=allk + allq, writes=[pS.b()])
            fw.op("act", lambda e: e.copy(out=ktm.t[0:L, :, :], in_=pv[0:L, :].rearrange("p (k n) -> p k n", k=4)),
                  reads=[pT.b()], writes=[ktm.b()])
            fw.op("dve", lambda e: e.tensor_tensor(
                out=stm.t[0:L, :, 0:L], in0=pS.t[0:L, :].rearrange("p (k n) -> p k n", k=4)[:, :, 0:L],
                in1=self.maskT.t[0:L, 0:L].unsqueeze(1).to_broadcast([L, 4, L]), op=ALU.mult),
                reads=[pS.b(), self.maskT.b()], writes=[stm.b()])
            for k_, hm in enumerate(hms):
                mix, h = hm // 4, hm % 4
                es, wc, rd_g = self.gate_aps(s, mix, h, c, L)
                vsrc = self.vview(s, mix)
                fw.op("act", lambda e, vsrc=vsrc, es=es, h=h, k_=k_: e.activation(
                    out=vp.t[0:L, k_, 0:256], in_=vsrc[0:L, c, h * 256:(h + 1) * 256], func=AF.Copy, scale=es),
                    reads=[s.xres.b(c)] + rd_g, writes=[vp.b()])
                fw.op("act", lambda e, es=es, k_=k_: e.copy(out=vp.t[0:L, k_, 256:257], in_=es),
                      reads=rd_g + [vp.b()], writes=[vp.b()])
            return P

        def body(c, g, P):
            deferred = []
            ktm, vp, stm = P.ktm, P.vp, P.stm
            for k_, hm in enumerate(P.hms):
                mix, h = hm // 4, hm % 4
                es, wc, rd_g = self.gate_aps(s, mix, h, c, L)
                pA = self.ps[(k_ % 2) * 2]
                pB = self.ps[(k_ % 2) * 2 + 1]
                qa = P.qa[hm]

                def mmn(e, pA=pA, qa=qa, hm=hm, k_=k_):
                    for j in range(2):
                        e.matmul(pA.t[0:L, 128:385], lhsT=qa[j], rhs=Cb.t[:, hm, j, :], start=(j == 0), stop=False)
                    return e.matmul(pA.t[0:L, 128:385], lhsT=stm.t[0:L, k_, 0:L], rhs=vp.t[0:L, k_, :],
                                    start=False, stop=True)
                fw.op("pe", mmn, reads=P.qb[hm] + [Cb.b(hm), stm.b(), vp.b()], writes=[pA.b()])

                def mmp(e, pB=pB, pA=pA, k_=k_):
                    for j in range(2):
                        e.matmul(pB.t[:, j * 256:(j + 1) * 256], lhsT=ktm.t[0:L, k_, j * 128:(j + 1) * 128],
                                 rhs=vp.t[0:L, k_, 0:256], start=True, stop=True)
                    for j in range(2):
                        ins = e.matmul(pA.t[:, 400 + j:401 + j], lhsT=ktm.t[0:L, k_, j * 128:(j + 1) * 128],
                                       rhs=vp.t[0:L, k_, 256:257], start=True, stop=True)
                    return ins
                fw.op("pe", mmp, reads=[ktm.b(), vp.b()], writes=[pB.b(), pA.b()])
                fw.op("dve", lambda e, pA=pA, hm=hm: e.bn_stats(out=st6.t[0:L, hm, :], in_=pA.t[0:L, 128:384]),
                      reads=[pA.b()], writes=[st6.b(hm)])
                fw.op("dve", lambda e, hm=hm: e.bn_aggr(out=mv.t[0:L, hm, :], in_=st6.t[0:L, hm, :]),
                      reads=[st6.b(hm)], writes=[mv.b(hm)])
                if mix == 0:
                    fw.op("dve", lambda e, pA=pA, h=h: e.tensor_copy(out=self.den.t[0:L, h:h + 1], in_=pA.t[0:L, 384:385]),
                          reads=[pA.b()], writes=[self.den.b(h)])
                G = s.G[mix]
                fw.op("dve", lambda e, pA=pA, hm=hm, G=G, h=h: e.scalar_tensor_tensor(
                    out=u.t[0:L, hm, :], in0=pA.t[0:L, 128:384], scalar=mv.t[0:L, hm, 0:1],
                    in1=G.t[0:L, c, h * 256:(h + 1) * 256], op0=ALU.subtract, op1=ALU.mult),
                    reads=[pA.b(), mv.b(hm), G.b((c, h // 2))], writes=[u.b(hm)])
                fw.op("dve", lambda e, pB=pB, hm=hm, wc=wc: e.scalar_tensor_tensor(
                    out=Cf.t[:, hm, :, 0:256], in0=Cf.t[:, hm, :, 0:256], scalar=wc,
                    in1=pB.t[:, :].rearrange("p (j n) -> p j n", j=2), op0=ALU.mult, op1=ALU.add),
                    reads=[pB.b(), Cf.b(hm)] + rd_g, writes=[Cf.b(hm)])
                fw.op("dve", lambda e, pA=pA, hm=hm, wc=wc: e.scalar_tensor_tensor(
                    out=Cf.t[:, hm, :, 256], in0=Cf.t[:, hm, :, 256], scalar=wc, in1=pA.t[:, 400:402],
                    op0=ALU.mult, op1=ALU.add),
                    reads=[pA.b(), Cf.b(hm)] + rd_g, writes=[Cf.b(hm)])
                if c + 1 < nt:
                    deferred.append(lambda hm=hm: self.make_cb(s, hm, c + 1, L))
            return deferred

        def pm(c, g):
            lowb = self.gm.t[0:L, c, s.gofs + 4:s.gofs + 8]
            keps = K_EPS128 if L == 128 else K_EPS16
            self.post_merge(s, L, 0, c, lowb, cst.t[0:L, keps:keps + 4], [self.gm.b(), cst.b()], 2 * g, 2 * g + 2)

        steps = [(c, g) for c in range(nt) for g in range(2)]
        P_next = prologue(*steps[0])
        pending = []
        prev = None
        for idx, (c, g) in enumerate(steps):
            P_cur = P_next
            if idx + 1 < len(steps):
                P_next = prologue(*steps[idx + 1])
            for f in pending:
                f()
            pending = body(c, g, P_cur)
            if prev is not None:
                pm(*prev)
            prev = (c, g)
        pm(*prev)
        for f in pending:
            f()

    def post_merge(self, s, L, p0, c, lowb, epsr, rd, h0=0, h1=4):
        fw = self.fw
        sm = self.sm
        R = slice(p0, p0 + L)
        H = slice(h0, h1)
        nh = h1 - h0
        fw.op("dve", lambda e: e.scalar_tensor_tensor(out=sm.t[R, 4 + h0:4 + h1], in0=self.den.t[R, H], scalar=-1.0,
                                                      in1=self.den.t[R, H], op0=ALU.mult, op1=ALU.max),
              reads=self.den.bs(range(h0, h1)) + [sm.b()], writes=[sm.b()])
        fw.op("dve", lambda e: e.tensor_tensor(out=sm.t[R, H], in0=sm.t[R, 4 + h0:4 + h1], in1=lowb[:, H], op=ALU.max),
              reads=[sm.b()] + rd, writes=[sm.b()])
        fw.op("dve", lambda e: e.scalar_tensor_tensor(out=sm.t[R, 8 + h0:8 + h1], in0=sm.t[R, H], scalar=LN_EPS,
                                                      in1=sm.t[R, H], op0=ALU.mult, op1=ALU.mult),
              reads=[sm.b()], writes=[sm.b()])
        fw.op("dve", lambda e: e.tensor_tensor(out=sm.t[R, 16 + h0:16 + h1], in0=sm.t[R, 8 + h0:8 + h1],
                                               in1=self.mv.t[R, H, 1], op=ALU.add),
              reads=[sm.b()] + self.mv.bs(range(h0, h1)), writes=[sm.b()])
        fw.op("dve", lambda e: e.tensor_tensor(out=sm.t[R, 20 + h0:20 + h1], in0=epsr[:, H],
                                               in1=self.mv.t[R, 4 + h0:4 + h1, 1], op=ALU.add),
              reads=[sm.b()] + rd + self.mv.bs(range(4 + h0, 4 + h1)), writes=[sm.b()])
        vin = sm.t[R, 16:24].rearrange("p (m h) -> p m h", m=2)[:, :, H]
        vout = sm.t[R, 24:32].rearrange("p (m h) -> p m h", m=2)[:, :, H]
        self.rstd(vout, vin, sm)
        ta = self.tmpa
        u = self.u
        for h in range(h0, h1):
            fw.op("pool", lambda e, h=h: e.tensor_scalar(out=ta.t[R, :], in0=u.t[R, h, :], scalar1=sm.t[R, 24 + h:25 + h],
                                                        scalar2=0.0, op0=ALU.mult, op1=ALU.add),
                  reads=[u.b(h), sm.b()], writes=[ta.b()])
            fw.op("pool", lambda e, h=h: e.tensor_scalar(out=u.t[R, 4 + h, :], in0=u.t[R, 4 + h, :],
                                                        scalar1=sm.t[R, 28 + h:29 + h], scalar2=0.0,
                                                        op0=ALU.mult, op1=ALU.add),
                  reads=[u.b(4 + h), sm.b()], writes=[u.b(4 + h)])
            fw.op("pool", lambda e, h=h: e.tensor_tensor(out=s.merged.t[R, c, h * 256:(h + 1) * 256], in0=ta.t[R, :],
                                                        in1=u.t[R, 4 + h, :], op=ALU.add),
                  reads=[u.b(4 + h), ta.b()], writes=[s.merged.b(c)])

    def store_prompt_state(self):
        d = self.dram
        Cf = self.Cf
        for h in range(4):
            self.store("sp", Cf, h, d["pC"][h].rearrange("(j p) e -> p j e", p=128), Cf.t[:, h, :, 0:256])
            self.store("sp", Cf, 4 + h, d["pS"][h].rearrange("(j p) e -> p j e", p=128), Cf.t[:, 4 + h, :, 0:256])
        tok = self.fw.dma("sp", lambda e: e.dma_start(out=d["pn"].rearrange("h (j p) -> p h j", p=128),
                                                      in_=Cf.t[:, 0:4, :, 256], allow_slow_non_contiguous=True),
                          Cf.b("o"), reads=Cf.bs(range(4)))
        self.fw.out_toks.append(tok)
        self.store("sp", self.mcur, 0, d["pm"], self.mcur.t[:])

    def sample_mixers(self):
        fw = self.fw
        s = self.MINI
        d = self.dram
        cst = self.cst
        R = slice(SP0, SP0 + NS)
        ssm = self.s_sm
        gcol = s.gcol
        def tm_transposes(w):
            ps = self.nextps()
            pv = ps.t[:].bitcast(BF16)

            def tr(e, w=w, pv=pv):
                for k in range(8):
                    ins = e.transpose(out=pv[0:64, k * 128:(k + 1) * 128], in_=self.fm(s, w, k)[:, 0:64],
                                      identity=self.ident_b.t[:])
                return ins
            fw.op("pe", tr, reads=s.fmbig.bs(range(8 * w, 8 * w + 8)) + [self.ident_b.b()], writes=[ps.b()])
            fw.op("act", lambda e, w=w, pv=pv: e.copy(out=self.s_tm[w].t[:, :], in_=pv[0:64, :]),
                  reads=[ps.b()], writes=[self.s_tm[w].b()])
        tm_transposes(0)
        tm_transposes(1)
        self.load("sp", self.s_n, 0, self.s_n.t[R, :], d["sn"])
        self.load("sp", ssm, "m", ssm.t[R, 0:4], d["sm"])
        one = cst.t[R, K_ONE:K_ONE + 1]
        sb_ = [ssm.b(), ssm.b("m")]
        fw.op("act", lambda e: e.activation(out=ssm.t[R, 28:32], in_=gcol.t[R, 0, 4:8], func=AF.Exp, scale=-1.0),
              reads=[gcol.b()] + sb_, writes=[ssm.b()])
        fw.op("act", lambda e: e.activation(out=ssm.t[R, 4:8], in_=ssm.t[R, 28:32], func=AF.Ln, bias=one, scale=1.0),
              reads=[ssm.b(), cst.b()], writes=[ssm.b()])
        fw.op("dve", lambda e: e.tensor_tensor(out=ssm.t[R, 8:12], in0=ssm.t[R, 0:4], in1=ssm.t[R, 4:8], op=ALU.subtract),
              reads=sb_, writes=[ssm.b()])
        fw.op("dve", lambda e: e.tensor_tensor(out=ssm.t[R, 12:16], in0=ssm.t[R, 8:12], in1=gcol.t[R, 0, 0:4], op=ALU.max),
              reads=[ssm.b(), gcol.b()], writes=[ssm.b()])
        fw.op("dve", lambda e: e.tensor_tensor(out=ssm.t[R, 16:20], in0=gcol.t[R, 0, 0:4], in1=ssm.t[R, 12:16],
                                               op=ALU.subtract),
              reads=[ssm.b(), gcol.b()], writes=[ssm.b()])
        fw.op("dve", lambda e: e.tensor_tensor(out=ssm.t[R, 20:24], in0=ssm.t[R, 8:12], in1=ssm.t[R, 12:16],
                                               op=ALU.subtract),
              reads=[ssm.b()], writes=[ssm.b()])
        fw.op("act", lambda e: e.activation(out=ssm.t[R, 16:24], in_=ssm.t[R, 16:24], func=AF.Exp),
              reads=[ssm.b()], writes=[ssm.b()])
        fw.op("act", lambda e: e.activation(out=ssm.t[R, 24:28], in_=ssm.t[R, 12:16], func=AF.Exp, scale=-1.0),
              reads=[ssm.b()], writes=[ssm.b()])
        self.store("sp", ssm, 0, d["om"], ssm.t[R, 12:16])
        big = self.u.t[0:64, :, :].bitcast(F32).rearrange("p a n -> p (a n)")
        bigb = self.u.bs(range(8))
        for (a_, b_, col, bt) in ((self.s_tm[0], self.s_tm[1], 32, None), (self.s_tm[0], self.s_n, 36, None),
                                  (self.s_tm[2], self.s_tm[3], 40, None)):
            if col == 40:
                tm_transposes(2)
                tm_transposes(3)
            fw.op("dve", lambda e, a_=a_, b_=b_: e.tensor_tensor(out=big[R, :], in0=a_.t[R, :], in1=b_.t[R, :], op=ALU.mult),
                  reads=[a_.b(), b_.b()], writes=bigb)
            fw.op("dve", lambda e, col=col: e.tensor_reduce(out=ssm.t[R, col:col + 4],
                                                           in_=big[R, :].rearrange("p (h n) -> p h n", h=4),
                                                           axis=AX.X, op=ALU.add),
                  reads=bigb + [ssm.b()], writes=[ssm.b()])
        fw.op("dve", lambda e: e.tensor_tensor(out=ssm.t[R, 44:48], in0=ssm.t[R, 32:36], in1=ssm.t[R, 16:20], op=ALU.mult),
              reads=[ssm.b()], writes=[ssm.b()])
        fw.op("dve", lambda e: e.tensor_tensor(out=ssm.t[R, 48:52], in0=ssm.t[R, 36:40], in1=ssm.t[R, 20:24], op=ALU.mult),
              reads=[ssm.b()], writes=[ssm.b()])
        fw.op("dve", lambda e: e.tensor_tensor(out=self.den.t[R, :], in0=ssm.t[R, 48:52], in1=ssm.t[R, 44:48], op=ALU.add),
              reads=[ssm.b()], writes=self.den.bs(range(4)))
        kp = self.s_tm[1]
        for h in range(4):
            hs = slice(h * 256, (h + 1) * 256)
            fw.op("act", lambda e, h=h, hs=hs: e.activation(out=kp.t[R, hs], in_=self.s_tm[1].t[R, hs], func=AF.Copy,
                                                           scale=ssm.t[R, 16 + h:17 + h]),
                  reads=[self.s_tm[1].b(), ssm.b()], writes=[kp.b()])
            fw.op("dve", lambda e, h=h, hs=hs: e.scalar_tensor_tensor(
                out=self.s_n.t[R, hs], in0=self.s_n.t[R, hs], scalar=ssm.t[R, 20 + h:21 + h], in1=kp.t[R, hs],
                op0=ALU.mult, op1=ALU.add),
                reads=[self.s_n.b(), ssm.b(), kp.b()], writes=[self.s_n.b()])
        self.store("sp", self.s_n, 0, d["on"], self.s_n.t[R, :])
        dg = self.s_dg
        idr = self.ident_f.t[R, SP0:SP0 + NS]
        fw.op("dve", lambda e: e.tensor_tensor(
            out=dg.t[R, :, 0:4], in0=idr.unsqueeze(2).to_broadcast([NS, NS, 4]),
            in1=ssm.t[R, 20:24].unsqueeze(1).to_broadcast([NS, NS, 4]), op=ALU.mult),
            reads=[ssm.b(), self.ident_f.b()], writes=[dg.b()])
        fw.op("dve", lambda e: e.tensor_tensor(
            out=dg.t[R, :, 4:8], in0=idr.unsqueeze(2).to_broadcast([NS, NS, 4]),
            in1=cst.t[R, K_G1:K_G1 + 4].unsqueeze(1).to_broadcast([NS, NS, 4]), op=ALU.mult),
            reads=[cst.b(), self.ident_f.b(), dg.b()], writes=[dg.b()])
        ps = self.nextps()
        fw.op("pe", lambda e, ps=ps: e.matmul(ps.t[:, 0:128], lhsT=self.ones_f.t[R, :],
                                            rhs=dg.t[R, :, :].rearrange("p a b -> p (a b)"), start=True, stop=True),
              reads=[dg.b(), self.ones_f.b()], writes=[ps.b()])
        fw.op("dve", lambda e, ps=ps: e.tensor_copy(out=self.s_wb.t[:, :], in_=ps.t[:, 0:128]),
              reads=[ps.b()], writes=[self.s_wb.b()])
        ci = 0
        for mix in range(2):
            src, dst = (d["sC"], d["oC"]) if mix == 0 else (d["sS"], d["oS"])
            kpt = kp if mix == 0 else self.s_tm[3]
            vsrc = self.vview(s, mix)
            acc = [self.ps[4 + h] for h in range(4)]
            for b in range(NS):
                qm, vm = (self.qm[0], self.vmk[0]) if b % 2 == 0 else (self.qm2, self.vm2)
                first2 = (mix == 0 and b == 1)
                xw = [self.wbuf[-1].b(0)] if first2 else []
                fw.op("dve", lambda e, qm=qm, b=b, mix=mix: e.tensor_tensor(
                    out=qm.t[:, :, :], in0=s.fmbig.t[:, 16 * mix:16 * mix + 8, :],
                    in1=self.identrow.t[:, b, :].unsqueeze(1).to_broadcast([128, 8, 64]), op=ALU.mult),
                    reads=s.fmbig.bs(range(16 * mix, 16 * mix + 8)) + [self.identrow.b()], writes=[qm.b()], extra=xw)
                fw.op("act", lambda e, vm=vm, b=b, vsrc=vsrc: e.activation(
                    out=vm.t[R, :], in_=vsrc[R, 0, :], func=AF.Copy, scale=self.ident_f.t[R, SP0 + b:SP0 + b + 1]),
                    reads=[s.xres.b(0), self.ident_f.b()], writes=[vm.b()], extra=xw)
                for h in range(4):
                    cin = self.cin[ci % len(self.cin)]
                    cbf = self.cbf[ci % 2]
                    first = ci < len(self.cin) and cin.name in self.cin_owner
                    ci += 1
                    self.fw.dma("sp", lambda e, cin=cin, b=b, h=h, src=src: e.dma_start(
                        out=cin.t[:, :, :], in_=src[b, h].rearrange("(j p) e -> p j e", p=128)),
                        cin.b(0), writes=[cin.b(0)], extra=[self.cin_owner[cin.name].b(0)] if first else ())
                    fw.op("act", lambda e, cin=cin, cbf=cbf: e.copy(out=cbf.t[:, :, :], in_=cin.t[:, :, :]),
                          reads=[cin.b()], writes=[cbf.b()])

                    def mmq(e, qm=qm, cbf=cbf, h=h, b=b):
                        for j in range(2):
                            ins = e.matmul(acc[h].t[0:64, 0:256], lhsT=qm.t[:, 2 * h + j, :], rhs=cbf.t[:, j, :],
                                           start=(b == 0 and j == 0), stop=(b == NS - 1 and j == 1))
                        return ins
                    fw.op("pe", mmq, reads=[qm.b(), cbf.b()], writes=[acc[h].b()])
                    pr = self.ps[ci % 4]

                    def mmr(e, pr=pr, kpt=kpt, vm=vm, h=h):
                        for j in range(2):
                            ins = e.matmul(pr.t[:, j * 256:(j + 1) * 256],
                                           lhsT=kpt.t[R, h * 256 + j * 128:h * 256 + (j + 1) * 128],
                                           rhs=vm.t[R, h * 256:(h + 1) * 256], start=True, stop=True)
                        return ins
                    fw.op("pe", mmr, reads=[kpt.b(), vm.b()], writes=[pr.b()])
                    wcol = b * 8 + mix * 4 + h
                    fw.op("dve", lambda e, cin=cin, pr=pr, wcol=wcol: e.scalar_tensor_tensor(
                        out=cin.t[:, :, :], in0=cin.t[:, :, :], scalar=self.s_wb.t[:, wcol:wcol + 1],
                        in1=pr.t[:, :].rearrange("p (j n) -> p j n", j=2), op0=ALU.mult, op1=ALU.add),
                        reads=[cin.b(), pr.b(), self.s_wb.b()], writes=[cin.b()])
                    self.store("pool", cin, 0, dst[b, h].rearrange("(j p) e -> p j e", p=128), cin.t[:, :, :],
                               semkey="st")
            for h in range(4):
                hm = 4 * mix + h
                ta = self.tmpa
                scol = (44 if mix == 0 else 40) + h
                fw.op("act", lambda e, h=h, scol=scol, vsrc=vsrc: e.activation(
                    out=ta.t[R, :], in_=vsrc[R, 0, h * 256:(h + 1) * 256], func=AF.Copy,
                    scale=ssm.t[R, scol:scol + 1]),
                    reads=[s.xres.b(0), ssm.b()], writes=[ta.b()])
                wsc = ssm.t[R, 20 + h:21 + h] if mix == 0 else cst.t[R, K_G1 + h:K_G1 + h + 1]
                num = self.s_num
                fw.op("dve", lambda e, h=h, wsc=wsc: e.scalar_tensor_tensor(
                    out=num.t[R, :], in0=acc[h].t[R, 0:256], scalar=wsc, in1=ta.t[R, :], op0=ALU.mult, op1=ALU.add),
                    reads=[acc[h].b(), ta.b(), ssm.b(), cst.b()], writes=[num.b()])
                fw.op("dve", lambda e, hm=hm: e.bn_stats(out=self.st6.t[R, hm, :], in_=num.t[R, :]),
                      reads=[num.b()], writes=[self.st6.b(hm)])
                fw.op("dve", lambda e, hm=hm: e.bn_aggr(out=self.mv.t[R, hm, :], in_=self.st6.t[R, hm, :]),
                      reads=[self.st6.b(hm)], writes=[self.mv.b(hm)])
                G = s.G[mix]
                fw.op("dve", lambda e, hm=hm, h=h, G=G: e.scalar_tensor_tensor(
                    out=self.u.t[R, hm, :], in0=num.t[R, :], scalar=self.mv.t[R, hm, 0:1],
                    in1=G.t[R, 0, h * 256:(h + 1) * 256], op0=ALU.subtract, op1=ALU.mult),
                    reads=[num.b(), self.mv.b(hm), G.b((0, h // 2))], writes=[self.u.b(hm)])
        for tl in self.borrowed2:
            wb = self.wbuf[-1].b(0)
            for k_, tok in tl.b(0).r.items():
                wb.r["smp_%s_%s" % (tl.name, k_)] = tok
            if tl.b(0).w is not None:
                wb.r["smpw_" + tl.name] = tl.b(0).w
        for tl in self.cin:
            if tl.name in self.cin_owner:
                wb = self.cin_owner[tl.name].b(0)
                for k_, tok in tl.b(0).r.items():
                    wb.r["smp_%s_%s" % (tl.name, k_)] = tok
                if tl.b(0).w is not None:
                    wb.r["smpw_" + tl.name] = tl.b(0).w
        self.post_merge(s, NS, SP0, 0, ssm.t[R, 24:28], cst.t[R, K_EPS1:K_EPS1 + 4], [ssm.b(), cst.b()])

    def ln_evac(self, s, i, pss):
        fw = self.fw
        tsz = s.tsz
        xr = s.xres
        z = xr.t[:, i, :]
        for hf in range(2):
            cs = slice(hf * 512, (hf + 1) * 512)
            fw.op("dve", lambda e, hf=hf, cs=cs: e.scalar_tensor_tensor(
                out=z[:, cs], in0=z[:, cs], scalar=ALPHA, in1=pss[hf].t[0:tsz, :], op0=ALU.mult, op1=ALU.add),
                reads=[pss[hf].b(), xr.b(i)], writes=[xr.b(i)])

    def ln_tile(self, s, i, g, b):
        fw = self.fw
        tsz = s.tsz
        xr = s.xres
        z = xr.t[:, i, :]
        lmv = self.lmv
        for hf in range(2):
            cs = slice(hf * 512, (hf + 1) * 512)
            fw.op("dve", lambda e, hf=hf, cs=cs: e.bn_stats(out=self.lst.t[0:tsz, hf, :], in_=z[:, cs]),
                  reads=[xr.b(i)], writes=[self.lst.b()])
        fw.op("dve", lambda e: e.bn_aggr(out=lmv.t[0:tsz, 0:2], in_=self.lst.t[0:tsz, :, :].rearrange("p a b -> p (a b)")),
              reads=[self.lst.b()], writes=[lmv.b()])
        fw.op("dve", lambda e: e.tensor_scalar(out=lmv.t[0:tsz, 2:3], in0=lmv.t[0:tsz, 1:2], scalar1=LN_EPS, scalar2=None,
                                               op0=ALU.add),
              reads=[lmv.b()], writes=[lmv.b()])
        self.rstd(lmv.t[0:tsz, 3:4], lmv.t[0:tsz, 2:3], lmv)
        fw.op("dve", lambda e: e.scalar_tensor_tensor(out=z, in0=z, scalar=lmv.t[0:tsz, 0:1], in1=g.t[0:tsz, :],
                                                      op0=ALU.subtract, op1=ALU.mult),
              reads=[xr.b(i), lmv.b(), g.b()], writes=[xr.b(i)])
        fw.op("dve", lambda e: e.scalar_tensor_tensor(out=z, in0=z, scalar=lmv.t[0:tsz, 3:4], in1=b.t[0:tsz, :],
                                                      op0=ALU.mult, op1=ALU.add),
              reads=[xr.b(i), lmv.b(), b.b()], writes=[xr.b(i)])
        return z

    def outproj_ln1(self, sets):
        fw = self.fw
        d = self.dram
        if DEBUG and self.MINI in sets:
            s = self.MINI
            big = self.rot.t[0:64, :, :].rearrange("p a n -> p (a n)")
            fw.op("act", lambda e, s=s: e.copy(out=big, in_=s.merged.t[:, 0, :]), reads=[s.merged.b(0)], writes=self.rot.bs(range(4)))
            tok = fw.dma("sp", lambda e: e.dma_start(out=d["dbg_mrg"], in_=big), self.rot.b("o"), reads=self.rot.bs(range(4)))
            fw.out_toks.append(tok)
        mTb = lambda s: s.fmbig.bs(range(8))
        for s in sets:
            tsz = s.tsz
            for i in range(s.ntile):
                self.transpose_in(s, [s.merged.b(i)], lambda k, s=s, i=i: s.merged.t[:, i, k * 128:(k + 1) * 128],
                                  s.fmbig.t[:, 0:8, i * tsz:(i + 1) * tsz], mTb(s))
        wts = [self.wnext("out", 0), self.wnext("out", 512, hold=1)]
        for s in sets:
            tsz = s.tsz
            allb = s.xres.bs(range(s.ntile))
            if s is self.MAIN:
                r0 = self.pass_idx * 512
                self.fw.dma("sp", lambda e, s=s, r0=r0: e.dma_start(
                    out=s.xres.t[:, :, :], in_=d["xp"][r0:r0 + 512, :].rearrange("(i p) n -> p i n", p=128)),
                    s.xres.b("ld"), writes=allb)
            else:
                self.fw.dma("sp", lambda e, s=s: e.dma_start(out=s.xres.t[0:NMETA, 0, :], in_=d["meta"]),
                            s.xres.b("ld"), writes=allb)
                self.fw.dma("sp", lambda e, s=s: e.dma_start(out=s.xres.t[SP0:SP0 + NS, 0, :], in_=d["xs"]),
                            s.xres.b("ld"), writes=allb, nowait=True)
            pss = []
            for i in range(s.ntile):
                pp = []
                for hf in range(2):
                    ps = self.nextps()
                    pp.append(ps)
                    w3 = wts[hf].t[:, 0:8 * 512].rearrange("p (k n) -> p k n", k=8)

                    def mm(e, ps=ps, w3=w3, i=i, s=s):
                        for k in range(8):
                            ins = e.matmul(ps.t[0:s.tsz, :], lhsT=s.fmbig.t[:, k, i * s.tsz:(i + 1) * s.tsz],
                                           rhs=w3[:, k, :], start=(k == 0), stop=(k == 7))
                        return ins
                    fw.op("pe", mm, reads=mTb(s) + [wts[hf].b()], writes=[ps.b()])
                pss.append(pp)
            for i in range(s.ntile):
                self.ln_evac(s, i, pss[i])
            for i in range(s.ntile):
                z = self.ln_tile(s, i, self.lnp[0], self.lnp[1])
                fw.op("act", lambda e, z=z, s=s, i=i: e.copy(out=s.merged.t[:, i, :], in_=z),
                      reads=[s.xres.b(i)], writes=[s.merged.b(i)])
                if DEBUG and s is self.MINI:
                    self.store("sp", s.xres, i, d["dbg_x1"], s.xres.t[:, 0, :])
                self.transpose_in(s, [s.merged.b(i)], lambda k, s=s, i=i: s.merged.t[:, i, k * 128:(k + 1) * 128],
                                  s.fmbig.t[:, 0:8, i * tsz:(i + 1) * tsz], mTb(s))

    def ffn(self, sets, hook=None):
        fw = self.fw
        for c0 in range(0, DFF, 512):
            wg = self.wnext("gate", c0)
            wu = self.wnext("up", c0, hold=1)
            nch = min(4, (DFF - c0) // 128)
            g3 = wg.t[:, 0:8 * 512].rearrange("p (k n) -> p k n", k=8)
            u3 = wu.t[:, 0:8 * 512].rearrange("p (k n) -> p k n", k=8)
            for s in sets:
                T = s.T
                x1r = s.fmbig.bs(range(8))

                def grp(ps, w3, wt, m, s=s, T=T):
                    def mm(e):
                        for k in range(8):
                            ins = e.matmul(ps.t[:, 0:T], lhsT=w3[:, k, m * 128:(m + 1) * 128], rhs=s.fmbig.t[:, k, :],
                                           start=(k == 0), stop=(k == 7))
                        return ins
                    fw.op("pe", mm, reads=x1r + [wt.b()], writes=[ps.b()])
                pgs = []
                for m in range(nch):
                    pg = self.nextps()
                    pgs.append(pg)
                    grp(pg, g3, wg, m)
                tbs = []
                for m in range(nch):
                    fc = c0 // 128 + m
                    pu = self.nextps()
                    grp(pu, u3, wu, m)
                    pg = pgs[m]
                    tb = self.tmpb[self.tbi % 2]
                    self.tbi += 1
                    fw.op("act", lambda e, tb=tb, pg=pg, T=T: e.activation(out=tb.t[:, 0:T], in_=pg.t[:, 0:T], func=AF.Silu),
                          reads=[pg.b()], writes=[tb.b()])
                    fw.op("dve", lambda e, tb=tb, pu=pu, T=T, s=s, fc=fc: e.tensor_tensor(
                        out=s.fmbig.t[:, 8 + fc, :], in0=tb.t[:, 0:T], in1=pu.t[:, 0:T], op=ALU.mult),
                        reads=[tb.b(), pu.b()], writes=[s.fmbig.b(8 + fc)])
        small = [(s, i) for s in sets if s is not self.MAIN for i in range(s.ntile)]
        if small:
            self.mini_down = small
        if hook is not None:
            hook()
        return self.down([(self.MAIN, i) for i in range(4)], defer=hook is not None)

    def down(self, tiles, defer=False, banks=None, blocks=None):
        fw = self.fw
        d = self.dram
        if banks is None:
            banks = [[self.ps[2 * n], self.ps[2 * n + 1]] for n in range(len(tiles))]
        todo = list(range(0, DFF, 512)) if blocks is None else ([] if blocks == "finish" else [blocks])
        for r0 in todo:
            wt = self.wnext("down", r0)
            nch = min(4, (DFF - r0) // 128)
            w4 = wt.t[:, 0:4 * 1024].rearrange("p (c n) -> p c n", c=4)
            for n, (s, i) in enumerate(tiles):
                tsz = s.tsz
                for hf in range(2):
                    ps = banks[n][hf]

                    def mm(e, ps=ps, i=i, hf=hf, s=s, nch=nch, r0=r0, tsz=tsz, w4=w4):
                        for m in range(nch):
                            fc = r0 // 128 + m
                            ins = e.matmul(ps.t[0:tsz, :], lhsT=s.fmbig.t[:, 8 + fc, i * tsz:(i + 1) * tsz],
                                           rhs=w4[:, m, hf * 512:(hf + 1) * 512],
                                           start=(fc == 0), stop=(fc == NFF - 1))
                        return ins
                    fw.op("pe", mm, reads=s.fmbig.bs(range(8 + r0 // 128, 8 + r0 // 128 + nch)) + [wt.b()],
                          writes=[ps.b()])
        if blocks is not None and blocks != "finish":
            return []
        for n, (s, i) in enumerate(tiles):
            self.ln_evac(s, i, banks[n])

        def finish(s, i, r0):
            def f():
                self.ln_tile(s, i, self.lnp[2], self.lnp[3])
                if s is self.MAIN:
                    self.store("sp", s.xres, i, d["yp"][r0:r0 + 128, :], s.xres.t[:, i, :])
                else:
                    self.store("sp", s.xres, i, d["ys"], s.xres.t[SP0:SP0 + NS, 0, :])
            return f
        fins = [finish(s, i, self.pass_idx * 512 + i * 128) for (s, i) in tiles]
        if defer:
            return fins
        for f in fins:
            f()
        return []

    def program(self):
        NP = 4
        fw = self.fw
        self.wspecs = self.wspec_list(NP)
        self.wi = 0
        self.wloaded = 0
        self.wscr = self.nc.dram_tensor("wscr", [40, 128, 9 * 512], BF16).ap()
        self.scr_slot = {}
        self.scr_buf = {}
        self.no_prefetch_beyond = 19
        self.gw_loaded = False
        self.load_consts()
        fw.op("dve", lambda e: e.memset(self.MINI.merged.t[:], 0.0), writes=[self.MINI.merged.b(0)])
        fw.op("dve", lambda e: e.memset(self.MINI.xres.t[:], 0.0), writes=self.MINI.xres.bs(range(1)))
        for p in range(NP):
            self.pass_idx = p
            sets = [self.MINI, self.MAIN] if p == 0 else [self.MAIN]
            if p == 0:
                self.load_x_mini()
                self.load_x_main(0)
                gw = self.gw
                self.load("pool", gw, 0, gw.t[:, 0:8, :],
                          self.dram["w_in"][:, C_MI:C_MI + 8].rearrange("(k p) n -> p k n", p=128))
                self.load("pool", gw, 0, gw.t[0:1, 8, :], self.dram["b_in"][:, C_MI:C_MI + 8], nowait=True)
                self.gw_loaded = True
                while self.wloaded < 2:
                    self.wload(self.wloaded)
                    self.wloaded += 1
                self.load_consts_late()
            self.projection(sets)
            if p == 0:
                self.mixers(self.MINI)
                self.sample_mixers()
                self.no_prefetch_beyond = None
                while self.wloaded < min(len(self.wspecs), self.wi + self.NWB):
                    self.wload(self.wloaded)
                    self.wloaded += 1
            self.mixers(self.MAIN)
            if p == NP - 1:
                self.store_prompt_state()
            self.outproj_ln1(sets)
            self.pending_ln = self.ffn(sets, (lambda p=p: self.load_x_main(p + 1)) if p + 1 < NP else None)
        assert self.wi == len(self.wspecs)


def _consts():
    f32 = np.float32
    ident = np.eye(128, dtype=f32)
    maskT = np.triu(np.ones((128, 128), dtype=f32))
    lg = np.log1p(-np.exp2(-5.0 - np.arange(4, dtype=np.float64)))
    cst = np.zeros((128, NCST), dtype=np.float64)
    s128 = np.arange(128)[:, None]
    cst[:, K_ES128:K_ES128 + 4] = np.exp((127 - s128) * lg[None, :])
    cst[:, K_ES16:K_ES16 + 4] = np.exp((15 - s128) * lg[None, :])
    cst[:, K_WC128:K_WC128 + 4] = np.exp(128 * lg)[None, :]
    cst[:, K_WC16:K_WC16 + 4] = np.exp(16 * lg)[None, :]
    cst[:, K_EPS128:K_EPS128 + 4] = LN_EPS * np.exp(2 * (127 - s128) * lg[None, :])
    cst[:, K_EPS16:K_EPS16 + 4] = LN_EPS * np.exp(2 * (15 - s128) * lg[None, :])
    cst[:, K_G1:K_G1 + 4] = np.exp(lg)[None, :]
    cst[:, K_EPS1:K_EPS1 + 4] = LN_EPS
    cst[:, K_ONE] = 1.0
    cst = cst.astype(f32)
    inv = 10000.0 ** (-np.arange(0, 256, 2, dtype=np.float64) / 256.0)

    def tables(pos):
        ang = np.asarray(pos, dtype=np.float64)[:, None] * inv[None, :]
        return np.ascontiguousarray(np.cos(ang).T.astype(f32)), np.ascontiguousarray(np.sin(ang).T.astype(f32))
    ropeC, ropeS = tables(np.arange(NMETA, NMETA + SEQ))
    pm = np.zeros(64)
    pm[0:NMETA] = np.arange(NMETA)
    pm[SP0:SP0 + NS] = PAST
    ropeCm, ropeSm = tables(pm)
    identrow = np.zeros((128, 16, 64), dtype=f32)
    for b in range(16):
        identrow[:, b, SP0 + b] = 1.0
    return dict(ident=ident, maskT=maskT, cst=cst, ropeC=ropeC, ropeS=ropeS, ropeCm=ropeCm, ropeSm=ropeSm,
                identrow=identrow.reshape(128, 1024))


_NC_CACHE = {}


def kernel(x_prompt, x_sample, state_mlstm_C, state_mlstm_n, state_mlstm_m, state_ret_S,
           meta_tokens, w_in, b_in, ml_norm_g, rt_norm_g, w_out,
           ln1_g, ln1_b, w_gate, w_up, w_down, ln2_g, ln2_b):
    f = lambda a: np.ascontiguousarray(np.asarray(a, dtype=np.float32))
    if "nc" not in _NC_CACHE:
        _NC_CACHE["nc"] = Prog().build()
    nc = _NC_CACHE["nc"]
    cs = _consts()
    shared = dict(meta=f(meta_tokens), w_in=f(w_in)[0], b_in=f(b_in), ml_g=f(ml_norm_g), rt_g=f(rt_norm_g),
                  w_out=f(w_out)[0], ln1_g=f(ln1_g), ln1_b=f(ln1_b), w_gate=f(w_gate)[0], w_up=f(w_up)[0],
                  w_down=f(w_down)[0], ln2_g=f(ln2_g), ln2_b=f(ln2_b), **cs)
    xp, xs = f(x_prompt), f(x_sample)
    sC, sn, sm, sS = f(state_mlstm_C)[0], f(state_mlstm_n)[0], f(state_mlstm_m)[0], f(state_ret_S)[0]
    in_maps = []
    for c in range(NCORES):
        sl = slice(c * NS, (c + 1) * NS)
        m = dict(shared)
        m.update(xp=xp[c], xs=np.ascontiguousarray(xs[sl, 0, :]), sC=np.ascontiguousarray(sC[sl]),
                 sn=np.ascontiguousarray(sn[sl].reshape(NS, 1024)), sm=np.ascontiguousarray(sm[sl]),
                 sS=np.ascontiguousarray(sS[sl]))
        in_maps.append(m)
    res = run_bass_kernel_spmd(nc, in_maps, core_ids=list(range(NCORES)))
    R = res.results
    if DEBUG:
        _NC_CACHE["dbg"] = dict(mrg=R[0]["dbg_mrg"], x1=R[0]["dbg_x1"])
    cat = lambda k: np.concatenate([r[k] for r in R], axis=0)
    stk = lambda k: np.stack([r[k] for r in R], axis=0)
    y_prompt = stk("yp")
    y_sample = cat("ys").reshape(128, 1, D)
    p_C = stk("pC")[None]
    p_n = stk("pn")[None]
    p_m = stk("pm").reshape(1, NCORES, 4)
    p_S = stk("pS")[None]
    s_C = cat("oC")[None]
    s_n = cat("on").reshape(1, 128, 4, 256)
    s_m = cat("om")[None]
    s_S = cat("oS")[None]
    return (y_prompt, y_sample, p_C, p_n, p_m, p_S, s_C, s_n, s_m, s_S)
```

```python
from contextlib import ExitStack

import numpy as np
import concourse.bass as bass
import concourse.mybir as mybir
from concourse.bass_utils import run_bass_kernel_spmd

F32 = mybir.dt.float32
BF16 = mybir.dt.bfloat16
AF = mybir.ActivationFunctionType
ALU = mybir.AluOpType
AX = mybir.AxisListType

NCORES = 8
D = 1024
SEQ = 2048
NMETA = 16
NS = 16
SP0 = 32
DIN = 10248
DFF = 2816
NFF = DFF // 128
LN_EPS = 1e-5
ALPHA = 2.0 ** 0.25
PAST = 16384
C_MI = 4096

K_ES128, K_ES16, K_WC128, K_WC16, K_EPS128, K_EPS16, K_G1, K_EPS1, K_ONE = (0, 4, 8, 12, 16, 20, 24, 28, 32)
NCST = 40
DEBUG = False


class Buf:
    __slots__ = ("name", "w", "r", "sem", "cnt")

    def __init__(self, name):
        self.name = name
        self.w = None
        self.r = {}
        self.sem = None
        self.cnt = 0


class FW:
    ENG = ("pe", "act", "dve", "pool", "sp")

    def __init__(self, nc, stack):
        self.nc = nc
        self.stack = stack
        self.sem = {e: stack.enter_context(nc.semaphore("s_" + e)) for e in self.ENG}
        self.n = {e: 0 for e in self.ENG}
        self.known = {e: {} for e in self.ENG}
        self.rec = {e: [] for e in self.ENG}
        self.nsem = len(self.ENG)
        self.out_toks = []

    def _waits(self, E, reads, writes):
        need = {}

        def add(tok):
            if tok is None:
                return
            s, v = tok
            if need.get(s, 0) < v:
                need[s] = v
        for b in reads:
            add(b.w)
        for b in writes:
            add(b.w)
            for t in b.r.values():
                add(t)
        out = []
        kn = self.known[E]
        for s, v in need.items():
            if E == "pe" and s is self.sem["pe"]:
                continue
            if kn.get(s, 0) < v:
                kn[s] = v
                out.append((s, v))
        return out

    def op(self, E, fn, reads=(), writes=(), extra=()):
        waits = self._waits(E, reads, list(writes) + list(extra))
        self.n[E] += 1
        sem = self.sem[E]
        tok = (sem, self.n[E])
        self.rec[E].append((waits, fn, sem, 1))
        for b in reads:
            b.r[E] = tok
        for b in writes:
            b.w = tok
            b.r = {}
        return tok

    def dma(self, Q, fn, sb, reads=(), writes=(), nowait=False, extra=()):
        waits = [] if nowait else self._waits(Q, reads, list(writes) + list(extra))
        if sb.sem is None:
            sb.sem = self.stack.enter_context(self.nc.semaphore("d_" + sb.name))
            self.nsem += 1
        sb.cnt += 16
        tok = (sb.sem, sb.cnt)
        self.rec[Q].append((waits, fn, sb.sem, 16))
        for b in reads:
            b.r["dma_" + sb.name] = tok
        for b in writes:
            b.w = tok
            b.r = {}
        return tok

    def emit(self):
        nc = self.nc
        need = {}
        for s, v in self.out_toks:
            if need.get(s, 0) < v:
                need[s] = v
        self.rec["sp"].append((list(need.items()), None, None, 0))
        with nc.Block() as block:
            def run(eng, lst):
                for waits, fn, sem, inc in lst:
                    for s, v in waits:
                        eng.wait_ge(s, v)
                    if fn is not None:
                        fn(eng).then_inc(sem, inc)

            @block.tensor
            def _(e):
                run(e, self.rec["pe"])

            @block.scalar
            def _(e):
                run(e, self.rec["act"])

            @block.vector
            def _(e):
                run(e, self.rec["dve"])

            @block.gpsimd
            def _(e):
                run(e, self.rec["pool"])

            @block.sync
            def _(e):
                run(e, self.rec["sp"])


class Tl:
    def __init__(self, t, name):
        self.t = t
        self.name = name
        self._b = {}
        self.coarse = False

    def b(self, key=0):
        if self.coarse:
            key = 0
        if key not in self._b:
            self._b[key] = Buf("%s_%s" % (self.name, key))
        return self._b[key]

    def bs(self, keys):
        return [self.b(k) for k in keys]


class TokSet:
    pass


class Prog:
    def __init__(self):
        self.nc = bass.Bass("TRN2", target_bir_lowering=False)
        self.st = ExitStack()
        self.dram = {}
        self.xbi = 0
        self.pending_ln = []
        self.ps_reserved = set()
        self.mini_down = None
        self.mgi = 0
        self.tbi = 0
        self.psi = 0
        self.sbytes = 0

    def din(self, name, shape):
        self.dram[name] = self.nc.dram_tensor(name, list(shape), F32, kind="ExternalInput").ap()

    def dout(self, name, shape):
        self.dram[name] = self.nc.dram_tensor(name, list(shape), F32, kind="ExternalOutput").ap()

    def sb(self, name, shape, dt):
        n = 1
        for x in shape[1:]:
            n *= x
        self.sbytes += n * (4 if dt == F32 else 2)
        return Tl(self.st.enter_context(self.nc.sbuf_tensor("sb_" + name, list(shape), dt)), name)

    def load(self, Q, tl, key, out_ap, in_ap, nowait=False):
        return self.fw.dma(Q, lambda e: e.dma_start(out=out_ap, in_=in_ap), tl.b(key),
                           writes=[tl.b(key)], nowait=nowait)

    def store(self, Q, tl, key, out_ap, in_ap, reads=None, semkey=None):
        tok = self.fw.dma(Q, lambda e: e.dma_start(out=out_ap, in_=in_ap), tl.b(key if semkey is None else semkey),
                          reads=[tl.b(key)] if reads is None else reads)
        self.fw.out_toks.append(tok)
        return tok

    def build(self):
        nc = self.nc
        with self.st:
            self.fw = FW(nc, self.st)
            self.declare()
            self.alloc()
            self.program()
            self.fw.emit()
        return nc

    def declare(self):
        d = self.din
        d("xp", (SEQ, D)); d("meta", (NMETA, D)); d("xs", (NS, D))
        d("sC", (NS, 4, 256, 256)); d("sn", (NS, 1024)); d("sm", (NS, 4)); d("sS", (NS, 4, 256, 256))
        d("w_in", (D, DIN)); d("b_in", (1, DIN))
        d("ml_g", (1, D)); d("rt_g", (1, D)); d("w_out", (D, D))
        d("ln1_g", (1, D)); d("ln1_b", (1, D))
        d("w_gate", (D, DFF)); d("w_up", (D, DFF)); d("w_down", (DFF, D))
        d("ln2_g", (1, D)); d("ln2_b", (1, D))
        d("ident", (128, 128)); d("maskT", (128, 128)); d("cst", (128, NCST))
        d("ropeC", (128, SEQ)); d("ropeS", (128, SEQ))
        d("ropeCm", (128, 64)); d("ropeSm", (128, 64))
        d("identrow", (128, 1024))
        o = self.dout
        o("yp", (SEQ, D)); o("ys", (NS, D))
        if DEBUG:
            o("dbg_mrg", (64, D)); o("dbg_x1", (64, D))
        o("pC", (4, 256, 256)); o("pn", (4, 256)); o("pm", (4, 1)); o("pS", (4, 256, 256))
        o("oC", (NS, 4, 256, 256)); o("on", (NS, 1024)); o("om", (NS, 4)); o("oS", (NS, 4, 256, 256))

    def mkset(self, name, T, tsz):
        s = TokSet()
        s.name, s.T, s.tsz, s.ntile = name, T, tsz, T // tsz
        sb = self.sb
        s.alias = (tsz == 128)
        s.xT = sb(name + "xT", [128, 8, T], BF16)
        s.fmbig = sb(name + "fm", [128, 32, T], BF16)
        s.xres = sb(name + "xres", [tsz, s.ntile, 1024], F32)
        s.G = [sb(name + "G%d" % i, [tsz, s.ntile, 1024], BF16) for i in range(2)]
        s.gcol = sb(name + "gcol", [tsz, s.ntile, 8], F32)
        if s.alias:
            s.xT.coarse = True
            mv_ = s.xT.t[:, :, :].rearrange("p k t -> p (k t)").rearrange("p (i n) -> p i n", i=s.ntile)
            s.merged = Tl(mv_, s.xT.name)
            s.merged._b = s.xT._b
            s.merged.coarse = True
        else:
            s.merged = sb(name + "mrg", [tsz, s.ntile, 1024], BF16)
        s.rc = sb(name + "rc", [128, T], F32)
        s.rs = sb(name + "rs", [128, T], F32)
        return s

    def fm(self, s, which, chunk):
        return s.fmbig.t[:, 8 * which + chunk, :]

    def vview(self, s, mix):
        nt = s.ntile
        half = s.xres.t[:, :, :].rearrange("p a n -> p (a n)").bitcast(BF16)
        return half[:, mix * nt * 1024:(mix + 1) * nt * 1024].rearrange("p (a n) -> p a n", a=nt)

    def alloc(self):
        nc, st, sb = self.nc, self.st, self.sb
        self.ps = [Tl(st.enter_context(nc.psum_tensor("ps%d" % i, [128, 512], F32)), "ps%d" % i)
                   for i in range(8)]
        self.MAIN = self.mkset("M", 512, 128)
        self.MINI = self.mkset("E", 64, 64)
        self.NWB = 3
        self.wbuf = [sb("wb%d" % i, [128, 9 * 512], BF16) for i in range(self.NWB)]
        self.gw = sb("gw", [128, 9, 8], BF16)
        self.ident_b = sb("ident_b", [128, 128], BF16)
        self.ident_f = sb("ident_f", [128, 128], F32)
        self.maskT = sb("maskT", [128, 128], F32)
        self.cst = sb("cst", [128, NCST], F32)
        self.ones_b = sb("ones_b", [128, 512], BF16)
        self.ones_f = sb("ones_f", [128, 128], F32)
        self.gbc = [sb("gbc%d" % i, [128, 1024], BF16) for i in range(2)]
        self.lnp = [sb("lnp%d" % i, [128, 1024], BF16) for i in range(4)]
        self.xbf = [sb("xbf%d" % i, [128, 1024], BF16) for i in range(2)]
        self.tmpb = [sb("tmpb%d" % i, [128, 512], BF16) for i in range(2)]
        self.rot = sb("rot", [128, 4, 256], F32)
        self.Cf = sb("Cf", [128, 8, 2, 257], F32)
        self.Cb = sb("Cb", [128, 8, 2, 257], BF16)
        self.ktm = [sb("ktm%d" % i, [128, 4, 256], BF16) for i in range(2)]
        self.vp = [sb("vp%d" % i, [128, 4, 257], BF16) for i in range(2)]
        self.stm = [sb("stm%d" % i, [128, 4, 128], BF16) for i in range(2)]
        self.u = sb("u", [128, 8, 256], BF16)
        self.st6 = sb("st6", [128, 8, 6], F32)
        self.mv = sb("mv", [128, 8, 2], F32)
        self.den = sb("den", [128, 4], F32)
        self.sm = sb("smalls", [128, 64], F32)
        self.tmpa = sb("tmpa", [128, 256], F32)
        self.gm = sb("gm", [128, 4, 40], F32)
        self.gsm = sb("gsm", [4, 96], F32)
        self.gbcst = sb("gbcst", [128, 5, 8], F32)
        self.wcb = sb("wcb", [128, 5, 4], F32)
        self.mcur = sb("mcur", [4, 1], F32)
        self.lst = sb("lst", [128, 2, 6], F32)
        self.lmv = sb("lmv", [128, 4], F32)
        self.NCIN = 3
        self.cin = [sb("cin%d" % i, [128, 2, 256], F32) for i in range(self.NCIN)]
        self.cin_owner = {}
        for i, wb in enumerate(self.wbuf):
            for q in range(4):
                v_ = wb.t[:, q * 1024:(q + 1) * 1024].bitcast(F32).rearrange("p (j e) -> p j e", j=2)
                tl = Tl(v_, "cs%d_%d" % (i, q))
                self.cin.append(tl)
                self.cin_owner[tl.name] = wb
        wb = self.wbuf[-1]
        for _ in range(2):
            tl = self.cin.pop()
            del self.cin_owner[tl.name]
        self.qm2 = Tl(wb.t[:, 3 * 1024:3 * 1024 + 512].rearrange("p (k n) -> p k n", k=8), "qm2")
        self.vm2 = Tl(wb.t[0:64, 2 * 1024:3 * 1024], "vm2")
        self.borrowed2 = [self.qm2, self.vm2]
        self.cbf = [sb("cbf%d" % i, [128, 2, 256], BF16) for i in range(2)]
        self.qm = [sb("qm0", [128, 8, 64], BF16)] * 2
        self.vmk = [sb("vmk0", [64, 1024], BF16)] * 2
        self.identrow = sb("identrow", [128, 16, 64], BF16)
        self.s_num = sb("s_num", [64, 256], F32)
        self.s_tm = [sb("s_tm%d" % i, [64, 1024], BF16) for i in range(3)]
        self.s_tm = [self.s_tm[0], self.s_tm[1], self.s_tm[0], self.s_tm[2]]
        self.s_n = Tl(self.rot.t[0:64, :, :].rearrange("p a n -> p (a n)"), "rot")
        self.s_n._b = self.rot._b
        self.s_n.coarse = True
        self.rot.coarse = True
        self.s_sm = sb("s_sm", [64, 64], F32)
        self.s_wb = sb("s_wb", [128, 128], F32)
        self.s_dg = sb("s_dg", [64, 16, 8], F32)

    def nextps(self):
        while (self.psi % 8) in self.ps_reserved:
            self.psi += 1
        p = self.ps[self.psi % 8]
        self.psi += 1
        return p

    def wspec_list(self, npass):
        L = []
        for p in range(npass):
            for n_, c0 in enumerate(list(range(0, 4096, 512)) + list(range(4104, DIN, 512))):
                L.append(("in", c0))
                if p == 1 and n_ < 6:
                    L.append(("down", n_ * 512))
            for c0 in (0, 512):
                L.append(("out", c0))
            for c0 in range(0, DFF, 512):
                L.append(("gate", c0))
                L.append(("up", c0))
            for r0 in range(0, DFF, 512):
                L.append(("down", r0))
        return L

    def wload(self, idx):
        kind, c0 = self.wspecs[idx]
        tl = self.wbuf[idx % self.NWB]
        d = self.dram
        uid = (kind, c0)
        def regions(t2):
            if kind == "in":
                return [t2[:, 0:8 * 512], t2[0:1, 8 * 512:9 * 512]]
            if kind == "out":
                return [t2[:, 0:8 * 512]]
            if kind in ("gate", "up"):
                n = min(512, DFF - c0)
                return [t2[:, 0:8 * 512].rearrange("p (k n) -> p k n", k=8)[:, :, 0:n]]
            nch = min(4, (DFF - c0) // 128)
            return [t2[:, 0:nch * 1024]]
        if uid in self.scr_slot:
            slot = self.scr_slot[uid]
            for n_, (o_, i_) in enumerate(zip(regions(tl.t), regions(self.wscr[slot]))):
                self.fw.dma("pool", lambda e, o_=o_, i_=i_: e.dma_start(out=o_, in_=i_), tl.b(0),
                            reads=[self.scr_buf[slot]], writes=[tl.b(0)], nowait=(n_ > 0))
            return
        self._wload_cast(idx)
        slot = len(self.scr_slot)
        self.scr_slot[uid] = slot
        self.scr_buf[slot] = Buf("scr%d" % slot)
        for n_, (o_, i_) in enumerate(zip(regions(self.wscr[slot]), regions(tl.t))):
            self.fw.dma("sp", lambda e, o_=o_, i_=i_: e.dma_start(out=o_, in_=i_), tl.b("st"),
                        reads=[tl.b(0)], writes=[self.scr_buf[slot]], nowait=(n_ > 0))

    def _wload_cast(self, idx):
        kind, c0 = self.wspecs[idx]
        tl = self.wbuf[idx % self.NWB]
        d = self.dram
        t3 = tl.t[:, 0:8 * 512].rearrange("p (k n) -> p k n", k=8)
        if kind == "in":
            self.load("pool", tl, 0, t3, d["w_in"][:, c0:c0 + 512].rearrange("(k p) n -> p k n", p=128))
            self.load("pool", tl, 0, tl.t[0:1, 8 * 512:9 * 512], d["b_in"][:, c0:c0 + 512], nowait=True)
        elif kind == "out":
            self.load("pool", tl, 0, t3, d["w_out"][:, c0:c0 + 512].rearrange("(k p) n -> p k n", p=128))
        elif kind in ("gate", "up"):
            w = d["w_gate"] if kind == "gate" else d["w_up"]
            n = min(512, DFF - c0)
            self.load("pool", tl, 0, t3[:, :, 0:n], w[:, c0:c0 + n].rearrange("(k p) n -> p k n", p=128))
        else:
            nch = min(4, (DFF - c0) // 128)
            t4 = tl.t[:, 0:4 * 1024].rearrange("p (c n) -> p c n", c=4)
            self.load("pool", tl, 0, t4[:, 0:nch, :],
                      d["w_down"][c0:c0 + nch * 128, :].rearrange("(c p) n -> p c n", p=128))

    def wnext(self, kind, c0, hold=0):
        i = self.wi
        assert self.wspecs[i] == (kind, c0), (self.wspecs[i], kind, c0)
        lim = min(len(self.wspecs), i + self.NWB - hold)
        if self.no_prefetch_beyond is not None:
            lim = min(lim, self.no_prefetch_beyond + 1)
        while self.wloaded < lim:
            self.wload(self.wloaded)
            self.wloaded += 1
        self.wi += 1
        return self.wbuf[i % self.NWB]

    def load_consts(self):
        d = self.dram
        fw = self.fw
        self.load("sp", self.ident_f, 0, self.ident_f.t[:], d["ident"])
        self.load("pool", self.ident_b, 0, self.ident_b.t[:], d["ident"])
        self.load("sp", self.maskT, 0, self.maskT.t[:], d["maskT"])
        self.load("sp", self.cst, 0, self.cst.t[:], d["cst"])
        self.load("pool", self.identrow, 0, self.identrow.t[:].rearrange("p a b -> p (a b)"), d["identrow"])
        fw.op("dve", lambda e: e.memset(self.ones_b.t[:], 1.0), writes=[self.ones_b.b()])
        fw.op("dve", lambda e: e.memset(self.ones_f.t[:], 1.0), writes=[self.ones_f.b()])
        fw.op("dve", lambda e: e.memset(self.Cf.t[:], 0.0), writes=self.Cf.bs(range(8)))
        fw.op("dve", lambda e: e.memset(self.mcur.t[:], 0.0), writes=[self.mcur.b()])

    def load_consts_late(self):
        d = self.dram
        for tl, nm in ((self.gbc[0], "ml_g"), (self.gbc[1], "rt_g"), (self.lnp[0], "ln1_g"),
                       (self.lnp[1], "ln1_b"), (self.lnp[2], "ln2_g"), (self.lnp[3], "ln2_b")):
            self.load("pool", tl, 0, tl.t[:], d[nm].partition_broadcast(128))

    def transpose_in(self, s, src_bufs, src_ap_fn, dst_ap, dst_bufs):
        tsz = s.tsz
        ps = self.nextps()
        pv = ps.t[:].bitcast(BF16).rearrange("p (k n) -> p k n", k=8)

        def tr(e):
            for k in range(8):
                ins = e.transpose(out=pv[:, k, 0:tsz], in_=src_ap_fn(k), identity=self.ident_b.t[0:tsz, 0:tsz])
            return ins
        self.fw.op("pe", tr, reads=list(src_bufs) + [self.ident_b.b()], writes=[ps.b()])
        self.fw.op("act", lambda e: e.copy(out=dst_ap, in_=pv[:, :, 0:tsz]), reads=[ps.b()], writes=list(dst_bufs))

    def load_x_main(self, p):
        s = self.MAIN
        d = self.dram
        for i in range(4):
            xb = self.xbf[self.xbi % 2]
            self.xbi += 1
            r0 = p * 512 + i * 128
            self.load("pool", xb, 0, xb.t[:, :], d["xp"][r0:r0 + 128, :])
            self.transpose_in(s, [xb.b()], lambda k, xb=xb: xb.t[:, k * 128:(k + 1) * 128],
                              s.xT.t[:, :, i * 128:(i + 1) * 128], [s.xT.b(i)])
        self.load("sp", s.rc, 0, s.rc.t[:], d["ropeC"][:, p * 512:(p + 1) * 512])
        self.load("sp", s.rs, 0, s.rs.t[:], d["ropeS"][:, p * 512:(p + 1) * 512])

    def load_x_mini(self):
        s = self.MINI
        d = self.dram
        xb = self.vmk[0]
        self.fw.op("dve", lambda e: e.memset(xb.t[:], 0.0), writes=[xb.b()])
        self.load("pool", xb, 0, xb.t[0:NMETA, :], d["meta"])
        self.load("pool", xb, 0, xb.t[SP0:SP0 + NS, :], d["xs"], nowait=True)
        self.transpose_in(s, [xb.b()], lambda k: xb.t[:, k * 128:(k + 1) * 128], s.xT.t[:, :, :], [s.xT.b(0)])
        self.load("sp", s.rc, 0, s.rc.t[:], d["ropeCm"])
        self.load("sp", s.rs, 0, s.rs.t[:], d["ropeSm"])

    def proj_gates(self, s):
        fw = self.fw
        gw = self.gw
        for i in range(s.ntile):
            ps = self.nextps()

            def mm(e, i=i, ps=ps):
                for k in range(8):
                    e.matmul(ps.t[0:s.tsz, 0:8], lhsT=s.xT.t[:, k, i * s.tsz:(i + 1) * s.tsz],
                             rhs=gw.t[:, k, :], start=(k == 0), stop=False)
                return e.matmul(ps.t[0:s.tsz, 0:8], lhsT=self.ones_b.t[0:1, 0:s.tsz],
                                rhs=gw.t[0:1, 8, :], start=False, stop=True)
            fw.op("pe", mm, reads=[s.xT.b(i), gw.b(), self.ones_b.b()], writes=[ps.b()])
            fw.op("act", lambda e, i=i, ps=ps: e.copy(out=s.gcol.t[:, i, :], in_=ps.t[0:s.tsz, 0:8]),
                  reads=[ps.b()], writes=[s.gcol.b()])

    def proj_fm(self, s, wt, cbase, which, scale, rot):
        fw = self.fw
        T = s.T
        w3 = wt.t[:, 0:8 * 512].rearrange("p (k n) -> p k n", k=8)
        brow = wt.t[0:1, 8 * 512:9 * 512]
        xr = s.xT.bs(range(s.ntile))
        pss = []
        for m in range(4):
            ps = self.nextps()
            pss.append(ps)

            def mm(e, m=m, ps=ps):
                for k in range(8):
                    e.matmul(ps.t[:, 0:T], lhsT=w3[:, k, m * 128:(m + 1) * 128], rhs=s.xT.t[:, k, :],
                             start=(k == 0), stop=False)
                return e.matmul(ps.t[:, 0:T], lhsT=brow[:, m * 128:(m + 1) * 128], rhs=self.ones_b.t[0:1, 0:T],
                                start=False, stop=True)
            fw.op("pe", mm, reads=xr + [wt.b(), self.ones_b.b()], writes=[ps.b()])
            if not rot:
                fw.op("act", lambda e, m=m, ps=ps: e.activation(out=self.fm(s, which, cbase + m), in_=ps.t[:, 0:T],
                                                               func=AF.Copy, scale=scale),
                      reads=[ps.b()], writes=[s.fmbig.b(8 * which + cbase + m)])
            elif m % 2 == 1:
                p1, p2 = pss[m - 1], ps
                c1, c2 = cbase + m - 1, cbase + m
                r = self.rot
                rd = [s.rc.b(), s.rs.b()]
                for q0 in range(0, T, 256):
                    n = min(256, T - q0)
                    cos, sin = s.rc.t[:, q0:q0 + n], s.rs.t[:, q0:q0 + n]
                    a1, a2 = p1.t[:, q0:q0 + n], p2.t[:, q0:q0 + n]

                    def stt(o, i0, i1):
                        return lambda e: e.scalar_tensor_tensor(out=o, in0=i0, scalar=scale, in1=i1,
                                                                op0=ALU.mult, op1=ALU.mult)
                    fw.op("dve", stt(r.t[:, 0, 0:n], a1, cos), reads=[p1.b()] + rd, writes=[r.b(0)])
                    fw.op("dve", stt(r.t[:, 1, 0:n], a2, sin), reads=[p2.b()] + rd, writes=[r.b(1)])
                    fw.op("dve", stt(r.t[:, 2, 0:n], a1, sin), reads=[p1.b()] + rd, writes=[r.b(2)])
                    fw.op("dve", stt(r.t[:, 3, 0:n], a2, cos), reads=[p2.b()] + rd, writes=[r.b(3)])
                    o1 = self.fm(s, which, c1)[:, q0:q0 + n]
                    o2 = self.fm(s, which, c2)[:, q0:q0 + n]
                    fw.op("dve", lambda e, o1=o1, n=n: e.tensor_tensor(out=o1, in0=r.t[:, 0, 0:n], in1=r.t[:, 1, 0:n],
                                                                      op=ALU.subtract),
                          reads=[r.b(0), r.b(1)], writes=[s.fmbig.b(8 * which + c1)])
                    fw.op("dve", lambda e, o2=o2, n=n: e.tensor_tensor(out=o2, in0=r.t[:, 2, 0:n], in1=r.t[:, 3, 0:n],
                                                                      op=ALU.add),
                          reads=[r.b(2), r.b(3)], writes=[s.fmbig.b(8 * which + c2)])

    def proj_tm(self, s, wt, kind, half):
        fw = self.fw
        w3 = wt.t[:, 0:8 * 512].rearrange("p (k n) -> p k n", k=8)
        brow = wt.t[0:1, 8 * 512:9 * 512]
        cs = slice(half * 512, (half + 1) * 512)
        tsz = s.tsz
        for i in range(s.ntile):
            ps = self.nextps()

            def mm(e, i=i, ps=ps):
                for k in range(8):
                    e.matmul(ps.t[0:tsz, :], lhsT=s.xT.t[:, k, i * tsz:(i + 1) * tsz], rhs=w3[:, k, :],
                             start=(k == 0), stop=False)
                return e.matmul(ps.t[0:tsz, :], lhsT=self.ones_b.t[0:1, 0:tsz], rhs=brow, start=False, stop=True)
            fw.op("pe", mm, reads=[s.xT.b(i), wt.b(), self.ones_b.b()], writes=[ps.b()])
            pin = ps.t[0:tsz, :]
            if kind in ("mv", "rv"):
                dst = self.vview(s, 0 if kind == "mv" else 1)
                fw.op("act", lambda e, dst=dst, i=i, pin=pin: e.copy(out=dst[:, i, cs], in_=pin),
                      reads=[ps.b()], writes=s.xres.bs(range(s.ntile)))
            elif kind in ("mo", "rg"):
                G = s.G[0 if kind == "mo" else 1]
                gb = self.gbc[0 if kind == "mo" else 1]
                tb = self.tmpb[self.tbi % 2]
                self.tbi += 1
                fn = AF.Sigmoid if kind == "mo" else AF.Silu
                fw.op("act", lambda e, tb=tb, pin=pin, fn=fn: e.activation(out=tb.t[0:tsz, :], in_=pin, func=fn),
                      reads=[ps.b()], writes=[tb.b()])
                fw.op("dve", lambda e, tb=tb, G=G, i=i, gb=gb: e.tensor_tensor(
                    out=G.t[:, i, cs], in0=tb.t[0:tsz, :], in1=gb.t[0:tsz, cs], op=ALU.mult),
                    reads=[tb.b(), gb.b()], writes=[G.b((i, half))])
            else:
                G = s.G[0 if kind == "ga" else 1]
                tb = self.tmpb[self.tbi % 2]
                self.tbi += 1
                fw.op("act", lambda e, tb=tb, pin=pin: e.activation(out=tb.t[0:tsz, :], in_=pin, func=AF.Sigmoid),
                      reads=[ps.b()], writes=[tb.b()])
                fw.op("dve", lambda e, tb=tb, G=G, i=i: e.tensor_tensor(
                    out=G.t[:, i, cs], in0=tb.t[0:tsz, :], in1=G.t[:, i, cs], op=ALU.mult),
                    reads=[tb.b(), G.b((i, half))], writes=[G.b((i, half))])

    def projection(self, sets):
        d = self.dram
        gw = self.gw
        if not self.gw_loaded:
            self.load("pool", gw, 0, gw.t[:, 0:8, :], d["w_in"][:, C_MI:C_MI + 8].rearrange("(k p) n -> p k n", p=128))
            self.load("pool", gw, 0, gw.t[0:1, 8, :], d["b_in"][:, C_MI:C_MI + 8], nowait=True)
            self.gw_loaded = True
        for s in sets:
            self.proj_gates(s)
        plan = [(0, "fm", 0, 0, 1.0, False), (512, "fm", 0, 4, 1.0, False),
                (1024, "fm", 1, 0, 1.0 / 16, False), (1536, "fm", 1, 4, 1.0 / 16, False),
                (2048, "tm", "mv", 0), (2560, "tm", "mv", 1), (3072, "tm", "mo", 0), (3584, "tm", "mo", 1),
                (4104, "fm", 2, 0, 1.0, True), (4616, "fm", 2, 4, 1.0, True),
                (5128, "fm", 3, 0, 1.0 / 16, True), (5640, "fm", 3, 4, 1.0 / 16, True),
                (6152, "tm", "rv", 0), (6664, "tm", "rv", 1), (7176, "tm", "rg", 0), (7688, "tm", "rg", 1),
                (8200, "tm", "ga", 0), (8712, "tm", "ga", 1), (9224, "tm", "gb", 0), (9736, "tm", "gb", 1)]
        md_banks = [[self.ps[6], self.ps[7]]]
        if self.mini_down:
            self.ps_reserved = {6, 7}
        for n_ent, ent in enumerate(plan):
            if self.pending_ln:
                self.pending_ln.pop(0)()
            if self.mini_down and n_ent == 6:
                self.down(self.mini_down, banks=md_banks, blocks="finish")
                self.mini_down = None
                self.ps_reserved = set()
            wt = self.wnext("in", ent[0])
            for s in sets:
                if ent[1] == "fm":
                    self.proj_fm(s, wt, ent[3], ent[2], ent[4], ent[5])
                else:
                    self.proj_tm(s, wt, ent[2], ent[3])
            if ent[0] == 1536:
                for s in sets:
                    if s is not self.MAIN:
                        for _ in self.gate_math(s):
                            pass
                gens = [self.gate_math(self.MAIN)]
            if ent[0] in (1536, 2048, 2560):
                for g_ in gens:
                    next(g_, None)
            if self.mini_down and n_ent < 6:
                self.down(self.mini_down, banks=md_banks, blocks=n_ent * 512)

    def gate_math(self, s):
        fw = self.fw
        main = s is self.MAIN
        L = 128 if main else NMETA
        nt = s.ntile
        gm, gcol = self.gm, s.gcol
        one = self.cst.t[0:L, K_ONE:K_ONE + 1]
        slot0 = 0 if main else 4
        o = s.gofs = (24 if main else 32)
        fw.op("act", lambda e: e.activation(out=gm.t[0:L, 0:nt, 20:24], in_=gcol.t[0:L, :, 4:8], func=AF.Exp, scale=-1.0),
              reads=[gcol.b()], writes=[gm.b()])
        fw.op("act", lambda e: e.activation(out=gm.t[0:L, 0:nt, 0:4], in_=gm.t[0:L, 0:nt, 20:24], func=AF.Ln,
                                            bias=one, scale=1.0),
              reads=[gm.b(), self.cst.b()], writes=[gm.b()])
        fw.op("dve", lambda e: e.tensor_scalar(out=gm.t[0:L, 0:nt, 0:4], in0=gm.t[0:L, 0:nt, 0:4], scalar1=-1.0,
                                               scalar2=None, op0=ALU.mult),
              reads=[gm.b()], writes=[gm.b()])
        gsm = self.gsm
        for i in range(nt):
            ps = self.nextps()

            def mmc(e, i=i, ps=ps):
                e.matmul(ps.t[0:L, 0:4], lhsT=self.maskT.t[0:L, 0:L], rhs=gm.t[0:L, i, 0:4], start=True, stop=True)
                return e.matmul(ps.t[0:4, 8:9], lhsT=gm.t[0:L, i, 0:4], rhs=self.ones_f.t[0:L, 0:1], start=True, stop=True)
            fw.op("pe", mmc, reads=[self.maskT.b(), gm.b(), self.ones_f.b()], writes=[ps.b()])
            fw.op("dve", lambda e, i=i, ps=ps: e.tensor_copy(out=gm.t[0:L, i, 4:8], in_=ps.t[0:L, 0:4]),
                  reads=[ps.b()], writes=[gm.b()])
            fw.op("dve", lambda e, i=i, ps=ps: e.tensor_copy(out=gsm.t[:, 4 + i:5 + i], in_=ps.t[0:4, 8:9]),
                  reads=[ps.b()], writes=[gsm.b()])
        fw.op("dve", lambda e: e.tensor_tensor(out=gm.t[0:L, 0:nt, 8:12], in0=gcol.t[0:L, :, 0:4],
                                               in1=gm.t[0:L, 0:nt, 4:8], op=ALU.subtract),
              reads=[gm.b(), gcol.b()], writes=[gm.b()])
        yield
        ps = self.nextps()

        def tr(e, ps=ps):
            for i in range(nt):
                ins = e.transpose(out=ps.t[0:4, i * L:(i + 1) * L], in_=gm.t[0:L, i, 8:12],
                                  identity=self.ident_f.t[0:L, 0:L])
            return ins
        fw.op("pe", tr, reads=[gm.b(), self.ident_f.b()], writes=[ps.b()])
        fw.op("dve", lambda e, ps=ps: e.tensor_reduce(out=gsm.t[:, 0:nt],
                                                      in_=ps.t[0:4, 0:nt * L].rearrange("p (c l) -> p c l", l=L),
                                                      axis=AX.X, op=ALU.max),
              reads=[ps.b()], writes=[gsm.b()])
        for c in range(nt):
            fw.op("dve", lambda e, c=c: e.tensor_copy(out=gsm.t[:, 9 + 2 * c:10 + 2 * c], in_=self.mcur.t[:]),
                  reads=[self.mcur.b(), gsm.b()], writes=[gsm.b()])
            fw.op("dve", lambda e, c=c: e.tensor_tensor(out=gsm.t[:, 8 + 2 * c:9 + 2 * c], in0=self.mcur.t[:],
                                                        in1=gsm.t[:, c:c + 1], op=ALU.max),
                  reads=[self.mcur.b(), gsm.b()], writes=[gsm.b()])
            fw.op("dve", lambda e, c=c: e.tensor_tensor(out=self.mcur.t[:], in0=gsm.t[:, 8 + 2 * c:9 + 2 * c],
                                                        in1=gsm.t[:, 4 + c:5 + c], op=ALU.add),
                  reads=[gsm.b(), self.mcur.b()], writes=[self.mcur.b()])
        dg = gsm.t[:, 24:24 + 8 * nt].rearrange("p (c h) -> p c h", h=4)
        fw.op("dve", lambda e: e.tensor_tensor(
            out=dg, in0=self.ident_f.t[0:4, 0:4].unsqueeze(1).to_broadcast([4, 2 * nt, 4]),
            in1=gsm.t[:, 8:8 + 2 * nt].unsqueeze(2).to_broadcast([4, 2 * nt, 4]), op=ALU.mult),
            reads=[gsm.b(), self.ident_f.b()], writes=[gsm.b()])
        yield
        ps = self.nextps()
        fw.op("pe", lambda e, ps=ps: e.matmul(ps.t[:, 0:8 * nt], lhsT=self.ones_f.t[0:4, :],
                                            rhs=gsm.t[:, 24:24 + 8 * nt], start=True, stop=True),
              reads=[gsm.b(), self.ones_f.b()], writes=[ps.b()])
        gb = self.gbcst
        fw.op("dve", lambda e, ps=ps: e.tensor_copy(out=gb.t[:, slot0:slot0 + nt, :],
                                                   in_=ps.t[:, 0:8 * nt].rearrange("p (c k) -> p c k", k=8)),
              reads=[ps.b()], writes=[gb.b()])
        wcb = self.wcb
        fw.op("dve", lambda e: e.tensor_tensor(out=wcb.t[:, slot0:slot0 + nt, :], in0=gb.t[:, slot0:slot0 + nt, 4:8],
                                               in1=gb.t[:, slot0:slot0 + nt, 0:4], op=ALU.subtract),
              reads=[gb.b(), wcb.b()], writes=[wcb.b()])
        fw.op("act", lambda e: e.activation(out=wcb.t[:, slot0:slot0 + nt, :], in_=wcb.t[:, slot0:slot0 + nt, :],
                                            func=AF.Exp),
              reads=[wcb.b()], writes=[wcb.b()])
        fw.op("dve", lambda e: e.tensor_tensor(out=gm.t[0:L, 0:nt, 12:16], in0=gm.t[0:L, 0:nt, 8:12],
                                               in1=gb.t[0:L, slot0:slot0 + nt, 0:4], op=ALU.subtract),
              reads=[gm.b(), gb.b()], writes=[gm.b()])
        fw.op("dve", lambda e: e.tensor_tensor(out=gm.t[0:L, 0:nt, 16:20], in0=gm.t[0:L, 0:nt, 4:8],
                                               in1=gb.t[0:L, slot0:slot0 + nt, 0:4], op=ALU.add),
              reads=[gm.b(), gb.b()], writes=[gm.b()])
        fw.op("act", lambda e: e.activation(out=gm.t[0:L, 0:nt, o:o + 4], in_=gm.t[0:L, 0:nt, 12:16], func=AF.Exp),
              reads=[gm.b()], writes=[gm.b()])
        fw.op("act", lambda e: e.activation(out=gm.t[0:L, 0:nt, o + 4:o + 8], in_=gm.t[0:L, 0:nt, 16:20], func=AF.Exp,
                                            scale=-1.0),
              reads=[gm.b()], writes=[gm.b()])

    def rstd(self, out_ap, in_ap, tl):
        fw = self.fw
        fw.op("act", lambda e: e.activation(out=out_ap, in_=in_ap, func=AF.Ln), reads=[tl.b()], writes=[tl.b()])
        fw.op("act", lambda e: e.activation(out=out_ap, in_=out_ap, func=AF.Exp, scale=-0.5),
              reads=[tl.b()], writes=[tl.b()])

    def gate_aps(self, s, mix, h, c, L):
        cst = self.cst
        if mix == 0:
            slot0 = 0 if s is self.MAIN else 4
            return (self.gm.t[0:L, c, s.gofs + h:s.gofs + h + 1], self.wcb.t[:, slot0 + c, h:h + 1],
                    [self.gm.b(), self.wcb.b()])
        ke, kw = (K_ES128, K_WC128) if L == 128 else (K_ES16, K_WC16)
        return cst.t[0:L, ke + h:ke + h + 1], cst.t[:, kw + h:kw + h + 1], [cst.b()]

    def make_cb(self, s, hm, c, L):
        fw = self.fw
        Cb, Cf = self.Cb, self.Cf
        _, wc, rd_g = self.gate_aps(s, hm // 4, hm % 4, c, L)
        fw.op("act", lambda e: e.activation(out=Cb.t[:, hm, :, :], in_=Cf.t[:, hm, :, :], func=AF.Copy, scale=wc),
              reads=[Cf.b(hm)] + rd_g, writes=[Cb.b(hm)])

    def mixers(self, s):
        fw = self.fw
        main = s is self.MAIN
        L = 128 if main else NMETA
        nt = s.ntile
        cst = self.cst
        Cb, Cf = self.Cb, self.Cf
        st6, mv, u = self.st6, self.mv, self.u
        for hm in range(8):
            self.make_cb(s, hm, 0, L)

        def prologue(c, g):
            t0 = c * L
            gi = self.mgi % 2
            self.mgi += 1
            P = TokSet()
            P.hms = hms = [2 * g, 4 + 2 * g, 2 * g + 1, 4 + 2 * g + 1]
            pT, pS = self.ps[4 + gi], self.ps[6 + gi]
            P.ktm, P.vp, P.stm = ktm, vp, stm = self.ktm[gi], self.vp[gi], self.stm[gi]
            pv = pT.t[:].bitcast(BF16)
            P.qa, P.qb = qa, qb = {}, {}
            ka, kb = {}, {}
            for hm in hms:
                mix, h = hm // 4, hm % 4
                qa[hm] = [self.fm(s, 2 * mix, 2 * h + j)[:, t0:t0 + L] for j in range(2)]
                ka[hm] = [self.fm(s, 2 * mix + 1, 2 * h + j)[:, t0:t0 + L] for j in range(2)]
                qb[hm] = s.fmbig.bs([16 * mix + 2 * h, 16 * mix + 2 * h + 1])
                kb[hm] = s.fmbig.bs([16 * mix + 8 + 2 * h, 16 * mix + 8 + 2 * h + 1])
            allq = [b for hm in hms for b in qb[hm]]
            allk = [b for hm in hms for b in kb[hm]]

            def trk(e):
                for k_, hm in enumerate(hms):
                    for j in range(2):
                        ins = e.transpose(out=pv[0:L, k_ * 256 + j * 128:k_ * 256 + (j + 1) * 128], in_=ka[hm][j],
                                          identity=self.ident_b.t[:])
                return ins
            fw.op("pe", trk, reads=allk + [self.ident_b.b()], writes=[pT.b()])

            def mms(e):
                for k_, hm in enumerate(hms):
                    for j in range(2):
                        ins = e.matmul(pS.t[0:L, k_ * 128:k_ * 128 + L], lhsT=ka[hm][j], rhs=qa[hm][j],
                                       start=(j == 0), stop=(j == 1))
                return ins
            fw.op("pe", mms, reads=allk + allq, writes=[pS.b()])
            fw.op("act", lambda e: e.copy(out=ktm.t[0:L, :, :], in_=pv[0:L, :].rearrange("p (k n) -> p k n", k=4)),
                  reads=[pT.b()], writes=[ktm.b()])
            fw.op("act", lambda e: e.copy(out=stm.t[0:L, :, 0:L],
                                          in_=pS.t[0:L, :].rearrange("p (k n) -> p k n", k=4)[:, :, 0:L]),
                  reads=[pS.b()], writes=[stm.b()])
            fw.op("pool", lambda e: e.tensor_tensor(
                out=stm.t[0:L, :, 0:L], in0=stm.t[0:L, :, 0:L],
                in1=self.maskT.t[0:L, 0:L].unsqueeze(1).to_broadcast([L, 4, L]), op=ALU.mult),
                reads=[stm.b(), self.maskT.b()], writes=[stm.b()])
            for k_, hm in enumerate(hms):
                mix, h = hm // 4, hm % 4
                es, wc, rd_g = self.gate_aps(s, mix, h, c, L)
                vsrc = self.vview(s, mix)
                fw.op("act", lambda e, vsrc=vsrc, es=es, h=h, k_=k_: e.activation(
                    out=vp.t[0:L, k_, 0:256], in_=vsrc[0:L, c, h * 256:(h + 1) * 256], func=AF.Copy, scale=es),
                    reads=[s.xres.b(c)] + rd_g, writes=[vp.b()])
                fw.op("act", lambda e, es=es, k_=k_: e.copy(out=vp.t[0:L, k_, 256:257], in_=es),
                      reads=rd_g + [vp.b()], writes=[vp.b()])
            return P

        def body(c, g, P):
            deferred = []
            ktm, vp, stm = P.ktm, P.vp, P.stm
            for k_, hm in enumerate(P.hms):
                mix, h = hm // 4, hm % 4
                es, wc, rd_g = self.gate_aps(s, mix, h, c, L)
                pA = self.ps[(k_ % 2) * 2]
                pB = self.ps[(k_ % 2) * 2 + 1]
                qa = P.qa[hm]

                def mmn(e, pA=pA, qa=qa, hm=hm, k_=k_):
                    for j in range(2):
                        e.matmul(pA.t[0:L, 128:385], lhsT=qa[j], rhs=Cb.t[:, hm, j, :], start=(j == 0), stop=False)
                    return e.matmul(pA.t[0:L, 128:385], lhsT=stm.t[0:L, k_, 0:L], rhs=vp.t[0:L, k_, :],
                                    start=False, stop=True)
                fw.op("pe", mmn, reads=P.qb[hm] + [Cb.b(hm), stm.b(), vp.b()], writes=[pA.b()])

                def mmp(e, pB=pB, pA=pA, k_=k_):
                    for j in range(2):
                        e.matmul(pB.t[:, j * 256:(j + 1) * 256], lhsT=ktm.t[0:L, k_, j * 128:(j + 1) * 128],
                                 rhs=vp.t[0:L, k_, 0:256], start=True, stop=True)
                    for j in range(2):
                        ins = e.matmul(pA.t[:, 400 + j:401 + j], lhsT=ktm.t[0:L, k_, j * 128:(j + 1) * 128],
                                       rhs=vp.t[0:L, k_, 256:257], start=True, stop=True)
                    return ins
                fw.op("pe", mmp, reads=[ktm.b(), vp.b()], writes=[pB.b(), pA.b()])
                fw.op("dve", lambda e, pA=pA, hm=hm: e.bn_stats(out=st6.t[0:L, hm, :], in_=pA.t[0:L, 128:384]),
                      reads=[pA.b()], writes=[st6.b(hm)])
                fw.op("dve", lambda e, hm=hm: e.bn_aggr(out=mv.t[0:L, hm, :], in_=st6.t[0:L, hm, :]),
                      reads=[st6.b(hm)], writes=[mv.b(hm)])
                if mix == 0:
                    fw.op("dve", lambda e, pA=pA, h=h: e.tensor_copy(out=self.den.t[0:L, h:h + 1], in_=pA.t[0:L, 384:385]),
                          reads=[pA.b()], writes=[self.den.b(h)])
                G = s.G[mix]
                fw.op("dve", lambda e, pA=pA, hm=hm, G=G, h=h: e.scalar_tensor_tensor(
                    out=u.t[0:L, hm, :], in0=pA.t[0:L, 128:384], scalar=mv.t[0:L, hm, 0:1],
                    in1=G.t[0:L, c, h * 256:(h + 1) * 256], op0=ALU.subtract, op1=ALU.mult),
                    reads=[pA.b(), mv.b(hm), G.b((c, h // 2))], writes=[u.b(hm)])
                fw.op("dve", lambda e, pB=pB, hm=hm, wc=wc: e.scalar_tensor_tensor(
                    out=Cf.t[:, hm, :, 0:256], in0=Cf.t[:, hm, :, 0:256], scalar=wc,
                    in1=pB.t[:, :].rearrange("p (j n) -> p j n", j=2), op0=ALU.mult, op1=ALU.add),
                    reads=[pB.b(), Cf.b(hm)] + rd_g, writes=[Cf.b(hm)])
                fw.op("dve", lambda e, pA=pA, hm=hm, wc=wc: e.scalar_tensor_tensor(
                    out=Cf.t[:, hm, :, 256], in0=Cf.t[:, hm, :, 256], scalar=wc, in1=pA.t[:, 400:402],
                    op0=ALU.mult, op1=ALU.add),
                    reads=[pA.b(), Cf.b(hm)] + rd_g, writes=[Cf.b(hm)])
                if c + 1 < nt:
                    deferred.append(lambda hm=hm: self.make_cb(s, hm, c + 1, L))
            return deferred

        def pm(c, g):
            lowb = self.gm.t[0:L, c, s.gofs + 4:s.gofs + 8]
            keps = K_EPS128 if L == 128 else K_EPS16
            self.post_merge(s, L, 0, c, lowb, cst.t[0:L, keps:keps + 4], [self.gm.b(), cst.b()], 2 * g, 2 * g + 2)

        steps = [(c, g) for c in range(nt) for g in range(2)]
        P_next = prologue(*steps[0])
        pending = []
        prev = None
        for idx, (c, g) in enumerate(steps):
            P_cur = P_next
            if idx + 1 < len(steps):
                P_next = prologue(*steps[idx + 1])
            for f in pending:
                f()
            pending = body(c, g, P_cur)
            if prev is not None:
                pm(*prev)
            prev = (c, g)
        pm(*prev)
        for f in pending:
            f()

    def post_merge(self, s, L, p0, c, lowb, epsr, rd, h0=0, h1=4):
        fw = self.fw
        sm = self.sm
        R = slice(p0, p0 + L)
        H = slice(h0, h1)
        nh = h1 - h0
        fw.op("dve", lambda e: e.scalar_tensor_tensor(out=sm.t[R, 4 + h0:4 + h1], in0=self.den.t[R, H], scalar=-1.0,
                                                      in1=self.den.t[R, H], op0=ALU.mult, op1=ALU.max),
              reads=self.den.bs(range(h0, h1)) + [sm.b()], writes=[sm.b()])
        fw.op("dve", lambda e: e.tensor_tensor(out=sm.t[R, H], in0=sm.t[R, 4 + h0:4 + h1], in1=lowb[:, H], op=ALU.max),
              reads=[sm.b()] + rd, writes=[sm.b()])
        fw.op("dve", lambda e: e.scalar_tensor_tensor(out=sm.t[R, 8 + h0:8 + h1], in0=sm.t[R, H], scalar=LN_EPS,
                                                      in1=sm.t[R, H], op0=ALU.mult, op1=ALU.mult),
              reads=[sm.b()], writes=[sm.b()])
        fw.op("dve", lambda e: e.tensor_tensor(out=sm.t[R, 16 + h0:16 + h1], in0=sm.t[R, 8 + h0:8 + h1],
                                               in1=self.mv.t[R, H, 1], op=ALU.add),
              reads=[sm.b()] + self.mv.bs(range(h0, h1)), writes=[sm.b()])
        fw.op("dve", lambda e: e.tensor_tensor(out=sm.t[R, 20 + h0:20 + h1], in0=epsr[:, H],
                                               in1=self.mv.t[R, 4 + h0:4 + h1, 1], op=ALU.add),
              reads=[sm.b()] + rd + self.mv.bs(range(4 + h0, 4 + h1)), writes=[sm.b()])
        vin = sm.t[R, 16:24].rearrange("p (m h) -> p m h", m=2)[:, :, H]
        vout = sm.t[R, 24:32].rearrange("p (m h) -> p m h", m=2)[:, :, H]
        self.rstd(vout, vin, sm)
        ta = self.tmpa
        u = self.u
        for h in range(h0, h1):
            fw.op("pool", lambda e, h=h: e.tensor_scalar(out=ta.t[R, :], in0=u.t[R, h, :], scalar1=sm.t[R, 24 + h:25 + h],
                                                        scalar2=0.0, op0=ALU.mult, op1=ALU.add),
                  reads=[u.b(h), sm.b()], writes=[ta.b()])
            fw.op("pool", lambda e, h=h: e.tensor_scalar(out=u.t[R, 4 + h, :], in0=u.t[R, 4 + h, :],
                                                        scalar1=sm.t[R, 28 + h:29 + h], scalar2=0.0,
                                                        op0=ALU.mult, op1=ALU.add),
                  reads=[u.b(4 + h), sm.b()], writes=[u.b(4 + h)])
            fw.op("pool", lambda e, h=h: e.tensor_tensor(out=s.merged.t[R, c, h * 256:(h + 1) * 256], in0=ta.t[R, :],
                                                        in1=u.t[R, 4 + h, :], op=ALU.add),
                  reads=[u.b(4 + h), ta.b()], writes=[s.merged.b(c)])

    def store_prompt_state(self):
        d = self.dram
        Cf = self.Cf
        for h in range(4):
            self.store("sp", Cf, h, d["pC"][h].rearrange("(j p) e -> p j e", p=128), Cf.t[:, h, :, 0:256])
            self.store("sp", Cf, 4 + h, d["pS"][h].rearrange("(j p) e -> p j e", p=128), Cf.t[:, 4 + h, :, 0:256])
        tok = self.fw.dma("sp", lambda e: e.dma_start(out=d["pn"].rearrange("h (j p) -> p h j", p=128),
                                                      in_=Cf.t[:, 0:4, :, 256], allow_slow_non_contiguous=True),
                          Cf.b("o"), reads=Cf.bs(range(4)))
        self.fw.out_toks.append(tok)
        self.store("sp", self.mcur, 0, d["pm"], self.mcur.t[:])

    def sample_mixers(self):
        fw = self.fw
        s = self.MINI
        d = self.dram
        cst = self.cst
        R = slice(SP0, SP0 + NS)
        ssm = self.s_sm
        gcol = s.gcol
        def tm_transposes(w):
            ps = self.nextps()
            pv = ps.t[:].bitcast(BF16)

            def tr(e, w=w, pv=pv):
                for k in range(8):
                    ins = e.transpose(out=pv[0:64, k * 128:(k + 1) * 128], in_=self.fm(s, w, k)[:, 0:64],
                                      identity=self.ident_b.t[:])
                return ins
            fw.op("pe", tr, reads=s.fmbig.bs(range(8 * w, 8 * w + 8)) + [self.ident_b.b()], writes=[ps.b()])
            fw.op("act", lambda e, w=w, pv=pv: e.copy(out=self.s_tm[w].t[:, :], in_=pv[0:64, :]),
                  reads=[ps.b()], writes=[self.s_tm[w].b()])
        tm_transposes(0)
        tm_transposes(1)
        self.load("sp", self.s_n, 0, self.s_n.t[R, :], d["sn"])
        self.load("sp", ssm, "m", ssm.t[R, 0:4], d["sm"])
        one = cst.t[R, K_ONE:K_ONE + 1]
        sb_ = [ssm.b(), ssm.b("m")]
        fw.op("act", lambda e: e.activation(out=ssm.t[R, 28:32], in_=gcol.t[R, 0, 4:8], func=AF.Exp, scale=-1.0),
              reads=[gcol.b()] + sb_, writes=[ssm.b()])
        fw.op("act", lambda e: e.activation(out=ssm.t[R, 4:8], in_=ssm.t[R, 28:32], func=AF.Ln, bias=one, scale=1.0),
              reads=[ssm.b(), cst.b()], writes=[ssm.b()])
        fw.op("dve", lambda e: e.tensor_tensor(out=ssm.t[R, 8:12], in0=ssm.t[R, 0:4], in1=ssm.t[R, 4:8], op=ALU.subtract),
              reads=sb_, writes=[ssm.b()])
        fw.op("dve", lambda e: e.tensor_tensor(out=ssm.t[R, 12:16], in0=ssm.t[R, 8:12], in1=gcol.t[R, 0, 0:4], op=ALU.max),
              reads=[ssm.b(), gcol.b()], writes=[ssm.b()])
        fw.op("dve", lambda e: e.tensor_tensor(out=ssm.t[R, 16:20], in0=gcol.t[R, 0, 0:4], in1=ssm.t[R, 12:16],
                                               op=ALU.subtract),
              reads=[ssm.b(), gcol.b()], writes=[ssm.b()])
        fw.op("dve", lambda e: e.tensor_tensor(out=ssm.t[R, 20:24], in0=ssm.t[R, 8:12], in1=ssm.t[R, 12:16],
                                               op=ALU.subtract),
              reads=[ssm.b()], writes=[ssm.b()])
        fw.op("act", lambda e: e.activation(out=ssm.t[R, 16:24], in_=ssm.t[R, 16:24], func=AF.Exp),
              reads=[ssm.b()], writes=[ssm.b()])
        fw.op("act", lambda e: e.activation(out=ssm.t[R, 24:28], in_=ssm.t[R, 12:16], func=AF.Exp, scale=-1.0),
              reads=[ssm.b()], writes=[ssm.b()])
        self.store("sp", ssm, 0, d["om"], ssm.t[R, 12:16])
        big = self.u.t[0:64, :, :].bitcast(F32).rearrange("p a n -> p (a n)")
        bigb = self.u.bs(range(8))
        for (a_, b_, col, bt) in ((self.s_tm[0], self.s_tm[1], 32, None), (self.s_tm[0], self.s_n, 36, None),
                                  (self.s_tm[2], self.s_tm[3], 40, None)):
            if col == 40:
                tm_transposes(2)
                tm_transposes(3)
            fw.op("dve", lambda e, a_=a_, b_=b_: e.tensor_tensor(out=big[R, :], in0=a_.t[R, :], in1=b_.t[R, :], op=ALU.mult),
                  reads=[a_.b(), b_.b()], writes=bigb)
            fw.op("dve", lambda e, col=col: e.tensor_reduce(out=ssm.t[R, col:col + 4],
                                                           in_=big[R, :].rearrange("p (h n) -> p h n", h=4),
                                                           axis=AX.X, op=ALU.add),
                  reads=bigb + [ssm.b()], writes=[ssm.b()])
        fw.op("dve", lambda e: e.tensor_tensor(out=ssm.t[R, 44:48], in0=ssm.t[R, 32:36], in1=ssm.t[R, 16:20], op=ALU.mult),
              reads=[ssm.b()], writes=[ssm.b()])
        fw.op("dve", lambda e: e.tensor_tensor(out=ssm.t[R, 48:52], in0=ssm.t[R, 36:40], in1=ssm.t[R, 20:24], op=ALU.mult),
              reads=[ssm.b()], writes=[ssm.b()])
        fw.op("dve", lambda e: e.tensor_tensor(out=self.den.t[R, :], in0=ssm.t[R, 48:52], in1=ssm.t[R, 44:48], op=ALU.add),
              reads=[ssm.b()], writes=self.den.bs(range(4)))
        kp = self.s_tm[1]
        for h in range(4):
            hs = slice(h * 256, (h + 1) * 256)
            fw.op("act", lambda e, h=h, hs=hs: e.activation(out=kp.t[R, hs], in_=self.s_tm[1].t[R, hs], func=AF.Copy,
                                                           scale=ssm.t[R, 16 + h:17 + h]),
                  reads=[self.s_tm[1].b(), ssm.b()], writes=[kp.b()])
            fw.op("dve", lambda e, h=h, hs=hs: e.scalar_tensor_tensor(
                out=self.s_n.t[R, hs], in0=self.s_n.t[R, hs], scalar=ssm.t[R, 20 + h:21 + h], in1=kp.t[R, hs],
                op0=ALU.mult, op1=ALU.add),
                reads=[self.s_n.b(), ssm.b(), kp.b()], writes=[self.s_n.b()])
        self.store("sp", self.s_n, 0, d["on"], self.s_n.t[R, :])
        dg = self.s_dg
        idr = self.ident_f.t[R, SP0:SP0 + NS]
        fw.op("dve", lambda e: e.tensor_tensor(
            out=dg.t[R, :, 0:4], in0=idr.unsqueeze(2).to_broadcast([NS, NS, 4]),
            in1=ssm.t[R, 20:24].unsqueeze(1).to_broadcast([NS, NS, 4]), op=ALU.mult),
            reads=[ssm.b(), self.ident_f.b()], writes=[dg.b()])
        fw.op("dve", lambda e: e.tensor_tensor(
            out=dg.t[R, :, 4:8], in0=idr.unsqueeze(2).to_broadcast([NS, NS, 4]),
            in1=cst.t[R, K_G1:K_G1 + 4].unsqueeze(1).to_broadcast([NS, NS, 4]), op=ALU.mult),
            reads=[cst.b(), self.ident_f.b(), dg.b()], writes=[dg.b()])
        ps = self.nextps()
        fw.op("pe", lambda e, ps=ps: e.matmul(ps.t[:, 0:128], lhsT=self.ones_f.t[R, :],
                                            rhs=dg.t[R, :, :].rearrange("p a b -> p (a b)"), start=True, stop=True),
              reads=[dg.b(), self.ones_f.b()], writes=[ps.b()])
        fw.op("dve", lambda e, ps=ps: e.tensor_copy(out=self.s_wb.t[:, :], in_=ps.t[:, 0:128]),
              reads=[ps.b()], writes=[self.s_wb.b()])
        ci = 0
        for mix in range(2):
            src, dst = (d["sC"], d["oC"]) if mix == 0 else (d["sS"], d["oS"])
            kpt = kp if mix == 0 else self.s_tm[3]
            vsrc = self.vview(s, mix)
            acc = [self.ps[4 + h] for h in range(4)]
            for b in range(NS):
                qm, vm = (self.qm[0], self.vmk[0]) if b % 2 == 0 else (self.qm2, self.vm2)
                first2 = (mix == 0 and b == 1)
                xw = [self.wbuf[-1].b(0)] if first2 else []
                fw.op("dve", lambda e, qm=qm, b=b, mix=mix: e.tensor_tensor(
                    out=qm.t[:, :, :], in0=s.fmbig.t[:, 16 * mix:16 * mix + 8, :],
                    in1=self.identrow.t[:, b, :].unsqueeze(1).to_broadcast([128, 8, 64]), op=ALU.mult),
                    reads=s.fmbig.bs(range(16 * mix, 16 * mix + 8)) + [self.identrow.b()], writes=[qm.b()], extra=xw)
                fw.op("act", lambda e, vm=vm, b=b, vsrc=vsrc: e.activation(
                    out=vm.t[R, :], in_=vsrc[R, 0, :], func=AF.Copy, scale=self.ident_f.t[R, SP0 + b:SP0 + b + 1]),
                    reads=[s.xres.b(0), self.ident_f.b()], writes=[vm.b()], extra=xw)
                for h in range(4):
                    cin = self.cin[ci % len(self.cin)]
                    cbf = self.cbf[ci % 2]
                    first = ci < len(self.cin) and cin.name in self.cin_owner
                    ci += 1
                    self.fw.dma("sp", lambda e, cin=cin, b=b, h=h, src=src: e.dma_start(
                        out=cin.t[:, :, :], in_=src[b, h].rearrange("(j p) e -> p j e", p=128)),
                        cin.b(0), writes=[cin.b(0)], extra=[self.cin_owner[cin.name].b(0)] if first else ())
                    fw.op("act", lambda e, cin=cin, cbf=cbf: e.copy(out=cbf.t[:, :, :], in_=cin.t[:, :, :]),
                          reads=[cin.b()], writes=[cbf.b()])

                    def mmq(e, qm=qm, cbf=cbf, h=h, b=b):
                        for j in range(2):
                            ins = e.matmul(acc[h].t[0:64, 0:256], lhsT=qm.t[:, 2 * h + j, :], rhs=cbf.t[:, j, :],
                                           start=(b == 0 and j == 0), stop=(b == NS - 1 and j == 1))
                        return ins
                    fw.op("pe", mmq, reads=[qm.b(), cbf.b()], writes=[acc[h].b()])
                    pr = self.ps[ci % 4]

                    def mmr(e, pr=pr, kpt=kpt, vm=vm, h=h):
                        for j in range(2):
                            ins = e.matmul(pr.t[:, j * 256:(j + 1) * 256],
                                           lhsT=kpt.t[R, h * 256 + j * 128:h * 256 + (j + 1) * 128],
                                           rhs=vm.t[R, h * 256:(h + 1) * 256], start=True, stop=True)
                        return ins
                    fw.op("pe", mmr, reads=[kpt.b(), vm.b()], writes=[pr.b()])
                    wcol = b * 8 + mix * 4 + h
                    fw.op("dve", lambda e, cin=cin, pr=pr, wcol=wcol: e.scalar_tensor_tensor(
                        out=cin.t[:, :, :], in0=cin.t[:, :, :], scalar=self.s_wb.t[:, wcol:wcol + 1],
                        in1=pr.t[:, :].rearrange("p (j n) -> p j n", j=2), op0=ALU.mult, op1=ALU.add),
                        reads=[cin.b(), pr.b(), self.s_wb.b()], writes=[cin.b()])
                    self.store("pool", cin, 0, dst[b, h].rearrange("(j p) e -> p j e", p=128), cin.t[:, :, :],
                               semkey="st")
            for h in range(4):
                hm = 4 * mix + h
                ta = self.tmpa
                scol = (44 if mix == 0 else 40) + h
                fw.op("act", lambda e, h=h, scol=scol, vsrc=vsrc: e.activation(
                    out=ta.t[R, :], in_=vsrc[R, 0, h * 256:(h + 1) * 256], func=AF.Copy,
                    scale=ssm.t[R, scol:scol + 1]),
                    reads=[s.xres.b(0), ssm.b()], writes=[ta.b()])
                wsc = ssm.t[R, 20 + h:21 + h] if mix == 0 else cst.t[R, K_G1 + h:K_G1 + h + 1]
                num = self.s_num
                fw.op("dve", lambda e, h=h, wsc=wsc: e.scalar_tensor_tensor(
                    out=num.t[R, :], in0=acc[h].t[R, 0:256], scalar=wsc, in1=ta.t[R, :], op0=ALU.mult, op1=ALU.add),
                    reads=[acc[h].b(), ta.b(), ssm.b(), cst.b()], writes=[num.b()])
                fw.op("dve", lambda e, hm=hm: e.bn_stats(out=self.st6.t[R, hm, :], in_=num.t[R, :]),
                      reads=[num.b()], writes=[self.st6.b(hm)])
                fw.op("dve", lambda e, hm=hm: e.bn_aggr(out=self.mv.t[R, hm, :], in_=self.st6.t[R, hm, :]),
                      reads=[self.st6.b(hm)], writes=[self.mv.b(hm)])
                G = s.G[mix]
                fw.op("dve", lambda e, hm=hm, h=h, G=G: e.scalar_tensor_tensor(
                    out=self.u.t[R, hm, :], in0=num.t[R, :], scalar=self.mv.t[R, hm, 0:1],
                    in1=G.t[R, 0, h * 256:(h + 1) * 256], op0=ALU.subtract, op1=ALU.mult),
                    reads=[num.b(), self.mv.b(hm), G.b((0, h // 2))], writes=[self.u.b(hm)])
        for tl in self.borrowed2:
            wb = self.wbuf[-1].b(0)
            for k_, tok in tl.b(0).r.items():
                wb.r["smp_%s_%s" % (tl.name, k_)] = tok
            if tl.b(0).w is not None:
                wb.r["smpw_" + tl.name] = tl.b(0).w
        for tl in self.cin:
            if tl.name in self.cin_owner:
                wb = self.cin_owner[tl.name].b(0)
                for k_, tok in tl.b(0).r.items():
                    wb.r["smp_%s_%s" % (tl.name, k_)] = tok
                if tl.b(0).w is not None:
                    wb.r["smpw_" + tl.name] = tl.b(0).w
        self.post_merge(s, NS, SP0, 0, ssm.t[R, 24:28], cst.t[R, K_EPS1:K_EPS1 + 4], [ssm.b(), cst.b()])

    def ln_evac(self, s, i, pss):
        fw = self.fw
        tsz = s.tsz
        xr = s.xres
        z = xr.t[:, i, :]
        for hf in range(2):
            cs = slice(hf * 512, (hf + 1) * 512)
            fw.op("dve", lambda e, hf=hf, cs=cs: e.scalar_tensor_tensor(
                out=z[:, cs], in0=z[:, cs], scalar=ALPHA, in1=pss[hf].t[0:tsz, :], op0=ALU.mult, op1=ALU.add),
                reads=[pss[hf].b(), xr.b(i)], writes=[xr.b(i)])

    def ln_tile(self, s, i, g, b):
        fw = self.fw
        tsz = s.tsz
        xr = s.xres
        z = xr.t[:, i, :]
        lmv = self.lmv
        for hf in range(2):
            cs = slice(hf * 512, (hf + 1) * 512)
            fw.op("dve", lambda e, hf=hf, cs=cs: e.bn_stats(out=self.lst.t[0:tsz, hf, :], in_=z[:, cs]),
                  reads=[xr.b(i)], writes=[self.lst.b()])
        fw.op("dve", lambda e: e.bn_aggr(out=lmv.t[0:tsz, 0:2], in_=self.lst.t[0:tsz, :, :].rearrange("p a b -> p (a b)")),
              reads=[self.lst.b()], writes=[lmv.b()])
        fw.op("dve", lambda e: e.tensor_scalar(out=lmv.t[0:tsz, 2:3], in0=lmv.t[0:tsz, 1:2], scalar1=LN_EPS, scalar2=None,
                                               op0=ALU.add),
              reads=[lmv.b()], writes=[lmv.b()])
        self.rstd(lmv.t[0:tsz, 3:4], lmv.t[0:tsz, 2:3], lmv)
        fw.op("dve", lambda e: e.scalar_tensor_tensor(out=z, in0=z, scalar=lmv.t[0:tsz, 0:1], in1=g.t[0:tsz, :],
                                                      op0=ALU.subtract, op1=ALU.mult),
              reads=[xr.b(i), lmv.b(), g.b()], writes=[xr.b(i)])
        fw.op("dve", lambda e: e.scalar_tensor_tensor(out=z, in0=z, scalar=lmv.t[0:tsz, 3:4], in1=b.t[0:tsz, :],
                                                      op0=ALU.mult, op1=ALU.add),
              reads=[xr.b(i), lmv.b(), b.b()], writes=[xr.b(i)])
        return z

    def outproj_ln1(self, sets):
        fw = self.fw
        d = self.dram
        if DEBUG and self.MINI in sets:
            s = self.MINI
            big = self.rot.t[0:64, :, :].rearrange("p a n -> p (a n)")
            fw.op("act", lambda e, s=s: e.copy(out=big, in_=s.merged.t[:, 0, :]), reads=[s.merged.b(0)], writes=self.rot.bs(range(4)))
            tok = fw.dma("sp", lambda e: e.dma_start(out=d["dbg_mrg"], in_=big), self.rot.b("o"), reads=self.rot.bs(range(4)))
            fw.out_toks.append(tok)
        mTb = lambda s: s.fmbig.bs(range(8))
        for s in sets:
            tsz = s.tsz
            for i in range(s.ntile):
                self.transpose_in(s, [s.merged.b(i)], lambda k, s=s, i=i: s.merged.t[:, i, k * 128:(k + 1) * 128],
                                  s.fmbig.t[:, 0:8, i * tsz:(i + 1) * tsz], mTb(s))
        wts = [self.wnext("out", 0), self.wnext("out", 512, hold=1)]
        for s in sets:
            tsz = s.tsz
            allb = s.xres.bs(range(s.ntile))
            if s is self.MAIN:
                r0 = self.pass_idx * 512
                self.fw.dma("sp", lambda e, s=s, r0=r0: e.dma_start(
                    out=s.xres.t[:, :, :], in_=d["xp"][r0:r0 + 512, :].rearrange("(i p) n -> p i n", p=128)),
                    s.xres.b("ld"), writes=allb)
            else:
                self.fw.dma("sp", lambda e, s=s: e.dma_start(out=s.xres.t[0:NMETA, 0, :], in_=d["meta"]),
                            s.xres.b("ld"), writes=allb)
                self.fw.dma("sp", lambda e, s=s: e.dma_start(out=s.xres.t[SP0:SP0 + NS, 0, :], in_=d["xs"]),
                            s.xres.b("ld"), writes=allb, nowait=True)
            pss = []
            for i in range(s.ntile):
                pp = []
                for hf in range(2):
                    ps = self.nextps()
                    pp.append(ps)
                    w3 = wts[hf].t[:, 0:8 * 512].rearrange("p (k n) -> p k n", k=8)

                    def mm(e, ps=ps, w3=w3, i=i, s=s):
                        for k in range(8):
                            ins = e.matmul(ps.t[0:s.tsz, :], lhsT=s.fmbig.t[:, k, i * s.tsz:(i + 1) * s.tsz],
                                           rhs=w3[:, k, :], start=(k == 0), stop=(k == 7))
                        return ins
                    fw.op("pe", mm, reads=mTb(s) + [wts[hf].b()], writes=[ps.b()])
                pss.append(pp)
            for i in range(s.ntile):
                self.ln_evac(s, i, pss[i])
            for i in range(s.ntile):
                z = self.ln_tile(s, i, self.lnp[0], self.lnp[1])
                fw.op("act", lambda e, z=z, s=s, i=i: e.copy(out=s.merged.t[:, i, :], in_=z),
                      reads=[s.xres.b(i)], writes=[s.merged.b(i)])
                if DEBUG and s is self.MINI:
                    self.store("sp", s.xres, i, d["dbg_x1"], s.xres.t[:, 0, :])
                self.transpose_in(s, [s.merged.b(i)], lambda k, s=s, i=i: s.merged.t[:, i, k * 128:(k + 1) * 128],
                                  s.fmbig.t[:, 0:8, i * tsz:(i + 1) * tsz], mTb(s))

    def ffn(self, sets, hook=None):
        fw = self.fw
        for c0 in range(0, DFF, 512):
            wg = self.wnext("gate", c0)
            wu = self.wnext("up", c0, hold=1)
            nch = min(4, (DFF - c0) // 128)
            g3 = wg.t[:, 0:8 * 512].rearrange("p (k n) -> p k n", k=8)
            u3 = wu.t[:, 0:8 * 512].rearrange("p (k n) -> p k n", k=8)
            for s in sets:
                T = s.T
                x1r = s.fmbig.bs(range(8))

                def grp(ps, w3, wt, m, s=s, T=T):
                    def mm(e):
                        for k in range(8):
                            ins = e.matmul(ps.t[:, 0:T], lhsT=w3[:, k, m * 128:(m + 1) * 128], rhs=s.fmbig.t[:, k, :],
                                           start=(k == 0), stop=(k == 7))
                        return ins
                    fw.op("pe", mm, reads=x1r + [wt.b()], writes=[ps.b()])
                pgs = []
                for m in range(nch):
                    pg = self.nextps()
                    pgs.append(pg)
                    grp(pg, g3, wg, m)
                tbs = []
                for m in range(nch):
                    fc = c0 // 128 + m
                    pu = self.nextps()
                    grp(pu, u3, wu, m)
                    pg = pgs[m]
                    tb = self.tmpb[self.tbi % 2]
                    self.tbi += 1
                    fw.op("act", lambda e, tb=tb, pg=pg, T=T: e.activation(out=tb.t[:, 0:T], in_=pg.t[:, 0:T], func=AF.Silu),
                          reads=[pg.b()], writes=[tb.b()])
                    fw.op("dve", lambda e, tb=tb, pu=pu, T=T, s=s, fc=fc: e.tensor_tensor(
                        out=s.fmbig.t[:, 8 + fc, :], in0=tb.t[:, 0:T], in1=pu.t[:, 0:T], op=ALU.mult),
                        reads=[tb.b(), pu.b()], writes=[s.fmbig.b(8 + fc)])
        small = [(s, i) for s in sets if s is not self.MAIN for i in range(s.ntile)]
        if small:
            self.mini_down = small
        if hook is not None:
            hook()
        return self.down([(self.MAIN, i) for i in range(4)], defer=hook is not None)

    def down(self, tiles, defer=False, banks=None, blocks=None):
        fw = self.fw
        d = self.dram
        if banks is None:
            banks = [[self.ps[2 * n], self.ps[2 * n + 1]] for n in range(len(tiles))]
        todo = list(range(0, DFF, 512)) if blocks is None else ([] if blocks == "finish" else [blocks])
        for r0 in todo:
            wt = self.wnext("down", r0)
            nch = min(4, (DFF - r0) // 128)
            w4 = wt.t[:, 0:4 * 1024].rearrange("p (c n) -> p c n", c=4)
            for n, (s, i) in enumerate(tiles):
                tsz = s.tsz
                for hf in range(2):
                    ps = banks[n][hf]

                    def mm(e, ps=ps, i=i, hf=hf, s=s, nch=nch, r0=r0, tsz=tsz, w4=w4):
                        for m in range(nch):
                            fc = r0 // 128 + m
                            ins = e.matmul(ps.t[0:tsz, :], lhsT=s.fmbig.t[:, 8 + fc, i * tsz:(i + 1) * tsz],
                                           rhs=w4[:, m, hf * 512:(hf + 1) * 512],
                                           start=(fc == 0), stop=(fc == NFF - 1))
                        return ins
                    fw.op("pe", mm, reads=s.fmbig.bs(range(8 + r0 // 128, 8 + r0 // 128 + nch)) + [wt.b()],
                          writes=[ps.b()])
        if blocks is not None and blocks != "finish":
            return []
        for n, (s, i) in enumerate(tiles):
            self.ln_evac(s, i, banks[n])

        def finish(s, i, r0):
            def f():
                self.ln_tile(s, i, self.lnp[2], self.lnp[3])
                if s is self.MAIN:
                    self.store("sp", s.xres, i, d["yp"][r0:r0 + 128, :], s.xres.t[:, i, :])
                else:
                    self.store("sp", s.xres, i, d["ys"], s.xres.t[SP0:SP0 + NS, 0, :])
            return f
        fins = [finish(s, i, self.pass_idx * 512 + i * 128) for (s, i) in tiles]
        if defer:
            return fins
        for f in fins:
            f()
        return []

    def program(self):
        NP = 4
        fw = self.fw
        self.wspecs = self.wspec_list(NP)
        self.wi = 0
        self.wloaded = 0
        self.wscr = self.nc.dram_tensor("wscr", [40, 128, 9 * 512], BF16).ap()
        self.scr_slot = {}
        self.scr_buf = {}
        self.no_prefetch_beyond = 19
        self.gw_loaded = False
        self.load_consts()
        fw.op("dve", lambda e: e.memset(self.MINI.merged.t[:], 0.0), writes=[self.MINI.merged.b(0)])
        fw.op("dve", lambda e: e.memset(self.MINI.xres.t[:], 0.0), writes=self.MINI.xres.bs(range(1)))
        for p in range(NP):
            self.pass_idx = p
            sets = [self.MINI, self.MAIN] if p == 0 else [self.MAIN]
            if p == 0:
                self.load_x_mini()
                self.load_x_main(0)
                gw = self.gw
                self.load("pool", gw, 0, gw.t[:, 0:8, :],
                          self.dram["w_in"][:, C_MI:C_MI + 8].rearrange("(k p) n -> p k n", p=128))
                self.load("pool", gw, 0, gw.t[0:1, 8, :], self.dram["b_in"][:, C_MI:C_MI + 8], nowait=True)
                self.gw_loaded = True
                while self.wloaded < 2:
                    self.wload(self.wloaded)
                    self.wloaded += 1
                self.load_consts_late()
            self.projection(sets)
            if p == 0:
                self.mixers(self.MINI)
                self.sample_mixers()
                self.no_prefetch_beyond = None
                while self.wloaded < min(len(self.wspecs), self.wi + self.NWB):
                    self.wload(self.wloaded)
                    self.wloaded += 1
            self.mixers(self.MAIN)
            if p == NP - 1:
                self.store_prompt_state()
            self.outproj_ln1(sets)
            self.pending_ln = self.ffn(sets, (lambda p=p: self.load_x_main(p + 1)) if p + 1 < NP else None)
        assert self.wi == len(self.wspecs)


def _consts():
    f32 = np.float32
    ident = np.eye(128, dtype=f32)
    maskT = np.triu(np.ones((128, 128), dtype=f32))
    lg = np.log1p(-np.exp2(-5.0 - np.arange(4, dtype=np.float64)))
    cst = np.zeros((128, NCST), dtype=np.float64)
    s128 = np.arange(128)[:, None]
    cst[:, K_ES128:K_ES128 + 4] = np.exp((127 - s128) * lg[None, :])
    cst[:, K_ES16:K_ES16 + 4] = np.exp((15 - s128) * lg[None, :])
    cst[:, K_WC128:K_WC128 + 4] = np.exp(128 * lg)[None, :]
    cst[:, K_WC16:K_WC16 + 4] = np.exp(16 * lg)[None, :]
    cst[:, K_EPS128:K_EPS128 + 4] = LN_EPS * np.exp(2 * (127 - s128) * lg[None, :])
    cst[:, K_EPS16:K_EPS16 + 4] = LN_EPS * np.exp(2 * (15 - s128) * lg[None, :])
    cst[:, K_G1:K_G1 + 4] = np.exp(lg)[None, :]
    cst[:, K_EPS1:K_EPS1 + 4] = LN_EPS
    cst[:, K_ONE] = 1.0
    cst = cst.astype(f32)
    inv = 10000.0 ** (-np.arange(0, 256, 2, dtype=np.float64) / 256.0)

    def tables(pos):
        ang = np.asarray(pos, dtype=np.float64)[:, None] * inv[None, :]
        return np.ascontiguousarray(np.cos(ang).T.astype(f32)), np.ascontiguousarray(np.sin(ang).T.astype(f32))
    ropeC, ropeS = tables(np.arange(NMETA, NMETA + SEQ))
    pm = np.zeros(64)
    pm[0:NMETA] = np.arange(NMETA)
    pm[SP0:SP0 + NS] = PAST
    ropeCm, ropeSm = tables(pm)
    identrow = np.zeros((128, 16, 64), dtype=f32)
    for b in range(16):
        identrow[:, b, SP0 + b] = 1.0
    return dict(ident=ident, maskT=maskT, cst=cst, ropeC=ropeC, ropeS=ropeS, ropeCm=ropeCm, ropeSm=ropeSm,
                identrow=identrow.reshape(128, 1024))


_NC_CACHE = {}


def kernel(x_prompt, x_sample, state_mlstm_C, state_mlstm_n, state_mlstm_m, state_ret_S,
           meta_tokens, w_in, b_in, ml_norm_g, rt_norm_g, w_out,
           ln1_g, ln1_b, w_gate, w_up, w_down, ln2_g, ln2_b):
    f = lambda a: np.ascontiguousarray(np.asarray(a, dtype=np.float32))
    if "nc" not in _NC_CACHE:
        _NC_CACHE["nc"] = Prog().build()
    nc = _NC_CACHE["nc"]
    cs = _consts()
    shared = dict(meta=f(meta_tokens), w_in=f(w_in)[0], b_in=f(b_in), ml_g=f(ml_norm_g), rt_g=f(rt_norm_g),
                  w_out=f(w_out)[0], ln1_g=f(ln1_g), ln1_b=f(ln1_b), w_gate=f(w_gate)[0], w_up=f(w_up)[0],
                  w_down=f(w_down)[0], ln2_g=f(ln2_g), ln2_b=f(ln2_b), **cs)
    xp, xs = f(x_prompt), f(x_sample)
    sC, sn, sm, sS = f(state_mlstm_C)[0], f(state_mlstm_n)[0], f(state_mlstm_m)[0], f(state_ret_S)[0]
    in_maps = []
    for c in range(NCORES):
        sl = slice(c * NS, (c + 1) * NS)
        m = dict(shared)
        m.update(xp=xp[c], xs=np.ascontiguousarray(xs[sl, 0, :]), sC=np.ascontiguousarray(sC[sl]),
                 sn=np.ascontiguousarray(sn[sl].reshape(NS, 1024)), sm=np.ascontiguousarray(sm[sl]),
                 sS=np.ascontiguousarray(sS[sl]))
        in_maps.append(m)
    res = run_bass_kernel_spmd(nc, in_maps, core_ids=list(range(NCORES)))
    R = res.results
    if DEBUG:
        _NC_CACHE["dbg"] = dict(mrg=R[0]["dbg_mrg"], x1=R[0]["dbg_x1"])
    cat = lambda k: np.concatenate([r[k] for r in R], axis=0)
    stk = lambda k: np.stack([r[k] for r in R], axis=0)
    y_prompt = stk("yp")
    y_sample = cat("ys").reshape(128, 1, D)
    p_C = stk("pC")[None]
    p_n = stk("pn")[None]
    p_m = stk("pm").reshape(1, NCORES, 4)
    p_S = stk("pS")[None]
    s_C = cat("oC")[None]
    s_n = cat("on").reshape(1, 128, 4, 256)
    s_m = cat("om")[None]
    s_S = cat("oS")[None]
    return (y_prompt, y_sample, p_C, p_n, p_m, p_S, s_C, s_n, s_m, s_S)
```

```python
from contextlib import ExitStack

import numpy as np
import concourse.bass as bass
import concourse.mybir as mybir
from concourse.bass_utils import run_bass_kernel_spmd

F32 = mybir.dt.float32
BF16 = mybir.dt.bfloat16
AF = mybir.ActivationFunctionType
ALU = mybir.AluOpType
AX = mybir.AxisListType

NCORES = 8
D = 1024
SEQ = 2048
NMETA = 16
NS = 16
SP0 = 32
DIN = 10248
DFF = 2816
NFF = DFF // 128
LN_EPS = 1e-5
ALPHA = 2.0 ** 0.25
PAST = 16384
C_MI = 4096

K_ES128, K_ES16, K_WC128, K_WC16, K_EPS128, K_EPS16, K_G1, K_EPS1, K_ONE = (0, 4, 8, 12, 16, 20, 24, 28, 32)
NCST = 40
DEBUG = False


class Buf:
    __slots__ = ("name", "w", "r", "sem", "cnt")

    def __init__(self, name):
        self.name = name
        self.w = None
        self.r = {}
        self.sem = None
        self.cnt = 0


class FW:
    ENG = ("pe", "act", "dve", "pool", "sp")

    def __init__(self, nc, stack):
        self.nc = nc
        self.stack = stack
        self.sem = {e: stack.enter_context(nc.semaphore("s_" + e)) for e in self.ENG}
        self.n = {e: 0 for e in self.ENG}
        self.known = {e: {} for e in self.ENG}
        self.rec = {e: [] for e in self.ENG}
        self.nsem = len(self.ENG)
        self.out_toks = []

    def _waits(self, E, reads, writes):
        need = {}

        def add(tok):
            if tok is None:
                return
            s, v = tok
            if need.get(s, 0) < v:
                need[s] = v
        for b in reads:
            add(b.w)
        for b in writes:
            add(b.w)
            for t in b.r.values():
                add(t)
        out = []
        kn = self.known[E]
        for s, v in need.items():
            if E == "pe" and s is self.sem["pe"]:
                continue
            if kn.get(s, 0) < v:
                kn[s] = v
                out.append((s, v))
        return out

    def op(self, E, fn, reads=(), writes=(), extra=()):
        waits = self._waits(E, reads, list(writes) + list(extra))
        self.n[E] += 1
        sem = self.sem[E]
        tok = (sem, self.n[E])
        self.rec[E].append((waits, fn, sem, 1))
        for b in reads:
            b.r[E] = tok
        for b in writes:
            b.w = tok
            b.r = {}
        return tok

    def dma(self, Q, fn, sb, reads=(), writes=(), nowait=False, extra=()):
        waits = [] if nowait else self._waits(Q, reads, list(writes) + list(extra))
        if sb.sem is None:
            sb.sem = self.stack.enter_context(self.nc.semaphore("d_" + sb.name))
            self.nsem += 1
        sb.cnt += 16
        tok = (sb.sem, sb.cnt)
        self.rec[Q].append((waits, fn, sb.sem, 16))
        for b in reads:
            b.r["dma_" + sb.name] = tok
        for b in writes:
            b.w = tok
            b.r = {}
        return tok

    def emit(self):
        nc = self.nc
        need = {}
        for s, v in self.out_toks:
            if need.get(s, 0) < v:
                need[s] = v
        self.rec["sp"].append((list(need.items()), None, None, 0))
        with nc.Block() as block:
            def run(eng, lst):
                for waits, fn, sem, inc in lst:
                    for s, v in waits:
                        eng.wait_ge(s, v)
                    if fn is not None:
                        fn(eng).then_inc(sem, inc)

            @block.tensor
            def _(e):
                run(e, self.rec["pe"])

            @block.scalar
            def _(e):
                run(e, self.rec["act"])

            @block.vector
            def _(e):
                run(e, self.rec["dve"])

            @block.gpsimd
            def _(e):
                run(e, self.rec["pool"])

            @block.sync
            def _(e):
                run(e, self.rec["sp"])


class Tl:
    def __init__(self, t, name):
        self.t = t
        self.name = name
        self._b = {}
        self.coarse = False

    def b(self, key=0):
        if self.coarse:
            key = 0
        if key not in self._b:
            self._b[key] = Buf("%s_%s" % (self.name, key))
        return self._b[key]

    def bs(self, keys):
        return [self.b(k) for k in keys]


class TokSet:
    pass


class Prog:
    def __init__(self):
        self.nc = bass.Bass("TRN2", target_bir_lowering=False)
        self.st = ExitStack()
        self.dram = {}
        self.xbi = 0
        self.pending_ln = []
        self.ps_reserved = set()
        self.mini_down = None
        self.mgi = 0
        self.tbi = 0
        self.psi = 0
        self.sbytes = 0

    def din(self, name, shape):
        self.dram[name] = self.nc.dram_tensor(name, list(shape), F32, kind="ExternalInput").ap()

    def dout(self, name, shape):
        self.dram[name] = self.nc.dram_tensor(name, list(shape), F32, kind="ExternalOutput").ap()

    def sb(self, name, shape, dt):
        n = 1
        for x in shape[1:]:
            n *= x
        self.sbytes += n * (4 if dt == F32 else 2)
        return Tl(self.st.enter_context(self.nc.sbuf_tensor("sb_" + name, list(shape), dt)), name)

    def load(self, Q, tl, key, out_ap, in_ap, nowait=False):
        return self.fw.dma(Q, lambda e: e.dma_start(out=out_ap, in_=in_ap), tl.b(key),
                           writes=[tl.b(key)], nowait=nowait)

    def store(self, Q, tl, key, out_ap, in_ap, reads=None, semkey=None):
        tok = self.fw.dma(Q, lambda e: e.dma_start(out=out_ap, in_=in_ap), tl.b(key if semkey is None else semkey),
                          reads=[tl.b(key)] if reads is None else reads)
        self.fw.out_toks.append(tok)
        return tok

    def build(self):
        nc = self.nc
        with self.st:
            self.fw = FW(nc, self.st)
            self.declare()
            self.alloc()
            self.program()
            self.fw.emit()
        return nc

    def declare(self):
        d = self.din
        d("xp", (SEQ, D)); d("meta", (NMETA, D)); d("xs", (NS, D))
        d("sC", (NS, 4, 256, 256)); d("sn", (NS, 1024)); d("sm", (NS, 4)); d("sS", (NS, 4, 256, 256))
        d("w_in", (D, DIN)); d("b_in", (1, DIN))
        d("ml_g", (1, D)); d("rt_g", (1, D)); d("w_out", (D, D))
        d("ln1_g", (1, D)); d("ln1_b", (1, D))
        d("w_gate", (D, DFF)); d("w_up", (D, DFF)); d("w_down", (DFF, D))
        d("ln2_g", (1, D)); d("ln2_b", (1, D))
        d("ident", (128, 128)); d("maskT", (128, 128)); d("cst", (128, NCST))
        d("ropeC", (128, SEQ)); d("ropeS", (128, SEQ))
        d("ropeCm", (128, 64)); d("ropeSm", (128, 64))
        d("identrow", (128, 1024))
        o = self.dout
        o("yp", (SEQ, D)); o("ys", (NS, D))
        if DEBUG:
            o("dbg_mrg", (64, D)); o("dbg_x1", (64, D))
        o("pC", (4, 256, 256)); o("pn", (4, 256)); o("pm", (4, 1)); o("pS", (4, 256, 256))
        o("oC", (NS, 4, 256, 256)); o("on", (NS, 1024)); o("om", (NS, 4)); o("oS", (NS, 4, 256, 256))

    def mkset(self, name, T, tsz):
        s = TokSet()
        s.name, s.T, s.tsz, s.ntile = name, T, tsz, T // tsz
        sb = self.sb
        s.alias = (tsz == 128)
        s.xT = sb(name + "xT", [128, 8, T], BF16)
        s.fmbig = sb(name + "fm", [128, 32, T], BF16)
        s.xres = sb(name + "xres", [tsz, s.ntile, 1024], F32)
        s.G = [sb(name + "G%d" % i, [tsz, s.ntile, 1024], BF16) for i in range(2)]
        s.gcol = sb(name + "gcol", [tsz, s.ntile, 8], F32)
        if s.alias:
            s.xT.coarse = True
            mv_ = s.xT.t[:, :, :].rearrange("p k t -> p (k t)").rearrange("p (i n) -> p i n", i=s.ntile)
            s.merged = Tl(mv_, s.xT.name)
            s.merged._b = s.xT._b
            s.merged.coarse = True
        else:
            s.merged = sb(name + "mrg", [tsz, s.ntile, 1024], BF16)
        s.rc = sb(name + "rc", [128, T], F32)
        s.rs = sb(name + "rs", [128, T], F32)
        return s

    def fm(self, s, which, chunk):
        return s.fmbig.t[:, 8 * which + chunk, :]

    def vview(self, s, mix):
        nt = s.ntile
        half = s.xres.t[:, :, :].rearrange("p a n -> p (a n)").bitcast(BF16)
        return half[:, mix * nt * 1024:(mix + 1) * nt * 1024].rearrange("p (a n) -> p a n", a=nt)

    def alloc(self):
        nc, st, sb = self.nc, self.st, self.sb
        self.ps = [Tl(st.enter_context(nc.psum_tensor("ps%d" % i, [128, 512], F32)), "ps%d" % i)
                   for i in range(8)]
        self.MAIN = self.mkset("M", 512, 128)
        self.MINI = self.mkset("E", 64, 64)
        self.NWB = 3
        self.wbuf = [sb("wb%d" % i, [128, 9 * 512], BF16) for i in range(self.NWB)]
        self.gw = sb("gw", [128, 9, 8], BF16)
        self.ident_b = sb("ident_b", [128, 128], BF16)
        self.ident_f = sb("ident_f", [128, 128], F32)
        self.maskT = sb("maskT", [128, 128], F32)
        self.cst = sb("cst", [128, NCST], F32)
        self.ones_b = sb("ones_b", [128, 512], BF16)
        self.ones_f = sb("ones_f", [128, 128], F32)
        self.gbc = [sb("gbc%d" % i, [128, 1024], BF16) for i in range(2)]
        self.lnp = [sb("lnp%d" % i, [128, 1024], BF16) for i in range(4)]
        self.xbf = [sb("xbf%d" % i, [128, 1024], BF16) for i in range(2)]
        self.tmpb = [sb("tmpb%d" % i, [128, 512], BF16) for i in range(2)]
        self.rot = sb("rot", [128, 4, 256], F32)
        self.Cf = sb("Cf", [128, 8, 2, 257], F32)
        self.Cb = sb("Cb", [128, 8, 2, 257], BF16)
        self.ktm = [sb("ktm%d" % i, [128, 4, 256], BF16) for i in range(2)]
        self.vp = [sb("vp%d" % i, [128, 4, 257], BF16) for i in range(2)]
        self.stm = [sb("stm%d" % i, [128, 4, 128], BF16) for i in range(2)]
        self.u = sb("u", [128, 8, 256], BF16)
        self.st6 = sb("st6", [128, 8, 6], F32)
        self.mv = sb("mv", [128, 8, 2], F32)
        self.den = sb("den", [128, 4], F32)
        self.sm = sb("smalls", [128, 64], F32)
        self.tmpa = sb("tmpa", [128, 256], F32)
        self.gm = sb("gm", [128, 4, 40], F32)
        self.gsm = sb("gsm", [4, 96], F32)
        self.gbcst = sb("gbcst", [128, 5, 8], F32)
        self.wcb = sb("wcb", [128, 5, 4], F32)
        self.mcur = sb("mcur", [4, 1], F32)
        self.lst = sb("lst", [128, 2, 6], F32)
        self.lmv = sb("lmv", [128, 4], F32)
        self.NCIN = 3
        self.cin = [sb("cin%d" % i, [128, 2, 256], F32) for i in range(self.NCIN)]
        self.cin_owner = {}
        for i, wb in enumerate(self.wbuf):
            for q in range(4):
                v_ = wb.t[:, q * 1024:(q + 1) * 1024].bitcast(F32).rearrange("p (j e) -> p j e", j=2)
                tl = Tl(v_, "cs%d_%d" % (i, q))
                self.cin.append(tl)
                self.cin_owner[tl.name] = wb
        wb = self.wbuf[-1]
        for _ in range(2):
            tl = self.cin.pop()
            del self.cin_owner[tl.name]
        self.qm2 = Tl(wb.t[:, 3 * 1024:3 * 1024 + 512].rearrange("p (k n) -> p k n", k=8), "qm2")
        self.vm2 = Tl(wb.t[0:64, 2 * 1024:3 * 1024], "vm2")
        self.borrowed2 = [self.qm2, self.vm2]
        self.cbf = [sb("cbf%d" % i, [128, 2, 256], BF16) for i in range(2)]
        self.qm = [sb("qm0", [128, 8, 64], BF16)] * 2
        self.vmk = [sb("vmk0", [64, 1024], BF16)] * 2
        self.identrow = sb("identrow", [128, 16, 64], BF16)
        self.s_num = sb("s_num", [64, 256], F32)
        self.s_tm = [sb("s_tm%d" % i, [64, 1024], BF16) for i in range(3)]
        self.s_tm = [self.s_tm[0], self.s_tm[1], self.s_tm[0], self.s_tm[2]]
        self.s_n = Tl(self.rot.t[0:64, :, :].rearrange("p a n -> p (a n)"), "rot")
        self.s_n._b = self.rot._b
        self.s_n.coarse = True
        self.rot.coarse = True
        self.s_sm = sb("s_sm", [64, 64], F32)
        self.s_wb = sb("s_wb", [128, 128], F32)
        self.s_dg = sb("s_dg", [64, 16, 8], F32)

    def nextps(self):
        while (self.psi % 8) in self.ps_reserved:
            self.psi += 1
        p = self.ps[self.psi % 8]
        self.psi += 1
        return p

    def wspec_list(self, npass):
        L = []
        for p in range(npass):
            for n_, c0 in enumerate(list(range(0, 4096, 512)) + list(range(4104, DIN, 512))):
                L.append(("in", c0))
                if p == 1 and n_ < 6:
                    L.append(("down", n_ * 512))
            for c0 in (0, 512):
                L.append(("out", c0))
            for c0 in range(0, DFF, 512):
                L.append(("gate", c0))
                L.append(("up", c0))
            for r0 in range(0, DFF, 512):
                L.append(("down", r0))
        return L

    def wload(self, idx):
        kind, c0 = self.wspecs[idx]
        tl = self.wbuf[idx % self.NWB]
        d = self.dram
        uid = (kind, c0)
        def regions(t2):
            if kind == "in":
                return [t2[:, 0:8 * 512], t2[0:1, 8 * 512:9 * 512]]
            if kind == "out":
                return [t2[:, 0:8 * 512]]
            if kind in ("gate", "up"):
                n = min(512, DFF - c0)
                return [t2[:, 0:8 * 512].rearrange("p (k n) -> p k n", k=8)[:, :, 0:n]]
            nch = min(4, (DFF - c0) // 128)
            return [t2[:, 0:nch * 1024]]
        if uid in self.scr_slot:
            slot = self.scr_slot[uid]
            for n_, (o_, i_) in enumerate(zip(regions(tl.t), regions(self.wscr[slot]))):
                self.fw.dma("pool", lambda e, o_=o_, i_=i_: e.dma_start(out=o_, in_=i_), tl.b(0),
                            reads=[self.scr_buf[slot]], writes=[tl.b(0)], nowait=(n_ > 0))
            return
        self._wload_cast(idx)
        slot = len(self.scr_slot)
        self.scr_slot[uid] = slot
        self.scr_buf[slot] = Buf("scr%d" % slot)
        for n_, (o_, i_) in enumerate(zip(regions(self.wscr[slot]), regions(tl.t))):
            self.fw.dma("sp", lambda e, o_=o_, i_=i_: e.dma_start(out=o_, in_=i_), tl.b("st"),
                        reads=[tl.b(0)], writes=[self.scr_buf[slot]], nowait=(n_ > 0))

    def _wload_cast(self, idx):
        kind, c0 = self.wspecs[idx]
        tl = self.wbuf[idx % self.NWB]
        d = self.dram
        t3 = tl.t[:, 0:8 * 512].rearrange("p (k n) -> p k n", k=8)
        if kind == "in":
            self.load("pool", tl, 0, t3, d["w_in"][:, c0:c0 + 512].rearrange("(k p) n -> p k n", p=128))
            self.load("pool", tl, 0, tl.t[0:1, 8 * 512:9 * 512], d["b_in"][:, c0:c0 + 512], nowait=True)
        elif kind == "out":
            self.load("pool", tl, 0, t3, d["w_out"][:, c0:c0 + 512].rearrange("(k p) n -> p k n", p=128))
        elif kind in ("gate", "up"):
            w = d["w_gate"] if kind == "gate" else d["w_up"]
            n = min(512, DFF - c0)
            self.load("pool", tl, 0, t3[:, :, 0:n], w[:, c0:c0 + n].rearrange("(k p) n -> p k n", p=128))
        else:
            nch = min(4, (DFF - c0) // 128)
            t4 = tl.t[:, 0:4 * 1024].rearrange("p (c n) -> p c n", c=4)
            self.load("pool", tl, 0, t4[:, 0:nch, :],
                      d["w_down"][c0:c0 + nch * 128, :].rearrange("(c p) n -> p c n", p=128))

    def wnext(self, kind, c0, hold=0):
        i = self.wi
        assert self.wspecs[i] == (kind, c0), (self.wspecs[i], kind, c0)
        lim = min(len(self.wspecs), i + self.NWB - hold)
        if self.no_prefetch_beyond is not None:
            lim = min(lim, self.no_prefetch_beyond + 1)
        while self.wloaded < lim:
            self.wload(self.wloaded)
            self.wloaded += 1
        self.wi += 1
        return self.wbuf[i % self.NWB]

    def load_consts(self):
        d = self.dram
        fw = self.fw
        self.load("sp", self.ident_f, 0, self.ident_f.t[:], d["ident"])
        self.load("pool", self.ident_b, 0, self.ident_b.t[:], d["ident"])
        self.load("sp", self.maskT, 0, self.maskT.t[:], d["maskT"])
        self.load("sp", self.cst, 0, self.cst.t[:], d["cst"])
        self.load("pool", self.identrow, 0, self.identrow.t[:].rearrange("p a b -> p (a b)"), d["identrow"])
        fw.op("dve", lambda e: e.memset(self.ones_b.t[:], 1.0), writes=[self.ones_b.b()])
        fw.op("dve", lambda e: e.memset(self.ones_f.t[:], 1.0), writes=[self.ones_f.b()])
        fw.op("dve", lambda e: e.memset(self.Cf.t[:], 0.0), writes=self.Cf.bs(range(8)))
        fw.op("dve", lambda e: e.memset(self.mcur.t[:], 0.0), writes=[self.mcur.b()])

    def load_consts_late(self):
        d = self.dram
        for tl, nm in ((self.gbc[0], "ml_g"), (self.gbc[1], "rt_g"), (self.lnp[0], "ln1_g"),
                       (self.lnp[1], "ln1_b"), (self.lnp[2], "ln2_g"), (self.lnp[3], "ln2_b")):
            self.load("pool", tl, 0, tl.t[:], d[nm].partition_broadcast(128))

    def transpose_in(self, s, src_bufs, src_ap_fn, dst_ap, dst_bufs):
        tsz = s.tsz
        ps = self.nextps()
        pv = ps.t[:].bitcast(BF16).rearrange("p (k n) -> p k n", k=8)

        def tr(e):
            for k in range(8):
                ins = e.transpose(out=pv[:, k, 0:tsz], in_=src_ap_fn(k), identity=self.ident_b.t[0:tsz, 0:tsz])
            return ins
        self.fw.op("pe", tr, reads=list(src_bufs) + [self.ident_b.b()], writes=[ps.b()])
        self.fw.op("act", lambda e: e.copy(out=dst_ap, in_=pv[:, :, 0:tsz]), reads=[ps.b()], writes=list(dst_bufs))

    def load_x_main(self, p):
        s = self.MAIN
        d = self.dram
        for i in range(4):
            xb = self.xbf[self.xbi % 2]
            self.xbi += 1
            r0 = p * 512 + i * 128
            self.load("pool", xb, 0, xb.t[:, :], d["xp"][r0:r0 + 128, :])
            self.transpose_in(s, [xb.b()], lambda k, xb=xb: xb.t[:, k * 128:(k + 1) * 128],
                              s.xT.t[:, :, i * 128:(i + 1) * 128], [s.xT.b(i)])
        self.load("sp", s.rc, 0, s.rc.t[:], d["ropeC"][:, p * 512:(p + 1) * 512])
        self.load("sp", s.rs, 0, s.rs.t[:], d["ropeS"][:, p * 512:(p + 1) * 512])

    def load_x_mini(self):
        s = self.MINI
        d = self.dram
        xb = self.vmk[0]
        self.fw.op("dve", lambda e: e.memset(xb.t[:], 0.0), writes=[xb.b()])
        self.load("pool", xb, 0, xb.t[0:NMETA, :], d["meta"])
        self.load("pool", xb, 0, xb.t[SP0:SP0 + NS, :], d["xs"], nowait=True)
        self.transpose_in(s, [xb.b()], lambda k: xb.t[:, k * 128:(k + 1) * 128], s.xT.t[:, :, :], [s.xT.b(0)])
        self.load("sp", s.rc, 0, s.rc.t[:], d["ropeCm"])
        self.load("sp", s.rs, 0, s.rs.t[:], d["ropeSm"])

    def proj_gates(self, s):
        fw = self.fw
        gw = self.gw
        for i in range(s.ntile):
            ps = self.nextps()

            def mm(e, i=i, ps=ps):
                for k in range(8):
                    e.matmul(ps.t[0:s.tsz, 0:8], lhsT=s.xT.t[:, k, i * s.tsz:(i + 1) * s.tsz],
                             rhs=gw.t[:, k, :], start=(k == 0), stop=False)
                return e.matmul(ps.t[0:s.tsz, 0:8], lhsT=self.ones_b.t[0:1, 0:s.tsz],
                                rhs=gw.t[0:1, 8, :], start=False, stop=True)
            fw.op("pe", mm, reads=[s.xT.b(i), gw.b(), self.ones_b.b()], writes=[ps.b()])
            fw.op("act", lambda e, i=i, ps=ps: e.copy(out=s.gcol.t[:, i, :], in_=ps.t[0:s.tsz, 0:8]),
                  reads=[ps.b()], writes=[s.gcol.b()])

    def proj_fm(self, s, wt, cbase, which, scale, rot):
        fw = self.fw
        T = s.T
        w3 = wt.t[:, 0:8 * 512].rearrange("p (k n) -> p k n", k=8)
        brow = wt.t[0:1, 8 * 512:9 * 512]
        xr = s.xT.bs(range(s.ntile))
        pss = []
        for m in range(4):
            ps = self.nextps()
            pss.append(ps)

            def mm(e, m=m, ps=ps):
                for k in range(8):
                    e.matmul(ps.t[:, 0:T], lhsT=w3[:, k, m * 128:(m + 1) * 128], rhs=s.xT.t[:, k, :],
                             start=(k == 0), stop=False)
                return e.matmul(ps.t[:, 0:T], lhsT=brow[:, m * 128:(m + 1) * 128], rhs=self.ones_b.t[0:1, 0:T],
                                start=False, stop=True)
            fw.op("pe", mm, reads=xr + [wt.b(), self.ones_b.b()], writes=[ps.b()])
            if not rot:
                fw.op("act", lambda e, m=m, ps=ps: e.activation(out=self.fm(s, which, cbase + m), in_=ps.t[:, 0:T],
                                                               func=AF.Copy, scale=scale),
                      reads=[ps.b()], writes=[s.fmbig.b(8 * which + cbase + m)])
            elif m % 2 == 1:
                p1, p2 = pss[m - 1], ps
                c1, c2 = cbase + m - 1, cbase + m
                r = self.rot
                rd = [s.rc.b(), s.rs.b()]
                for q0 in range(0, T, 256):
                    n = min(256, T - q0)
                    cos, sin = s.rc.t[:, q0:q0 + n], s.rs.t[:, q0:q0 + n]
                    a1, a2 = p1.t[:, q0:q0 + n], p2.t[:, q0:q0 + n]

                    def stt(o, i0, i1):
                        return lambda e: e.scalar_tensor_tensor(out=o, in0=i0, scalar=scale, in1=i1,
                                                                op0=ALU.mult, op1=ALU.mult)
                    fw.op("dve", stt(r.t[:, 0, 0:n], a1, cos), reads=[p1.b()] + rd, writes=[r.b(0)])
                    fw.op("dve", stt(r.t[:, 1, 0:n], a2, sin), reads=[p2.b()] + rd, writes=[r.b(1)])
                    fw.op("dve", stt(r.t[:, 2, 0:n], a1, sin), reads=[p1.b()] + rd, writes=[r.b(2)])
                    fw.op("dve", stt(r.t[:, 3, 0:n], a2, cos), reads=[p2.b()] + rd, writes=[r.b(3)])
                    o1 = self.fm(s, which, c1)[:, q0:q0 + n]
                    o2 = self.fm(s, which, c2)[:, q0:q0 + n]
                    fw.op("dve", lambda e, o1=o1, n=n: e.tensor_tensor(out=o1, in0=r.t[:, 0, 0:n], in1=r.t[:, 1, 0:n],
                                                                      op=ALU.subtract),
                          reads=[r.b(0), r.b(1)], writes=[s.fmbig.b(8 * which + c1)])
                    fw.op("dve", lambda e, o2=o2, n=n: e.tensor_tensor(out=o2, in0=r.t[:, 2, 0:n], in1=r.t[:, 3, 0:n],
                                                                      op=ALU.add),
                          reads=[r.b(2), r.b(3)], writes=[s.fmbig.b(8 * which + c2)])

    def proj_tm(self, s, wt, kind, half):
        fw = self.fw
        w3 = wt.t[:, 0:8 * 512].rearrange("p (k n) -> p k n", k=8)
        brow = wt.t[0:1, 8 * 512:9 * 512]
        cs = slice(half * 512, (half + 1) * 512)
        tsz = s.tsz
        for i in range(s.ntile):
            ps = self.nextps()

            def mm(e, i=i, ps=ps):
                for k in range(8):
                    e.matmul(ps.t[0:tsz, :], lhsT=s.xT.t[:, k, i * tsz:(i + 1) * tsz], rhs=w3[:, k, :],
                             start=(k == 0), stop=False)
                return e.matmul(ps.t[0:tsz, :], lhsT=self.ones_b.t[0:1, 0:tsz], rhs=brow, start=False, stop=True)
            fw.op("pe", mm, reads=[s.xT.b(i), wt.b(), self.ones_b.b()], writes=[ps.b()])
            pin = ps.t[0:tsz, :]
            if kind in ("mv", "rv"):
                dst = self.vview(s, 0 if kind == "mv" else 1)
                fw.op("act", lambda e, dst=dst, i=i, pin=pin: e.copy(out=dst[:, i, cs], in_=pin),
                      reads=[ps.b()], writes=s.xres.bs(range(s.ntile)))
            elif kind in ("mo", "rg"):
                G = s.G[0 if kind == "mo" else 1]
                gb = self.gbc[0 if kind == "mo" else 1]
                tb = self.tmpb[self.tbi % 2]
                self.tbi += 1
                fn = AF.Sigmoid if kind == "mo" else AF.Silu
                fw.op("act", lambda e, tb=tb, pin=pin, fn=fn: e.activation(out=tb.t[0:tsz, :], in_=pin, func=fn),
                      reads=[ps.b()], writes=[tb.b()])
                fw.op("dve", lambda e, tb=tb, G=G, i=i, gb=gb: e.tensor_tensor(
                    out=G.t[:, i, cs], in0=tb.t[0:tsz, :], in1=gb.t[0:tsz, cs], op=ALU.mult),
                    reads=[tb.b(), gb.b()], writes=[G.b((i, half))])
            else:
                G = s.G[0 if kind == "ga" else 1]
                tb = self.tmpb[self.tbi % 2]
                self.tbi += 1
                fw.op("act", lambda e, tb=tb, pin=pin: e.activation(out=tb.t[0:tsz, :], in_=pin, func=AF.Sigmoid),
                      reads=[ps.b()], writes=[tb.b()])
                fw.op("dve", lambda e, tb=tb, G=G, i=i: e.tensor_tensor(
                    out=G.t[:, i, cs], in0=tb.t[0:tsz, :], in1=G.t[:, i, cs], op=ALU.mult),
                    reads=[tb.b(), G.b((i, half))], writes=[G.b((i, half))])

    def projection(self, sets):
        d = self.dram
        gw = self.gw
        if not self.gw_loaded:
            self.load("pool", gw, 0, gw.t[:, 0:8, :], d["w_in"][:, C_MI:C_MI + 8].rearrange("(k p) n -> p k n", p=128))
            self.load("pool", gw, 0, gw.t[0:1, 8, :], d["b_in"][:, C_MI:C_MI + 8], nowait=True)
            self.gw_loaded = True
        for s in sets:
            self.proj_gates(s)
        plan = [(0, "fm", 0, 0, 1.0, False), (512, "fm", 0, 4, 1.0, False),
                (1024, "fm", 1, 0, 1.0 / 16, False), (1536, "fm", 1, 4, 1.0 / 16, False),
                (2048, "tm", "mv", 0), (2560, "tm", "mv", 1), (3072, "tm", "mo", 0), (3584, "tm", "mo", 1),
                (4104, "fm", 2, 0, 1.0, True), (4616, "fm", 2, 4, 1.0, True),
                (5128, "fm", 3, 0, 1.0 / 16, True), (5640, "fm", 3, 4, 1.0 / 16, True),
                (6152, "tm", "rv", 0), (6664, "tm", "rv", 1), (7176, "tm", "rg", 0), (7688, "tm", "rg", 1),
                (8200, "tm", "ga", 0), (8712, "tm", "ga", 1), (9224, "tm", "gb", 0), (9736, "tm", "gb", 1)]
        md_banks = [[self.ps[6], self.ps[7]]]
        if self.mini_down:
            self.ps_reserved = {6, 7}
        for n_ent, ent in enumerate(plan):
            if self.pending_ln:
                self.pending_ln.pop(0)()
            if self.mini_down and n_ent == 6:
                self.down(self.mini_down, banks=md_banks, blocks="finish")
                self.mini_down = None
                self.ps_reserved = set()
            wt = self.wnext("in", ent[0])
            for s in sets:
                if ent[1] == "fm":
                    self.proj_fm(s, wt, ent[3], ent[2], ent[4], ent[5])
                else:
                    self.proj_tm(s, wt, ent[2], ent[3])
            if ent[0] == 1536:
                for s in sets:
                    if s is not self.MAIN:
                        for _ in self.gate_math(s):
                            pass
                gens = [self.gate_math(self.MAIN)]
            if ent[0] in (1536, 2048, 2560):
                for g_ in gens:
                    next(g_, None)
            if self.mini_down and n_ent < 6:
                self.down(self.mini_down, banks=md_banks, blocks=n_ent * 512)

    def gate_math(self, s):
        fw = self.fw
        main = s is self.MAIN
        L = 128 if main else NMETA
        nt = s.ntile
        gm, gcol = self.gm, s.gcol
        one = self.cst.t[0:L, K_ONE:K_ONE + 1]
        slot0 = 0 if main else 4
        o = s.gofs = (24 if main else 32)
        fw.op("act", lambda e: e.activation(out=gm.t[0:L, 0:nt, 20:24], in_=gcol.t[0:L, :, 4:8], func=AF.Exp, scale=-1.0),
              reads=[gcol.b()], writes=[gm.b()])
        fw.op("act", lambda e: e.activation(out=gm.t[0:L, 0:nt, 0:4], in_=gm.t[0:L, 0:nt, 20:24], func=AF.Ln,
                                            bias=one, scale=1.0),
              reads=[gm.b(), self.cst.b()], writes=[gm.b()])
        fw.op("dve", lambda e: e.tensor_scalar(out=gm.t[0:L, 0:nt, 0:4], in0=gm.t[0:L, 0:nt, 0:4], scalar1=-1.0,
                                               scalar2=None, op0=ALU.mult),
              reads=[gm.b()], writes=[gm.b()])
        gsm = self.gsm
        for i in range(nt):
            ps = self.nextps()

            def mmc(e, i=i, ps=ps):
                e.matmul(ps.t[0:L, 0:4], lhsT=self.maskT.t[0:L, 0:L], rhs=gm.t[0:L, i, 0:4], start=True, stop=True)
                return e.matmul(ps.t[0:4, 8:9], lhsT=gm.t[0:L, i, 0:4], rhs=self.ones_f.t[0:L, 0:1], start=True, stop=True)
            fw.op("pe", mmc, reads=[self.maskT.b(), gm.b(), self.ones_f.b()], writes=[ps.b()])
            fw.op("dve", lambda e, i=i, ps=ps: e.tensor_copy(out=gm.t[0:L, i, 4:8], in_=ps.t[0:L, 0:4]),
                  reads=[ps.b()], writes=[gm.b()])
            fw.op("dve", lambda e, i=i, ps=ps: e.tensor_copy(out=gsm.t[:, 4 + i:5 + i], in_=ps.t[0:4, 8:9]),
                  reads=[ps.b()], writes=[gsm.b()])
        fw.op("dve", lambda e: e.tensor_tensor(out=gm.t[0:L, 0:nt, 8:12], in0=gcol.t[0:L, :, 0:4],
                                               in1=gm.t[0:L, 0:nt, 4:8], op=ALU.subtract),
              reads=[gm.b(), gcol.b()], writes=[gm.b()])
        yield
        ps = self.nextps()

        def tr(e, ps=ps):
            for i in range(nt):
                ins = e.transpose(out=ps.t[0:4, i * L:(i + 1) * L], in_=gm.t[0:L, i, 8:12],
                                  identity=self.ident_f.t[0:L, 0:L])
            return ins
        fw.op("pe", tr, reads=[gm.b(), self.ident_f.b()], writes=[ps.b()])
        fw.op("dve", lambda e, ps=ps: e.tensor_reduce(out=gsm.t[:, 0:nt],
                                                      in_=ps.t[0:4, 0:nt * L].rearrange("p (c l) -> p c l", l=L),
                                                      axis=AX.X, op=ALU.max),
              reads=[ps.b()], writes=[gsm.b()])
        for c in range(nt):
            fw.op("dve", lambda e, c=c: e.tensor_copy(out=gsm.t[:, 9 + 2 * c:10 + 2 * c], in_=self.mcur.t[:]),
                  reads=[self.mcur.b(), gsm.b()], writes=[gsm.b()])
            fw.op("dve", lambda e, c=c: e.tensor_tensor(out=gsm.t[:, 8 + 2 * c:9 + 2 * c], in0=self.mcur.t[:],
                                                        in1=gsm.t[:, c:c + 1], op=ALU.max),
                  reads=[self.mcur.b(), gsm.b()], writes=[gsm.b()])
            fw.op("dve", lambda e, c=c: e.tensor_tensor(out=self.mcur.t[:], in0=gsm.t[:, 8 + 2 * c:9 + 2 * c],
                                                        in1=gsm.t[:, 4 + c:5 + c], op=ALU.add),
                  reads=[gsm.b(), self.mcur.b()], writes=[self.mcur.b()])
        dg = gsm.t[:, 24:24 + 8 * nt].rearrange("p (c h) -> p c h", h=4)
        fw.op("dve", lambda e: e.tensor_tensor(
            out=dg, in0=self.ident_f.t[0:4, 0:4].unsqueeze(1).to_broadcast([4, 2 * nt, 4]),
            in1=gsm.t[:, 8:8 + 2 * nt].unsqueeze(2).to_broadcast([4, 2 * nt, 4]), op=ALU.mult),
            reads=[gsm.b(), self.ident_f.b()], writes=[gsm.b()])
        yield
        ps = self.nextps()
        fw.op("pe", lambda e, ps=ps: e.matmul(ps.t[:, 0:8 * nt], lhsT=self.ones_f.t[0:4, :],
                                            rhs=gsm.t[:, 24:24 + 8 * nt], start=True, stop=True),
              reads=[gsm.b(), self.ones_f.b()], writes=[ps.b()])
        gb = self.gbcst
        fw.op("dve", lambda e, ps=ps: e.tensor_copy(out=gb.t[:, slot0:slot0 + nt, :],
                                                   in_=ps.t[:, 0:8 * nt].rearrange("p (c k) -> p c k", k=8)),
              reads=[ps.b()], writes=[gb.b()])
        wcb = self.wcb
        fw.op("dve", lambda e: e.tensor_tensor(out=wcb.t[:, slot0:slot0 + nt, :], in0=gb.t[:, slot0:slot0 + nt, 4:8],
                                               in1=gb.t[:, slot0:slot0 + nt, 0:4], op=ALU.subtract),
              reads=[gb.b(), wcb.b()], writes=[wcb.b()])
        fw.op("act", lambda e: e.activation(out=wcb.t[:, slot0:slot0 + nt, :], in_=wcb.t[:, slot0:slot0 + nt, :],
                                            func=AF.Exp),
              reads=[wcb.b()], writes=[wcb.b()])
        fw.op("dve", lambda e: e.tensor_tensor(out=gm.t[0:L, 0:nt, 12:16], in0=gm.t[0:L, 0:nt, 8:12],
                                               in1=gb.t[0:L, slot0:slot0 + nt, 0:4], op=ALU.subtract),
              reads=[gm.b(), gb.b()], writes=[gm.b()])
        fw.op("dve", lambda e: e.tensor_tensor(out=gm.t[0:L, 0:nt, 16:20], in0=gm.t[0:L, 0:nt, 4:8],
                                               in1=gb.t[0:L, slot0:slot0 + nt, 0:4], op=ALU.add),
              reads=[gm.b(), gb.b()], writes=[gm.b()])
        fw.op("act", lambda e: e.activation(out=gm.t[0:L, 0:nt, o:o + 4], in_=gm.t[0:L, 0:nt, 12:16], func=AF.Exp),
              reads=[gm.b()], writes=[gm.b()])
        fw.op("act", lambda e: e.activation(out=gm.t[0:L, 0:nt, o + 4:o + 8], in_=gm.t[0:L, 0:nt, 16:20], func=AF.Exp,
                                            scale=-1.0),
              reads=[gm.b()], writes=[gm.b()])

    def rstd(self, out_ap, in_ap, tl):
        fw = self.fw
        fw.op("act", lambda e: e.activation(out=out_ap, in_=in_ap, func=AF.Ln), reads=[tl.b()], writes=[tl.b()])
        fw.op("act", lambda e: e.activation(out=out_ap, in_=out_ap, func=AF.Exp, scale=-0.5),
              reads=[tl.b()], writes=[tl.b()])

    def gate_aps(self, s, mix, h, c, L):
        cst = self.cst
        if mix == 0:
            slot0 = 0 if s is self.MAIN else 4
            return (self.gm.t[0:L, c, s.gofs + h:s.gofs + h + 1], self.wcb.t[:, slot0 + c, h:h + 1],
                    [self.gm.b(), self.wcb.b()])
        ke, kw = (K_ES128, K_WC128) if L == 128 else (K_ES16, K_WC16)
        return cst.t[0:L, ke + h:ke + h + 1], cst.t[:, kw + h:kw + h + 1], [cst.b()]

    def make_cb(self, s, hm, c, L):
        fw = self.fw
        Cb, Cf = self.Cb, self.Cf
        _, wc, rd_g = self.gate_aps(s, hm // 4, hm % 4, c, L)
        fw.op("act", lambda e: e.activation(out=Cb.t[:, hm, :, :], in_=Cf.t[:, hm, :, :], func=AF.Copy, scale=wc),
              reads=[Cf.b(hm)] + rd_g, writes=[Cb.b(hm)])

    def mixers(self, s):
        fw = self.fw
        main = s is self.MAIN
        L = 128 if main else NMETA
        nt = s.ntile
        cst = self.cst
        Cb, Cf = self.Cb, self.Cf
        st6, mv, u = self.st6, self.mv, self.u
        for hm in range(8):
            self.make_cb(s, hm, 0, L)

        def prologue(c, g):
            t0 = c * L
            gi = self.mgi % 2
            self.mgi += 1
            P = TokSet()
            P.hms = hms = [2 * g, 4 + 2 * g, 2 * g + 1, 4 + 2 * g + 1]
            pT, pS = self.ps[4 + gi], self.ps[6 + gi]
            P.ktm, P.vp, P.stm = ktm, vp, stm = self.ktm[gi], self.vp[gi], self.stm[gi]
            pv = pT.t[:].bitcast(BF16)
            P.qa, P.qb = qa, qb = {}, {}
            ka, kb = {}, {}
            for hm in hms:
                mix, h = hm // 4, hm % 4
                qa[hm] = [self.fm(s, 2 * mix, 2 * h + j)[:, t0:t0 + L] for j in range(2)]
                ka[hm] = [self.fm(s, 2 * mix + 1, 2 * h + j)[:, t0:t0 + L] for j in range(2)]
                qb[hm] = s.fmbig.bs([16 * mix + 2 * h, 16 * mix + 2 * h + 1])
                kb[hm] = s.fmbig.bs([16 * mix + 8 + 2 * h, 16 * mix + 8 + 2 * h + 1])
            allq = [b for hm in hms for b in qb[hm]]
            allk = [b for hm in hms for b in kb[hm]]

            def trk(e):
                for k_, hm in enumerate(hms):
                    for j in range(2):
                        ins = e.transpose(out=pv[0:L, k_ * 256 + j * 128:k_ * 256 + (j + 1) * 128], in_=ka[hm][j],
                                          identity=self.ident_b.t[:])
                return ins
            fw.op("pe", trk, reads=allk + [self.ident_b.b()], writes=[pT.b()])

            def mms(e):
                for k_, hm in enumerate(hms):
                    for j in range(2):
                        ins = e.matmul(pS.t[0:L, k_ * 128:k_ * 128 + L], lhsT=ka[hm][j], rhs=qa[hm][j],
                                       start=(j == 0), stop=(j == 1))
                return ins
            fw.op("pe", mms, reads=allk + allq, writes=[pS.b()])
            fw.op("act", lambda e: e.copy(out=ktm.t[0:L, :, :], in_=pv[0:L, :].rearrange("p (k n) -> p k n", k=4)),
                  reads=[pT.b()], writes=[ktm.b()])
            fw.op("dve", lambda e: e.tensor_tensor(
                out=stm.t[0:L, :, 0:L], in0=pS.t[0:L, :].rearrange("p (k n) -> p k n", k=4)[:, :, 0:L],
                in1=self.maskT.t[0:L, 0:L].unsqueeze(1).to_broadcast([L, 4, L]), op=ALU.mult),
                reads=[pS.b(), self.maskT.b()], writes=[stm.b()])
            for k_, hm in enumerate(hms):
                mix, h = hm // 4, hm % 4
                es, wc, rd_g = self.gate_aps(s, mix, h, c, L)
                vsrc = self.vview(s, mix)
                fw.op("act", lambda e, vsrc=vsrc, es=es, h=h, k_=k_: e.activation(
                    out=vp.t[0:L, k_, 0:256], in_=vsrc[0:L, c, h * 256:(h + 1) * 256], func=AF.Copy, scale=es),
                    reads=[s.xres.b(c)] + rd_g, writes=[vp.b()])
                fw.op("act", lambda e, es=es, k_=k_: e.copy(out=vp.t[0:L, k_, 256:257], in_=es),
                      reads=rd_g + [vp.b()], writes=[vp.b()])
            return P

        def body(c, g, P):
            deferred = []
            ktm, vp, stm = P.ktm, P.vp, P.stm
            for k_, hm in enumerate(P.hms):
                mix, h = hm // 4, hm % 4
                es, wc, rd_g = self.gate_aps(s, mix, h, c, L)
                pA = self.ps[(k_ % 2) * 2]
                pB = self.ps[(k_ % 2) * 2 + 1]
                qa = P.qa[hm]

                def mmn(e, pA=pA, qa=qa, hm=hm, k_=k_):
                    for j in range(2):
                        e.matmul(pA.t[0:L, 128:385], lhsT=qa[j], rhs=Cb.t[:, hm, j, :], start=(j == 0), stop=False)
                    return e.matmul(pA.t[0:L, 128:385], lhsT=stm.t[0:L, k_, 0:L], rhs=vp.t[0:L, k_, :],
                                    start=False, stop=True)
                fw.op("pe", mmn, reads=P.qb[hm] + [Cb.b(hm), stm.b(), vp.b()], writes=[pA.b()])

                def mmp(e, pB=pB, pA=pA, k_=k_):
                    for j in range(2):
                        e.matmul(pB.t[:, j * 256:(j + 1) * 256], lhsT=ktm.t[0:L, k_, j * 128:(j + 1) * 128],
                                 rhs=vp.t[0:L, k_, 0:256], start=True, stop=True)
                    for j in range(2):
                        ins = e.matmul(pA.t[:, 400 + j:401 + j], lhsT=ktm.t[0:L, k_, j * 128:(j + 1) * 128],
                                       rhs=vp.t[0:L, k_, 256:257], start=True, stop=True)
                    return ins
                fw.op("pe", mmp, reads=[ktm.b(), vp.b()], writes=[pB.b(), pA.b()])
                fw.op("dve", lambda e, pA=pA, hm=hm: e.bn_stats(out=st6.t[0:L, hm, :], in_=pA.t[0:L, 128:384]),
                      reads=[pA.b()], writes=[st6.b(hm)])
                fw.op("dve", lambda e, hm=hm: e.bn_aggr(out=mv.t[0:L, hm, :], in_=st6.t[0:L, hm, :]),
                      reads=[st6.b(hm)], writes=[mv.b(hm)])
                if mix == 0:
                    fw.op("dve", lambda e, pA=pA, h=h: e.tensor_copy(out=self.den.t[0:L, h:h + 1], in_=pA.t[0:L, 384:385]),
                          reads=[pA.b()], writes=[self.den.b(h)])
                G = s.G[mix]
                fw.op("dve", lambda e, pA=pA, hm=hm, G=G, h=h: e.scalar_tensor_tensor(
                    out=u.t[0:L, hm, :], in0=pA.t[0:L, 128:384], scalar=mv.t[0:L, hm, 0:1],
                    in1=G.t[0:L, c, h * 256:(h + 1) * 256], op0=ALU.subtract, op1=ALU.mult),
                    reads=[pA.b(), mv.b(hm), G.b((c, h // 2))], writes=[u.b(hm)])
                fw.op("dve", lambda e, pA=pA, hm=hm, wc=wc: e.scalar_tensor_tensor(
                    out=Cf.t[:, hm, :, 256], in0=Cf.t[:, hm, :, 256], scalar=wc, in1=pA.t[:, 400:402],
                    op0=ALU.mult, op1=ALU.add),
                    reads=[pA.b(), Cf.b(hm)] + rd_g, writes=[Cf.b(hm)])
                fw.op("dve", lambda e, pB=pB, hm=hm, wc=wc: e.scalar_tensor_tensor(
                    out=Cf.t[:, hm, :, 0:256], in0=Cf.t[:, hm, :, 0:256], scalar=wc,
                    in1=pB.t[:, :].rearrange("p (j n) -> p j n", j=2), op0=ALU.mult, op1=ALU.add),
                    reads=[pB.b(), Cf.b(hm)] + rd_g, writes=[Cf.b(hm)])
                if c + 1 < nt:
                    deferred.append(lambda hm=hm: self.make_cb(s, hm, c + 1, L))
            return deferred

        def pm(c, g):
            lowb = self.gm.t[0:L, c, s.gofs + 4:s.gofs + 8]
            keps = K_EPS128 if L == 128 else K_EPS16
            self.post_merge(s, L, 0, c, lowb, cst.t[0:L, keps:keps + 4], [self.gm.b(), cst.b()], 2 * g, 2 * g + 2)

        steps = [(c, g) for c in range(nt) for g in range(2)]
        P_next = prologue(*steps[0])
        pending = []
        prev = None
        for idx, (c, g) in enumerate(steps):
            P_cur = P_next
            if idx + 1 < len(steps):
                P_next = prologue(*steps[idx + 1])
            for f in pending:
                f()
            pending = body(c, g, P_cur)
            if prev is not None:
                pm(*prev)
            prev = (c, g)
        pm(*prev)
        for f in pending:
            f()

    def post_merge(self, s, L, p0, c, lowb, epsr, rd, h0=0, h1=4):
        fw = self.fw
        sm = self.sm
        R = slice(p0, p0 + L)
        H = slice(h0, h1)
        nh = h1 - h0
        fw.op("dve", lambda e: e.scalar_tensor_tensor(out=sm.t[R, 4 + h0:4 + h1], in0=self.den.t[R, H], scalar=-1.0,
                                                      in1=self.den.t[R, H], op0=ALU.mult, op1=ALU.max),
              reads=self.den.bs(range(h0, h1)) + [sm.b()], writes=[sm.b()])
        fw.op("dve", lambda e: e.tensor_tensor(out=sm.t[R, H], in0=sm.t[R, 4 + h0:4 + h1], in1=lowb[:, H], op=ALU.max),
              reads=[sm.b()] + rd, writes=[sm.b()])
        fw.op("dve", lambda e: e.scalar_tensor_tensor(out=sm.t[R, 8 + h0:8 + h1], in0=sm.t[R, H], scalar=LN_EPS,
                                                      in1=sm.t[R, H], op0=ALU.mult, op1=ALU.mult),
              reads=[sm.b()], writes=[sm.b()])
        fw.op("dve", lambda e: e.tensor_tensor(out=sm.t[R, 16 + h0:16 + h1], in0=sm.t[R, 8 + h0:8 + h1],
                                               in1=self.mv.t[R, H, 1], op=ALU.add),
              reads=[sm.b()] + self.mv.bs(range(h0, h1)), writes=[sm.b()])
        fw.op("dve", lambda e: e.tensor_tensor(out=sm.t[R, 20 + h0:20 + h1], in0=epsr[:, H],
                                               in1=self.mv.t[R, 4 + h0:4 + h1, 1], op=ALU.add),
              reads=[sm.b()] + rd + self.mv.bs(range(4 + h0, 4 + h1)), writes=[sm.b()])
        vin = sm.t[R, 16:24].rearrange("p (m h) -> p m h", m=2)[:, :, H]
        vout = sm.t[R, 24:32].rearrange("p (m h) -> p m h", m=2)[:, :, H]
        self.rstd(vout, vin, sm)
        ta = self.tmpa
        u = self.u
        for h in range(h0, h1):
            fw.op("pool", lambda e, h=h: e.tensor_scalar(out=ta.t[R, :], in0=u.t[R, h, :], scalar1=sm.t[R, 24 + h:25 + h],
                                                        scalar2=0.0, op0=ALU.mult, op1=ALU.add),
                  reads=[u.b(h), sm.b()], writes=[ta.b()])
            fw.op("pool", lambda e, h=h: e.tensor_scalar(out=u.t[R, 4 + h, :], in0=u.t[R, 4 + h, :],
                                                        scalar1=sm.t[R, 28 + h:29 + h], scalar2=0.0,
                                                        op0=ALU.mult, op1=ALU.add),
                  reads=[u.b(4 + h), sm.b()], writes=[u.b(4 + h)])
            fw.op("pool", lambda e, h=h: e.tensor_tensor(out=s.merged.t[R, c, h * 256:(h + 1) * 256], in0=ta.t[R, :],
                                                        in1=u.t[R, 4 + h, :], op=ALU.add),
                  reads=[u.b(4 + h), ta.b()], writes=[s.merged.b(c)])

    def store_prompt_state(self):
        d = self.dram
        Cf = self.Cf
        for h in range(4):
            self.store("sp", Cf, h, d["pC"][h].rearrange("(j p) e -> p j e", p=128), Cf.t[:, h, :, 0:256])
            self.store("sp", Cf, 4 + h, d["pS"][h].rearrange("(j p) e -> p j e", p=128), Cf.t[:, 4 + h, :, 0:256])
        tok = self.fw.dma("sp", lambda e: e.dma_start(out=d["pn"].rearrange("h (j p) -> p h j", p=128),
                                                      in_=Cf.t[:, 0:4, :, 256], allow_slow_non_contiguous=True),
                          Cf.b("o"), reads=Cf.bs(range(4)))
        self.fw.out_toks.append(tok)
        self.store("sp", self.mcur, 0, d["pm"], self.mcur.t[:])

    def sample_mixers(self):
        fw = self.fw
        s = self.MINI
        d = self.dram
        cst = self.cst
        R = slice(SP0, SP0 + NS)
        ssm = self.s_sm
        gcol = s.gcol
        def tm_transposes(w):
            ps = self.nextps()
            pv = ps.t[:].bitcast(BF16)

            def tr(e, w=w, pv=pv):
                for k in range(8):
                    ins = e.transpose(out=pv[0:64, k * 128:(k + 1) * 128], in_=self.fm(s, w, k)[:, 0:64],
                                      identity=self.ident_b.t[:])
                return ins
            fw.op("pe", tr, reads=s.fmbig.bs(range(8 * w, 8 * w + 8)) + [self.ident_b.b()], writes=[ps.b()])
            fw.op("act", lambda e, w=w, pv=pv: e.copy(out=self.s_tm[w].t[:, :], in_=pv[0:64, :]),
                  reads=[ps.b()], writes=[self.s_tm[w].b()])
        tm_transposes(0)
        tm_transposes(1)
        self.load("sp", self.s_n, 0, self.s_n.t[R, :], d["sn"])
        self.load("sp", ssm, "m", ssm.t[R, 0:4], d["sm"])
        one = cst.t[R, K_ONE:K_ONE + 1]
        sb_ = [ssm.b(), ssm.b("m")]
        fw.op("act", lambda e: e.activation(out=ssm.t[R, 28:32], in_=gcol.t[R, 0, 4:8], func=AF.Exp, scale=-1.0),
              reads=[gcol.b()] + sb_, writes=[ssm.b()])
        fw.op("act", lambda e: e.activation(out=ssm.t[R, 4:8], in_=ssm.t[R, 28:32], func=AF.Ln, bias=one, scale=1.0),
              reads=[ssm.b(), cst.b()], writes=[ssm.b()])
        fw.op("dve", lambda e: e.tensor_tensor(out=ssm.t[R, 8:12], in0=ssm.t[R, 0:4], in1=ssm.t[R, 4:8], op=ALU.subtract),
              reads=sb_, writes=[ssm.b()])
        fw.op("dve", lambda e: e.tensor_tensor(out=ssm.t[R, 12:16], in0=ssm.t[R, 8:12], in1=gcol.t[R, 0, 0:4], op=ALU.max),
              reads=[ssm.b(), gcol.b()], writes=[ssm.b()])
        fw.op("dve", lambda e: e.tensor_tensor(out=ssm.t[R, 16:20], in0=gcol.t[R, 0, 0:4], in1=ssm.t[R, 12:16],
                                               op=ALU.subtract),
              reads=[ssm.b(), gcol.b()], writes=[ssm.b()])
        fw.op("dve", lambda e: e.tensor_tensor(out=ssm.t[R, 20:24], in0=ssm.t[R, 8:12], in1=ssm.t[R, 12:16],
                                               op=ALU.subtract),
              reads=[ssm.b()], writes=[ssm.b()])
        fw.op("act", lambda e: e.activation(out=ssm.t[R, 16:24], in_=ssm.t[R, 16:24], func=AF.Exp),
              reads=[ssm.b()], writes=[ssm.b()])
        fw.op("act", lambda e: e.activation(out=ssm.t[R, 24:28], in_=ssm.t[R, 12:16], func=AF.Exp, scale=-1.0),
              reads=[ssm.b()], writes=[ssm.b()])
        self.store("sp", ssm, 0, d["om"], ssm.t[R, 12:16])
        big = self.u.t[0:64, :, :].bitcast(F32).rearrange("p a n -> p (a n)")
        bigb = self.u.bs(range(8))
        for (a_, b_, col, bt) in ((self.s_tm[0], self.s_tm[1], 32, None), (self.s_tm[0], self.s_n, 36, None),
                                  (self.s_tm[2], self.s_tm[3], 40, None)):
            if col == 40:
                tm_transposes(2)
                tm_transposes(3)
            fw.op("dve", lambda e, a_=a_, b_=b_: e.tensor_tensor(out=big[R, :], in0=a_.t[R, :], in1=b_.t[R, :], op=ALU.mult),
                  reads=[a_.b(), b_.b()], writes=bigb)
            fw.op("dve", lambda e, col=col: e.tensor_reduce(out=ssm.t[R, col:col + 4],
                                                           in_=big[R, :].rearrange("p (h n) -> p h n", h=4),
                                                           axis=AX.X, op=ALU.add),
                  reads=bigb + [ssm.b()], writes=[ssm.b()])
        fw.op("dve", lambda e: e.tensor_tensor(out=ssm.t[R, 44:48], in0=ssm.t[R, 32:36], in1=ssm.t[R, 16:20], op=ALU.mult),
              reads=[ssm.b()], writes=[ssm.b()])
        fw.op("dve", lambda e: e.tensor_tensor(out=ssm.t[R, 48:52], in0=ssm.t[R, 36:40], in1=ssm.t[R, 20:24], op=ALU.mult),
              reads=[ssm.b()], writes=[ssm.b()])
        fw.op("dve", lambda e: e.tensor_tensor(out=self.den.t[R, :], in0=ssm.t[R, 48:52], in1=ssm.t[R, 44:48], op=ALU.add),
              reads=[ssm.b()], writes=self.den.bs(range(4)))
        kp = self.s_tm[1]
        for h in range(4):
            hs = slice(h * 256, (h + 1) * 256)
            fw.op("act", lambda e, h=h, hs=hs: e.activation(out=kp.t[R, hs], in_=self.s_tm[1].t[R, hs], func=AF.Copy,
                                                           scale=ssm.t[R, 16 + h:17 + h]),
                  reads=[self.s_tm[1].b(), ssm.b()], writes=[kp.b()])
            fw.op("dve", lambda e, h=h, hs=hs: e.scalar_tensor_tensor(
                out=self.s_n.t[R, hs], in0=self.s_n.t[R, hs], scalar=ssm.t[R, 20 + h:21 + h], in1=kp.t[R, hs],
                op0=ALU.mult, op1=ALU.add),
                reads=[self.s_n.b(), ssm.b(), kp.b()], writes=[self.s_n.b()])
        self.store("sp", self.s_n, 0, d["on"], self.s_n.t[R, :])
        dg = self.s_dg
        idr = self.ident_f.t[R, SP0:SP0 + NS]
        fw.op("dve", lambda e: e.tensor_tensor(
            out=dg.t[R, :, 0:4], in0=idr.unsqueeze(2).to_broadcast([NS, NS, 4]),
            in1=ssm.t[R, 20:24].unsqueeze(1).to_broadcast([NS, NS, 4]), op=ALU.mult),
            reads=[ssm.b(), self.ident_f.b()], writes=[dg.b()])
        fw.op("dve", lambda e: e.tensor_tensor(
            out=dg.t[R, :, 4:8], in0=idr.unsqueeze(2).to_broadcast([NS, NS, 4]),
            in1=cst.t[R, K_G1:K_G1 + 4].unsqueeze(1).to_broadcast([NS, NS, 4]), op=ALU.mult),
            reads=[cst.b(), self.ident_f.b(), dg.b()], writes=[dg.b()])
        ps = self.nextps()
        fw.op("pe", lambda e, ps=ps: e.matmul(ps.t[:, 0:128], lhsT=self.ones_f.t[R, :],
                                            rhs=dg.t[R, :, :].rearrange("p a b -> p (a b)"), start=True, stop=True),
              reads=[dg.b(), self.ones_f.b()], writes=[ps.b()])
        fw.op("dve", lambda e, ps=ps: e.tensor_copy(out=self.s_wb.t[:, :], in_=ps.t[:, 0:128]),
              reads=[ps.b()], writes=[self.s_wb.b()])
        ci = 0
        for mix in range(2):
            src, dst = (d["sC"], d["oC"]) if mix == 0 else (d["sS"], d["oS"])
            kpt = kp if mix == 0 else self.s_tm[3]
            vsrc = self.vview(s, mix)
            acc = [self.ps[4 + h] for h in range(4)]
            for b in range(NS):
                qm, vm = (self.qm[0], self.vmk[0]) if b % 2 == 0 else (self.qm2, self.vm2)
                first2 = (mix == 0 and b == 1)
                xw = [self.wbuf[-1].b(0)] if first2 else []
                fw.op("dve", lambda e, qm=qm, b=b, mix=mix: e.tensor_tensor(
                    out=qm.t[:, :, :], in0=s.fmbig.t[:, 16 * mix:16 * mix + 8, :],
                    in1=self.identrow.t[:, b, :].unsqueeze(1).to_broadcast([128, 8, 64]), op=ALU.mult),
                    reads=s.fmbig.bs(range(16 * mix, 16 * mix + 8)) + [self.identrow.b()], writes=[qm.b()], extra=xw)
                fw.op("act", lambda e, vm=vm, b=b, vsrc=vsrc: e.activation(
                    out=vm.t[R, :], in_=vsrc[R, 0, :], func=AF.Copy, scale=self.ident_f.t[R, SP0 + b:SP0 + b + 1]),
                    reads=[s.xres.b(0), self.ident_f.b()], writes=[vm.b()], extra=xw)
                for h in range(4):
                    cin = self.cin[ci % len(self.cin)]
                    cbf = self.cbf[ci % 2]
                    first = ci < len(self.cin) and cin.name in self.cin_owner
                    ci += 1
                    self.fw.dma("sp", lambda e, cin=cin, b=b, h=h, src=src: e.dma_start(
                        out=cin.t[:, :, :], in_=src[b, h].rearrange("(j p) e -> p j e", p=128)),
                        cin.b(0), writes=[cin.b(0)], extra=[self.cin_owner[cin.name].b(0)] if first else ())
                    fw.op("act", lambda e, cin=cin, cbf=cbf: e.copy(out=cbf.t[:, :, :], in_=cin.t[:, :, :]),
                          reads=[cin.b()], writes=[cbf.b()])

                    def mmq(e, qm=qm, cbf=cbf, h=h, b=b):
                        for j in range(2):
                            ins = e.matmul(acc[h].t[0:64, 0:256], lhsT=qm.t[:, 2 * h + j, :], rhs=cbf.t[:, j, :],
                                           start=(b == 0 and j == 0), stop=(b == NS - 1 and j == 1))
                        return ins
                    fw.op("pe", mmq, reads=[qm.b(), cbf.b()], writes=[acc[h].b()])
                    pr = self.ps[ci % 4]

                    def mmr(e, pr=pr, kpt=kpt, vm=vm, h=h):
                        for j in range(2):
                            ins = e.matmul(pr.t[:, j * 256:(j + 1) * 256],
                                           lhsT=kpt.t[R, h * 256 + j * 128:h * 256 + (j + 1) * 128],
                                           rhs=vm.t[R, h * 256:(h + 1) * 256], start=True, stop=True)
                        return ins
                    fw.op("pe", mmr, reads=[kpt.b(), vm.b()], writes=[pr.b()])
                    wcol = b * 8 + mix * 4 + h
                    fw.op("dve", lambda e, cin=cin, pr=pr, wcol=wcol: e.scalar_tensor_tensor(
                        out=cin.t[:, :, :], in0=cin.t[:, :, :], scalar=self.s_wb.t[:, wcol:wcol + 1],
                        in1=pr.t[:, :].rearrange("p (j n) -> p j n", j=2), op0=ALU.mult, op1=ALU.add),
                        reads=[cin.b(), pr.b(), self.s_wb.b()], writes=[cin.b()])
                    self.store("pool", cin, 0, dst[b, h].rearrange("(j p) e -> p j e", p=128), cin.t[:, :, :],
                               semkey="st")
            for h in range(4):
                hm = 4 * mix + h
                ta = self.tmpa
                scol = (44 if mix == 0 else 40) + h
                fw.op("act", lambda e, h=h, scol=scol, vsrc=vsrc: e.activation(
                    out=ta.t[R, :], in_=vsrc[R, 0, h * 256:(h + 1) * 256], func=AF.Copy,
                    scale=ssm.t[R, scol:scol + 1]),
                    reads=[s.xres.b(0), ssm.b()], writes=[ta.b()])
                wsc = ssm.t[R, 20 + h:21 + h] if mix == 0 else cst.t[R, K_G1 + h:K_G1 + h + 1]
                num = self.s_num
                fw.op("dve", lambda e, h=h, wsc=wsc: e.scalar_tensor_tensor(
                    out=num.t[R, :], in0=acc[h].t[R, 0:256], scalar=wsc, in1=ta.t[R, :], op0=ALU.mult, op1=ALU.add),
                    reads=[acc[h].b(), ta.b(), ssm.b(), cst.b()], writes=[num.b()])
                fw.op("dve", lambda e, hm=hm: e.bn_stats(out=self.st6.t[R, hm, :], in_=num.t[R, :]),
                      reads=[num.b()], writes=[self.st6.b(hm)])
                fw.op("dve", lambda e, hm=hm: e.bn_aggr(out=self.mv.t[R, hm, :], in_=self.st6.t[R, hm, :]),
                      reads=[self.st6.b(hm)], writes=[self.mv.b(hm)])
                G = s.G[mix]
                fw.op("dve", lambda e, hm=hm, h=h, G=G: e.scalar_tensor_tensor(
                    out=self.u.t[R, hm, :], in0=num.t[R, :], scalar=self.mv.t[R, hm, 0:1],
                    in1=G.t[R, 0, h * 256:(h + 1) * 256], op0=ALU.subtract, op1=ALU.mult),
                    reads=[num.b(), self.mv.b(hm), G.b((0, h // 2))], writes=[self.u.b(hm)])
        for tl in self.borrowed2:
            wb = self.wbuf[-1].b(0)
            for k_, tok in tl.b(0).r.items():
                wb.r["smp_%s_%s" % (tl.name, k_)] = tok
            if tl.b(0).w is not None:
                wb.r["smpw_" + tl.name] = tl.b(0).w
        for tl in self.cin:
            if tl.name in self.cin_owner:
                wb = self.cin_owner[tl.name].b(0)
                for k_, tok in tl.b(0).r.items():
                    wb.r["smp_%s_%s" % (tl.name, k_)] = tok
                if tl.b(0).w is not None:
                    wb.r["smpw_" + tl.name] = tl.b(0).w
        self.post_merge(s, NS, SP0, 0, ssm.t[R, 24:28], cst.t[R, K_EPS1:K_EPS1 + 4], [ssm.b(), cst.b()])

    def ln_evac(self, s, i, pss):
        fw = self.fw
        tsz = s.tsz
        xr = s.xres
        z = xr.t[:, i, :]
        for hf in range(2):
            cs = slice(hf * 512, (hf + 1) * 512)
            fw.op("dve", lambda e, hf=hf, cs=cs: e.scalar_tensor_tensor(
                out=z[:, cs], in0=z[:, cs], scalar=ALPHA, in1=pss[hf].t[0:tsz, :], op0=ALU.mult, op1=ALU.add),
                reads=[pss[hf].b(), xr.b(i)], writes=[xr.b(i)])

    def ln_tile(self, s, i, g, b):
        fw = self.fw
        tsz = s.tsz
        xr = s.xres
        z = xr.t[:, i, :]
        lmv = self.lmv
        for hf in range(2):
            cs = slice(hf * 512, (hf + 1) * 512)
            fw.op("dve", lambda e, hf=hf, cs=cs: e.bn_stats(out=self.lst.t[0:tsz, hf, :], in_=z[:, cs]),
                  reads=[xr.b(i)], writes=[self.lst.b()])
        fw.op("dve", lambda e: e.bn_aggr(out=lmv.t[0:tsz, 0:2], in_=self.lst.t[0:tsz, :, :].rearrange("p a b -> p (a b)")),
              reads=[self.lst.b()], writes=[lmv.b()])
        fw.op("dve", lambda e: e.tensor_scalar(out=lmv.t[0:tsz, 2:3], in0=lmv.t[0:tsz, 1:2], scalar1=LN_EPS, scalar2=None,
                                               op0=ALU.add),
              reads=[lmv.b()], writes=[lmv.b()])
        self.rstd(lmv.t[0:tsz, 3:4], lmv.t[0:tsz, 2:3], lmv)
        fw.op("dve", lambda e: e.scalar_tensor_tensor(out=z, in0=z, scalar=lmv.t[0:tsz, 0:1], in1=g.t[0:tsz, :],
                                                      op0=ALU.subtract, op1=ALU.mult),
              reads=[xr.b(i), lmv.b(), g.b()], writes=[xr.b(i)])
        fw.op("dve", lambda e: e.scalar_tensor_tensor(out=z, in0=z, scalar=lmv.t[0:tsz, 3:4], in1=b.t[0:tsz, :],
                                                      op0=ALU.mult, op1=ALU.add),
              reads=[xr.b(i), lmv.b(), b.b()], writes=[xr.b(i)])
        return z

    def outproj_ln1(self, sets):
        fw = self.fw
        d = self.dram
        if DEBUG and self.MINI in sets:
            s = self.MINI
            big = self.rot.t[0:64, :, :].rearrange("p a n -> p (a n)")
            fw.op("act", lambda e, s=s: e.copy(out=big, in_=s.merged.t[:, 0, :]), reads=[s.merged.b(0)], writes=self.rot.bs(range(4)))
            tok = fw.dma("sp", lambda e: e.dma_start(out=d["dbg_mrg"], in_=big), self.rot.b("o"), reads=self.rot.bs(range(4)))
            fw.out_toks.append(tok)
        mTb = lambda s: s.fmbig.bs(range(8))
        for s in sets:
            tsz = s.tsz
            for i in range(s.ntile):
                self.transpose_in(s, [s.merged.b(i)], lambda k, s=s, i=i: s.merged.t[:, i, k * 128:(k + 1) * 128],
                                  s.fmbig.t[:, 0:8, i * tsz:(i + 1) * tsz], mTb(s))
        wts = [self.wnext("out", 0), self.wnext("out", 512, hold=1)]
        for s in sets:
            tsz = s.tsz
            allb = s.xres.bs(range(s.ntile))
            if s is self.MAIN:
                r0 = self.pass_idx * 512
                self.fw.dma("sp", lambda e, s=s, r0=r0: e.dma_start(
                    out=s.xres.t[:, :, :], in_=d["xp"][r0:r0 + 512, :].rearrange("(i p) n -> p i n", p=128)),
                    s.xres.b("ld"), writes=allb)
            else:
                self.fw.dma("sp", lambda e, s=s: e.dma_start(out=s.xres.t[0:NMETA, 0, :], in_=d["meta"]),
                            s.xres.b("ld"), writes=allb)
                self.fw.dma("sp", lambda e, s=s: e.dma_start(out=s.xres.t[SP0:SP0 + NS, 0, :], in_=d["xs"]),
                            s.xres.b("ld"), writes=allb, nowait=True)
            pss = []
            for i in range(s.ntile):
                pp = []
                for hf in range(2):
                    ps = self.nextps()
                    pp.append(ps)
                    w3 = wts[hf].t[:, 0:8 * 512].rearrange("p (k n) -> p k n", k=8)

                    def mm(e, ps=ps, w3=w3, i=i, s=s):
                        for k in range(8):
                            ins = e.matmul(ps.t[0:s.tsz, :], lhsT=s.fmbig.t[:, k, i * s.tsz:(i + 1) * s.tsz],
                                           rhs=w3[:, k, :], start=(k == 0), stop=(k == 7))
                        return ins
                    fw.op("pe", mm, reads=mTb(s) + [wts[hf].b()], writes=[ps.b()])
                pss.append(pp)
            for i in range(s.ntile):
                self.ln_evac(s, i, pss[i])
            for i in range(s.ntile):
                z = self.ln_tile(s, i, self.lnp[0], self.lnp[1])
                fw.op("act", lambda e, z=z, s=s, i=i: e.copy(out=s.merged.t[:, i, :], in_=z),
                      reads=[s.xres.b(i)], writes=[s.merged.b(i)])
                if DEBUG and s is self.MINI:
                    self.store("sp", s.xres, i, d["dbg_x1"], s.xres.t[:, 0, :])
                self.transpose_in(s, [s.merged.b(i)], lambda k, s=s, i=i: s.merged.t[:, i, k * 128:(k + 1) * 128],
                                  s.fmbig.t[:, 0:8, i * tsz:(i + 1) * tsz], mTb(s))

    def ffn(self, sets, hook=None):
        fw = self.fw
        for c0 in range(0, DFF, 512):
            wg = self.wnext("gate", c0)
            wu = self.wnext("up", c0, hold=1)
            nch = min(4, (DFF - c0) // 128)
            g3 = wg.t[:, 0:8 * 512].rearrange("p (k n) -> p k n", k=8)
            u3 = wu.t[:, 0:8 * 512].rearrange("p (k n) -> p k n", k=8)
            for s in sets:
                T = s.T
                x1r = s.fmbig.bs(range(8))

                def grp(ps, w3, wt, m, s=s, T=T):
                    def mm(e):
                        for k in range(8):
                            ins = e.matmul(ps.t[:, 0:T], lhsT=w3[:, k, m * 128:(m + 1) * 128], rhs=s.fmbig.t[:, k, :],
                                           start=(k == 0), stop=(k == 7))
                        return ins
                    fw.op("pe", mm, reads=x1r + [wt.b()], writes=[ps.b()])
                pgs = []
                for m in range(nch):
                    pg = self.nextps()
                    pgs.append(pg)
                    grp(pg, g3, wg, m)
                tbs = []
                for m in range(nch):
                    fc = c0 // 128 + m
                    pu = self.nextps()
                    grp(pu, u3, wu, m)
                    pg = pgs[m]
                    tb = self.tmpb[self.tbi % 2]
                    self.tbi += 1
                    fw.op("act", lambda e, tb=tb, pg=pg, T=T: e.activation(out=tb.t[:, 0:T], in_=pg.t[:, 0:T], func=AF.Silu),
                          reads=[pg.b()], writes=[tb.b()])
                    fw.op("dve", lambda e, tb=tb, pu=pu, T=T, s=s, fc=fc: e.tensor_tensor(
                        out=s.fmbig.t[:, 8 + fc, :], in0=tb.t[:, 0:T], in1=pu.t[:, 0:T], op=ALU.mult),
                        reads=[tb.b(), pu.b()], writes=[s.fmbig.b(8 + fc)])
        small = [(s, i) for s in sets if s is not self.MAIN for i in range(s.ntile)]
        if small:
            self.mini_down = small
        if hook is not None:
            hook()
        return self.down([(self.MAIN, i) for i in range(4)], defer=hook is not None)

    def down(self, tiles, defer=False, banks=None, blocks=None):
        fw = self.fw
        d = self.dram
        if banks is None:
            banks = [[self.ps[2 * n], self.ps[2 * n + 1]] for n in range(len(tiles))]
        todo = list(range(0, DFF, 512)) if blocks is None else ([] if blocks == "finish" else [blocks])
        for r0 in todo:
            wt = self.wnext("down", r0)
            nch = min(4, (DFF - r0) // 128)
            w4 = wt.t[:, 0:4 * 1024].rearrange("p (c n) -> p c n", c=4)
            for n, (s, i) in enumerate(tiles):
                tsz = s.tsz
                for hf in range(2):
                    ps = banks[n][hf]

                    def mm(e, ps=ps, i=i, hf=hf, s=s, nch=nch, r0=r0, tsz=tsz, w4=w4):
                        for m in range(nch):
                            fc = r0 // 128 + m
                            ins = e.matmul(ps.t[0:tsz, :], lhsT=s.fmbig.t[:, 8 + fc, i * tsz:(i + 1) * tsz],
                                           rhs=w4[:, m, hf * 512:(hf + 1) * 512],
                                           start=(fc == 0), stop=(fc == NFF - 1))
                        return ins
                    fw.op("pe", mm, reads=s.fmbig.bs(range(8 + r0 // 128, 8 + r0 // 128 + nch)) + [wt.b()],
                          writes=[ps.b()])
        if blocks is not None and blocks != "finish":
            return []
        for n, (s, i) in enumerate(tiles):
            self.ln_evac(s, i, banks[n])

        def finish(s, i, r0):
            def f():
                self.ln_tile(s, i, self.lnp[2], self.lnp[3])
                if s is self.MAIN:
                    self.store("sp", s.xres, i, d["yp"][r0:r0 + 128, :], s.xres.t[:, i, :])
                else:
                    self.store("sp", s.xres, i, d["ys"], s.xres.t[SP0:SP0 + NS, 0, :])
            return f
        fins = [finish(s, i, self.pass_idx * 512 + i * 128) for (s, i) in tiles]
        if defer:
            return fins
        for f in fins:
            f()
        return []

    def program(self):
        NP = 4
        fw = self.fw
        self.wspecs = self.wspec_list(NP)
        self.wi = 0
        self.wloaded = 0
        self.wscr = self.nc.dram_tensor("wscr", [40, 128, 9 * 512], BF16).ap()
        self.scr_slot = {}
        self.scr_buf = {}
        self.no_prefetch_beyond = 19
        self.gw_loaded = False
        self.load_consts()
        fw.op("dve", lambda e: e.memset(self.MINI.merged.t[:], 0.0), writes=[self.MINI.merged.b(0)])
        fw.op("dve", lambda e: e.memset(self.MINI.xres.t[:], 0.0), writes=self.MINI.xres.bs(range(1)))
        for p in range(NP):
            self.pass_idx = p
            sets = [self.MINI, self.MAIN] if p == 0 else [self.MAIN]
            if p == 0:
                self.load_x_mini()
                self.load_x_main(0)
                gw = self.gw
                self.load("pool", gw, 0, gw.t[:, 0:8, :],
                          self.dram["w_in"][:, C_MI:C_MI + 8].rearrange("(k p) n -> p k n", p=128))
                self.load("pool", gw, 0, gw.t[0:1, 8, :], self.dram["b_in"][:, C_MI:C_MI + 8], nowait=True)
                self.gw_loaded = True
                while self.wloaded < 2:
                    self.wload(self.wloaded)
                    self.wloaded += 1
                self.load_consts_late()
            self.projection(sets)
            if p == 0:
                self.mixers(self.MINI)
                self.sample_mixers()
                self.no_prefetch_beyond = None
                while self.wloaded < min(len(self.wspecs), self.wi + self.NWB):
                    self.wload(self.wloaded)
                    self.wloaded += 1
            self.mixers(self.MAIN)
            if p == NP - 1:
                self.store_prompt_state()
            self.outproj_ln1(sets)
            self.pending_ln = self.ffn(sets, (lambda p=p: self.load_x_main(p + 1)) if p + 1 < NP else None)
        assert self.wi == len(self.wspecs)


def _consts():
    f32 = np.float32
    ident = np.eye(128, dtype=f32)
    maskT = np.triu(np.ones((128, 128), dtype=f32))
    lg = np.log1p(-np.exp2(-5.0 - np.arange(4, dtype=np.float64)))
    cst = np.zeros((128, NCST), dtype=np.float64)
    s128 = np.arange(128)[:, None]
    cst[:, K_ES128:K_ES128 + 4] = np.exp((127 - s128) * lg[None, :])
    cst[:, K_ES16:K_ES16 + 4] = np.exp((15 - s128) * lg[None, :])
    cst[:, K_WC128:K_WC128 + 4] = np.exp(128 * lg)[None, :]
    cst[:, K_WC16:K_WC16 + 4] = np.exp(16 * lg)[None, :]
    cst[:, K_EPS128:K_EPS128 + 4] = LN_EPS * np.exp(2 * (127 - s128) * lg[None, :])
    cst[:, K_EPS16:K_EPS16 + 4] = LN_EPS * np.exp(2 * (15 - s128) * lg[None, :])
    cst[:, K_G1:K_G1 + 4] = np.exp(lg)[None, :]
    cst[:, K_EPS1:K_EPS1 + 4] = LN_EPS
    cst[:, K_ONE] = 1.0
    cst = cst.astype(f32)
    inv = 10000.0 ** (-np.arange(0, 256, 2, dtype=np.float64) / 256.0)

    def tables(pos):
        ang = np.asarray(pos, dtype=np.float64)[:, None] * inv[None, :]
        return np.ascontiguousarray(np.cos(ang).T.astype(f32)), np.ascontiguousarray(np.sin(ang).T.astype(f32))
    ropeC, ropeS = tables(np.arange(NMETA, NMETA + SEQ))
    pm = np.zeros(64)
    pm[0:NMETA] = np.arange(NMETA)
    pm[SP0:SP0 + NS] = PAST
    ropeCm, ropeSm = tables(pm)
    identrow = np.zeros((128, 16, 64), dtype=f32)
    for b in range(16):
        identrow[:, b, SP0 + b] = 1.0
    return dict(ident=ident, maskT=maskT, cst=cst, ropeC=ropeC, ropeS=ropeS, ropeCm=ropeCm, ropeSm=ropeSm,
                identrow=identrow.reshape(128, 1024))


_NC_CACHE = {}


def kernel(x_prompt, x_sample, state_mlstm_C, state_mlstm_n, state_mlstm_m, state_ret_S,
           meta_tokens, w_in, b_in, ml_norm_g, rt_norm_g, w_out,
           ln1_g, ln1_b, w_gate, w_up, w_down, ln2_g, ln2_b):
    f = lambda a: np.ascontiguousarray(np.asarray(a, dtype=np.float32))
    if "nc" not in _NC_CACHE:
        _NC_CACHE["nc"] = Prog().build()
    nc = _NC_CACHE["nc"]
    cs = _consts()
    shared = dict(meta=f(meta_tokens), w_in=f(w_in)[0], b_in=f(b_in), ml_g=f(ml_norm_g), rt_g=f(rt_norm_g),
                  w_out=f(w_out)[0], ln1_g=f(ln1_g), ln1_b=f(ln1_b), w_gate=f(w_gate)[0], w_up=f(w_up)[0],
                  w_down=f(w_down)[0], ln2_g=f(ln2_g), ln2_b=f(ln2_b), **cs)
    xp, xs = f(x_prompt), f(x_sample)
    sC, sn, sm, sS = f(state_mlstm_C)[0], f(state_mlstm_n)[0], f(state_mlstm_m)[0], f(state_ret_S)[0]
    in_maps = []
    for c in range(NCORES):
        sl = slice(c * NS, (c + 1) * NS)
        m = dict(shared)
        m.update(xp=xp[c], xs=np.ascontiguousarray(xs[sl, 0, :]), sC=np.ascontiguousarray(sC[sl]),
                 sn=np.ascontiguousarray(sn[sl].reshape(NS, 1024)), sm=np.ascontiguousarray(sm[sl]),
                 sS=np.ascontiguousarray(sS[sl]))
        in_maps.append(m)
    res = run_bass_kernel_spmd(nc, in_maps, core_ids=list(range(NCORES)))
    R = res.results
    if DEBUG:
        _NC_CACHE["dbg"] = dict(mrg=R[0]["dbg_mrg"], x1=R[0]["dbg_x1"])
    cat = lambda k: np.concatenate([r[k] for r in R], axis=0)
    stk = lambda k: np.stack([r[k] for r in R], axis=0)
    y_prompt = stk("yp")
    y_sample = cat("ys").reshape(128, 1, D)
    p_C = stk("pC")[None]
    p_n = stk("pn")[None]
    p_m = stk("pm").reshape(1, NCORES, 4)
    p_S = stk("pS")[None]
    s_C = cat("oC")[None]
    s_n = cat("on").reshape(1, 128, 4, 256)
    s_m = cat("om")[None]
    s_S = cat("oS")[None]
    return (y_prompt, y_sample, p_C, p_n, p_m, p_S, s_C, s_n, s_m, s_S)
```

```python
from contextlib import ExitStack

import numpy as np
import concourse.bass as bass
import concourse.mybir as mybir
from concourse.bass_utils import run_bass_kernel_spmd

F32 = mybir.dt.float32
BF16 = mybir.dt.bfloat16
AF = mybir.ActivationFunctionType
ALU = mybir.AluOpType
AX = mybir.AxisListType

NCORES = 8
D = 1024
SEQ = 2048
NMETA = 16
NS = 16
SP0 = 32
DIN = 10248
DFF = 2816
NFF = DFF // 128
LN_EPS = 1e-5
ALPHA = 2.0 ** 0.25
PAST = 16384
C_MI = 4096

K_ES128, K_ES16, K_WC128, K_WC16, K_EPS128, K_EPS16, K_G1, K_EPS1, K_ONE = (0, 4, 8, 12, 16, 20, 24, 28, 32)
NCST = 40
DEBUG = False


class Buf:
    __slots__ = ("name", "w", "r", "sem", "cnt")

    def __init__(self, name):
        self.name = name
        self.w = None
        self.r = {}
        self.sem = None
        self.cnt = 0


class FW:
    ENG = ("pe", "act", "dve", "pool", "sp")

    def __init__(self, nc, stack):
        self.nc = nc
        self.stack = stack
        self.sem = {e: stack.enter_context(nc.semaphore("s_" + e)) for e in self.ENG}
        self.n = {e: 0 for e in self.ENG}
        self.known = {e: {} for e in self.ENG}
        self.rec = {e: [] for e in self.ENG}
        self.nsem = len(self.ENG)
        self.out_toks = []

    def _waits(self, E, reads, writes):
        need = {}

        def add(tok):
            if tok is None:
                return
            s, v = tok
            if need.get(s, 0) < v:
                need[s] = v
        for b in reads:
            add(b.w)
        for b in writes:
            add(b.w)
            for t in b.r.values():
                add(t)
        out = []
        kn = self.known[E]
        for s, v in need.items():
            if E == "pe" and s is self.sem["pe"]:
                continue
            if kn.get(s, 0) < v:
                kn[s] = v
                out.append((s, v))
        return out

    def op(self, E, fn, reads=(), writes=(), extra=()):
        waits = self._waits(E, reads, list(writes) + list(extra))
        self.n[E] += 1
        sem = self.sem[E]
        tok = (sem, self.n[E])
        self.rec[E].append((waits, fn, sem, 1))
        for b in reads:
            b.r[E] = tok
        for b in writes:
            b.w = tok
            b.r = {}
        return tok

    def dma(self, Q, fn, sb, reads=(), writes=(), nowait=False, extra=()):
        waits = [] if nowait else self._waits(Q, reads, list(writes) + list(extra))
        if sb.sem is None:
            sb.sem = self.stack.enter_context(self.nc.semaphore("d_" + sb.name))
            self.nsem += 1
        sb.cnt += 16
        tok = (sb.sem, sb.cnt)
        self.rec[Q].append((waits, fn, sb.sem, 16))
        for b in reads:
            b.r["dma_" + sb.name] = tok
        for b in writes:
            b.w = tok
            b.r = {}
        return tok

    def emit(self):
        nc = self.nc
        need = {}
        for s, v in self.out_toks:
            if need.get(s, 0) < v:
                need[s] = v
        self.rec["sp"].append((list(need.items()), None, None, 0))
        with nc.Block() as block:
            def run(eng, lst):
                for waits, fn, sem, inc in lst:
                    for s, v in waits:
                        eng.wait_ge(s, v)
                    if fn is not None:
                        fn(eng).then_inc(sem, inc)

            @block.tensor
            def _(e):
                run(e, self.rec["pe"])

            @block.scalar
            def _(e):
                run(e, self.rec["act"])

            @block.vector
            def _(e):
                run(e, self.rec["dve"])

            @block.gpsimd
            def _(e):
                run(e, self.rec["pool"])

            @block.sync
            def _(e):
                run(e, self.rec["sp"])


class Tl:
    def __init__(self, t, name):
        self.t = t
        self.name = name
        self._b = {}
        self.coarse = False

    def b(self, key=0):
        if self.coarse:
            key = 0
        if key not in self._b:
            self._b[key] = Buf("%s_%s" % (self.name, key))
        return self._b[key]

    def bs(self, keys):
        return [self.b(k) for k in keys]


class TokSet:
    pass


class Prog:
    def __init__(self):
        self.nc = bass.Bass("TRN2", target_bir_lowering=False)
        self.st = ExitStack()
        self.dram = {}
        self.xbi = 0
        self.pending_ln = []
        self.ps_reserved = set()
        self.mini_down = None
        self.mgi = 0
        self.tbi = 0
        self.psi = 0
        self.sbytes = 0

    def din(self, name, shape):
        self.dram[name] = self.nc.dram_tensor(name, list(shape), F32, kind="ExternalInput").ap()

    def dout(self, name, shape):
        self.dram[name] = self.nc.dram_tensor(name, list(shape), F32, kind="ExternalOutput").ap()

    def sb(self, name, shape, dt):
        n = 1
        for x in shape[1:]:
            n *= x
        self.sbytes += n * (4 if dt == F32 else 2)
        return Tl(self.st.enter_context(self.nc.sbuf_tensor("sb_" + name, list(shape), dt)), name)

    def load(self, Q, tl, key, out_ap, in_ap, nowait=False):
        return self.fw.dma(Q, lambda e: e.dma_start(out=out_ap, in_=in_ap), tl.b(key),
                           writes=[tl.b(key)], nowait=nowait)

    def store(self, Q, tl, key, out_ap, in_ap, reads=None, semkey=None):
        tok = self.fw.dma(Q, lambda e: e.dma_start(out=out_ap, in_=in_ap), tl.b(key if semkey is None else semkey),
                          reads=[tl.b(key)] if reads is None else reads)
        self.fw.out_toks.append(tok)
        return tok

    def build(self):
        nc = self.nc
        with self.st:
            self.fw = FW(nc, self.st)
            self.declare()
            self.alloc()
            self.program()
            self.fw.emit()
        return nc

    def declare(self):
        d = self.din
        d("xp", (SEQ, D)); d("meta", (NMETA, D)); d("xs", (NS, D))
        d("sC", (NS, 4, 256, 256)); d("sn", (NS, 1024)); d("sm", (NS, 4)); d("sS", (NS, 4, 256, 256))
        d("w_in", (D, DIN)); d("b_in", (1, DIN))
        d("ml_g", (1, D)); d("rt_g", (1, D)); d("w_out", (D, D))
        d("ln1_g", (1, D)); d("ln1_b", (1, D))
        d("w_gate", (D, DFF)); d("w_up", (D, DFF)); d("w_down", (DFF, D))
        d("ln2_g", (1, D)); d("ln2_b", (1, D))
        d("ident", (128, 128)); d("maskT", (128, 128)); d("cst", (128, NCST))
        d("ropeC", (128, SEQ)); d("ropeS", (128, SEQ))
        d("ropeCm", (128, 64)); d("ropeSm", (128, 64))
        d("identrow", (128, 1024))
        o = self.dout
        o("yp", (SEQ, D)); o("ys", (NS, D))
        if DEBUG:
            o("dbg_mrg", (64, D)); o("dbg_x1", (64, D))
        o("pC", (4, 256, 256)); o("pn", (4, 256)); o("pm", (4, 1)); o("pS", (4, 256, 256))
        o("oC", (NS, 4, 256, 256)); o("on", (NS, 1024)); o("om", (NS, 4)); o("oS", (NS, 4, 256, 256))

    def mkset(self, name, T, tsz):
        s = TokSet()
        s.name, s.T, s.tsz, s.ntile = name, T, tsz, T // tsz
        sb = self.sb
        s.alias = (tsz == 128)
        s.xT = sb(name + "xT", [128, 8, T], BF16)
        s.fmbig = sb(name + "fm", [128, 32, T], BF16)
        s.xres = sb(name + "xres", [tsz, s.ntile, 1024], F32)
        s.G = [sb(name + "G%d" % i, [tsz, s.ntile, 1024], BF16) for i in range(2)]
        s.gcol = sb(name + "gcol", [tsz, s.ntile, 8], F32)
        if s.alias:
            s.xT.coarse = True
            mv_ = s.xT.t[:, :, :].rearrange("p k t -> p (k t)").rearrange("p (i n) -> p i n", i=s.ntile)
            s.merged = Tl(mv_, s.xT.name)
            s.merged._b = s.xT._b
            s.merged.coarse = True
        else:
            s.merged = sb(name + "mrg", [tsz, s.ntile, 1024], BF16)
        s.rc = sb(name + "rc", [128, T], F32)
        s.rs = sb(name + "rs", [128, T], F32)
        return s

    def fm(self, s, which, chunk):
        return s.fmbig.t[:, 8 * which + chunk, :]

    def vview(self, s, mix):
        nt = s.ntile
        half = s.xres.t[:, :, :].rearrange("p a n -> p (a n)").bitcast(BF16)
        return half[:, mix * nt * 1024:(mix + 1) * nt * 1024].rearrange("p (a n) -> p a n", a=nt)

    def alloc(self):
        nc, st, sb = self.nc, self.st, self.sb
        self.ps = [Tl(st.enter_context(nc.psum_tensor("ps%d" % i, [128, 512], F32)), "ps%d" % i)
                   for i in range(8)]
        self.MAIN = self.mkset("M", 512, 128)
        self.MINI = self.mkset("E", 64, 64)
        self.NWB = 3
        self.wbuf = [sb("wb%d" % i, [128, 9 * 512], BF16) for i in range(self.NWB)]
        self.gw = sb("gw", [128, 9, 8], BF16)
        self.ident_b = sb("ident_b", [128, 128], BF16)
        self.ident_f = sb("ident_f", [128, 128], F32)
        self.maskT = sb("maskT", [128, 128], F32)
        self.cst = sb("cst", [128, NCST], F32)
        self.ones_b = sb("ones_b", [128, 512], BF16)
        self.ones_f = sb("ones_f", [128, 128], F32)
        self.gbc = [sb("gbc%d" % i, [128, 1024], BF16) for i in range(2)]
        self.lnp = [sb("lnp%d" % i, [128, 1024], BF16) for i in range(4)]
        self.xbf = [sb("xbf%d" % i, [128, 1024], BF16) for i in range(2)]
        self.tmpb = [sb("tmpb%d" % i, [128, 512], BF16) for i in range(2)]
        self.rot = sb("rot", [128, 4, 256], F32)
        self.Cf = sb("Cf", [128, 8, 2, 257], F32)
        self.Cb = sb("Cb", [128, 8, 2, 257], BF16)
        self.ktm = [sb("ktm%d" % i, [128, 4, 256], BF16) for i in range(2)]
        self.vp = [sb("vp%d" % i, [128, 4, 257], BF16) for i in range(2)]
        self.stm = [sb("stm%d" % i, [128, 4, 128], BF16) for i in range(2)]
        self.u = sb("u", [128, 8, 256], BF16)
        self.st6 = sb("st6", [128, 8, 6], F32)
        self.mv = sb("mv", [128, 8, 2], F32)
        self.den = sb("den", [128, 4], F32)
        self.sm = sb("smalls", [128, 64], F32)
        self.tmpa = sb("tmpa", [128, 256], F32)
        self.gm = sb("gm", [128, 4, 40], F32)
        self.gsm = sb("gsm", [4, 96], F32)
        self.gbcst = sb("gbcst", [128, 5, 8], F32)
        self.wcb = sb("wcb", [128, 5, 4], F32)
        self.mcur = sb("mcur", [4, 1], F32)
        self.lst = sb("lst", [128, 2, 6], F32)
        self.lmv = sb("lmv", [128, 4], F32)
        self.NCIN = 3
        self.cin = [sb("cin%d" % i, [128, 2, 256], F32) for i in range(self.NCIN)]
        self.cin_owner = {}
        for i, wb in enumerate(self.wbuf):
            for q in range(4):
                v_ = wb.t[:, q * 1024:(q + 1) * 1024].bitcast(F32).rearrange("p (j e) -> p j e", j=2)
                tl = Tl(v_, "cs%d_%d" % (i, q))
                self.cin.append(tl)
                self.cin_owner[tl.name] = wb
        wb = self.wbuf[-1]
        for _ in range(2):
            tl = self.cin.pop()
            del self.cin_owner[tl.name]
        self.qm2 = Tl(wb.t[:, 3 * 1024:3 * 1024 + 512].rearrange("p (k n) -> p k n", k=8), "qm2")
        self.vm2 = Tl(wb.t[0:64, 2 * 1024:3 * 1024], "vm2")
        self.borrowed2 = [self.qm2, self.vm2]
        self.cbf = [sb("cbf%d" % i, [128, 2, 256], BF16) for i in range(2)]
        self.qm = [sb("qm0", [128, 8, 64], BF16)] * 2
        self.vmk = [sb("vmk0", [64, 1024], BF16)] * 2
        self.identrow = sb("identrow", [128, 16, 64], BF16)
        self.s_num = sb("s_num", [64, 256], F32)
        self.s_tm = [sb("s_tm%d" % i, [64, 1024], BF16) for i in range(3)]
        self.s_tm = [self.s_tm[0], self.s_tm[1], self.s_tm[0], self.s_tm[2]]
        self.s_n = Tl(self.rot.t[0:64, :, :].rearrange("p a n -> p (a n)"), "rot")
        self.s_n._b = self.rot._b
        self.s_n.coarse = True
        self.rot.coarse = True
        self.s_sm = sb("s_sm", [64, 64], F32)
        self.s_wb = sb("s_wb", [128, 128], F32)
        self.s_dg = sb("s_dg", [64, 16, 8], F32)

    def nextps(self):
        while (self.psi % 8) in self.ps_reserved:
            self.psi += 1
        p = self.ps[self.psi % 8]
        self.psi += 1
        return p

    def wspec_list(self, npass):
        L = []
        for p in range(npass):
            for n_, c0 in enumerate(list(range(0, 4096, 512)) + list(range(4104, DIN, 512))):
                L.append(("in", c0))
                if p == 1 and n_ < 6:
                    L.append(("down", n_ * 512))
            for c0 in (0, 512):
                L.append(("out", c0))
            for c0 in range(0, DFF, 512):
                L.append(("gate", c0))
                L.append(("up", c0))
            for r0 in range(0, DFF, 512):
                L.append(("down", r0))
        return L

    def wload(self, idx):
        kind, c0 = self.wspecs[idx]
        tl = self.wbuf[idx % self.NWB]
        d = self.dram
        uid = (kind, c0)
        def regions(t2):
            if kind == "in":
                return [t2[:, 0:8 * 512], t2[0:1, 8 * 512:9 * 512]]
            if kind == "out":
                return [t2[:, 0:8 * 512]]
            if kind in ("gate", "up"):
                n = min(512, DFF - c0)
                return [t2[:, 0:8 * 512].rearrange("p (k n) -> p k n", k=8)[:, :, 0:n]]
            nch = min(4, (DFF - c0) // 128)
            return [t2[:, 0:nch * 1024]]
        if uid in self.scr_slot:
            slot = self.scr_slot[uid]
            for n_, (o_, i_) in enumerate(zip(regions(tl.t), regions(self.wscr[slot]))):
                self.fw.dma("pool", lambda e, o_=o_, i_=i_: e.dma_start(out=o_, in_=i_), tl.b(0),
                            reads=[self.scr_buf[slot]], writes=[tl.b(0)], nowait=(n_ > 0))
            return
        self._wload_cast(idx)
        slot = len(self.scr_slot)
        self.scr_slot[uid] = slot
        self.scr_buf[slot] = Buf("scr%d" % slot)
        for n_, (o_, i_) in enumerate(zip(regions(self.wscr[slot]), regions(tl.t))):
            self.fw.dma("sp", lambda e, o_=o_, i_=i_: e.dma_start(out=o_, in_=i_), tl.b("st"),
                        reads=[tl.b(0)], writes=[self.scr_buf[slot]], nowait=(n_ > 0))

    def _wload_cast(self, idx):
        kind, c0 = self.wspecs[idx]
        tl = self.wbuf[idx % self.NWB]
        d = self.dram
        t3 = tl.t[:, 0:8 * 512].rearrange("p (k n) -> p k n", k=8)
        if kind == "in":
            self.load("pool", tl, 0, t3, d["w_in"][:, c0:c0 + 512].rearrange("(k p) n -> p k n", p=128))
            self.load("pool", tl, 0, tl.t[0:1, 8 * 512:9 * 512], d["b_in"][:, c0:c0 + 512], nowait=True)
        elif kind == "out":
            self.load("pool", tl, 0, t3, d["w_out"][:, c0:c0 + 512].rearrange("(k p) n -> p k n", p=128))
        elif kind in ("gate", "up"):
            w = d["w_gate"] if kind == "gate" else d["w_up"]
            n = min(512, DFF - c0)
            self.load("pool", tl, 0, t3[:, :, 0:n], w[:, c0:c0 + n].rearrange("(k p) n -> p k n", p=128))
        else:
            nch = min(4, (DFF - c0) // 128)
            t4 = tl.t[:, 0:4 * 1024].rearrange("p (c n) -> p c n", c=4)
            self.load("pool", tl, 0, t4[:, 0:nch, :],
                      d["w_down"][c0:c0 + nch * 128, :].rearrange("(c p) n -> p c n", p=128))

    def wnext(self, kind, c0, hold=0):
        i = self.wi
        assert self.wspecs[i] == (kind, c0), (self.wspecs[i], kind, c0)
        lim = min(len(self.wspecs), i + self.NWB - hold)
        if self.no_prefetch_beyond is not None:
            lim = min(lim, self.no_prefetch_beyond + 1)
        while self.wloaded < lim:
            self.wload(self.wloaded)
            self.wloaded += 1
        self.wi += 1
        return self.wbuf[i % self.NWB]

    def load_consts(self):
        d = self.dram
        fw = self.fw
        self.load("sp", self.ident_f, 0, self.ident_f.t[:], d["ident"])
        self.load("pool", self.ident_b, 0, self.ident_b.t[:], d["ident"])
        self.load("sp", self.maskT, 0, self.maskT.t[:], d["maskT"])
        self.load("sp", self.cst, 0, self.cst.t[:], d["cst"])
        self.load("pool", self.identrow, 0, self.identrow.t[:].rearrange("p a b -> p (a b)"), d["identrow"])
        fw.op("dve", lambda e: e.memset(self.ones_b.t[:], 1.0), writes=[self.ones_b.b()])
        fw.op("dve", lambda e: e.memset(self.ones_f.t[:], 1.0), writes=[self.ones_f.b()])
        fw.op("dve", lambda e: e.memset(self.Cf.t[:], 0.0), writes=self.Cf.bs(range(8)))
        fw.op("dve", lambda e: e.memset(self.mcur.t[:], 0.0), writes=[self.mcur.b()])

    def load_consts_late(self):
        d = self.dram
        for tl, nm in ((self.gbc[0], "ml_g"), (self.gbc[1], "rt_g"), (self.lnp[0], "ln1_g"),
                       (self.lnp[1], "ln1_b"), (self.lnp[2], "ln2_g"), (self.lnp[3], "ln2_b")):
            self.load("pool", tl, 0, tl.t[:], d[nm].partition_broadcast(128))

    def transpose_in(self, s, src_bufs, src_ap_fn, dst_ap, dst_bufs):
        tsz = s.tsz
        ps = self.nextps()
        pv = ps.t[:].bitcast(BF16).rearrange("p (k n) -> p k n", k=8)

        def tr(e):
            for k in range(8):
                ins = e.transpose(out=pv[:, k, 0:tsz], in_=src_ap_fn(k), identity=self.ident_b.t[0:tsz, 0:tsz])
            return ins
        self.fw.op("pe", tr, reads=list(src_bufs) + [self.ident_b.b()], writes=[ps.b()])
        self.fw.op("act", lambda e: e.copy(out=dst_ap, in_=pv[:, :, 0:tsz]), reads=[ps.b()], writes=list(dst_bufs))

    def load_x_main(self, p):
        s = self.MAIN
        d = self.dram
        for i in range(4):
            xb = self.xbf[self.xbi % 2]
            self.xbi += 1
            r0 = p * 512 + i * 128
            self.load("pool", xb, 0, xb.t[:, :], d["xp"][r0:r0 + 128, :])
            self.transpose_in(s, [xb.b()], lambda k, xb=xb: xb.t[:, k * 128:(k + 1) * 128],
                              s.xT.t[:, :, i * 128:(i + 1) * 128], [s.xT.b(i)])
        self.load("sp", s.rc, 0, s.rc.t[:], d["ropeC"][:, p * 512:(p + 1) * 512])
        self.load("sp", s.rs, 0, s.rs.t[:], d["ropeS"][:, p * 512:(p + 1) * 512])

    def load_x_mini(self):
        s = self.MINI
        d = self.dram
        xb = self.vmk[0]
        self.fw.op("dve", lambda e: e.memset(xb.t[:], 0.0), writes=[xb.b()])
        self.load("pool", xb, 0, xb.t[0:NMETA, :], d["meta"])
        self.load("pool", xb, 0, xb.t[SP0:SP0 + NS, :], d["xs"], nowait=True)
        self.transpose_in(s, [xb.b()], lambda k: xb.t[:, k * 128:(k + 1) * 128], s.xT.t[:, :, :], [s.xT.b(0)])
        self.load("sp", s.rc, 0, s.rc.t[:], d["ropeCm"])
        self.load("sp", s.rs, 0, s.rs.t[:], d["ropeSm"])

    def proj_gates(self, s):
        fw = self.fw
        gw = self.gw
        for i in range(s.ntile):
            ps = self.nextps()

            def mm(e, i=i, ps=ps):
                for k in range(8):
                    e.matmul(ps.t[0:s.tsz, 0:8], lhsT=s.xT.t[:, k, i * s.tsz:(i + 1) * s.tsz],
                             rhs=gw.t[:, k, :], start=(k == 0), stop=False)
                return e.matmul(ps.t[0:s.tsz, 0:8], lhsT=self.ones_b.t[0:1, 0:s.tsz],
                                rhs=gw.t[0:1, 8, :], start=False, stop=True)
            fw.op("pe", mm, reads=[s.xT.b(i), gw.b(), self.ones_b.b()], writes=[ps.b()])
            fw.op("act", lambda e, i=i, ps=ps: e.copy(out=s.gcol.t[:, i, :], in_=ps.t[0:s.tsz, 0:8]),
                  reads=[ps.b()], writes=[s.gcol.b()])

    def proj_fm(self, s, wt, cbase, which, scale, rot):
        fw = self.fw
        T = s.T
        w3 = wt.t[:, 0:8 * 512].rearrange("p (k n) -> p k n", k=8)
        brow = wt.t[0:1, 8 * 512:9 * 512]
        xr = s.xT.bs(range(s.ntile))
        pss = []
        for m in range(4):
            ps = self.nextps()
            pss.append(ps)

            def mm(e, m=m, ps=ps):
                for k in range(8):
                    e.matmul(ps.t[:, 0:T], lhsT=w3[:, k, m * 128:(m + 1) * 128], rhs=s.xT.t[:, k, :],
                             start=(k == 0), stop=False)
                return e.matmul(ps.t[:, 0:T], lhsT=brow[:, m * 128:(m + 1) * 128], rhs=self.ones_b.t[0:1, 0:T],
                                start=False, stop=True)
            fw.op("pe", mm, reads=xr + [wt.b(), self.ones_b.b()], writes=[ps.b()])
            if not rot:
                fw.op("act", lambda e, m=m, ps=ps: e.activation(out=self.fm(s, which, cbase + m), in_=ps.t[:, 0:T],
                                                               func=AF.Copy, scale=scale),
                      reads=[ps.b()], writes=[s.fmbig.b(8 * which + cbase + m)])
            elif m % 2 == 1:
                p1, p2 = pss[m - 1], ps
                c1, c2 = cbase + m - 1, cbase + m
                r = self.rot
                rd = [s.rc.b(), s.rs.b()]
                for q0 in range(0, T, 256):
                    n = min(256, T - q0)
                    cos, sin = s.rc.t[:, q0:q0 + n], s.rs.t[:, q0:q0 + n]
                    a1, a2 = p1.t[:, q0:q0 + n], p2.t[:, q0:q0 + n]

                    def stt(o, i0, i1):
                        return lambda e: e.scalar_tensor_tensor(out=o, in0=i0, scalar=scale, in1=i1,
                                                                op0=ALU.mult, op1=ALU.mult)
                    fw.op("dve", stt(r.t[:, 0, 0:n], a1, cos), reads=[p1.b()] + rd, writes=[r.b(0)])
                    fw.op("dve", stt(r.t[:, 1, 0:n], a2, sin), reads=[p2.b()] + rd, writes=[r.b(1)])
                    fw.op("dve", stt(r.t[:, 2, 0:n], a1, sin), reads=[p1.b()] + rd, writes=[r.b(2)])
                    fw.op("dve", stt(r.t[:, 3, 0:n], a2, cos), reads=[p2.b()] + rd, writes=[r.b(3)])
                    o1 = self.fm(s, which, c1)[:, q0:q0 + n]
                    o2 = self.fm(s, which, c2)[:, q0:q0 + n]
                    fw.op("dve", lambda e, o1=o1, n=n: e.tensor_tensor(out=o1, in0=r.t[:, 0, 0:n], in1=r.t[:, 1, 0:n],
                                                                      op=ALU.subtract),
                          reads=[r.b(0), r.b(1)], writes=[s.fmbig.b(8 * which + c1)])
                    fw.op("dve", lambda e, o2=o2, n=n: e.tensor_tensor(out=o2, in0=r.t[:, 2, 0:n], in1=r.t[:, 3, 0:n],
                                                                      op=ALU.add),
                          reads=[r.b(2), r.b(3)], writes=[s.fmbig.b(8 * which + c2)])

    def proj_tm(self, s, wt, kind, half):
        fw = self.fw
        w3 = wt.t[:, 0:8 * 512].rearrange("p (k n) -> p k n", k=8)
        brow = wt.t[0:1, 8 * 512:9 * 512]
        cs = slice(half * 512, (half + 1) * 512)
        tsz = s.tsz
        for i in range(s.ntile):
            ps = self.nextps()

            def mm(e, i=i, ps=ps):
                for k in range(8):
                    e.matmul(ps.t[0:tsz, :], lhsT=s.xT.t[:, k, i * tsz:(i + 1) * tsz], rhs=w3[:, k, :],
                             start=(k == 0), stop=False)
                return e.matmul(ps.t[0:tsz, :], lhsT=self.ones_b.t[0:1, 0:tsz], rhs=brow, start=False, stop=True)
            fw.op("pe", mm, reads=[s.xT.b(i), wt.b(), self.ones_b.b()], writes=[ps.b()])
            pin = ps.t[0:tsz, :]
            if kind in ("mv", "rv"):
                dst = self.vview(s, 0 if kind == "mv" else 1)
                fw.op("act", lambda e, dst=dst, i=i, pin=pin: e.copy(out=dst[:, i, cs], in_=pin),
                      reads=[ps.b()], writes=s.xres.bs(range(s.ntile)))
            elif kind in ("mo", "rg"):
                G = s.G[0 if kind == "mo" else 1]
                gb = self.gbc[0 if kind == "mo" else 1]
                tb = self.tmpb[self.tbi % 2]
                self.tbi += 1
                fn = AF.Sigmoid if kind == "mo" else AF.Silu
                fw.op("act", lambda e, tb=tb, pin=pin, fn=fn: e.activation(out=tb.t[0:tsz, :], in_=pin, func=fn),
                      reads=[ps.b()], writes=[tb.b()])
                fw.op("dve", lambda e, tb=tb, G=G, i=i, gb=gb: e.tensor_tensor(
                    out=G.t[:, i, cs], in0=tb.t[0:tsz, :], in1=gb.t[0:tsz, cs], op=ALU.mult),
                    reads=[tb.b(), gb.b()], writes=[G.b((i, half))])
            else:
                G = s.G[0 if kind == "ga" else 1]
                tb = self.tmpb[self.tbi % 2]
                self.tbi += 1
                fw.op("act", lambda e, tb=tb, pin=pin: e.activation(out=tb.t[0:tsz, :], in_=pin, func=AF.Sigmoid),
                      reads=[ps.b()], writes=[tb.b()])
                fw.op("dve", lambda e, tb=tb, G=G, i=i: e.tensor_tensor(
                    out=G.t[:, i, cs], in0=tb.t[0:tsz, :], in1=G.t[:, i, cs], op=ALU.mult),
                    reads=[tb.b(), G.b((i, half))], writes=[G.b((i, half))])

    def projection(self, sets):
        d = self.dram
        gw = self.gw
        if not self.gw_loaded:
            self.load("pool", gw, 0, gw.t[:, 0:8, :], d["w_in"][:, C_MI:C_MI + 8].rearrange("(k p) n -> p k n", p=128))
            self.load("pool", gw, 0, gw.t[0:1, 8, :], d["b_in"][:, C_MI:C_MI + 8], nowait=True)
            self.gw_loaded = True
        for s in sets:
            self.proj_gates(s)
        plan = [(0, "fm", 0, 0, 1.0, False), (512, "fm", 0, 4, 1.0, False),
                (1024, "fm", 1, 0, 1.0 / 16, False), (1536, "fm", 1, 4, 1.0 / 16, False),
                (2048, "tm", "mv", 0), (2560, "tm", "mv", 1), (3072, "tm", "mo", 0), (3584, "tm", "mo", 1),
                (4104, "fm", 2, 0, 1.0, True), (4616, "fm", 2, 4, 1.0, True),
                (5128, "fm", 3, 0, 1.0 / 16, True), (5640, "fm", 3, 4, 1.0 / 16, True),
                (6152, "tm", "rv", 0), (6664, "tm", "rv", 1), (7176, "tm", "rg", 0), (7688, "tm", "rg", 1),
                (8200, "tm", "ga", 0), (8712, "tm", "ga", 1), (9224, "tm", "gb", 0), (9736, "tm", "gb", 1)]
        md_banks = [[self.ps[6], self.ps[7]]]
        if self.mini_down:
            self.ps_reserved = {6, 7}
        for n_ent, ent in enumerate(plan):
            if self.pending_ln:
                self.pending_ln.pop(0)()
            if self.mini_down and n_ent == 6:
                self.down(self.mini_down, banks=md_banks, blocks="finish")
                self.mini_down = None
                self.ps_reserved = set()
            wt = self.wnext("in", ent[0])
            for s in sets:
                if ent[1] == "fm":
                    self.proj_fm(s, wt, ent[3], ent[2], ent[4], ent[5])
                else:
                    self.proj_tm(s, wt, ent[2], ent[3])
            if ent[0] == 1536:
                for s in sets:
                    if s is not self.MAIN:
                        for _ in self.gate_math(s):
                            pass
                gens = [self.gate_math(self.MAIN)]
            if ent[0] in (1536, 2048, 2560):
                for g_ in gens:
                    next(g_, None)
            if self.mini_down and n_ent < 6:
                self.down(self.mini_down, banks=md_banks, blocks=n_ent * 512)

    def gate_math(self, s):
        fw = self.fw
        main = s is self.MAIN
        L = 128 if main else NMETA
        nt = s.ntile
        gm, gcol = self.gm, s.gcol
        one = self.cst.t[0:L, K_ONE:K_ONE + 1]
        slot0 = 0 if main else 4
        o = s.gofs = (24 if main else 32)
        fw.op("act", lambda e: e.activation(out=gm.t[0:L, 0:nt, 20:24], in_=gcol.t[0:L, :, 4:8], func=AF.Exp, scale=-1.0),
              reads=[gcol.b()], writes=[gm.b()])
        fw.op("act", lambda e: e.activation(out=gm.t[0:L, 0:nt, 0:4], in_=gm.t[0:L, 0:nt, 20:24], func=AF.Ln,
                                            bias=one, scale=1.0),
              reads=[gm.b(), self.cst.b()], writes=[gm.b()])
        fw.op("dve", lambda e: e.tensor_scalar(out=gm.t[0:L, 0:nt, 0:4], in0=gm.t[0:L, 0:nt, 0:4], scalar1=-1.0,
                                               scalar2=None, op0=ALU.mult),
              reads=[gm.b()], writes=[gm.b()])
        gsm = self.gsm
        for i in range(nt):
            ps = self.nextps()

            def mmc(e, i=i, ps=ps):
                e.matmul(ps.t[0:L, 0:4], lhsT=self.maskT.t[0:L, 0:L], rhs=gm.t[0:L, i, 0:4], start=True, stop=True)
                return e.matmul(ps.t[0:4, 8:9], lhsT=gm.t[0:L, i, 0:4], rhs=self.ones_f.t[0:L, 0:1], start=True, stop=True)
            fw.op("pe", mmc, reads=[self.maskT.b(), gm.b(), self.ones_f.b()], writes=[ps.b()])
            fw.op("dve", lambda e, i=i, ps=ps: e.tensor_copy(out=gm.t[0:L, i, 4:8], in_=ps.t[0:L, 0:4]),
                  reads=[ps.b()], writes=[gm.b()])
            fw.op("dve", lambda e, i=i, ps=ps: e.tensor_copy(out=gsm.t[:, 4 + i:5 + i], in_=ps.t[0:4, 8:9]),
                  reads=[ps.b()], writes=[gsm.b()])
        fw.op("dve", lambda e: e.tensor_tensor(out=gm.t[0:L, 0:nt, 8:12], in0=gcol.t[0:L, :, 0:4],
                                               in1=gm.t[0:L, 0:nt, 4:8], op=ALU.subtract),
              reads=[gm.b(), gcol.b()], writes=[gm.b()])
        yield
        ps = self.nextps()

        def tr(e, ps=ps):
            for i in range(nt):
                ins = e.transpose(out=ps.t[0:4, i * L:(i + 1) * L], in_=gm.t[0:L, i, 8:12],
                                  identity=self.ident_f.t[0:L, 0:L])
            return ins
        fw.op("pe", tr, reads=[gm.b(), self.ident_f.b()], writes=[ps.b()])
        fw.op("dve", lambda e, ps=ps: e.tensor_reduce(out=gsm.t[:, 0:nt],
                                                      in_=ps.t[0:4, 0:nt * L].rearrange("p (c l) -> p c l", l=L),
                                                      axis=AX.X, op=ALU.max),
              reads=[ps.b()], writes=[gsm.b()])
        for c in range(nt):
            fw.op("dve", lambda e, c=c: e.tensor_copy(out=gsm.t[:, 9 + 2 * c:10 + 2 * c], in_=self.mcur.t[:]),
                  reads=[self.mcur.b(), gsm.b()], writes=[gsm.b()])
            fw.op("dve", lambda e, c=c: e.tensor_tensor(out=gsm.t[:, 8 + 2 * c:9 + 2 * c], in0=self.mcur.t[:],
                                                        in1=gsm.t[:, c:c + 1], op=ALU.max),
                  reads=[self.mcur.b(), gsm.b()], writes=[gsm.b()])
            fw.op("dve", lambda e, c=c: e.tensor_tensor(out=self.mcur.t[:], in0=gsm.t[:, 8 + 2 * c:9 + 2 * c],
                                                        in1=gsm.t[:, 4 + c:5 + c], op=ALU.add),
                  reads=[gsm.b(), self.mcur.b()], writes=[self.mcur.b()])
        dg = gsm.t[:, 24:24 + 8 * nt].rearrange("p (c h) -> p c h", h=4)
        fw.op("dve", lambda e: e.tensor_tensor(
            out=dg, in0=self.ident_f.t[0:4, 0:4].unsqueeze(1).to_broadcast([4, 2 * nt, 4]),
            in1=gsm.t[:, 8:8 + 2 * nt].unsqueeze(2).to_broadcast([4, 2 * nt, 4]), op=ALU.mult),
            reads=[gsm.b(), self.ident_f.b()], writes=[gsm.b()])
        yield
        ps = self.nextps()
        fw.op("pe", lambda e, ps=ps: e.matmul(ps.t[:, 0:8 * nt], lhsT=self.ones_f.t[0:4, :],
                                            rhs=gsm.t[:, 24:24 + 8 * nt], start=True, stop=True),
              reads=[gsm.b(), self.ones_f.b()], writes=[ps.b()])
        gb = self.gbcst
        fw.op("dve", lambda e, ps=ps: e.tensor_copy(out=gb.t[:, slot0:slot0 + nt, :],
                                                   in_=ps.t[:, 0:8 * nt].rearrange("p (c k) -> p c k", k=8)),
              reads=[ps.b()], writes=[gb.b()])
        wcb = self.wcb
        fw.op("dve", lambda e: e.tensor_tensor(out=wcb.t[:, slot0:slot0 + nt, :], in0=gb.t[:, slot0:slot0 + nt, 4:8],
                                               in1=gb.t[:, slot0:slot0 + nt, 0:4], op=ALU.subtract),
              reads=[gb.b(), wcb.b()], writes=[wcb.b()])
        fw.op("act", lambda e: e.activation(out=wcb.t[:, slot0:slot0 + nt, :], in_=wcb.t[:, slot0:slot0 + nt, :],
                                            func=AF.Exp),
              reads=[wcb.b()], writes=[wcb.b()])
        fw.op("dve", lambda e: e.tensor_tensor(out=gm.t[0:L, 0:nt, 12:16], in0=gm.t[0:L, 0:nt, 8:12],
                                               in1=gb.t[0:L, slot0:slot0 + nt, 0:4], op=ALU.subtract),
              reads=[gm.b(), gb.b()], writes=[gm.b()])
        fw.op("dve", lambda e: e.tensor_tensor(out=gm.t[0:L, 0:nt, 16:20], in0=gm.t[0:L, 0:nt, 4:8],
                                               in1=gb.t[0:L, slot0:slot0 + nt, 0:4], op=ALU.add),
              reads=[gm.b(), gb.b()], writes=[gm.b()])
        fw.op("act", lambda e: e.activation(out=gm.t[0:L, 0:nt, o:o + 4], in_=gm.t[0:L, 0:nt, 12:16], func=AF.Exp),
              reads=[gm.b()], writes=[gm.b()])
        fw.op("act", lambda e: e.activation(out=gm.t[0:L, 0:nt, o + 4:o + 8], in_=gm.t[0:L, 0:nt, 16:20], func=AF.Exp,
                                            scale=-1.0),
              reads=[gm.b()], writes=[gm.b()])

    def rstd(self, out_ap, in_ap, tl):
        fw = self.fw
        fw.op("act", lambda e: e.activation(out=out_ap, in_=in_ap, func=AF.Ln), reads=[tl.b()], writes=[tl.b()])
        fw.op("act", lambda e: e.activation(out=out_ap, in_=out_ap, func=AF.Exp, scale=-0.5),
              reads=[tl.b()], writes=[tl.b()])

    def gate_aps(self, s, mix, h, c, L):
        cst = self.cst
        if mix == 0:
            slot0 = 0 if s is self.MAIN else 4
            return (self.gm.t[0:L, c, s.gofs + h:s.gofs + h + 1], self.wcb.t[:, slot0 + c, h:h + 1],
                    [self.gm.b(), self.wcb.b()])
        ke, kw = (K_ES128, K_WC128) if L == 128 else (K_ES16, K_WC16)
        return cst.t[0:L, ke + h:ke + h + 1], cst.t[:, kw + h:kw + h + 1], [cst.b()]

    def make_cb(self, s, hm, c, L):
        fw = self.fw
        Cb, Cf = self.Cb, self.Cf
        _, wc, rd_g = self.gate_aps(s, hm // 4, hm % 4, c, L)
        fw.op("act", lambda e: e.activation(out=Cb.t[:, hm, :, :], in_=Cf.t[:, hm, :, :], func=AF.Copy, scale=wc),
              reads=[Cf.b(hm)] + rd_g, writes=[Cb.b(hm)])

    def mixers(self, s):
        fw = self.fw
        main = s is self.MAIN
        L = 128 if main else NMETA
        nt = s.ntile
        cst = self.cst
        Cb, Cf = self.Cb, self.Cf
        st6, mv, u = self.st6, self.mv, self.u
        for hm in range(8):
            self.make_cb(s, hm, 0, L)

        def prologue(c, g):
            t0 = c * L
            gi = self.mgi % 2
            self.mgi += 1
            P = TokSet()
            P.hms = hms = [2 * g, 4 + 2 * g, 2 * g + 1, 4 + 2 * g + 1]
            pT, pS = self.ps[4 + gi], self.ps[6 + gi]
            P.ktm, P.vp, P.stm = ktm, vp, stm = self.ktm[gi], self.vp[gi], self.stm[gi]
            pv = pT.t[:].bitcast(BF16)
            P.qa, P.qb = qa, qb = {}, {}
            ka, kb = {}, {}
            for hm in hms:
                mix, h = hm // 4, hm % 4
                qa[hm] = [self.fm(s, 2 * mix, 2 * h + j)[:, t0:t0 + L] for j in range(2)]
                ka[hm] = [self.fm(s, 2 * mix + 1, 2 * h + j)[:, t0:t0 + L] for j in range(2)]
                qb[hm] = s.fmbig.bs([16 * mix + 2 * h, 16 * mix + 2 * h + 1])
                kb[hm] = s.fmbig.bs([16 * mix + 8 + 2 * h, 16 * mix + 8 + 2 * h + 1])
            allq = [b for hm in hms for b in qb[hm]]
            allk = [b for hm in hms for b in kb[hm]]

            def trk(e):
                for k_, hm in enumerate(hms):
                    for j in range(2):
                        ins = e.transpose(out=pv[0:L, k_ * 256 + j * 128:k_ * 256 + (j + 1) * 128], in_=ka[hm][j],
                                          identity=self.ident_b.t[:])
                return ins
            fw.op("pe", trk, reads=allk + [self.ident_b.b()], writes=[pT.b()])

            def mms(e):
                for k_, hm in enumerate(hms):
                    for j in range(2):
                        ins = e.matmul(pS.t[0:L, k_ * 128:k_ * 128 + L], lhsT=ka[hm][j], rhs=qa[hm][j],
                                       start=(j == 0), stop=(j == 1))
                return ins
            fw.op("pe", mms, reads=allk + allq, writes=[pS.b()])
            fw.op("act", lambda e: e.copy(out=ktm.t[0:L, :, :], in_=pv[0:L, :].rearrange("p (k n) -> p k n", k=4)),
                  reads=[pT.b()], writes=[ktm.b()])
            fw.op("act", lambda e: e.copy(out=stm.t[0:L, :, 0:L],
                                          in_=pS.t[0:L, :].rearrange("p (k n) -> p k n", k=4)[:, :, 0:L]),
                  reads=[pS.b()], writes=[stm.b()])
            fw.op("pool", lambda e: e.tensor_tensor(
                out=stm.t[0:L, :, 0:L], in0=stm.t[0:L, :, 0:L],
                in1=self.maskT.t[0:L, 0:L].unsqueeze(1).to_broadcast([L, 4, L]), op=ALU.mult),
                reads=[stm.b(), self.maskT.b()], writes=[stm.b()])
            for k_, hm in enumerate(hms):
                mix, h = hm // 4, hm % 4
                es, wc, rd_g = self.gate_aps(s, mix, h, c, L)
                vsrc = self.vview(s, mix)
                fw.op("act", lambda e, vsrc=vsrc, es=es, h=h, k_=k_: e.activation(
                    out=vp.t[0:L, k_, 0:256], in_=vsrc[0:L, c, h * 256:(h + 1) * 256], func=AF.Copy, scale=es),
                    reads=[s.xres.b(c)] + rd_g, writes=[vp.b()])
                fw.op("act", lambda e, es=es, k_=k_: e.copy(out=vp.t[0:L, k_, 256:257], in_=es),
                      reads=rd_g + [vp.b()], writes=[vp.b()])
            return P

        def body(c, g, P):
            deferred = []
            ktm, vp, stm = P.ktm, P.vp, P.stm
            for k_, hm in enumerate(P.hms):
                mix, h = hm // 4, hm % 4
                es, wc, rd_g = self.gate_aps(s, mix, h, c, L)
                pA = self.ps[(k_ % 2) * 2]
                pB = self.ps[(k_ % 2) * 2 + 1]
                qa = P.qa[hm]

                def mmn(e, pA=pA, qa=qa, hm=hm, k_=k_):
                    for j in range(2):
                        e.matmul(pA.t[0:L, 128:385], lhsT=qa[j], rhs=Cb.t[:, hm, j, :], start=(j == 0), stop=False)
                    return e.matmul(pA.t[0:L, 128:385], lhsT=stm.t[0:L, k_, 0:L], rhs=vp.t[0:L, k_, :],
                                    start=False, stop=True)
                fw.op("pe", mmn, reads=P.qb[hm] + [Cb.b(hm), stm.b(), vp.b()], writes=[pA.b()])

                def mmp(e, pB=pB, pA=pA, k_=k_):
                    for j in range(2):
                        e.matmul(pB.t[:, j * 256:(j + 1) * 256], lhsT=ktm.t[0:L, k_, j * 128:(j + 1) * 128],
                                 rhs=vp.t[0:L, k_, 0:256], start=True, stop=True)
                    for j in range(2):
                        ins = e.matmul(pA.t[:, 400 + j:401 + j], lhsT=ktm.t[0:L, k_, j * 128:(j + 1) * 128],
                                       rhs=vp.t[0:L, k_, 256:257], start=True, stop=True)
                    return ins
                fw.op("pe", mmp, reads=[ktm.b(), vp.b()], writes=[pB.b(), pA.b()])
                fw.op("dve", lambda e, pA=pA, hm=hm: e.bn_stats(out=st6.t[0:L, hm, :], in_=pA.t[0:L, 128:384]),
                      reads=[pA.b()], writes=[st6.b(hm)])
                fw.op("dve", lambda e, hm=hm: e.bn_aggr(out=mv.t[0:L, hm, :], in_=st6.t[0:L, hm, :]),
                      reads=[st6.b(hm)], writes=[mv.b(hm)])
                if mix == 0:
                    fw.op("dve", lambda e, pA=pA, h=h: e.tensor_copy(out=self.den.t[0:L, h:h + 1], in_=pA.t[0:L, 384:385]),
                          reads=[pA.b()], writes=[self.den.b(h)])
                G = s.G[mix]
                fw.op("dve", lambda e, pA=pA, hm=hm, G=G, h=h: e.scalar_tensor_tensor(
                    out=u.t[0:L, hm, :], in0=pA.t[0:L, 128:384], scalar=mv.t[0:L, hm, 0:1],
                    in1=G.t[0:L, c, h * 256:(h + 1) * 256], op0=ALU.subtract, op1=ALU.mult),
                    reads=[pA.b(), mv.b(hm), G.b((c, h // 2))], writes=[u.b(hm)])
                fw.op("dve", lambda e, pA=pA, hm=hm, wc=wc: e.scalar_tensor_tensor(
                    out=Cf.t[:, hm, :, 256], in0=Cf.t[:, hm, :, 256], scalar=wc, in1=pA.t[:, 400:402],
                    op0=ALU.mult, op1=ALU.add),
                    reads=[pA.b(), Cf.b(hm)] + rd_g, writes=[Cf.b(hm)])
                fw.op("dve", lambda e, pB=pB, hm=hm, wc=wc: e.scalar_tensor_tensor(
                    out=Cf.t[:, hm, :, 0:256], in0=Cf.t[:, hm, :, 0:256], scalar=wc,
                    in1=pB.t[:, :].rearrange("p (j n) -> p j n", j=2), op0=ALU.mult, op1=ALU.add),
                    reads=[pB.b(), Cf.b(hm)] + rd_g, writes=[Cf.b(hm)])
                if c + 1 < nt:
                    deferred.append(lambda hm=hm: self.make_cb(s, hm, c + 1, L))
            return deferred

        def pm(c, g):
            lowb = self.gm.t[0:L, c, s.gofs + 4:s.gofs + 8]
            keps = K_EPS128 if L == 128 else K_EPS16
            self.post_merge(s, L, 0, c, lowb, cst.t[0:L, keps:keps + 4], [self.gm.b(), cst.b()], 2 * g, 2 * g + 2)

        steps = [(c, g) for c in range(nt) for g in range(2)]
        P_next = prologue(*steps[0])
        pending = []
        prev = None
        for idx, (c, g) in enumerate(steps):
            P_cur = P_next
            if idx + 1 < len(steps):
                P_next = prologue(*steps[idx + 1])
            for f in pending:
                f()
            pending = body(c, g, P_cur)
            if prev is not None:
                pm(*prev)
            prev = (c, g)
        pm(*prev)
        for f in pending:
            f()

    def post_merge(self, s, L, p0, c, lowb, epsr, rd, h0=0, h1=4):
        fw = self.fw
        sm = self.sm
        R = slice(p0, p0 + L)
        H = slice(h0, h1)
        nh = h1 - h0
        fw.op("dve", lambda e: e.scalar_tensor_tensor(out=sm.t[R, 4 + h0:4 + h1], in0=self.den.t[R, H], scalar=-1.0,
                                                      in1=self.den.t[R, H], op0=ALU.mult, op1=ALU.max),
              reads=self.den.bs(range(h0, h1)) + [sm.b()], writes=[sm.b()])
        fw.op("dve", lambda e: e.tensor_tensor(out=sm.t[R, H], in0=sm.t[R, 4 + h0:4 + h1], in1=lowb[:, H], op=ALU.max),
              reads=[sm.b()] + rd, writes=[sm.b()])
        fw.op("dve", lambda e: e.scalar_tensor_tensor(out=sm.t[R, 8 + h0:8 + h1], in0=sm.t[R, H], scalar=LN_EPS,
                                                      in1=sm.t[R, H], op0=ALU.mult, op1=ALU.mult),
              reads=[sm.b()], writes=[sm.b()])
        fw.op("dve", lambda e: e.tensor_tensor(out=sm.t[R, 16 + h0:16 + h1], in0=sm.t[R, 8 + h0:8 + h1],
                                               in1=self.mv.t[R, H, 1], op=ALU.add),
              reads=[sm.b()] + self.mv.bs(range(h0, h1)), writes=[sm.b()])
        fw.op("dve", lambda e: e.tensor_tensor(out=sm.t[R, 20 + h0:20 + h1], in0=epsr[:, H],
                                               in1=self.mv.t[R, 4 + h0:4 + h1, 1], op=ALU.add),
              reads=[sm.b()] + rd + self.mv.bs(range(4 + h0, 4 + h1)), writes=[sm.b()])
        vin = sm.t[R, 16:24].rearrange("p (m h) -> p m h", m=2)[:, :, H]
        vout = sm.t[R, 24:32].rearrange("p (m h) -> p m h", m=2)[:, :, H]
        self.rstd(vout, vin, sm)
        ta = self.tmpa
        u = self.u
        for h in range(h0, h1):
            fw.op("pool", lambda e, h=h: e.tensor_scalar(out=ta.t[R, :], in0=u.t[R, h, :], scalar1=sm.t[R, 24 + h:25 + h],
                                                        scalar2=0.0, op0=ALU.mult, op1=ALU.add),
                  reads=[u.b(h), sm.b()], writes=[ta.b()])
            fw.op("pool", lambda e, h=h: e.tensor_scalar(out=u.t[R, 4 + h, :], in0=u.t[R, 4 + h, :],
                                                        scalar1=sm.t[R, 28 + h:29 + h], scalar2=0.0,
                                                        op0=ALU.mult, op1=ALU.add),
                  reads=[u.b(4 + h), sm.b()], writes=[u.b(4 + h)])
            fw.op("pool", lambda e, h=h: e.tensor_tensor(out=s.merged.t[R, c, h * 256:(h + 1) * 256], in0=ta.t[R, :],
                                                        in1=u.t[R, 4 + h, :], op=ALU.add),
                  reads=[u.b(4 + h), ta.b()], writes=[s.merged.b(c)])

    def store_prompt_state(self):
        d = self.dram
        Cf = self.Cf
        for h in range(4):
            self.store("sp", Cf, h, d["pC"][h].rearrange("(j p) e -> p j e", p=128), Cf.t[:, h, :, 0:256])
            self.store("sp", Cf, 4 + h, d["pS"][h].rearrange("(j p) e -> p j e", p=128), Cf.t[:, 4 + h, :, 0:256])
        tok = self.fw.dma("sp", lambda e: e.dma_start(out=d["pn"].rearrange("h (j p) -> p h j", p=128),
                                                      in_=Cf.t[:, 0:4, :, 256], allow_slow_non_contiguous=True),
                          Cf.b("o"), reads=Cf.bs(range(4)))
        self.fw.out_toks.append(tok)
        self.store("sp", self.mcur, 0, d["pm"], self.mcur.t[:])

    def sample_mixers(self):
        fw = self.fw
        s = self.MINI
        d = self.dram
        cst = self.cst
        R = slice(SP0, SP0 + NS)
        ssm = self.s_sm
        gcol = s.gcol
        def tm_transposes(w):
            ps = self.nextps()
            pv = ps.t[:].bitcast(BF16)

            def tr(e, w=w, pv=pv):
                for k in range(8):
                    ins = e.transpose(out=pv[0:64, k * 128:(k + 1) * 128], in_=self.fm(s, w, k)[:, 0:64],
                                      identity=self.ident_b.t[:])
                return ins
            fw.op("pe", tr, reads=s.fmbig.bs(range(8 * w, 8 * w + 8)) + [self.ident_b.b()], writes=[ps.b()])
            fw.op("act", lambda e, w=w, pv=pv: e.copy(out=self.s_tm[w].t[:, :], in_=pv[0:64, :]),
                  reads=[ps.b()], writes=[self.s_tm[w].b()])
        tm_transposes(0)
        tm_transposes(1)
        self.load("sp", self.s_n, 0, self.s_n.t[R, :], d["sn"])
        self.load("sp", ssm, "m", ssm.t[R, 0:4], d["sm"])
        one = cst.t[R, K_ONE:K_ONE + 1]
        sb_ = [ssm.b(), ssm.b("m")]
        fw.op("act", lambda e: e.activation(out=ssm.t[R, 28:32], in_=gcol.t[R, 0, 4:8], func=AF.Exp, scale=-1.0),
              reads=[gcol.b()] + sb_, writes=[ssm.b()])
        fw.op("act", lambda e: e.activation(out=ssm.t[R, 4:8], in_=ssm.t[R, 28:32], func=AF.Ln, bias=one, scale=1.0),
              reads=[ssm.b(), cst.b()], writes=[ssm.b()])
        fw.op("dve", lambda e: e.tensor_tensor(out=ssm.t[R, 8:12], in0=ssm.t[R, 0:4], in1=ssm.t[R, 4:8], op=ALU.subtract),
              reads=sb_, writes=[ssm.b()])
        fw.op("dve", lambda e: e.tensor_tensor(out=ssm.t[R, 12:16], in0=ssm.t[R, 8:12], in1=gcol.t[R, 0, 0:4], op=ALU.max),
              reads=[ssm.b(), gcol.b()], writes=[ssm.b()])
        fw.op("dve", lambda e: e.tensor_tensor(out=ssm.t[R, 16:20], in0=gcol.t[R, 0, 0:4], in1=ssm.t[R, 12:16],
                                               op=ALU.subtract),
              reads=[ssm.b(), gcol.b()], writes=[ssm.b()])
        fw.op("dve", lambda e: e.tensor_tensor(out=ssm.t[R, 20:24], in0=ssm.t[R, 8:12], in1=ssm.t[R, 12:16],
                                               op=ALU.subtract),
              reads=[ssm.b()], writes=[ssm.b()])
        fw.op("act", lambda e: e.activation(out=ssm.t[R, 16:24], in_=ssm.t[R, 16:24], func=AF.Exp),
              reads=[ssm.b()], writes=[ssm.b()])
        fw.op("act", lambda e: e.activation(out=ssm.t[R, 24:28], in_=ssm.t[R, 12:16], func=AF.Exp, scale=-1.0),
              reads=[ssm.b()], writes=[ssm.b()])
        self.store("sp", ssm, 0, d["om"], ssm.t[R, 12:16])
        big = self.u.t[0:64, :, :].bitcast(F32).rearrange("p a n -> p (a n)")
        bigb = self.u.bs(range(8))
        for (a_, b_, col, bt) in ((self.s_tm[0], self.s_tm[1], 32, None), (self.s_tm[0], self.s_n, 36, None),
                                  (self.s_tm[2], self.s_tm[3], 40, None)):
            if col == 40:
                tm_transposes(2)
                tm_transposes(3)
            fw.op("dve", lambda e, a_=a_, b_=b_: e.tensor_tensor(out=big[R, :], in0=a_.t[R, :], in1=b_.t[R, :], op=ALU.mult),
                  reads=[a_.b(), b_.b()], writes=bigb)
            fw.op("dve", lambda e, col=col: e.tensor_reduce(out=ssm.t[R, col:col + 4],
                                                           in_=big[R, :].rearrange("p (h n) -> p h n", h=4),
                                                           axis=AX.X, op=ALU.add),
                  reads=bigb + [ssm.b()], writes=[ssm.b()])
        fw.op("dve", lambda e: e.tensor_tensor(out=ssm.t[R, 44:48], in0=ssm.t[R, 32:36], in1=ssm.t[R, 16:20], op=ALU.mult),
              reads=[ssm.b()], writes=[ssm.b()])
        fw.op("dve", lambda e: e.tensor_tensor(out=ssm.t[R, 48:52], in0=ssm.t[R, 36:40], in1=ssm.t[R, 20:24], op=ALU.mult),
              reads=[ssm.b()], writes=[ssm.b()])
        fw.op("dve", lambda e: e.tensor_tensor(out=self.den.t[R, :], in0=ssm.t[R, 48:52], in1=ssm.t[R, 44:48], op=ALU.add),
              reads=[ssm.b()], writes=self.den.bs(range(4)))
        kp = self.s_tm[1]
        for h in range(4):
            hs = slice(h * 256, (h + 1) * 256)
            fw.op("act", lambda e, h=h, hs=hs: e.activation(out=kp.t[R, hs], in_=self.s_tm[1].t[R, hs], func=AF.Copy,
                                                           scale=ssm.t[R, 16 + h:17 + h]),
                  reads=[self.s_tm[1].b(), ssm.b()], writes=[kp.b()])
            fw.op("dve", lambda e, h=h, hs=hs: e.scalar_tensor_tensor(
                out=self.s_n.t[R, hs], in0=self.s_n.t[R, hs], scalar=ssm.t[R, 20 + h:21 + h], in1=kp.t[R, hs],
                op0=ALU.mult, op1=ALU.add),
                reads=[self.s_n.b(), ssm.b(), kp.b()], writes=[self.s_n.b()])
        self.store("sp", self.s_n, 0, d["on"], self.s_n.t[R, :])
        dg = self.s_dg
        idr = self.ident_f.t[R, SP0:SP0 + NS]
        fw.op("dve", lambda e: e.tensor_tensor(
            out=dg.t[R, :, 0:4], in0=idr.unsqueeze(2).to_broadcast([NS, NS, 4]),
            in1=ssm.t[R, 20:24].unsqueeze(1).to_broadcast([NS, NS, 4]), op=ALU.mult),
            reads=[ssm.b(), self.ident_f.b()], writes=[dg.b()])
        fw.op("dve", lambda e: e.tensor_tensor(
            out=dg.t[R, :, 4:8], in0=idr.unsqueeze(2).to_broadcast([NS, NS, 4]),
            in1=cst.t[R, K_G1:K_G1 + 4].unsqueeze(1).to_broadcast([NS, NS, 4]), op=ALU.mult),
            reads=[cst.b(), self.ident_f.b(), dg.b()], writes=[dg.b()])
        ps = self.nextps()
        fw.op("pe", lambda e, ps=ps: e.matmul(ps.t[:, 0:128], lhsT=self.ones_f.t[R, :],
                                            rhs=dg.t[R, :, :].rearrange("p a b -> p (a b)"), start=True, stop=True),
              reads=[dg.b(), self.ones_f.b()], writes=[ps.b()])
        fw.op("dve", lambda e, ps=ps: e.tensor_copy(out=self.s_wb.t[:, :], in_=ps.t[:, 0:128]),
              reads=[ps.b()], writes=[self.s_wb.b()])
        ci = 0
        for mix in range(2):
            src, dst = (d["sC"], d["oC"]) if mix == 0 else (d["sS"], d["oS"])
            kpt = kp if mix == 0 else self.s_tm[3]
            vsrc = self.vview(s, mix)
            acc = [self.ps[4 + h] for h in range(4)]
            for b in range(NS):
                qm, vm = (self.qm[0], self.vmk[0]) if b % 2 == 0 else (self.qm2, self.vm2)
                first2 = (mix == 0 and b == 1)
                xw = [self.wbuf[-1].b(0)] if first2 else []
                fw.op("dve", lambda e, qm=qm, b=b, mix=mix: e.tensor_tensor(
                    out=qm.t[:, :, :], in0=s.fmbig.t[:, 16 * mix:16 * mix + 8, :],
                    in1=self.identrow.t[:, b, :].unsqueeze(1).to_broadcast([128, 8, 64]), op=ALU.mult),
                    reads=s.fmbig.bs(range(16 * mix, 16 * mix + 8)) + [self.identrow.b()], writes=[qm.b()], extra=xw)
                fw.op("act", lambda e, vm=vm, b=b, vsrc=vsrc: e.activation(
                    out=vm.t[R, :], in_=vsrc[R, 0, :], func=AF.Copy, scale=self.ident_f.t[R, SP0 + b:SP0 + b + 1]),
                    reads=[s.xres.b(0), self.ident_f.b()], writes=[vm.b()], extra=xw)
                for h in range(4):
                    cin = self.cin[ci % len(self.cin)]
                    cbf = self.cbf[ci % 2]
                    first = ci < len(self.cin) and cin.name in self.cin_owner
                    ci += 1
                    self.fw.dma("sp", lambda e, cin=cin, b=b, h=h, src=src: e.dma_start(
                        out=cin.t[:, :, :], in_=src[b, h].rearrange("(j p) e -> p j e", p=128)),
                        cin.b(0), writes=[cin.b(0)], extra=[self.cin_owner[cin.name].b(0)] if first else ())
                    fw.op("act", lambda e, cin=cin, cbf=cbf: e.copy(out=cbf.t[:, :, :], in_=cin.t[:, :, :]),
                          reads=[cin.b()], writes=[cbf.b()])

                    def mmq(e, qm=qm, cbf=cbf, h=h, b=b):
                        for j in range(2):
                            ins = e.matmul(acc[h].t[0:64, 0:256], lhsT=qm.t[:, 2 * h + j, :], rhs=cbf.t[:, j, :],
                                           start=(b == 0 and j == 0), stop=(b == NS - 1 and j == 1))
                        return ins
                    fw.op("pe", mmq, reads=[qm.b(), cbf.b()], writes=[acc[h].b()])
                    pr = self.ps[ci % 4]

                    def mmr(e, pr=pr, kpt=kpt, vm=vm, h=h):
                        for j in range(2):
                            ins = e.matmul(pr.t[:, j * 256:(j + 1) * 256],
                                           lhsT=kpt.t[R, h * 256 + j * 128:h * 256 + (j + 1) * 128],
                                           rhs=vm.t[R, h * 256:(h + 1) * 256], start=True, stop=True)
                        return ins
                    fw.op("pe", mmr, reads=[kpt.b(), vm.b()], writes=[pr.b()])
                    wcol = b * 8 + mix * 4 + h
                    fw.op("dve", lambda e, cin=cin, pr=pr, wcol=wcol: e.scalar_tensor_tensor(
                        out=cin.t[:, :, :], in0=cin.t[:, :, :], scalar=self.s_wb.t[:, wcol:wcol + 1],
                        in1=pr.t[:, :].rearrange("p (j n) -> p j n", j=2), op0=ALU.mult, op1=ALU.add),
                        reads=[cin.b(), pr.b(), self.s_wb.b()], writes=[cin.b()])
                    self.store("pool", cin, 0, dst[b, h].rearrange("(j p) e -> p j e", p=128), cin.t[:, :, :],
                               semkey="st")
            for h in range(4):
                hm = 4 * mix + h
                ta = self.tmpa
                scol = (44 if mix == 0 else 40) + h
                fw.op("act", lambda e, h=h, scol=scol, vsrc=vsrc: e.activation(
                    out=ta.t[R, :], in_=vsrc[R, 0, h * 256:(h + 1) * 256], func=AF.Copy,
                    scale=ssm.t[R, scol:scol + 1]),
                    reads=[s.xres.b(0), ssm.b()], writes=[ta.b()])
                wsc = ssm.t[R, 20 + h:21 + h] if mix == 0 else cst.t[R, K_G1 + h:K_G1 + h + 1]
                num = self.s_num
                fw.op("dve", lambda e, h=h, wsc=wsc: e.scalar_tensor_tensor(
                    out=num.t[R, :], in0=acc[h].t[R, 0:256], scalar=wsc, in1=ta.t[R, :], op0=ALU.mult, op1=ALU.add),
                    reads=[acc[h].b(), ta.b(), ssm.b(), cst.b()], writes=[num.b()])
                fw.op("dve", lambda e, hm=hm: e.bn_stats(out=self.st6.t[R, hm, :], in_=num.t[R, :]),
                      reads=[num.b()], writes=[self.st6.b(hm)])
                fw.op("dve", lambda e, hm=hm: e.bn_aggr(out=self.mv.t[R, hm, :], in_=self.st6.t[R, hm, :]),
                      reads=[self.st6.b(hm)], writes=[self.mv.b(hm)])
                G = s.G[mix]
                fw.op("dve", lambda e, hm=hm, h=h, G=G: e.scalar_tensor_tensor(
                    out=self.u.t[R, hm, :], in0=num.t[R, :], scalar=self.mv.t[R, hm, 0:1],
                    in1=G.t[R, 0, h * 256:(h + 1) * 256], op0=ALU.subtract, op1=ALU.mult),
                    reads=[num.b(), self.mv.b(hm), G.b((0, h // 2))], writes=[self.u.b(hm)])
        for tl in self.borrowed2:
            wb = self.wbuf[-1].b(0)
            for k_, tok in tl.b(0).r.items():
                wb.r["smp_%s_%s" % (tl.name, k_)] = tok
            if tl.b(0).w is not None:
                wb.r["smpw_" + tl.name] = tl.b(0).w
        for tl in self.cin:
            if tl.name in self.cin_owner:
                wb = self.cin_owner[tl.name].b(0)
                for k_, tok in tl.b(0).r.items():
                    wb.r["smp_%s_%s" % (tl.name, k_)] = tok
                if tl.b(0).w is not None:
                    wb.r["smpw_" + tl.name] = tl.b(0).w
        self.post_merge(s, NS, SP0, 0, ssm.t[R, 24:28], cst.t[R, K_EPS1:K_EPS1 + 4], [ssm.b(), cst.b()])

    def ln_evac(self, s, i, pss):
        fw = self.fw
        tsz = s.tsz
        xr = s.xres
        z = xr.t[:, i, :]
        for hf in range(2):
            cs = slice(hf * 512, (hf + 1) * 512)
            fw.op("dve", lambda e, hf=hf, cs=cs: e.scalar_tensor_tensor(
                out=z[:, cs], in0=z[:, cs], scalar=ALPHA, in1=pss[hf].t[0:tsz, :], op0=ALU.mult, op1=ALU.add),
                reads=[pss[hf].b(), xr.b(i)], writes=[xr.b(i)])

    def ln_tile(self, s, i, g, b):
        fw = self.fw
        tsz = s.tsz
        xr = s.xres
        z = xr.t[:, i, :]
        lmv = self.lmv
        for hf in range(2):
            cs = slice(hf * 512, (hf + 1) * 512)
            fw.op("dve", lambda e, hf=hf, cs=cs: e.bn_stats(out=self.lst.t[0:tsz, hf, :], in_=z[:, cs]),
                  reads=[xr.b(i)], writes=[self.lst.b()])
        fw.op("dve", lambda e: e.bn_aggr(out=lmv.t[0:tsz, 0:2], in_=self.lst.t[0:tsz, :, :].rearrange("p a b -> p (a b)")),
              reads=[self.lst.b()], writes=[lmv.b()])
        fw.op("dve", lambda e: e.tensor_scalar(out=lmv.t[0:tsz, 2:3], in0=lmv.t[0:tsz, 1:2], scalar1=LN_EPS, scalar2=None,
                                               op0=ALU.add),
              reads=[lmv.b()], writes=[lmv.b()])
        self.rstd(lmv.t[0:tsz, 3:4], lmv.t[0:tsz, 2:3], lmv)
        fw.op("dve", lambda e: e.scalar_tensor_tensor(out=z, in0=z, scalar=lmv.t[0:tsz, 0:1], in1=g.t[0:tsz, :],
                                                      op0=ALU.subtract, op1=ALU.mult),
              reads=[xr.b(i), lmv.b(), g.b()], writes=[xr.b(i)])
        fw.op("dve", lambda e: e.scalar_tensor_tensor(out=z, in0=z, scalar=lmv.t[0:tsz, 3:4], in1=b.t[0:tsz, :],
                                                      op0=ALU.mult, op1=ALU.add),
              reads=[xr.b(i), lmv.b(), b.b()], writes=[xr.b(i)])
        return z

    def outproj_ln1(self, sets):
        fw = self.fw
        d = self.dram
        if DEBUG and self.MINI in sets:
            s = self.MINI
            big = self.rot.t[0:64, :, :].rearrange("p a n -> p (a n)")
            fw.op("act", lambda e, s=s: e.copy(out=big, in_=s.merged.t[:, 0, :]), reads=[s.merged.b(0)], writes=self.rot.bs(range(4)))
            tok = fw.dma("sp", lambda e: e.dma_start(out=d["dbg_mrg"], in_=big), self.rot.b("o"), reads=self.rot.bs(range(4)))
            fw.out_toks.append(tok)
        mTb = lambda s: s.fmbig.bs(range(8))
        for s in sets:
            tsz = s.tsz
            for i in range(s.ntile):
                self.transpose_in(s, [s.merged.b(i)], lambda k, s=s, i=i: s.merged.t[:, i, k * 128:(k + 1) * 128],
                                  s.fmbig.t[:, 0:8, i * tsz:(i + 1) * tsz], mTb(s))
        wts = [self.wnext("out", 0), self.wnext("out", 512, hold=1)]
        for s in sets:
            tsz = s.tsz
            allb = s.xres.bs(range(s.ntile))
            if s is self.MAIN:
                r0 = self.pass_idx * 512
                self.fw.dma("sp", lambda e, s=s, r0=r0: e.dma_start(
                    out=s.xres.t[:, :, :], in_=d["xp"][r0:r0 + 512, :].rearrange("(i p) n -> p i n", p=128)),
                    s.xres.b("ld"), writes=allb)
            else:
                self.fw.dma("sp", lambda e, s=s: e.dma_start(out=s.xres.t[0:NMETA, 0, :], in_=d["meta"]),
                            s.xres.b("ld"), writes=allb)
                self.fw.dma("sp", lambda e, s=s: e.dma_start(out=s.xres.t[SP0:SP0 + NS, 0, :], in_=d["xs"]),
                            s.xres.b("ld"), writes=allb, nowait=True)
            pss = []
            for i in range(s.ntile):
                pp = []
                for hf in range(2):
                    ps = self.nextps()
                    pp.append(ps)
                    w3 = wts[hf].t[:, 0:8 * 512].rearrange("p (k n) -> p k n", k=8)

                    def mm(e, ps=ps, w3=w3, i=i, s=s):
                        for k in range(8):
                            ins = e.matmul(ps.t[0:s.tsz, :], lhsT=s.fmbig.t[:, k, i * s.tsz:(i + 1) * s.tsz],
                                           rhs=w3[:, k, :], start=(k == 0), stop=(k == 7))
                        return ins
                    fw.op("pe", mm, reads=mTb(s) + [wts[hf].b()], writes=[ps.b()])
                pss.append(pp)
            for i in range(s.ntile):
                self.ln_evac(s, i, pss[i])
            for i in range(s.ntile):
                z = self.ln_tile(s, i, self.lnp[0], self.lnp[1])
                fw.op("act", lambda e, z=z, s=s, i=i: e.copy(out=s.merged.t[:, i, :], in_=z),
                      reads=[s.xres.b(i)], writes=[s.merged.b(i)])
                if DEBUG and s is self.MINI:
                    self.store("sp", s.xres, i, d["dbg_x1"], s.xres.t[:, 0, :])
                self.transpose_in(s, [s.merged.b(i)], lambda k, s=s, i=i: s.merged.t[:, i, k * 128:(k + 1) * 128],
                                  s.fmbig.t[:, 0:8, i * tsz:(i + 1) * tsz], mTb(s))

    def ffn(self, sets, hook=None):
        fw = self.fw
        for c0 in range(0, DFF, 512):
            wg = self.wnext("gate", c0)
            wu = self.wnext("up", c0, hold=1)
            nch = min(4, (DFF - c0) // 128)
            g3 = wg.t[:, 0:8 * 512].rearrange("p (k n) -> p k n", k=8)
            u3 = wu.t[:, 0:8 * 512].rearrange("p (k n) -> p k n", k=8)
            for s in sets:
                T = s.T
                x1r = s.fmbig.bs(range(8))

                def grp(ps, w3, wt, m, s=s, T=T):
                    def mm(e):
                        for k in range(8):
                            ins = e.matmul(ps.t[:, 0:T], lhsT=w3[:, k, m * 128:(m + 1) * 128], rhs=s.fmbig.t[:, k, :],
                                           start=(k == 0), stop=(k == 7))
                        return ins
                    fw.op("pe", mm, reads=x1r + [wt.b()], writes=[ps.b()])
                pgs = []
                for m in range(nch):
                    pg = self.nextps()
                    pgs.append(pg)
                    grp(pg, g3, wg, m)
                tbs = []
                for m in range(nch):
                    fc = c0 // 128 + m
                    pu = self.nextps()
                    grp(pu, u3, wu, m)
                    pg = pgs[m]
                    tb = self.tmpb[self.tbi % 2]
                    self.tbi += 1
                    fw.op("act", lambda e, tb=tb, pg=pg, T=T: e.activation(out=tb.t[:, 0:T], in_=pg.t[:, 0:T], func=AF.Silu),
                          reads=[pg.b()], writes=[tb.b()])
                    fw.op("dve", lambda e, tb=tb, pu=pu, T=T, s=s, fc=fc: e.tensor_tensor(
                        out=s.fmbig.t[:, 8 + fc, :], in0=tb.t[:, 0:T], in1=pu.t[:, 0:T], op=ALU.mult),
                        reads=[tb.b(), pu.b()], writes=[s.fmbig.b(8 + fc)])
        small = [(s, i) for s in sets if s is not self.MAIN for i in range(s.ntile)]
        if small:
            self.mini_down = small
        if hook is not None:
            hook()
        return self.down([(self.MAIN, i) for i in range(4)], defer=hook is not None)

    def down(self, tiles, defer=False, banks=None, blocks=None):
        fw = self.fw
        d = self.dram
        if banks is None:
            banks = [[self.ps[2 * n], self.ps[2 * n + 1]] for n in range(len(tiles))]
        todo = list(range(0, DFF, 512)) if blocks is None else ([] if blocks == "finish" else [blocks])
        for r0 in todo:
            wt = self.wnext("down", r0)
            nch = min(4, (DFF - r0) // 128)
            w4 = wt.t[:, 0:4 * 1024].rearrange("p (c n) -> p c n", c=4)
            for n, (s, i) in enumerate(tiles):
                tsz = s.tsz
                for hf in range(2):
                    ps = banks[n][hf]

                    def mm(e, ps=ps, i=i, hf=hf, s=s, nch=nch, r0=r0, tsz=tsz, w4=w4):
                        for m in range(nch):
                            fc = r0 // 128 + m
                            ins = e.matmul(ps.t[0:tsz, :], lhsT=s.fmbig.t[:, 8 + fc, i * tsz:(i + 1) * tsz],
                                           rhs=w4[:, m, hf * 512:(hf + 1) * 512],
                                           start=(fc == 0), stop=(fc == NFF - 1))
                        return ins
                    fw.op("pe", mm, reads=s.fmbig.bs(range(8 + r0 // 128, 8 + r0 // 128 + nch)) + [wt.b()],
                          writes=[ps.b()])
        if blocks is not None and blocks != "finish":
            return []
        for n, (s, i) in enumerate(tiles):
            self.ln_evac(s, i, banks[n])

        def finish(s, i, r0):
            def f():
                self.ln_tile(s, i, self.lnp[2], self.lnp[3])
                if s is self.MAIN:
                    self.store("sp", s.xres, i, d["yp"][r0:r0 + 128, :], s.xres.t[:, i, :])
                else:
                    self.store("sp", s.xres, i, d["ys"], s.xres.t[SP0:SP0 + NS, 0, :])
            return f
        fins = [finish(s, i, self.pass_idx * 512 + i * 128) for (s, i) in tiles]
        if defer:
            return fins
        for f in fins:
            f()
        return []

    def program(self):
        NP = 4
        fw = self.fw
        self.wspecs = self.wspec_list(NP)
        self.wi = 0
        self.wloaded = 0
        self.wscr = self.nc.dram_tensor("wscr", [40, 128, 9 * 512], BF16).ap()
        self.scr_slot = {}
        self.scr_buf = {}
        self.no_prefetch_beyond = 19
        self.gw_loaded = False
        self.load_consts()
        fw.op("dve", lambda e: e.memset(self.MINI.merged.t[:], 0.0), writes=[self.MINI.merged.b(0)])
        fw.op("dve", lambda e: e.memset(self.MINI.xres.t[:], 0.0), writes=self.MINI.xres.bs(range(1)))
        for p in range(NP):
            self.pass_idx = p
            sets = [self.MINI, self.MAIN] if p == 0 else [self.MAIN]
            if p == 0:
                self.load_x_mini()
                self.load_x_main(0)
                gw = self.gw
                self.load("pool", gw, 0, gw.t[:, 0:8, :],
                          self.dram["w_in"][:, C_MI:C_MI + 8].rearrange("(k p) n -> p k n", p=128))
                self.load("pool", gw, 0, gw.t[0:1, 8, :], self.dram["b_in"][:, C_MI:C_MI + 8], nowait=True)
                self.gw_loaded = True
                while self.wloaded < 2:
                    self.wload(self.wloaded)
                    self.wloaded += 1
                self.load_consts_late()
            self.projection(sets)
            if p == 0:
                self.mixers(self.MINI)
                self.sample_mixers()
                self.no_prefetch_beyond = None
                while self.wloaded < min(len(self.wspecs), self.wi + self.NWB):
                    self.wload(self.wloaded)
                    self.wloaded += 1
            self.mixers(self.MAIN)
            if p == NP - 1:
                self.store_prompt_state()
            self.outproj_ln1(sets)
            self.pending_ln = self.ffn(sets, (lambda p=p: self.load_x_main(p + 1)) if p + 1 < NP else None)
        assert self.wi == len(self.wspecs)


def _consts():
    f32 = np.float32
    ident = np.eye(128, dtype=f32)
    maskT = np.triu(np.ones((128, 128), dtype=f32))
    lg = np.log1p(-np.exp2(-5.0 - np.arange(4, dtype=np.float64)))
    cst = np.zeros((128, NCST), dtype=np.float64)
    s128 = np.arange(128)[:, None]
    cst[:, K_ES128:K_ES128 + 4] = np.exp((127 - s128) * lg[None, :])
    cst[:, K_ES16:K_ES16 + 4] = np.exp((15 - s128) * lg[None, :])
    cst[:, K_WC128:K_WC128 + 4] = np.exp(128 * lg)[None, :]
    cst[:, K_WC16:K_WC16 + 4] = np.exp(16 * lg)[None, :]
    cst[:, K_EPS128:K_EPS128 + 4] = LN_EPS * np.exp(2 * (127 - s128) * lg[None, :])
    cst[:, K_EPS16:K_EPS16 + 4] = LN_EPS * np.exp(2 * (15 - s128) * lg[None, :])
    cst[:, K_G1:K_G1 + 4] = np.exp(lg)[None, :]
    cst[:, K_EPS1:K_EPS1 + 4] = LN_EPS
    cst[:, K_ONE] = 1.0
    cst = cst.astype(f32)
    inv = 10000.0 ** (-np.arange(0, 256, 2, dtype=np.float64) / 256.0)

    def tables(pos):
        ang = np.asarray(pos, dtype=np.float64)[:, None] * inv[None, :]
        return np.ascontiguousarray(np.cos(ang).T.astype(f32)), np.ascontiguousarray(np.sin(ang).T.astype(f32))
    ropeC, ropeS = tables(np.arange(NMETA, NMETA + SEQ))
    pm = np.zeros(64)
    pm[0:NMETA] = np.arange(NMETA)
    pm[SP0:SP0 + NS] = PAST
    ropeCm, ropeSm = tables(pm)
    identrow = np.zeros((128, 16, 64), dtype=f32)
    for b in range(16):
        identrow[:, b, SP0 + b] = 1.0
    return dict(ident=ident, maskT=maskT, cst=cst, ropeC=ropeC, ropeS=ropeS, ropeCm=ropeCm, ropeSm=ropeSm,
                identrow=identrow.reshape(128, 1024))


_NC_CACHE = {}


def kernel(x_prompt, x_sample, state_mlstm_C, state_mlstm_n, state_mlstm_m, state_ret_S,
           meta_tokens, w_in, b_in, ml_norm_g, rt_norm_g, w_out,
           ln1_g, ln1_b, w_gate, w_up, w_down, ln2_g, ln2_b):
    f = lambda a: np.ascontiguousarray(np.asarray(a, dtype=np.float32))
    if "nc" not in _NC_CACHE:
        _NC_CACHE["nc"] = Prog().build()
    nc = _NC_CACHE["nc"]
    cs = _consts()
    shared = dict(meta=f(meta_tokens), w_in=f(w_in)[0], b_in=f(b_in), ml_g=f(ml_norm_g), rt_g=f(rt_norm_g),
                  w_out=f(w_out)[0], ln1_g=f(ln1_g), ln1_b=f(ln1_b), w_gate=f(w_gate)[0], w_up=f(w_up)[0],
                  w_down=f(w_down)[0], ln2_g=f(ln2_g), ln2_b=f(ln2_b), **cs)
    xp, xs = f(x_prompt), f(x_sample)
    sC, sn, sm, sS = f(state_mlstm_C)[0], f(state_mlstm_n)[0], f(state_mlstm_m)[0], f(state_ret_S)[0]
    in_maps = []
    for c in range(NCORES):
        sl = slice(c * NS, (c + 1) * NS)
        m = dict(shared)
        m.update(xp=xp[c], xs=np.ascontiguousarray(xs[sl, 0, :]), sC=np.ascontiguousarray(sC[sl]),
                 sn=np.ascontiguousarray(sn[sl].reshape(NS, 1024)), sm=np.ascontiguousarray(sm[sl]),
                 sS=np.ascontiguousarray(sS[sl]))
        in_maps.append(m)
    res = run_bass_kernel_spmd(nc, in_maps, core_ids=list(range(NCORES)))
    R = res.results
    if DEBUG:
        _NC_CACHE["dbg"] = dict(mrg=R[0]["dbg_mrg"], x1=R[0]["dbg_x1"])
    cat = lambda k: np.concatenate([r[k] for r in R], axis=0)
    stk = lambda k: np.stack([r[k] for r in R], axis=0)
    y_prompt = stk("yp")
    y_sample = cat("ys").reshape(128, 1, D)
    p_C = stk("pC")[None]
    p_n = stk("pn")[None]
    p_m = stk("pm").reshape(1, NCORES, 4)
    p_S = stk("pS")[None]
    s_C = cat("oC")[None]
    s_n = cat("on").reshape(1, 128, 4, 256)
    s_m = cat("om")[None]
    s_S = cat("oS")[None]
    return (y_prompt, y_sample, p_C, p_n, p_m, p_S, s_C, s_n, s_m, s_S)
```

```python
from contextlib import ExitStack

import numpy as np
import concourse.bass as bass
import concourse.mybir as mybir
from concourse.bass_utils import run_bass_kernel_spmd

F32 = mybir.dt.float32
BF16 = mybir.dt.bfloat16
AF = mybir.ActivationFunctionType
ALU = mybir.AluOpType
AX = mybir.AxisListType

NCORES = 8
D = 1024
SEQ = 2048
NMETA = 16
NS = 16
SP0 = 32
DIN = 10248
DFF = 2816
NFF = DFF // 128
LN_EPS = 1e-5
ALPHA = 2.0 ** 0.25
PAST = 16384
C_MI = 4096

K_ES128, K_ES16, K_WC128, K_WC16, K_EPS128, K_EPS16, K_G1, K_EPS1, K_ONE = (0, 4, 8, 12, 16, 20, 24, 28, 32)
NCST = 40
DEBUG = False


class Buf:
    __slots__ = ("name", "w", "r", "sem", "cnt")

    def __init__(self, name):
        self.name = name
        self.w = None
        self.r = {}
        self.sem = None
        self.cnt = 0


class FW:
    ENG = ("pe", "act", "dve", "pool", "sp")

    def __init__(self, nc, stack):
        self.nc = nc
        self.stack = stack
        self.sem = {e: stack.enter_context(nc.semaphore("s_" + e)) for e in self.ENG}
        self.n = {e: 0 for e in self.ENG}
        self.known = {e: {} for e in self.ENG}
        self.rec = {e: [] for e in self.ENG}
        self.nsem = len(self.ENG)
        self.out_toks = []

    def _waits(self, E, reads, writes):
        need = {}

        def add(tok):
            if tok is None:
                return
            s, v = tok
            if need.get(s, 0) < v:
                need[s] = v
        for b in reads:
            add(b.w)
        for b in writes:
            add(b.w)
            for t in b.r.values():
                add(t)
        out = []
        kn = self.known[E]
        for s, v in need.items():
            if E == "pe" and s is self.sem["pe"]:
                continue
            if kn.get(s, 0) < v:
                kn[s] = v
                out.append((s, v))
        return out

    def op(self, E, fn, reads=(), writes=(), extra=()):
        waits = self._waits(E, reads, list(writes) + list(extra))
        self.n[E] += 1
        sem = self.sem[E]
        tok = (sem, self.n[E])
        self.rec[E].append((waits, fn, sem, 1))
        for b in reads:
            b.r[E] = tok
        for b in writes:
            b.w = tok
            b.r = {}
        return tok

    def dma(self, Q, fn, sb, reads=(), writes=(), nowait=False, extra=()):
        waits = [] if nowait else self._waits(Q, reads, list(writes) + list(extra))
        if sb.sem is None:
            sb.sem = self.stack.enter_context(self.nc.semaphore("d_" + sb.name))
            self.nsem += 1
        sb.cnt += 16
        tok = (sb.sem, sb.cnt)
        self.rec[Q].append((waits, fn, sb.sem, 16))
        for b in reads:
            b.r["dma_" + sb.name] = tok
        for b in writes:
            b.w = tok
            b.r = {}
        return tok

    def emit(self):
        nc = self.nc
        need = {}
        for s, v in self.out_toks:
            if need.get(s, 0) < v:
                need[s] = v
        self.rec["sp"].append((list(need.items()), None, None, 0))
        with nc.Block() as block:
            def run(eng, lst):
                for waits, fn, sem, inc in lst:
                    for s, v in waits:
                        eng.wait_ge(s, v)
                    if fn is not None:
                        fn(eng).then_inc(sem, inc)

            @block.tensor
            def _(e):
                run(e, self.rec["pe"])

            @block.scalar
            def _(e):
                run(e, self.rec["act"])

            @block.vector
            def _(e):
                run(e, self.rec["dve"])

            @block.gpsimd
            def _(e):
                run(e, self.rec["pool"])

            @block.sync
            def _(e):
                run(e, self.rec["sp"])


class Tl:
    def __init__(self, t, name):
        self.t = t
        self.name = name
        self._b = {}
        self.coarse = False

    def b(self, key=0):
        if self.coarse:
            key = 0
        if key not in self._b:
            self._b[key] = Buf("%s_%s" % (self.name, key))
        return self._b[key]

    def bs(self, keys):
        return [self.b(k) for k in keys]


class TokSet:
    pass


class Prog:
    def __init__(self):
        self.nc = bass.Bass("TRN2", target_bir_lowering=False)
        self.st = ExitStack()
        self.dram = {}
        self.xbi = 0
        self.pending_ln = []
        self.ps_reserved = set()
        self.mini_down = None
        self.mgi = 0
        self.tbi = 0
        self.psi = 0
        self.sbytes = 0

    def din(self, name, shape):
        self.dram[name] = self.nc.dram_tensor(name, list(shape), F32, kind="ExternalInput").ap()

    def dout(self, name, shape):
        self.dram[name] = self.nc.dram_tensor(name, list(shape), F32, kind="ExternalOutput").ap()

    def sb(self, name, shape, dt):
        n = 1
        for x in shape[1:]:
            n *= x
        self.sbytes += n * (4 if dt == F32 else 2)
        return Tl(self.st.enter_context(self.nc.sbuf_tensor("sb_" + name, list(shape), dt)), name)

    def load(self, Q, tl, key, out_ap, in_ap, nowait=False):
        return self.fw.dma(Q, lambda e: e.dma_start(out=out_ap, in_=in_ap), tl.b(key),
                           writes=[tl.b(key)], nowait=nowait)

    def store(self, Q, tl, key, out_ap, in_ap, reads=None, semkey=None):
        tok = self.fw.dma(Q, lambda e: e.dma_start(out=out_ap, in_=in_ap), tl.b(key if semkey is None else semkey),
                          reads=[tl.b(key)] if reads is None else reads)
        self.fw.out_toks.append(tok)
        return tok

    def build(self):
        nc = self.nc
        with self.st:
            self.fw = FW(nc, self.st)
            self.declare()
            self.alloc()
            self.program()
            self.fw.emit()
        return nc

    def declare(self):
        d = self.din
        d("xp", (SEQ, D)); d("meta", (NMETA, D)); d("xs", (NS, D))
        d("sC", (NS, 4, 256, 256)); d("sn", (NS, 1024)); d("sm", (NS, 4)); d("sS", (NS, 4, 256, 256))
        d("w_in", (D, DIN)); d("b_in", (1, DIN))
        d("ml_g", (1, D)); d("rt_g", (1, D)); d("w_out", (D, D))
        d("ln1_g", (1, D)); d("ln1_b", (1, D))
        d("w_gate", (D, DFF)); d("w_up", (D, DFF)); d("w_down", (DFF, D))
        d("ln2_g", (1, D)); d("ln2_b", (1, D))
        d("ident", (128, 128)); d("maskT", (128, 128)); d("cst", (128, NCST))
        d("ropeC", (128, SEQ)); d("ropeS", (128, SEQ))
        d("ropeCm", (128, 64)); d("ropeSm", (128, 64))
        d("identrow", (128, 1024))
        o = self.dout
        o("yp", (SEQ, D)); o("ys", (NS, D))
        if DEBUG:
            o("dbg_mrg", (64, D)); o("dbg_x1", (64, D))
        o("pC", (4, 256, 256)); o("pn", (4, 256)); o("pm", (4, 1)); o("pS", (4, 256, 256))
        o("oC", (NS, 4, 256, 256)); o("on", (NS, 1024)); o("om", (NS, 4)); o("oS", (NS, 4, 256, 256))

    def mkset(self, name, T, tsz):
        s = TokSet()
        s.name, s.T, s.tsz, s.ntile = name, T, tsz, T // tsz
        sb = self.sb
        s.alias = (tsz == 128)
        s.xT = sb(name + "xT", [128, 8, T], BF16)
        s.fmbig = sb(name + "fm", [128, 32, T], BF16)
        s.xres = sb(name + "xres", [tsz, s.ntile, 1024], F32)
        s.G = [sb(name + "G%d" % i, [tsz, s.ntile, 1024], BF16) for i in range(2)]
        s.gcol = sb(name + "gcol", [tsz, s.ntile, 8], F32)
        if s.alias:
            s.xT.coarse = True
            mv_ = s.xT.t[:, :, :].rearrange("p k t -> p (k t)").rearrange("p (i n) -> p i n", i=s.ntile)
            s.merged = Tl(mv_, s.xT.name)
            s.merged._b = s.xT._b
            s.merged.coarse = True
        else:
            s.merged = sb(name + "mrg", [tsz, s.ntile, 1024], BF16)
        s.rc = sb(name + "rc", [128, T], F32)
        s.rs = sb(name + "rs", [128, T], F32)
        return s

    def fm(self, s, which, chunk):
        return s.fmbig.t[:, 8 * which + chunk, :]

    def vview(self, s, mix):
        nt = s.ntile
        half = s.xres.t[:, :, :].rearrange("p a n -> p (a n)").bitcast(BF16)
        return half[:, mix * nt * 1024:(mix + 1) * nt * 1024].rearrange("p (a n) -> p a n", a=nt)

    def alloc(self):
        nc, st, sb = self.nc, self.st, self.sb
        self.ps = [Tl(st.enter_context(nc.psum_tensor("ps%d" % i, [128, 512], F32)), "ps%d" % i)
                   for i in range(8)]
        self.MAIN = self.mkset("M", 512, 128)
        self.MINI = self.mkset("E", 64, 64)
        self.NWB = 3
        self.wbuf = [sb("wb%d" % i, [128, 9 * 512], BF16) for i in range(self.NWB)]
        self.gw = sb("gw", [128, 9, 8], BF16)
        self.ident_b = sb("ident_b", [128, 128], BF16)
        self.ident_f = sb("ident_f", [128, 128], F32)
        self.maskT = sb("maskT", [128, 128], F32)
        self.cst = sb("cst", [128, NCST], F32)
        self.ones_b = sb("ones_b", [128, 512], BF16)
        self.ones_f = sb("ones_f", [128, 128], F32)
        self.gbc = [sb("gbc%d" % i, [128, 1024], BF16) for i in range(2)]
        self.lnp = [sb("lnp%d" % i, [128, 1024], BF16) for i in range(4)]
        self.xbf = [sb("xbf%d" % i, [128, 1024], BF16) for i in range(2)]
        self.tmpb = [sb("tmpb%d" % i, [128, 512], BF16) for i in range(2)]
        self.rot = sb("rot", [128, 4, 256], F32)
        self.Cf = sb("Cf", [128, 8, 2, 257], F32)
        self.Cb = sb("Cb", [128, 8, 2, 257], BF16)
        self.ktm = [sb("ktm%d" % i, [128, 4, 256], BF16) for i in range(2)]
        self.vp = [sb("vp%d" % i, [128, 4, 257], BF16) for i in range(2)]
        self.stm = [sb("stm%d" % i, [128, 4, 128], BF16) for i in range(2)]
        self.u = sb("u", [128, 8, 256], BF16)
        self.st6 = sb("st6", [128, 8, 6], F32)
        self.mv = sb("mv", [128, 8, 2], F32)
        self.den = sb("den", [128, 4], F32)
        self.sm = sb("smalls", [128, 64], F32)
        self.tmpa = sb("tmpa", [128, 256], F32)
        self.gm = sb("gm", [128, 4, 40], F32)
        self.gsm = sb("gsm", [4, 96], F32)
        self.gbcst = sb("gbcst", [128, 5, 8], F32)
        self.wcb = sb("wcb", [128, 5, 4], F32)
        self.mcur = sb("mcur", [4, 1], F32)
        self.lst = sb("lst", [128, 4, 2, 6], F32)
        self.lmv = sb("lmv", [128, 4, 4], F32)
        self.NCIN = 3
        self.cin = [sb("cin%d" % i, [128, 2, 256], F32) for i in range(self.NCIN)]
        self.cin_owner = {}
        for i, wb in enumerate(self.wbuf):
            for q in range(4):
                v_ = wb.t[:, q * 1024:(q + 1) * 1024].bitcast(F32).rearrange("p (j e) -> p j e", j=2)
                tl = Tl(v_, "cs%d_%d" % (i, q))
                self.cin.append(tl)
                self.cin_owner[tl.name] = wb
        wb = self.wbuf[-1]
        for _ in range(2):
            tl = self.cin.pop()
            del self.cin_owner[tl.name]
        self.qm2 = Tl(wb.t[:, 3 * 1024:3 * 1024 + 512].rearrange("p (k n) -> p k n", k=8), "qm2")
        self.vm2 = Tl(wb.t[0:64, 2 * 1024:3 * 1024], "vm2")
        self.borrowed2 = [self.qm2, self.vm2]
        self.cbf = [sb("cbf%d" % i, [128, 2, 256], BF16) for i in range(2)]
        self.qm = [sb("qm0", [128, 8, 64], BF16)] * 2
        self.vmk = [sb("vmk0", [64, 1024], BF16)] * 2
        self.identrow = sb("identrow", [128, 16, 64], BF16)
        self.s_num = sb("s_num", [64, 256], F32)
        self.s_tm = [sb("s_tm%d" % i, [64, 1024], BF16) for i in range(3)]
        self.s_tm = [self.s_tm[0], self.s_tm[1], self.s_tm[0], self.s_tm[2]]
        self.s_n = Tl(self.rot.t[0:64, :, :].rearrange("p a n -> p (a n)"), "rot")
        self.s_n._b = self.rot._b
        self.s_n.coarse = True
        self.rot.coarse = True
        self.s_sm = sb("s_sm", [64, 64], F32)
        self.s_wb = sb("s_wb", [128, 128], F32)
        self.s_dg = sb("s_dg", [64, 16, 8], F32)

    def nextps(self):
        while (self.psi % 8) in self.ps_reserved:
            self.psi += 1
        p = self.ps[self.psi % 8]
        self.psi += 1
        return p

    def wspec_list(self, npass):
        L = []
        for p in range(npass):
            for n_, c0 in enumerate(list(range(0, 4096, 512)) + list(range(4104, DIN, 512))):
                L.append(("in", c0))
                if p == 1 and n_ < 6:
                    L.append(("down", n_ * 512))
            for c0 in (0, 512):
                L.append(("out", c0))
            for c0 in range(0, DFF, 512):
                L.append(("gate", c0))
                L.append(("up", c0))
            for r0 in range(0, DFF, 512):
                L.append(("down", r0))
        return L

    def wload(self, idx):
        kind, c0 = self.wspecs[idx]
        tl = self.wbuf[idx % self.NWB]
        d = self.dram
        uid = (kind, c0)
        def regions(t2):
            if kind == "in":
                return [t2[:, 0:8 * 512], t2[0:1, 8 * 512:9 * 512]]
            if kind == "out":
                return [t2[:, 0:8 * 512]]
            if kind in ("gate", "up"):
                n = min(512, DFF - c0)
                return [t2[:, 0:8 * 512].rearrange("p (k n) -> p k n", k=8)[:, :, 0:n]]
            nch = min(4, (DFF - c0) // 128)
            return [t2[:, 0:nch * 1024]]
        if uid in self.scr_slot:
            slot = self.scr_slot[uid]
            for n_, (o_, i_) in enumerate(zip(regions(tl.t), regions(self.wscr[slot]))):
                self.fw.dma("pool", lambda e, o_=o_, i_=i_: e.dma_start(out=o_, in_=i_), tl.b(0),
                            reads=[self.scr_buf[slot]], writes=[tl.b(0)], nowait=(n_ > 0))
            return
        self._wload_cast(idx)
        slot = len(self.scr_slot)
        self.scr_slot[uid] = slot
        self.scr_buf[slot] = Buf("scr%d" % slot)
        for n_, (o_, i_) in enumerate(zip(regions(self.wscr[slot]), regions(tl.t))):
            self.fw.dma("sp", lambda e, o_=o_, i_=i_: e.dma_start(out=o_, in_=i_), tl.b("st"),
                        reads=[tl.b(0)], writes=[self.scr_buf[slot]], nowait=(n_ > 0))

    def _wload_cast(self, idx):
        kind, c0 = self.wspecs[idx]
        tl = self.wbuf[idx % self.NWB]
        d = self.dram
        t3 = tl.t[:, 0:8 * 512].rearrange("p (k n) -> p k n", k=8)
        if kind == "in":
            self.load("pool", tl, 0, t3, d["w_in"][:, c0:c0 + 512].rearrange("(k p) n -> p k n", p=128))
            self.load("pool", tl, 0, tl.t[0:1, 8 * 512:9 * 512], d["b_in"][:, c0:c0 + 512], nowait=True)
        elif kind == "out":
            self.load("pool", tl, 0, t3, d["w_out"][:, c0:c0 + 512].rearrange("(k p) n -> p k n", p=128))
        elif kind in ("gate", "up"):
            w = d["w_gate"] if kind == "gate" else d["w_up"]
            n = min(512, DFF - c0)
            self.load("pool", tl, 0, t3[:, :, 0:n], w[:, c0:c0 + n].rearrange("(k p) n -> p k n", p=128))
        else:
            nch = min(4, (DFF - c0) // 128)
            t4 = tl.t[:, 0:4 * 1024].rearrange("p (c n) -> p c n", c=4)
            self.load("pool", tl, 0, t4[:, 0:nch, :],
                      d["w_down"][c0:c0 + nch * 128, :].rearrange("(c p) n -> p c n", p=128))

    def wnext(self, kind, c0, hold=0):
        i = self.wi
        assert self.wspecs[i] == (kind, c0), (self.wspecs[i], kind, c0)
        lim = min(len(self.wspecs), i + self.NWB - hold)
        if self.no_prefetch_beyond is not None:
            lim = min(lim, self.no_prefetch_beyond + 1)
        while self.wloaded < lim:
            self.wload(self.wloaded)
            self.wloaded += 1
        self.wi += 1
        return self.wbuf[i % self.NWB]

    def load_consts(self):
        d = self.dram
        fw = self.fw
        self.load("sp", self.ident_f, 0, self.ident_f.t[:], d["ident"])
        self.load("pool", self.ident_b, 0, self.ident_b.t[:], d["ident"])
        self.load("sp", self.maskT, 0, self.maskT.t[:], d["maskT"])
        self.load("sp", self.cst, 0, self.cst.t[:], d["cst"])
        self.load("pool", self.identrow, 0, self.identrow.t[:].rearrange("p a b -> p (a b)"), d["identrow"])
        fw.op("dve", lambda e: e.memset(self.ones_b.t[:], 1.0), writes=[self.ones_b.b()])
        fw.op("dve", lambda e: e.memset(self.ones_f.t[:], 1.0), writes=[self.ones_f.b()])
        fw.op("dve", lambda e: e.memset(self.Cf.t[:], 0.0), writes=self.Cf.bs(range(8)))
        fw.op("dve", lambda e: e.memset(self.mcur.t[:], 0.0), writes=[self.mcur.b()])

    def load_consts_late(self):
        d = self.dram
        for tl, nm in ((self.gbc[0], "ml_g"), (self.gbc[1], "rt_g"), (self.lnp[0], "ln1_g"),
                       (self.lnp[1], "ln1_b"), (self.lnp[2], "ln2_g"), (self.lnp[3], "ln2_b")):
            self.load("pool", tl, 0, tl.t[:], d[nm].partition_broadcast(128))

    def transpose_in(self, s, src_bufs, src_ap_fn, dst_ap, dst_bufs):
        tsz = s.tsz
        ps = self.nextps()
        pv = ps.t[:].bitcast(BF16).rearrange("p (k n) -> p k n", k=8)

        def tr(e):
            for k in range(8):
                ins = e.transpose(out=pv[:, k, 0:tsz], in_=src_ap_fn(k), identity=self.ident_b.t[0:tsz, 0:tsz])
            return ins
        self.fw.op("pe", tr, reads=list(src_bufs) + [self.ident_b.b()], writes=[ps.b()])
        self.fw.op("act", lambda e: e.copy(out=dst_ap, in_=pv[:, :, 0:tsz]), reads=[ps.b()], writes=list(dst_bufs))

    def load_x_main(self, p):
        s = self.MAIN
        d = self.dram
        for i in range(4):
            xb = self.xbf[self.xbi % 2]
            self.xbi += 1
            r0 = p * 512 + i * 128
            self.load("pool", xb, 0, xb.t[:, :], d["xp"][r0:r0 + 128, :])
            self.transpose_in(s, [xb.b()], lambda k, xb=xb: xb.t[:, k * 128:(k + 1) * 128],
                              s.xT.t[:, :, i * 128:(i + 1) * 128], [s.xT.b(i)])
        self.load("sp", s.rc, 0, s.rc.t[:], d["ropeC"][:, p * 512:(p + 1) * 512])
        self.load("sp", s.rs, 0, s.rs.t[:], d["ropeS"][:, p * 512:(p + 1) * 512])

    def load_x_mini(self):
        s = self.MINI
        d = self.dram
        xb = self.vmk[0]
        self.fw.op("dve", lambda e: e.memset(xb.t[:], 0.0), writes=[xb.b()])
        self.load("pool", xb, 0, xb.t[0:NMETA, :], d["meta"])
        self.load("pool", xb, 0, xb.t[SP0:SP0 + NS, :], d["xs"], nowait=True)
        self.transpose_in(s, [xb.b()], lambda k: xb.t[:, k * 128:(k + 1) * 128], s.xT.t[:, :, :], [s.xT.b(0)])
        self.load("sp", s.rc, 0, s.rc.t[:], d["ropeCm"])
        self.load("sp", s.rs, 0, s.rs.t[:], d["ropeSm"])

    def proj_gates(self, s):
        fw = self.fw
        gw = self.gw
        for i in range(s.ntile):
            ps = self.nextps()

            def mm(e, i=i, ps=ps):
                for k in range(8):
                    e.matmul(ps.t[0:s.tsz, 0:8], lhsT=s.xT.t[:, k, i * s.tsz:(i + 1) * s.tsz],
                             rhs=gw.t[:, k, :], start=(k == 0), stop=False)
                return e.matmul(ps.t[0:s.tsz, 0:8], lhsT=self.ones_b.t[0:1, 0:s.tsz],
                                rhs=gw.t[0:1, 8, :], start=False, stop=True)
            fw.op("pe", mm, reads=[s.xT.b(i), gw.b(), self.ones_b.b()], writes=[ps.b()])
            fw.op("act", lambda e, i=i, ps=ps: e.copy(out=s.gcol.t[:, i, :], in_=ps.t[0:s.tsz, 0:8]),
                  reads=[ps.b()], writes=[s.gcol.b()])

    def proj_fm(self, s, wt, cbase, which, scale, rot):
        fw = self.fw
        T = s.T
        w3 = wt.t[:, 0:8 * 512].rearrange("p (k n) -> p k n", k=8)
        brow = wt.t[0:1, 8 * 512:9 * 512]
        xr = s.xT.bs(range(s.ntile))
        pss = []
        for m in range(4):
            ps = self.nextps()
            pss.append(ps)

            def mm(e, m=m, ps=ps):
                for k in range(8):
                    e.matmul(ps.t[:, 0:T], lhsT=w3[:, k, m * 128:(m + 1) * 128], rhs=s.xT.t[:, k, :],
                             start=(k == 0), stop=False)
                return e.matmul(ps.t[:, 0:T], lhsT=brow[:, m * 128:(m + 1) * 128], rhs=self.ones_b.t[0:1, 0:T],
                                start=False, stop=True)
            fw.op("pe", mm, reads=xr + [wt.b(), self.ones_b.b()], writes=[ps.b()])
            if not rot:
                fw.op("act", lambda e, m=m, ps=ps: e.activation(out=self.fm(s, which, cbase + m), in_=ps.t[:, 0:T],
                                                               func=AF.Copy, scale=scale),
                      reads=[ps.b()], writes=[s.fmbig.b(8 * which + cbase + m)])
            elif m % 2 == 1:
                p1, p2 = pss[m - 1], ps
                c1, c2 = cbase + m - 1, cbase + m
                r = self.rot
                rd = [s.rc.b(), s.rs.b()]
                for q0 in range(0, T, 256):
                    n = min(256, T - q0)
                    cos, sin = s.rc.t[:, q0:q0 + n], s.rs.t[:, q0:q0 + n]
                    a1, a2 = p1.t[:, q0:q0 + n], p2.t[:, q0:q0 + n]

                    def stt(o, i0, i1):
                        return lambda e: e.scalar_tensor_tensor(out=o, in0=i0, scalar=scale, in1=i1,
                                                                op0=ALU.mult, op1=ALU.mult)
                    fw.op("dve", stt(r.t[:, 0, 0:n], a1, cos), reads=[p1.b()] + rd, writes=[r.b(0)])
                    fw.op("dve", stt(r.t[:, 1, 0:n], a2, sin), reads=[p2.b()] + rd, writes=[r.b(1)])
                    fw.op("dve", stt(r.t[:, 2, 0:n], a1, sin), reads=[p1.b()] + rd, writes=[r.b(2)])
                    fw.op("dve", stt(r.t[:, 3, 0:n], a2, cos), reads=[p2.b()] + rd, writes=[r.b(3)])
                    o1 = self.fm(s, which, c1)[:, q0:q0 + n]
                    o2 = self.fm(s, which, c2)[:, q0:q0 + n]
                    fw.op("dve", lambda e, o1=o1, n=n: e.tensor_tensor(out=o1, in0=r.t[:, 0, 0:n], in1=r.t[:, 1, 0:n],
                                                                      op=ALU.subtract),
                          reads=[r.b(0), r.b(1)], writes=[s.fmbig.b(8 * which + c1)])
                    fw.op("dve", lambda e, o2=o2, n=n: e.tensor_tensor(out=o2, in0=r.t[:, 2, 0:n], in1=r.t[:, 3, 0:n],
                                                                      op=ALU.add),
                          reads=[r.b(2), r.b(3)], writes=[s.fmbig.b(8 * which + c2)])

    def proj_tm(self, s, wt, kind, half):
        fw = self.fw
        w3 = wt.t[:, 0:8 * 512].rearrange("p (k n) -> p k n", k=8)
        brow = wt.t[0:1, 8 * 512:9 * 512]
        cs = slice(half * 512, (half + 1) * 512)
        tsz = s.tsz
        for i in range(s.ntile):
            ps = self.nextps()

            def mm(e, i=i, ps=ps):
                for k in range(8):
                    e.matmul(ps.t[0:tsz, :], lhsT=s.xT.t[:, k, i * tsz:(i + 1) * tsz], rhs=w3[:, k, :],
                             start=(k == 0), stop=False)
                return e.matmul(ps.t[0:tsz, :], lhsT=self.ones_b.t[0:1, 0:tsz], rhs=brow, start=False, stop=True)
            fw.op("pe", mm, reads=[s.xT.b(i), wt.b(), self.ones_b.b()], writes=[ps.b()])
            pin = ps.t[0:tsz, :]
            if kind in ("mv", "rv"):
                dst = self.vview(s, 0 if kind == "mv" else 1)
                fw.op("act", lambda e, dst=dst, i=i, pin=pin: e.copy(out=dst[:, i, cs], in_=pin),
                      reads=[ps.b()], writes=s.xres.bs(range(s.ntile)))
            elif kind in ("mo", "rg"):
                G = s.G[0 if kind == "mo" else 1]
                gb = self.gbc[0 if kind == "mo" else 1]
                tb = self.tmpb[self.tbi % 2]
                self.tbi += 1
                fn = AF.Sigmoid if kind == "mo" else AF.Silu
                fw.op("act", lambda e, tb=tb, pin=pin, fn=fn: e.activation(out=tb.t[0:tsz, :], in_=pin, func=fn),
                      reads=[ps.b()], writes=[tb.b()])
                fw.op("dve", lambda e, tb=tb, G=G, i=i, gb=gb: e.tensor_tensor(
                    out=G.t[:, i, cs], in0=tb.t[0:tsz, :], in1=gb.t[0:tsz, cs], op=ALU.mult),
                    reads=[tb.b(), gb.b()], writes=[G.b((i, half))])
            else:
                G = s.G[0 if kind == "ga" else 1]
                tb = self.tmpb[self.tbi % 2]
                self.tbi += 1
                fw.op("act", lambda e, tb=tb, pin=pin: e.activation(out=tb.t[0:tsz, :], in_=pin, func=AF.Sigmoid),
                      reads=[ps.b()], writes=[tb.b()])
                fw.op("dve", lambda e, tb=tb, G=G, i=i: e.tensor_tensor(
                    out=G.t[:, i, cs], in0=tb.t[0:tsz, :], in1=G.t[:, i, cs], op=ALU.mult),
                    reads=[tb.b(), G.b((i, half))], writes=[G.b((i, half))])

    def projection(self, sets):
        d = self.dram
        gw = self.gw
        if not self.gw_loaded:
            self.load("pool", gw, 0, gw.t[:, 0:8, :], d["w_in"][:, C_MI:C_MI + 8].rearrange("(k p) n -> p k n", p=128))
            self.load("pool", gw, 0, gw.t[0:1, 8, :], d["b_in"][:, C_MI:C_MI + 8], nowait=True)
            self.gw_loaded = True
        for s in sets:
            self.proj_gates(s)
        plan = [(0, "fm", 0, 0, 1.0, False), (512, "fm", 0, 4, 1.0, False),
                (1024, "fm", 1, 0, 1.0 / 16, False), (1536, "fm", 1, 4, 1.0 / 16, False),
                (2048, "tm", "mv", 0), (2560, "tm", "mv", 1), (3072, "tm", "mo", 0), (3584, "tm", "mo", 1),
                (4104, "fm", 2, 0, 1.0, True), (4616, "fm", 2, 4, 1.0, True),
                (5128, "fm", 3, 0, 1.0 / 16, True), (5640, "fm", 3, 4, 1.0 / 16, True),
                (6152, "tm", "rv", 0), (6664, "tm", "rv", 1), (7176, "tm", "rg", 0), (7688, "tm", "rg", 1),
                (8200, "tm", "ga", 0), (8712, "tm", "ga", 1), (9224, "tm", "gb", 0), (9736, "tm", "gb", 1)]
        md_banks = [[self.ps[6], self.ps[7]]]
        if self.mini_down:
            self.ps_reserved = {6, 7}
        for n_ent, ent in enumerate(plan):
            if self.pending_ln:
                self.pending_ln.pop(0)()
            if self.mini_down and n_ent == 6:
                self.down(self.mini_down, banks=md_banks, blocks="finish")
                self.mini_down = None
                self.ps_reserved = set()
            wt = self.wnext("in", ent[0])
            for s in sets:
                if ent[1] == "fm":
                    self.proj_fm(s, wt, ent[3], ent[2], ent[4], ent[5])
                else:
                    self.proj_tm(s, wt, ent[2], ent[3])
            if ent[0] == 1536:
                for s in sets:
                    if s is not self.MAIN:
                        for _ in self.gate_math(s):
                            pass
                gens = [self.gate_math(self.MAIN)]
            if ent[0] in (1536, 2048, 2560):
                for g_ in gens:
                    next(g_, None)
            if self.mini_down and n_ent < 6:
                self.down(self.mini_down, banks=md_banks, blocks=n_ent * 512)

    def gate_math(self, s):
        fw = self.fw
        main = s is self.MAIN
        L = 128 if main else NMETA
        nt = s.ntile
        gm, gcol = self.gm, s.gcol
        one = self.cst.t[0:L, K_ONE:K_ONE + 1]
        slot0 = 0 if main else 4
        o = s.gofs = (24 if main else 32)
        fw.op("act", lambda e: e.activation(out=gm.t[0:L, 0:nt, 20:24], in_=gcol.t[0:L, :, 4:8], func=AF.Exp, scale=-1.0),
              reads=[gcol.b()], writes=[gm.b()])
        fw.op("act", lambda e: e.activation(out=gm.t[0:L, 0:nt, 0:4], in_=gm.t[0:L, 0:nt, 20:24], func=AF.Ln,
                                            bias=one, scale=1.0),
              reads=[gm.b(), self.cst.b()], writes=[gm.b()])
        fw.op("dve", lambda e: e.tensor_scalar(out=gm.t[0:L, 0:nt, 0:4], in0=gm.t[0:L, 0:nt, 0:4], scalar1=-1.0,
                                               scalar2=None, op0=ALU.mult),
              reads=[gm.b()], writes=[gm.b()])
        gsm = self.gsm
        for i in range(nt):
            ps = self.nextps()

            def mmc(e, i=i, ps=ps):
                e.matmul(ps.t[0:L, 0:4], lhsT=self.maskT.t[0:L, 0:L], rhs=gm.t[0:L, i, 0:4], start=True, stop=True)
                return e.matmul(ps.t[0:4, 8:9], lhsT=gm.t[0:L, i, 0:4], rhs=self.ones_f.t[0:L, 0:1], start=True, stop=True)
            fw.op("pe", mmc, reads=[self.maskT.b(), gm.b(), self.ones_f.b()], writes=[ps.b()])
            fw.op("dve", lambda e, i=i, ps=ps: e.tensor_copy(out=gm.t[0:L, i, 4:8], in_=ps.t[0:L, 0:4]),
                  reads=[ps.b()], writes=[gm.b()])
            fw.op("dve", lambda e, i=i, ps=ps: e.tensor_copy(out=gsm.t[:, 4 + i:5 + i], in_=ps.t[0:4, 8:9]),
                  reads=[ps.b()], writes=[gsm.b()])
        fw.op("dve", lambda e: e.tensor_tensor(out=gm.t[0:L, 0:nt, 8:12], in0=gcol.t[0:L, :, 0:4],
                                               in1=gm.t[0:L, 0:nt, 4:8], op=ALU.subtract),
              reads=[gm.b(), gcol.b()], writes=[gm.b()])
        yield
        ps = self.nextps()

        def tr(e, ps=ps):
            for i in range(nt):
                ins = e.transpose(out=ps.t[0:4, i * L:(i + 1) * L], in_=gm.t[0:L, i, 8:12],
                                  identity=self.ident_f.t[0:L, 0:L])
            return ins
        fw.op("pe", tr, reads=[gm.b(), self.ident_f.b()], writes=[ps.b()])
        fw.op("dve", lambda e, ps=ps: e.tensor_reduce(out=gsm.t[:, 0:nt],
                                                      in_=ps.t[0:4, 0:nt * L].rearrange("p (c l) -> p c l", l=L),
                                                      axis=AX.X, op=ALU.max),
              reads=[ps.b()], writes=[gsm.b()])
        for c in range(nt):
            fw.op("dve", lambda e, c=c: e.tensor_copy(out=gsm.t[:, 9 + 2 * c:10 + 2 * c], in_=self.mcur.t[:]),
                  reads=[self.mcur.b(), gsm.b()], writes=[gsm.b()])
            fw.op("dve", lambda e, c=c: e.tensor_tensor(out=gsm.t[:, 8 + 2 * c:9 + 2 * c], in0=self.mcur.t[:],
                                                        in1=gsm.t[:, c:c + 1], op=ALU.max),
                  reads=[self.mcur.b(), gsm.b()], writes=[gsm.b()])
            fw.op("dve", lambda e, c=c: e.tensor_tensor(out=self.mcur.t[:], in0=gsm.t[:, 8 + 2 * c:9 + 2 * c],
                                                        in1=gsm.t[:, 4 + c:5 + c], op=ALU.add),
                  reads=[gsm.b(), self.mcur.b()], writes=[self.mcur.b()])
        dg = gsm.t[:, 24:24 + 8 * nt].rearrange("p (c h) -> p c h", h=4)
        fw.op("dve", lambda e: e.tensor_tensor(
            out=dg, in0=self.ident_f.t[0:4, 0:4].unsqueeze(1).to_broadcast([4, 2 * nt, 4]),
            in1=gsm.t[:, 8:8 + 2 * nt].unsqueeze(2).to_broadcast([4, 2 * nt, 4]), op=ALU.mult),
            reads=[gsm.b(), self.ident_f.b()], writes=[gsm.b()])
        yield
        ps = self.nextps()
        fw.op("pe", lambda e, ps=ps: e.matmul(ps.t[:, 0:8 * nt], lhsT=self.ones_f.t[0:4, :],
                                            rhs=gsm.t[:, 24:24 + 8 * nt], start=True, stop=True),
              reads=[gsm.b(), self.ones_f.b()], writes=[ps.b()])
        gb = self.gbcst
        fw.op("dve", lambda e, ps=ps: e.tensor_copy(out=gb.t[:, slot0:slot0 + nt, :],
                                                   in_=ps.t[:, 0:8 * nt].rearrange("p (c k) -> p c k", k=8)),
              reads=[ps.b()], writes=[gb.b()])
        wcb = self.wcb
        fw.op("dve", lambda e: e.tensor_tensor(out=wcb.t[:, slot0:slot0 + nt, :], in0=gb.t[:, slot0:slot0 + nt, 4:8],
                                               in1=gb.t[:, slot0:slot0 + nt, 0:4], op=ALU.subtract),
              reads=[gb.b(), wcb.b()], writes=[wcb.b()])
        fw.op("act", lambda e: e.activation(out=wcb.t[:, slot0:slot0 + nt, :], in_=wcb.t[:, slot0:slot0 + nt, :],
                                            func=AF.Exp),
              reads=[wcb.b()], writes=[wcb.b()])
        fw.op("dve", lambda e: e.tensor_tensor(out=gm.t[0:L, 0:nt, 12:16], in0=gm.t[0:L, 0:nt, 8:12],
                                               in1=gb.t[0:L, slot0:slot0 + nt, 0:4], op=ALU.subtract),
              reads=[gm.b(), gb.b()], writes=[gm.b()])
        fw.op("dve", lambda e: e.tensor_tensor(out=gm.t[0:L, 0:nt, 16:20], in0=gm.t[0:L, 0:nt, 4:8],
                                               in1=gb.t[0:L, slot0:slot0 + nt, 0:4], op=ALU.add),
              reads=[gm.b(), gb.b()], writes=[gm.b()])
        fw.op("act", lambda e: e.activation(out=gm.t[0:L, 0:nt, o:o + 4], in_=gm.t[0:L, 0:nt, 12:16], func=AF.Exp),
              reads=[gm.b()], writes=[gm.b()])
        fw.op("act", lambda e: e.activation(out=gm.t[0:L, 0:nt, o + 4:o + 8], in_=gm.t[0:L, 0:nt, 16:20], func=AF.Exp,
                                            scale=-1.0),
              reads=[gm.b()], writes=[gm.b()])

    def rstd(self, out_ap, in_ap, tl):
        fw = self.fw
        fw.op("act", lambda e: e.activation(out=out_ap, in_=in_ap, func=AF.Ln), reads=[tl.b()], writes=[tl.b()])
        fw.op("act", lambda e: e.activation(out=out_ap, in_=out_ap, func=AF.Exp, scale=-0.5),
              reads=[tl.b()], writes=[tl.b()])

    def gate_aps(self, s, mix, h, c, L):
        cst = self.cst
        if mix == 0:
            slot0 = 0 if s is self.MAIN else 4
            return (self.gm.t[0:L, c, s.gofs + h:s.gofs + h + 1], self.wcb.t[:, slot0 + c, h:h + 1],
                    [self.gm.b(), self.wcb.b()])
        ke, kw = (K_ES128, K_WC128) if L == 128 else (K_ES16, K_WC16)
        return cst.t[0:L, ke + h:ke + h + 1], cst.t[:, kw + h:kw + h + 1], [cst.b()]

    def make_cb(self, s, hm, c, L):
        fw = self.fw
        Cb, Cf = self.Cb, self.Cf
        _, wc, rd_g = self.gate_aps(s, hm // 4, hm % 4, c, L)
        fw.op("act", lambda e: e.activation(out=Cb.t[:, hm, :, :], in_=Cf.t[:, hm, :, :], func=AF.Copy, scale=wc),
              reads=[Cf.b(hm)] + rd_g, writes=[Cb.b(hm)])

    def mixers(self, s):
        fw = self.fw
        main = s is self.MAIN
        L = 128 if main else NMETA
        nt = s.ntile
        cst = self.cst
        Cb, Cf = self.Cb, self.Cf
        st6, mv, u = self.st6, self.mv, self.u
        for hm in range(8):
            self.make_cb(s, hm, 0, L)

        def prologue(c, g):
            t0 = c * L
            gi = self.mgi % 2
            self.mgi += 1
            P = TokSet()
            P.hms = hms = [2 * g, 4 + 2 * g, 2 * g + 1, 4 + 2 * g + 1]
            pT, pS = self.ps[4 + gi], self.ps[6 + gi]
            P.ktm, P.vp, P.stm = ktm, vp, stm = self.ktm[gi], self.vp[gi], self.stm[gi]
            pv = pT.t[:].bitcast(BF16)
            P.qa, P.qb = qa, qb = {}, {}
            ka, kb = {}, {}
            for hm in hms:
                mix, h = hm // 4, hm % 4
                qa[hm] = [self.fm(s, 2 * mix, 2 * h + j)[:, t0:t0 + L] for j in range(2)]
                ka[hm] = [self.fm(s, 2 * mix + 1, 2 * h + j)[:, t0:t0 + L] for j in range(2)]
                qb[hm] = s.fmbig.bs([16 * mix + 2 * h, 16 * mix + 2 * h + 1])
                kb[hm] = s.fmbig.bs([16 * mix + 8 + 2 * h, 16 * mix + 8 + 2 * h + 1])
            allq = [b for hm in hms for b in qb[hm]]
            allk = [b for hm in hms for b in kb[hm]]

            def trk(e):
                for k_, hm in enumerate(hms):
                    for j in range(2):
                        ins = e.transpose(out=pv[0:L, k_ * 256 + j * 128:k_ * 256 + (j + 1) * 128], in_=ka[hm][j],
                                          identity=self.ident_b.t[:])
                return ins
            fw.op("pe", trk, reads=allk + [self.ident_b.b()], writes=[pT.b()])

            def mms(e):
                for k_, hm in enumerate(hms):
                    for j in range(2):
                        ins = e.matmul(pS.t[0:L, k_ * 128:k_ * 128 + L], lhsT=ka[hm][j], rhs=qa[hm][j],
                                       start=(j == 0), stop=(j == 1))
                return ins
            fw.op("pe", mms, reads=allk + allq, writes=[pS.b()])
            fw.op("act", lambda e: e.copy(out=ktm.t[0:L, :, :], in_=pv[0:L, :].rearrange("p (k n) -> p k n", k=4)),
                  reads=[pT.b()], writes=[ktm.b()])
            fw.op("act", lambda e: e.copy(out=stm.t[0:L, :, 0:L],
                                          in_=pS.t[0:L, :].rearrange("p (k n) -> p k n", k=4)[:, :, 0:L]),
                  reads=[pS.b()], writes=[stm.b()])
            fw.op("pool", lambda e: e.tensor_tensor(
                out=stm.t[0:L, :, 0:L], in0=stm.t[0:L, :, 0:L],
                in1=self.maskT.t[0:L, 0:L].unsqueeze(1).to_broadcast([L, 4, L]), op=ALU.mult),
                reads=[stm.b(), self.maskT.b()], writes=[stm.b()])
            for k_, hm in enumerate(hms):
                mix, h = hm // 4, hm % 4
                es, wc, rd_g = self.gate_aps(s, mix, h, c, L)
                vsrc = self.vview(s, mix)
                fw.op("act", lambda e, vsrc=vsrc, es=es, h=h, k_=k_: e.activation(
                    out=vp.t[0:L, k_, 0:256], in_=vsrc[0:L, c, h * 256:(h + 1) * 256], func=AF.Copy, scale=es),
                    reads=[s.xres.b(c)] + rd_g, writes=[vp.b()])
                fw.op("act", lambda e, es=es, k_=k_: e.copy(out=vp.t[0:L, k_, 256:257], in_=es),
                      reads=rd_g + [vp.b()], writes=[vp.b()])
            return P

        def body(c, g, P):
            deferred = []
            ktm, vp, stm = P.ktm, P.vp, P.stm
            for k_, hm in enumerate(P.hms):
                mix, h = hm // 4, hm % 4
                es, wc, rd_g = self.gate_aps(s, mix, h, c, L)
                pA = self.ps[(k_ % 2) * 2]
                pB = self.ps[(k_ % 2) * 2 + 1]
                qa = P.qa[hm]

                def mmn(e, pA=pA, qa=qa, hm=hm, k_=k_):
                    for j in range(2):
                        e.matmul(pA.t[0:L, 128:385], lhsT=qa[j], rhs=Cb.t[:, hm, j, :], start=(j == 0), stop=False)
                    return e.matmul(pA.t[0:L, 128:385], lhsT=stm.t[0:L, k_, 0:L], rhs=vp.t[0:L, k_, :],
                                    start=False, stop=True)
                fw.op("pe", mmn, reads=P.qb[hm] + [Cb.b(hm), stm.b(), vp.b()], writes=[pA.b()])

                def mmp(e, pB=pB, pA=pA, k_=k_):
                    for j in range(2):
                        e.matmul(pB.t[:, j * 256:(j + 1) * 256], lhsT=ktm.t[0:L, k_, j * 128:(j + 1) * 128],
                                 rhs=vp.t[0:L, k_, 0:256], start=True, stop=True)
                    for j in range(2):
                        ins = e.matmul(pA.t[:, 400 + j:401 + j], lhsT=ktm.t[0:L, k_, j * 128:(j + 1) * 128],
                                       rhs=vp.t[0:L, k_, 256:257], start=True, stop=True)
                    return ins
                fw.op("pe", mmp, reads=[ktm.b(), vp.b()], writes=[pB.b(), pA.b()])
                fw.op("dve", lambda e, pA=pA, hm=hm: e.bn_stats(out=st6.t[0:L, hm, :], in_=pA.t[0:L, 128:384]),
                      reads=[pA.b()], writes=[st6.b(hm)])
                fw.op("dve", lambda e, hm=hm: e.bn_aggr(out=mv.t[0:L, hm, :], in_=st6.t[0:L, hm, :]),
                      reads=[st6.b(hm)], writes=[mv.b(hm)])
                if mix == 0:
                    fw.op("dve", lambda e, pA=pA, h=h: e.tensor_copy(out=self.den.t[0:L, h:h + 1], in_=pA.t[0:L, 384:385]),
                          reads=[pA.b()], writes=[self.den.b(h)])
                G = s.G[mix]
                fw.op("dve", lambda e, pA=pA, hm=hm, G=G, h=h: e.scalar_tensor_tensor(
                    out=u.t[0:L, hm, :], in0=pA.t[0:L, 128:384], scalar=mv.t[0:L, hm, 0:1],
                    in1=G.t[0:L, c, h * 256:(h + 1) * 256], op0=ALU.subtract, op1=ALU.mult),
                    reads=[pA.b(), mv.b(hm), G.b((c, h // 2))], writes=[u.b(hm)])
                fw.op("dve", lambda e, pA=pA, hm=hm, wc=wc: e.scalar_tensor_tensor(
                    out=Cf.t[:, hm, :, 256], in0=Cf.t[:, hm, :, 256], scalar=wc, in1=pA.t[:, 400:402],
                    op0=ALU.mult, op1=ALU.add),
                    reads=[pA.b(), Cf.b(hm)] + rd_g, writes=[Cf.b(hm)])
                fw.op("dve", lambda e, pB=pB, hm=hm, wc=wc: e.scalar_tensor_tensor(
                    out=Cf.t[:, hm, :, 0:256], in0=Cf.t[:, hm, :, 0:256], scalar=wc,
                    in1=pB.t[:, :].rearrange("p (j n) -> p j n", j=2), op0=ALU.mult, op1=ALU.add),
                    reads=[pB.b(), Cf.b(hm)] + rd_g, writes=[Cf.b(hm)])
                if c + 1 < nt:
                    deferred.append(lambda hm=hm: self.make_cb(s, hm, c + 1, L))
            return deferred

        def pm(c, g):
            lowb = self.gm.t[0:L, c, s.gofs + 4:s.gofs + 8]
            keps = K_EPS128 if L == 128 else K_EPS16
            self.post_merge(s, L, 0, c, lowb, cst.t[0:L, keps:keps + 4], [self.gm.b(), cst.b()], 2 * g, 2 * g + 2)

        steps = [(c, g) for c in range(nt) for g in range(2)]
        P_next = prologue(*steps[0])
        pending = []
        prev = None
        for idx, (c, g) in enumerate(steps):
            P_cur = P_next
            if idx + 1 < len(steps):
                P_next = prologue(*steps[idx + 1])
            for f in pending:
                f()
            pending = body(c, g, P_cur)
            if prev is not None:
                pm(*prev)
            prev = (c, g)
        pm(*prev)
        for f in pending:
            f()

    def post_merge(self, s, L, p0, c, lowb, epsr, rd, h0=0, h1=4):
        fw = self.fw
        sm = self.sm
        R = slice(p0, p0 + L)
        H = slice(h0, h1)
        nh = h1 - h0
        fw.op("dve", lambda e: e.scalar_tensor_tensor(out=sm.t[R, 4 + h0:4 + h1], in0=self.den.t[R, H], scalar=-1.0,
                                                      in1=self.den.t[R, H], op0=ALU.mult, op1=ALU.max),
              reads=self.den.bs(range(h0, h1)) + [sm.b()], writes=[sm.b()])
        fw.op("dve", lambda e: e.tensor_tensor(out=sm.t[R, H], in0=sm.t[R, 4 + h0:4 + h1], in1=lowb[:, H], op=ALU.max),
              reads=[sm.b()] + rd, writes=[sm.b()])
        fw.op("dve", lambda e: e.scalar_tensor_tensor(out=sm.t[R, 8 + h0:8 + h1], in0=sm.t[R, H], scalar=LN_EPS,
                                                      in1=sm.t[R, H], op0=ALU.mult, op1=ALU.mult),
              reads=[sm.b()], writes=[sm.b()])
        fw.op("dve", lambda e: e.tensor_tensor(out=sm.t[R, 16 + h0:16 + h1], in0=sm.t[R, 8 + h0:8 + h1],
                                               in1=self.mv.t[R, H, 1], op=ALU.add),
              reads=[sm.b()] + self.mv.bs(range(h0, h1)), writes=[sm.b()])
        fw.op("dve", lambda e: e.tensor_tensor(out=sm.t[R, 20 + h0:20 + h1], in0=epsr[:, H],
                                               in1=self.mv.t[R, 4 + h0:4 + h1, 1], op=ALU.add),
              reads=[sm.b()] + rd + self.mv.bs(range(4 + h0, 4 + h1)), writes=[sm.b()])
        vin = sm.t[R, 16:24].rearrange("p (m h) -> p m h", m=2)[:, :, H]
        vout = sm.t[R, 24:32].rearrange("p (m h) -> p m h", m=2)[:, :, H]
        self.rstd(vout, vin, sm)
        ta = self.tmpa
        u = self.u
        for h in range(h0, h1):
            fw.op("pool", lambda e, h=h: e.tensor_scalar(out=ta.t[R, :], in0=u.t[R, h, :], scalar1=sm.t[R, 24 + h:25 + h],
                                                        scalar2=0.0, op0=ALU.mult, op1=ALU.add),
                  reads=[u.b(h), sm.b()], writes=[ta.b()])
            fw.op("pool", lambda e, h=h: e.tensor_scalar(out=u.t[R, 4 + h, :], in0=u.t[R, 4 + h, :],
                                                        scalar1=sm.t[R, 28 + h:29 + h], scalar2=0.0,
                                                        op0=ALU.mult, op1=ALU.add),
                  reads=[u.b(4 + h), sm.b()], writes=[u.b(4 + h)])
            fw.op("pool", lambda e, h=h: e.tensor_tensor(out=s.merged.t[R, c, h * 256:(h + 1) * 256], in0=ta.t[R, :],
                                                        in1=u.t[R, 4 + h, :], op=ALU.add),
                  reads=[u.b(4 + h), ta.b()], writes=[s.merged.b(c)])

    def store_prompt_state(self):
        d = self.dram
        Cf = self.Cf
        for h in range(4):
            self.store("sp", Cf, h, d["pC"][h].rearrange("(j p) e -> p j e", p=128), Cf.t[:, h, :, 0:256])
            self.store("sp", Cf, 4 + h, d["pS"][h].rearrange("(j p) e -> p j e", p=128), Cf.t[:, 4 + h, :, 0:256])
        tok = self.fw.dma("sp", lambda e: e.dma_start(out=d["pn"].rearrange("h (j p) -> p h j", p=128),
                                                      in_=Cf.t[:, 0:4, :, 256], allow_slow_non_contiguous=True),
                          Cf.b("o"), reads=Cf.bs(range(4)))
        self.fw.out_toks.append(tok)
        self.store("sp", self.mcur, 0, d["pm"], self.mcur.t[:])

    def sample_mixers(self):
        fw = self.fw
        s = self.MINI
        d = self.dram
        cst = self.cst
        R = slice(SP0, SP0 + NS)
        ssm = self.s_sm
        gcol = s.gcol
        def tm_transposes(w):
            ps = self.nextps()
            pv = ps.t[:].bitcast(BF16)

            def tr(e, w=w, pv=pv):
                for k in range(8):
                    ins = e.transpose(out=pv[0:64, k * 128:(k + 1) * 128], in_=self.fm(s, w, k)[:, 0:64],
                                      identity=self.ident_b.t[:])
                return ins
            fw.op("pe", tr, reads=s.fmbig.bs(range(8 * w, 8 * w + 8)) + [self.ident_b.b()], writes=[ps.b()])
            fw.op("act", lambda e, w=w, pv=pv: e.copy(out=self.s_tm[w].t[:, :], in_=pv[0:64, :]),
                  reads=[ps.b()], writes=[self.s_tm[w].b()])
        tm_transposes(0)
        tm_transposes(1)
        self.load("sp", self.s_n, 0, self.s_n.t[R, :], d["sn"])
        self.load("sp", ssm, "m", ssm.t[R, 0:4], d["sm"])
        one = cst.t[R, K_ONE:K_ONE + 1]
        sb_ = [ssm.b(), ssm.b("m")]
        fw.op("act", lambda e: e.activation(out=ssm.t[R, 28:32], in_=gcol.t[R, 0, 4:8], func=AF.Exp, scale=-1.0),
              reads=[gcol.b()] + sb_, writes=[ssm.b()])
        fw.op("act", lambda e: e.activation(out=ssm.t[R, 4:8], in_=ssm.t[R, 28:32], func=AF.Ln, bias=one, scale=1.0),
              reads=[ssm.b(), cst.b()], writes=[ssm.b()])
        fw.op("dve", lambda e: e.tensor_tensor(out=ssm.t[R, 8:12], in0=ssm.t[R, 0:4], in1=ssm.t[R, 4:8], op=ALU.subtract),
              reads=sb_, writes=[ssm.b()])
        fw.op("dve", lambda e: e.tensor_tensor(out=ssm.t[R, 12:16], in0=ssm.t[R, 8:12], in1=gcol.t[R, 0, 0:4], op=ALU.max),
              reads=[ssm.b(), gcol.b()], writes=[ssm.b()])
        fw.op("dve", lambda e: e.tensor_tensor(out=ssm.t[R, 16:20], in0=gcol.t[R, 0, 0:4], in1=ssm.t[R, 12:16],
                                               op=ALU.subtract),
              reads=[ssm.b(), gcol.b()], writes=[ssm.b()])
        fw.op("dve", lambda e: e.tensor_tensor(out=ssm.t[R, 20:24], in0=ssm.t[R, 8:12], in1=ssm.t[R, 12:16],
                                               op=ALU.subtract),
              reads=[ssm.b()], writes=[ssm.b()])
        fw.op("act", lambda e: e.activation(out=ssm.t[R, 16:24], in_=ssm.t[R, 16:24], func=AF.Exp),
              reads=[ssm.b()], writes=[ssm.b()])
        fw.op("act", lambda e: e.activation(out=ssm.t[R, 24:28], in_=ssm.t[R, 12:16], func=AF.Exp, scale=-1.0),
              reads=[ssm.b()], writes=[ssm.b()])
        self.store("sp", ssm, 0, d["om"], ssm.t[R, 12:16])
        big = self.u.t[0:64, :, :].bitcast(F32).rearrange("p a n -> p (a n)")
        bigb = self.u.bs(range(8))
        for (a_, b_, col, bt) in ((self.s_tm[0], self.s_tm[1], 32, None), (self.s_tm[0], self.s_n, 36, None),
                                  (self.s_tm[2], self.s_tm[3], 40, None)):
            if col == 40:
                tm_transposes(2)
                tm_transposes(3)
            fw.op("dve", lambda e, a_=a_, b_=b_: e.tensor_tensor(out=big[R, :], in0=a_.t[R, :], in1=b_.t[R, :], op=ALU.mult),
                  reads=[a_.b(), b_.b()], writes=bigb)
            fw.op("dve", lambda e, col=col: e.tensor_reduce(out=ssm.t[R, col:col + 4],
                                                           in_=big[R, :].rearrange("p (h n) -> p h n", h=4),
                                                           axis=AX.X, op=ALU.add),
                  reads=bigb + [ssm.b()], writes=[ssm.b()])
        fw.op("dve", lambda e: e.tensor_tensor(out=ssm.t[R, 44:48], in0=ssm.t[R, 32:36], in1=ssm.t[R, 16:20], op=ALU.mult),
              reads=[ssm.b()], writes=[ssm.b()])
        fw.op("dve", lambda e: e.tensor_tensor(out=ssm.t[R, 48:52], in0=ssm.t[R, 36:40], in1=ssm.t[R, 20:24], op=ALU.mult),
              reads=[ssm.b()], writes=[ssm.b()])
        fw.op("dve", lambda e: e.tensor_tensor(out=self.den.t[R, :], in0=ssm.t[R, 48:52], in1=ssm.t[R, 44:48], op=ALU.add),
              reads=[ssm.b()], writes=self.den.bs(range(4)))
        kp = self.s_tm[1]
        for h in range(4):
            hs = slice(h * 256, (h + 1) * 256)
            fw.op("act", lambda e, h=h, hs=hs: e.activation(out=kp.t[R, hs], in_=self.s_tm[1].t[R, hs], func=AF.Copy,
                                                           scale=ssm.t[R, 16 + h:17 + h]),
                  reads=[self.s_tm[1].b(), ssm.b()], writes=[kp.b()])
            fw.op("dve", lambda e, h=h, hs=hs: e.scalar_tensor_tensor(
                out=self.s_n.t[R, hs], in0=self.s_n.t[R, hs], scalar=ssm.t[R, 20 + h:21 + h], in1=kp.t[R, hs],
                op0=ALU.mult, op1=ALU.add),
                reads=[self.s_n.b(), ssm.b(), kp.b()], writes=[self.s_n.b()])
        self.store("sp", self.s_n, 0, d["on"], self.s_n.t[R, :])
        dg = self.s_dg
        idr = self.ident_f.t[R, SP0:SP0 + NS]
        fw.op("dve", lambda e: e.tensor_tensor(
            out=dg.t[R, :, 0:4], in0=idr.unsqueeze(2).to_broadcast([NS, NS, 4]),
            in1=ssm.t[R, 20:24].unsqueeze(1).to_broadcast([NS, NS, 4]), op=ALU.mult),
            reads=[ssm.b(), self.ident_f.b()], writes=[dg.b()])
        fw.op("dve", lambda e: e.tensor_tensor(
            out=dg.t[R, :, 4:8], in0=idr.unsqueeze(2).to_broadcast([NS, NS, 4]),
            in1=cst.t[R, K_G1:K_G1 + 4].unsqueeze(1).to_broadcast([NS, NS, 4]), op=ALU.mult),
            reads=[cst.b(), self.ident_f.b(), dg.b()], writes=[dg.b()])
        ps = self.nextps()
        fw.op("pe", lambda e, ps=ps: e.matmul(ps.t[:, 0:128], lhsT=self.ones_f.t[R, :],
                                            rhs=dg.t[R, :, :].rearrange("p a b -> p (a b)"), start=True, stop=True),
              reads=[dg.b(), self.ones_f.b()], writes=[ps.b()])
        fw.op("dve", lambda e, ps=ps: e.tensor_copy(out=self.s_wb.t[:, :], in_=ps.t[:, 0:128]),
              reads=[ps.b()], writes=[self.s_wb.b()])
        ci = 0
        for mix in range(2):
            src, dst = (d["sC"], d["oC"]) if mix == 0 else (d["sS"], d["oS"])
            kpt = kp if mix == 0 else self.s_tm[3]
            vsrc = self.vview(s, mix)
            acc = [self.ps[4 + h] for h in range(4)]
            for b in range(NS):
                qm, vm = (self.qm[0], self.vmk[0]) if b % 2 == 0 else (self.qm2, self.vm2)
                first2 = (mix == 0 and b == 1)
                xw = [self.wbuf[-1].b(0)] if first2 else []
                fw.op("dve", lambda e, qm=qm, b=b, mix=mix: e.tensor_tensor(
                    out=qm.t[:, :, :], in0=s.fmbig.t[:, 16 * mix:16 * mix + 8, :],
                    in1=self.identrow.t[:, b, :].unsqueeze(1).to_broadcast([128, 8, 64]), op=ALU.mult),
                    reads=s.fmbig.bs(range(16 * mix, 16 * mix + 8)) + [self.identrow.b()], writes=[qm.b()], extra=xw)
                fw.op("act", lambda e, vm=vm, b=b, vsrc=vsrc: e.activation(
                    out=vm.t[R, :], in_=vsrc[R, 0, :], func=AF.Copy, scale=self.ident_f.t[R, SP0 + b:SP0 + b + 1]),
                    reads=[s.xres.b(0), self.ident_f.b()], writes=[vm.b()], extra=xw)
                for h in range(4):
                    cin = self.cin[ci % len(self.cin)]
                    cbf = self.cbf[ci % 2]
                    first = ci < len(self.cin) and cin.name in self.cin_owner
                    ci += 1
                    self.fw.dma("sp", lambda e, cin=cin, b=b, h=h, src=src: e.dma_start(
                        out=cin.t[:, :, :], in_=src[b, h].rearrange("(j p) e -> p j e", p=128)),
                        cin.b(0), writes=[cin.b(0)], extra=[self.cin_owner[cin.name].b(0)] if first else ())
                    fw.op("act", lambda e, cin=cin, cbf=cbf: e.copy(out=cbf.t[:, :, :], in_=cin.t[:, :, :]),
                          reads=[cin.b()], writes=[cbf.b()])

                    def mmq(e, qm=qm, cbf=cbf, h=h, b=b):
                        for j in range(2):
                            ins = e.matmul(acc[h].t[0:64, 0:256], lhsT=qm.t[:, 2 * h + j, :], rhs=cbf.t[:, j, :],
                                           start=(b == 0 and j == 0), stop=(b == NS - 1 and j == 1))
                        return ins
                    fw.op("pe", mmq, reads=[qm.b(), cbf.b()], writes=[acc[h].b()])
                    pr = self.ps[ci % 4]

                    def mmr(e, pr=pr, kpt=kpt, vm=vm, h=h):
                        for j in range(2):
                            ins = e.matmul(pr.t[:, j * 256:(j + 1) * 256],
                                           lhsT=kpt.t[R, h * 256 + j * 128:h * 256 + (j + 1) * 128],
                                           rhs=vm.t[R, h * 256:(h + 1) * 256], start=True, stop=True)
                        return ins
                    fw.op("pe", mmr, reads=[kpt.b(), vm.b()], writes=[pr.b()])
                    wcol = b * 8 + mix * 4 + h
                    fw.op("dve", lambda e, cin=cin, pr=pr, wcol=wcol: e.scalar_tensor_tensor(
                        out=cin.t[:, :, :], in0=cin.t[:, :, :], scalar=self.s_wb.t[:, wcol:wcol + 1],
                        in1=pr.t[:, :].rearrange("p (j n) -> p j n", j=2), op0=ALU.mult, op1=ALU.add),
                        reads=[cin.b(), pr.b(), self.s_wb.b()], writes=[cin.b()])
                    self.store("pool", cin, 0, dst[b, h].rearrange("(j p) e -> p j e", p=128), cin.t[:, :, :],
                               semkey="st")
            for h in range(4):
                hm = 4 * mix + h
                ta = self.tmpa
                scol = (44 if mix == 0 else 40) + h
                fw.op("act", lambda e, h=h, scol=scol, vsrc=vsrc: e.activation(
                    out=ta.t[R, :], in_=vsrc[R, 0, h * 256:(h + 1) * 256], func=AF.Copy,
                    scale=ssm.t[R, scol:scol + 1]),
                    reads=[s.xres.b(0), ssm.b()], writes=[ta.b()])
                wsc = ssm.t[R, 20 + h:21 + h] if mix == 0 else cst.t[R, K_G1 + h:K_G1 + h + 1]
                num = self.s_num
                fw.op("dve", lambda e, h=h, wsc=wsc: e.scalar_tensor_tensor(
                    out=num.t[R, :], in0=acc[h].t[R, 0:256], scalar=wsc, in1=ta.t[R, :], op0=ALU.mult, op1=ALU.add),
                    reads=[acc[h].b(), ta.b(), ssm.b(), cst.b()], writes=[num.b()])
                fw.op("dve", lambda e, hm=hm: e.bn_stats(out=self.st6.t[R, hm, :], in_=num.t[R, :]),
                      reads=[num.b()], writes=[self.st6.b(hm)])
                fw.op("dve", lambda e, hm=hm: e.bn_aggr(out=self.mv.t[R, hm, :], in_=self.st6.t[R, hm, :]),
                      reads=[self.st6.b(hm)], writes=[self.mv.b(hm)])
                G = s.G[mix]
                fw.op("dve", lambda e, hm=hm, h=h, G=G: e.scalar_tensor_tensor(
                    out=self.u.t[R, hm, :], in0=num.t[R, :], scalar=self.mv.t[R, hm, 0:1],
                    in1=G.t[R, 0, h * 256:(h + 1) * 256], op0=ALU.subtract, op1=ALU.mult),
                    reads=[num.b(), self.mv.b(hm), G.b((0, h // 2))], writes=[self.u.b(hm)])
        for tl in self.borrowed2:
            wb = self.wbuf[-1].b(0)
            for k_, tok in tl.b(0).r.items():
                wb.r["smp_%s_%s" % (tl.name, k_)] = tok
            if tl.b(0).w is not None:
                wb.r["smpw_" + tl.name] = tl.b(0).w
        for tl in self.cin:
            if tl.name in self.cin_owner:
                wb = self.cin_owner[tl.name].b(0)
                for k_, tok in tl.b(0).r.items():
                    wb.r["smp_%s_%s" % (tl.name, k_)] = tok
                if tl.b(0).w is not None:
                    wb.r["smpw_" + tl.name] = tl.b(0).w
        self.post_merge(s, NS, SP0, 0, ssm.t[R, 24:28], cst.t[R, K_EPS1:K_EPS1 + 4], [ssm.b(), cst.b()])

    def ln_evac(self, s, i, pss):
        fw = self.fw
        tsz = s.tsz
        xr = s.xres
        z = xr.t[:, i, :]
        for hf in range(2):
            cs = slice(hf * 512, (hf + 1) * 512)
            fw.op("dve", lambda e, hf=hf, cs=cs: e.scalar_tensor_tensor(
                out=z[:, cs], in0=z[:, cs], scalar=ALPHA, in1=pss[hf].t[0:tsz, :], op0=ALU.mult, op1=ALU.add),
                reads=[pss[hf].b(), xr.b(i)], writes=[xr.b(i)])

    def ln_stats(self, s, i):
        fw = self.fw
        tsz = s.tsz
        xr = s.xres
        z = xr.t[:, i, :]
        lst, lmv = self.lst, self.lmv
        for hf in range(2):
            cs = slice(hf * 512, (hf + 1) * 512)
            fw.op("dve", lambda e, hf=hf, cs=cs: e.bn_stats(out=lst.t[0:tsz, i, hf, :], in_=z[:, cs]),
                  reads=[xr.b(i)], writes=[lst.b(i)])
        fw.op("dve", lambda e: e.bn_aggr(out=lmv.t[0:tsz, i, 0:2],
                                         in_=lst.t[0:tsz, i, :, :].rearrange("p a b -> p (a b)")),
              reads=[lst.b(i)], writes=[lmv.b(i)])
        fw.op("dve", lambda e: e.tensor_scalar(out=lmv.t[0:tsz, i, 2:3], in0=lmv.t[0:tsz, i, 1:2], scalar1=LN_EPS,
                                               scalar2=None, op0=ALU.add),
              reads=[lmv.b(i)], writes=[lmv.b(i)])
        o_, i_ = lmv.t[0:tsz, i, 3:4], lmv.t[0:tsz, i, 2:3]
        fw.op("act", lambda e: e.activation(out=o_, in_=i_, func=AF.Ln), reads=[lmv.b(i)], writes=[lmv.b(i)])
        fw.op("act", lambda e: e.activation(out=o_, in_=o_, func=AF.Exp, scale=-0.5), reads=[lmv.b(i)], writes=[lmv.b(i)])

    def ln_norm(self, s, i, g, b):
        fw = self.fw
        tsz = s.tsz
        xr = s.xres
        z = xr.t[:, i, :]
        lmv = self.lmv
        fw.op("dve", lambda e: e.scalar_tensor_tensor(out=z, in0=z, scalar=lmv.t[0:tsz, i, 0:1], in1=g.t[0:tsz, :],
                                                      op0=ALU.subtract, op1=ALU.mult),
              reads=[xr.b(i), lmv.b(i), g.b()], writes=[xr.b(i)])
        fw.op("dve", lambda e: e.scalar_tensor_tensor(out=z, in0=z, scalar=lmv.t[0:tsz, i, 3:4], in1=b.t[0:tsz, :],
                                                      op0=ALU.mult, op1=ALU.add),
              reads=[xr.b(i), lmv.b(i), b.b()], writes=[xr.b(i)])
        return z

    def ln_tile(self, s, i, g, b):
        self.ln_stats(s, i)
        return self.ln_norm(s, i, g, b)

    def outproj_ln1(self, sets):
        fw = self.fw
        d = self.dram
        if DEBUG and self.MINI in sets:
            s = self.MINI
            big = self.rot.t[0:64, :, :].rearrange("p a n -> p (a n)")
            fw.op("act", lambda e, s=s: e.copy(out=big, in_=s.merged.t[:, 0, :]), reads=[s.merged.b(0)], writes=self.rot.bs(range(4)))
            tok = fw.dma("sp", lambda e: e.dma_start(out=d["dbg_mrg"], in_=big), self.rot.b("o"), reads=self.rot.bs(range(4)))
            fw.out_toks.append(tok)
        mTb = lambda s: s.fmbig.bs(range(8))
        for s in sets:
            tsz = s.tsz
            for i in range(s.ntile):
                self.transpose_in(s, [s.merged.b(i)], lambda k, s=s, i=i: s.merged.t[:, i, k * 128:(k + 1) * 128],
                                  s.fmbig.t[:, 0:8, i * tsz:(i + 1) * tsz], mTb(s))
        wts = [self.wnext("out", 0), self.wnext("out", 512, hold=1)]
        for s in sets:
            tsz = s.tsz
            allb = s.xres.bs(range(s.ntile))
            if s is self.MAIN:
                r0 = self.pass_idx * 512
                self.fw.dma("sp", lambda e, s=s, r0=r0: e.dma_start(
                    out=s.xres.t[:, :, :], in_=d["xp"][r0:r0 + 512, :].rearrange("(i p) n -> p i n", p=128)),
                    s.xres.b("ld"), writes=allb)
            else:
                self.fw.dma("sp", lambda e, s=s: e.dma_start(out=s.xres.t[0:NMETA, 0, :], in_=d["meta"]),
                            s.xres.b("ld"), writes=allb)
                self.fw.dma("sp", lambda e, s=s: e.dma_start(out=s.xres.t[SP0:SP0 + NS, 0, :], in_=d["xs"]),
                            s.xres.b("ld"), writes=allb, nowait=True)
            pss = []
            for i in range(s.ntile):
                pp = []
                for hf in range(2):
                    ps = self.nextps()
                    pp.append(ps)
                    w3 = wts[hf].t[:, 0:8 * 512].rearrange("p (k n) -> p k n", k=8)

                    def mm(e, ps=ps, w3=w3, i=i, s=s):
                        for k in range(8):
                            ins = e.matmul(ps.t[0:s.tsz, :], lhsT=s.fmbig.t[:, k, i * s.tsz:(i + 1) * s.tsz],
                                           rhs=w3[:, k, :], start=(k == 0), stop=(k == 7))
                        return ins
                    fw.op("pe", mm, reads=mTb(s) + [wts[hf].b()], writes=[ps.b()])
                pss.append(pp)
            for i in range(s.ntile):
                self.ln_evac(s, i, pss[i])
            self.ln_stats(s, 0)
            for i in range(s.ntile):
                if i + 1 < s.ntile:
                    self.ln_stats(s, i + 1)
                z = self.ln_norm(s, i, self.lnp[0], self.lnp[1])
                fw.op("act", lambda e, z=z, s=s, i=i: e.copy(out=s.merged.t[:, i, :], in_=z),
                      reads=[s.xres.b(i)], writes=[s.merged.b(i)])
                if DEBUG and s is self.MINI:
                    self.store("sp", s.xres, i, d["dbg_x1"], s.xres.t[:, 0, :])
                self.transpose_in(s, [s.merged.b(i)], lambda k, s=s, i=i: s.merged.t[:, i, k * 128:(k + 1) * 128],
                                  s.fmbig.t[:, 0:8, i * tsz:(i + 1) * tsz], mTb(s))

    def ffn(self, sets, hook=None):
        fw = self.fw
        for c0 in range(0, DFF, 512):
            wg = self.wnext("gate", c0)
            wu = self.wnext("up", c0, hold=1)
            nch = min(4, (DFF - c0) // 128)
            g3 = wg.t[:, 0:8 * 512].rearrange("p (k n) -> p k n", k=8)
            u3 = wu.t[:, 0:8 * 512].rearrange("p (k n) -> p k n", k=8)
            for s in sets:
                T = s.T
                x1r = s.fmbig.bs(range(8))

                def grp(ps, w3, wt, m, s=s, T=T):
                    def mm(e):
                        for k in range(8):
                            ins = e.matmul(ps.t[:, 0:T], lhsT=w3[:, k, m * 128:(m + 1) * 128], rhs=s.fmbig.t[:, k, :],
                                           start=(k == 0), stop=(k == 7))
                        return ins
                    fw.op("pe", mm, reads=x1r + [wt.b()], writes=[ps.b()])
                pgs = []
                for m in range(nch):
                    pg = self.nextps()
                    pgs.append(pg)
                    grp(pg, g3, wg, m)
                tbs = []
                for m in range(nch):
                    fc = c0 // 128 + m
                    pu = self.nextps()
                    grp(pu, u3, wu, m)
                    pg = pgs[m]
                    tb = self.tmpb[self.tbi % 2]
                    self.tbi += 1
                    fw.op("act", lambda e, tb=tb, pg=pg, T=T: e.activation(out=tb.t[:, 0:T], in_=pg.t[:, 0:T], func=AF.Silu),
                          reads=[pg.b()], writes=[tb.b()])
                    fw.op("dve", lambda e, tb=tb, pu=pu, T=T, s=s, fc=fc: e.tensor_tensor(
                        out=s.fmbig.t[:, 8 + fc, :], in0=tb.t[:, 0:T], in1=pu.t[:, 0:T], op=ALU.mult),
                        reads=[tb.b(), pu.b()], writes=[s.fmbig.b(8 + fc)])
        small = [(s, i) for s in sets if s is not self.MAIN for i in range(s.ntile)]
        if small:
            self.mini_down = small
        if hook is not None:
            hook()
        return self.down([(self.MAIN, i) for i in range(4)], defer=hook is not None)

    def down(self, tiles, defer=False, banks=None, blocks=None):
        fw = self.fw
        d = self.dram
        if banks is None:
            banks = [[self.ps[2 * n], self.ps[2 * n + 1]] for n in range(len(tiles))]
        todo = list(range(0, DFF, 512)) if blocks is None else ([] if blocks == "finish" else [blocks])
        for r0 in todo:
            wt = self.wnext("down", r0)
            nch = min(4, (DFF - r0) // 128)
            w4 = wt.t[:, 0:4 * 1024].rearrange("p (c n) -> p c n", c=4)
            for n, (s, i) in enumerate(tiles):
                tsz = s.tsz
                for hf in range(2):
                    ps = banks[n][hf]

                    def mm(e, ps=ps, i=i, hf=hf, s=s, nch=nch, r0=r0, tsz=tsz, w4=w4):
                        for m in range(nch):
                            fc = r0 // 128 + m
                            ins = e.matmul(ps.t[0:tsz, :], lhsT=s.fmbig.t[:, 8 + fc, i * tsz:(i + 1) * tsz],
                                           rhs=w4[:, m, hf * 512:(hf + 1) * 512],
                                           start=(fc == 0), stop=(fc == NFF - 1))
                        return ins
                    fw.op("pe", mm, reads=s.fmbig.bs(range(8 + r0 // 128, 8 + r0 // 128 + nch)) + [wt.b()],
                          writes=[ps.b()])
        if blocks is not None and blocks != "finish":
            return []
        for n, (s, i) in enumerate(tiles):
            self.ln_evac(s, i, banks[n])

        def finish(s, i, r0, pre=False):
            def f():
                if pre:
                    self.ln_norm(s, i, self.lnp[2], self.lnp[3])
                else:
                    self.ln_tile(s, i, self.lnp[2], self.lnp[3])
                if s is self.MAIN:
                    self.store("sp", s.xres, i, d["yp"][r0:r0 + 128, :], s.xres.t[:, i, :])
                else:
                    self.store("sp", s.xres, i, d["ys"], s.xres.t[SP0:SP0 + NS, 0, :])
            return f
        if defer:
            return [finish(s, i, self.pass_idx * 512 + i * 128) for (s, i) in tiles]
        fins = [finish(s, i, self.pass_idx * 512 + i * 128, pre=True) for (s, i) in tiles]
        self.ln_stats(*tiles[0])
        for n, f in enumerate(fins):
            if n + 1 < len(tiles):
                self.ln_stats(*tiles[n + 1])
            f()
        return []

    def program(self):
        NP = 4
        fw = self.fw
        self.wspecs = self.wspec_list(NP)
        self.wi = 0
        self.wloaded = 0
        self.wscr = self.nc.dram_tensor("wscr", [40, 128, 9 * 512], BF16).ap()
        self.scr_slot = {}
        self.scr_buf = {}
        self.no_prefetch_beyond = 19
        self.gw_loaded = False
        self.load_consts()
        fw.op("dve", lambda e: e.memset(self.MINI.merged.t[:], 0.0), writes=[self.MINI.merged.b(0)])
        fw.op("dve", lambda e: e.memset(self.MINI.xres.t[:], 0.0), writes=self.MINI.xres.bs(range(1)))
        for p in range(NP):
            self.pass_idx = p
            sets = [self.MINI, self.MAIN] if p == 0 else [self.MAIN]
            if p == 0:
                self.load_x_mini()
                self.load_x_main(0)
                gw = self.gw
                self.load("pool", gw, 0, gw.t[:, 0:8, :],
                          self.dram["w_in"][:, C_MI:C_MI + 8].rearrange("(k p) n -> p k n", p=128))
                self.load("pool", gw, 0, gw.t[0:1, 8, :], self.dram["b_in"][:, C_MI:C_MI + 8], nowait=True)
                self.gw_loaded = True
                while self.wloaded < 2:
                    self.wload(self.wloaded)
                    self.wloaded += 1
                self.load_consts_late()
            self.projection(sets)
            if p == 0:
                self.mixers(self.MINI)
                self.sample_mixers()
                self.no_prefetch_beyond = None
                while self.wloaded < min(len(self.wspecs), self.wi + self.NWB):
                    self.wload(self.wloaded)
                    self.wloaded += 1
            self.mixers(self.MAIN)
            if p == NP - 1:
                self.store_prompt_state()
            self.outproj_ln1(sets)
            self.pending_ln = self.ffn(sets, (lambda p=p: self.load_x_main(p + 1)) if p + 1 < NP else None)
        assert self.wi == len(self.wspecs)


def _consts():
    f32 = np.float32
    ident = np.eye(128, dtype=f32)
    maskT = np.triu(np.ones((128, 128), dtype=f32))
    lg = np.log1p(-np.exp2(-5.0 - np.arange(4, dtype=np.float64)))
    cst = np.zeros((128, NCST), dtype=np.float64)
    s128 = np.arange(128)[:, None]
    cst[:, K_ES128:K_ES128 + 4] = np.exp((127 - s128) * lg[None, :])
    cst[:, K_ES16:K_ES16 + 4] = np.exp((15 - s128) * lg[None, :])
    cst[:, K_WC128:K_WC128 + 4] = np.exp(128 * lg)[None, :]
    cst[:, K_WC16:K_WC16 + 4] = np.exp(16 * lg)[None, :]
    cst[:, K_EPS128:K_EPS128 + 4] = LN_EPS * np.exp(2 * (127 - s128) * lg[None, :])
    cst[:, K_EPS16:K_EPS16 + 4] = LN_EPS * np.exp(2 * (15 - s128) * lg[None, :])
    cst[:, K_G1:K_G1 + 4] = np.exp(lg)[None, :]
    cst[:, K_EPS1:K_EPS1 + 4] = LN_EPS
    cst[:, K_ONE] = 1.0
    cst = cst.astype(f32)
    inv = 10000.0 ** (-np.arange(0, 256, 2, dtype=np.float64) / 256.0)

    def tables(pos):
        ang = np.asarray(pos, dtype=np.float64)[:, None] * inv[None, :]
        return np.ascontiguousarray(np.cos(ang).T.astype(f32)), np.ascontiguousarray(np.sin(ang).T.astype(f32))
    ropeC, ropeS = tables(np.arange(NMETA, NMETA + SEQ))
    pm = np.zeros(64)
    pm[0:NMETA] = np.arange(NMETA)
    pm[SP0:SP0 + NS] = PAST
    ropeCm, ropeSm = tables(pm)
    identrow = np.zeros((128, 16, 64), dtype=f32)
    for b in range(16):
        identrow[:, b, SP0 + b] = 1.0
    return dict(ident=ident, maskT=maskT, cst=cst, ropeC=ropeC, ropeS=ropeS, ropeCm=ropeCm, ropeSm=ropeSm,
                identrow=identrow.reshape(128, 1024))


_NC_CACHE = {}


def kernel(x_prompt, x_sample, state_mlstm_C, state_mlstm_n, state_mlstm_m, state_ret_S,
           meta_tokens, w_in, b_in, ml_norm_g, rt_norm_g, w_out,
           ln1_g, ln1_b, w_gate, w_up, w_down, ln2_g, ln2_b):
    f = lambda a: np.ascontiguousarray(np.asarray(a, dtype=np.float32))
    if "nc" not in _NC_CACHE:
        _NC_CACHE["nc"] = Prog().build()
    nc = _NC_CACHE["nc"]
    cs = _consts()
    shared = dict(meta=f(meta_tokens), w_in=f(w_in)[0], b_in=f(b_in), ml_g=f(ml_norm_g), rt_g=f(rt_norm_g),
                  w_out=f(w_out)[0], ln1_g=f(ln1_g), ln1_b=f(ln1_b), w_gate=f(w_gate)[0], w_up=f(w_up)[0],
                  w_down=f(w_down)[0], ln2_g=f(ln2_g), ln2_b=f(ln2_b), **cs)
    xp, xs = f(x_prompt), f(x_sample)
    sC, sn, sm, sS = f(state_mlstm_C)[0], f(state_mlstm_n)[0], f(state_mlstm_m)[0], f(state_ret_S)[0]
    in_maps = []
    for c in range(NCORES):
        sl = slice(c * NS, (c + 1) * NS)
        m = dict(shared)
        m.update(xp=xp[c], xs=np.ascontiguousarray(xs[sl, 0, :]), sC=np.ascontiguousarray(sC[sl]),
                 sn=np.ascontiguousarray(sn[sl].reshape(NS, 1024)), sm=np.ascontiguousarray(sm[sl]),
                 sS=np.ascontiguousarray(sS[sl]))
        in_maps.append(m)
    res = run_bass_kernel_spmd(nc, in_maps, core_ids=list(range(NCORES)))
    R = res.results
    if DEBUG:
        _NC_CACHE["dbg"] = dict(mrg=R[0]["dbg_mrg"], x1=R[0]["dbg_x1"])
    cat = lambda k: np.concatenate([r[k] for r in R], axis=0)
    stk = lambda k: np.stack([r[k] for r in R], axis=0)
    y_prompt = stk("yp")
    y_sample = cat("ys").reshape(128, 1, D)
    p_C = stk("pC")[None]
    p_n = stk("pn")[None]
    p_m = stk("pm").reshape(1, NCORES, 4)
    p_S = stk("pS")[None]
    s_C = cat("oC")[None]
    s_n = cat("on").reshape(1, 128, 4, 256)
    s_m = cat("om")[None]
    s_S = cat("oS")[None]
    return (y_prompt, y_sample, p_C, p_n, p_m, p_S, s_C, s_n, s_m, s_S)
```
